# Optimizing a Trainium2 kernel written in Bass

```python
import math
import jax, jax.numpy as jnp
from jax import lax
import numpy as np

D_MODEL = 2048
BATCH = 2
SEQ = 4096
DEPTH = 1

PLE_DIM = 256
D_FF = ((8 * D_MODEL // 3 + 255) // 256) * 256
MIX_WIDTH = D_MODEL
GDN_HEAD_DIM = 128
GDN_WIDTH = MIX_WIDTH // 2
GDN_HEADS = GDN_WIDTH // GDN_HEAD_DIM
SSM_HEAD_DIM = 64
SSM_WIDTH = MIX_WIDTH - GDN_WIDTH
SSM_HEADS = SSM_WIDTH // SSM_HEAD_DIM
SSM_GROUPS = 2
SSM_STATE = 128
CONV_K = 4
CHUNK = 64
EPS = 1e-6
GDN_QKV_DIM = 3 * GDN_WIDTH
SSM_XBC_DIM = SSM_WIDTH + 2 * SSM_GROUPS * SSM_STATE
IN_SIZES = (GDN_QKV_DIM, GDN_WIDTH, GDN_HEADS, GDN_HEADS, SSM_WIDTH, SSM_XBC_DIM, SSM_HEADS)
IN_DIM = sum(IN_SIZES)
IN_SPLITS = tuple(int(s) for s in np.cumsum(IN_SIZES)[:-1])

kernel_name = 'hybrid_gdn_ssd_macaron_block'


def rmsnorm(x, w):
    xf = x.astype(jnp.float32)
    y = xf * lax.rsqrt(jnp.mean(xf * xf, axis=-1, keepdims=True) + EPS)
    return (y * w.astype(jnp.float32)).astype(x.dtype)


def l2norm(x):
    return x * lax.rsqrt(jnp.sum(x * x, axis=-1, keepdims=True) + EPS)


def swiglu(x, w_gate, w_up, w_down):
    return (jax.nn.silu(x @ w_gate) * (x @ w_up)) @ w_down


def causal_depthwise_conv(x, w):
    return lax.conv_general_dilated(
        x, w[:, None, :].astype(x.dtype), window_strides=(1,), padding=[(CONV_K - 1, 0)],
        dimension_numbers=('NWC', 'WIO', 'NWC'), feature_group_count=x.shape[-1])


def gated_delta_rule_chunked(q, k, v, g, beta):
    b, t, h, dk = q.shape
    dv = v.shape[-1]
    n = t // CHUNK

    def chunks(a):
        return jnp.swapaxes(a.reshape(b, n, CHUNK, h, *a.shape[3:]), 2, 3)

    q, k, v, g, beta = (chunks(a) for a in (q, k, v, g, beta))
    causal = jnp.tril(jnp.ones((CHUNK, CHUNK), dtype=bool))
    strict = jnp.tril(jnp.ones((CHUNK, CHUNK), dtype=bool), k=-1)
    gc = jnp.cumsum(g, axis=-1)
    decay = jnp.exp(jnp.where(causal, gc[..., :, None] - gc[..., None, :], -jnp.inf))
    kb = k * beta[..., None]
    lower = jnp.where(strict, jnp.einsum('bnhik,bnhjk->bnhij', kb, k) * decay, 0.0)
    eye = jnp.eye(CHUNK, dtype=q.dtype)
    t_inv = lax.linalg.triangular_solve(eye + lower, jnp.broadcast_to(eye, lower.shape),
                                        left_side=True, lower=True)
    u = jnp.einsum('bnhij,bnhjv->bnhiv', t_inv, v * beta[..., None])
    w = jnp.einsum('bnhij,bnhjk->bnhik', t_inv, kb * jnp.exp(gc)[..., None])
    qk = jnp.einsum('bnhik,bnhjk->bnhij', q, k) * decay
    q_dec = q * jnp.exp(gc)[..., None]
    k_dec = k * jnp.exp(gc[..., -1:] - gc)[..., None]
    g_tot = jnp.exp(gc[..., -1])

    def step(state, inp):
        u_c, w_c, qk_c, qd_c, kd_c, gt_c = inp
        v_new = u_c - jnp.einsum('bhck,bhkv->bhcv', w_c, state)
        o_c = (jnp.einsum('bhck,bhkv->bhcv', qd_c, state)
               + jnp.einsum('bhij,bhjv->bhiv', qk_c, v_new))
        state = state * gt_c[..., None, None] + jnp.einsum('bhck,bhcv->bhkv', kd_c, v_new)
        return state, o_c

    xs = tuple(jnp.moveaxis(a, 1, 0) for a in (u, w, qk, q_dec, k_dec, g_tot))
    s0 = jnp.zeros((b, h, dk, dv), q.dtype)
    _, o = lax.scan(step, s0, xs)
    return jnp.transpose(o, (1, 0, 3, 2, 4)).reshape(b, t, h, dv)


def ssd_chunked(x, dt, a_neg, bm, cm):
    b, t, nh, hp = x.shape
    ng, ns = bm.shape[2], bm.shape[3]
    r = nh // ng
    n = t // CHUNK
    xc = (x * dt[..., None]).reshape(b, n, CHUNK, ng, r, hp)
    la = jnp.moveaxis((dt * a_neg).reshape(b, n, CHUNK, ng, r), 2, -1)
    bc = bm.reshape(b, n, CHUNK, ng, ns)
    cc = cm.reshape(b, n, CHUNK, ng, ns)
    acs = jnp.cumsum(la, axis=-1)
    causal = jnp.tril(jnp.ones((CHUNK, CHUNK), dtype=bool))
    seg = jnp.exp(jnp.where(causal, acs[..., :, None] - acs[..., None, :], -jnp.inf))
    cb = jnp.einsum('bnlgd,bnsgd->bngls', cc, bc)
    y_diag = jnp.einsum('bngrls,bnsgrp->bnlgrp', cb[:, :, :, None] * seg, xc)
    decay_to_end = jnp.exp(acs[..., -1:] - acs)
    chunk_states = jnp.einsum('bnsgd,bngrs,bnsgrp->bngrpd', bc, decay_to_end, xc)
    chunk_decay = jnp.exp(acs[..., -1])

    def step(state, inp):
        cs, cd = inp
        return state * cd[..., None, None] + cs, state

    s0 = jnp.zeros((b, ng, r, hp, ns), x.dtype)
    _, prev = lax.scan(step, s0, (jnp.moveaxis(chunk_states, 1, 0), jnp.moveaxis(chunk_decay, 1, 0)))
    prev = jnp.moveaxis(prev, 0, 1)
    y_off = jnp.einsum('bnlgd,bngrpd,bngrl->bnlgrp', cc, prev, jnp.exp(acs))
    return (y_diag + y_off).reshape(b, t, nh, hp)


def gdn_mixer(qkv, gate, a, bl, conv_w, a_log, dt_bias, norm_w):
    b, t, _ = qkv.shape
    qkv = jax.nn.silu(causal_depthwise_conv(qkv, conv_w))
    q, k, v = jnp.split(qkv.astype(jnp.float32), 3, axis=-1)
    q = l2norm(q.reshape(b, t, GDN_HEADS, GDN_HEAD_DIM)) * (GDN_HEAD_DIM ** -0.5)
    k = l2norm(k.reshape(b, t, GDN_HEADS, GDN_HEAD_DIM))
    v = v.reshape(b, t, GDN_HEADS, GDN_HEAD_DIM)
    beta = jax.nn.sigmoid(bl.astype(jnp.float32))
    g = -jnp.exp(a_log.astype(jnp.float32)) * jax.nn.softplus(
        a.astype(jnp.float32) + dt_bias.astype(jnp.float32))
    o = gated_delta_rule_chunked(q, k, v, g, beta)
    o = rmsnorm(o, norm_w) * jax.nn.silu(
        gate.astype(jnp.float32).reshape(b, t, GDN_HEADS, GDN_HEAD_DIM))
    return o.reshape(b, t, GDN_WIDTH).astype(qkv.dtype)


def ssd_mixer(z, xbc, dt_raw, conv_w, conv_b, a_log, dt_bias, d_skip, norm_w):
    b, t, _ = xbc.shape
    xbc = jax.nn.silu(causal_depthwise_conv(xbc, conv_w) + conv_b.astype(xbc.dtype))
    xs, bm, cm = jnp.split(xbc.astype(jnp.float32),
                           [SSM_WIDTH, SSM_WIDTH + SSM_GROUPS * SSM_STATE], axis=-1)
    xs = xs.reshape(b, t, SSM_HEADS, SSM_HEAD_DIM)
    bm = bm.reshape(b, t, SSM_GROUPS, SSM_STATE)
    cm = cm.reshape(b, t, SSM_GROUPS, SSM_STATE)
    dt = jax.nn.softplus(dt_raw.astype(jnp.float32) + dt_bias.astype(jnp.float32))
    a_neg = -jnp.exp(a_log.astype(jnp.float32))
    y = ssd_chunked(xs, dt, a_neg, bm, cm) + xs * d_skip.astype(jnp.float32)[:, None]
    y = y.reshape(b, t, SSM_WIDTH) * jax.nn.silu(z.astype(jnp.float32))
    gsz = SSM_WIDTH // SSM_GROUPS
    y = rmsnorm(y.reshape(b, t, SSM_GROUPS, gsz), norm_w.reshape(SSM_GROUPS, gsz))
    return y.reshape(b, t, SSM_WIDTH).astype(xbc.dtype)


def setup_inputs(seed: int = 0) -> dict:
    key = jax.random.key(seed)
    ks = jax.random.split(key, 28)

    def normal(k, shape, scale):
        return jax.random.normal(k, shape, jnp.float32) * scale

    def gain(k, shape):
        return 1.0 + 0.02 * jax.random.normal(k, shape, jnp.float32)

    def dt_bias(k, shape):
        dt = jnp.exp(jax.random.uniform(k, shape, jnp.float32, math.log(1e-3), math.log(1e-1)))
        return dt + jnp.log(-jnp.expm1(-dt))

    def a_log(k, shape):
        return jnp.log(jax.random.uniform(k, shape, jnp.float32, 1.0, 16.0))

    def conv_w(k, shape):
        bound = CONV_K ** -0.5
        return jax.random.uniform(k, shape, jnp.float32, -bound, bound)

    L, D, F = DEPTH, D_MODEL, D_FF
    return {
        'x': normal(ks[0], (BATCH, SEQ, D), 1.0),
        'p': normal(ks[1], (L, BATCH, SEQ, PLE_DIM), 1.0),
        'ffn1_norm': gain(ks[2], (L, D)),
        'ffn1_w_gate': normal(ks[3], (L, D, F), D ** -0.5),
        'ffn1_w_up': normal(ks[4], (L, D, F), D ** -0.5),
        'ffn1_w_down': normal(ks[5], (L, F, D), F ** -0.5),
        'mix_norm': gain(ks[6], (L, D)),
        'w_in': normal(ks[7], (L, D, IN_DIM), D ** -0.5),
        'gdn_conv_w': conv_w(ks[8], (L, CONV_K, GDN_QKV_DIM)),
        'gdn_a_log': a_log(ks[9], (L, GDN_HEADS)),
        'gdn_dt_bias': dt_bias(ks[10], (L, GDN_HEADS)),
        'gdn_out_norm': gain(ks[11], (L, GDN_HEAD_DIM)),
        'ssm_conv_w': conv_w(ks[12], (L, CONV_K, SSM_XBC_DIM)),
        'ssm_conv_b': normal(ks[13], (L, SSM_XBC_DIM), 0.02),
        'ssm_a_log': a_log(ks[14], (L, SSM_HEADS)),
        'ssm_dt_bias': dt_bias(ks[15], (L, SSM_HEADS)),
        'ssm_d': gain(ks[16], (L, SSM_HEADS)),
        'ssm_out_norm': gain(ks[17], (L, SSM_WIDTH)),
        'w_out': normal(ks[18], (L, MIX_WIDTH, D), MIX_WIDTH ** -0.5),
        'ffn2_norm': gain(ks[19], (L, D)),
        'ffn2_w_gate': normal(ks[20], (L, D, F), D ** -0.5),
        'ffn2_w_up': normal(ks[21], (L, D, F), D ** -0.5),
        'ffn2_w_down': normal(ks[22], (L, F, D), F ** -0.5),
        'ple_norm': gain(ks[23], (L, D)),
        'ple_w_gate': normal(ks[24], (L, D, D), D ** -0.5),
        'ple_w_proj': normal(ks[25], (L, PLE_DIM, D), PLE_DIM ** -0.5),
        'ple_post_norm': gain(ks[26], (L, D)),
        'final_norm': gain(ks[27], (D,)),
    }


def reference(x, p, ffn1_norm, ffn1_w_gate, ffn1_w_up, ffn1_w_down, mix_norm, w_in,
              gdn_conv_w, gdn_a_log, gdn_dt_bias, gdn_out_norm,
              ssm_conv_w, ssm_conv_b, ssm_a_log, ssm_dt_bias, ssm_d, ssm_out_norm,
              w_out, ffn2_norm, ffn2_w_gate, ffn2_w_up, ffn2_w_down,
              ple_norm, ple_w_gate, ple_w_proj, ple_post_norm, final_norm):
    h = x
    for i in range(DEPTH):
        h = h + 0.5 * swiglu(rmsnorm(h, ffn1_norm[i]), ffn1_w_gate[i], ffn1_w_up[i], ffn1_w_down[i])
        proj = rmsnorm(h, mix_norm[i]) @ w_in[i]
        g_qkv, g_gate, g_a, g_b, s_z, s_xbc, s_dt = jnp.split(proj, IN_SPLITS, axis=-1)
        o_gdn = gdn_mixer(g_qkv, g_gate, g_a, g_b, gdn_conv_w[i], gdn_a_log[i],
                          gdn_dt_bias[i], gdn_out_norm[i])
        o_ssm = ssd_mixer(s_z, s_xbc, s_dt, ssm_conv_w[i], ssm_conv_b[i], ssm_a_log[i],
                          ssm_dt_bias[i], ssm_d[i], ssm_out_norm[i])
        h = h + jnp.concatenate([o_gdn, o_ssm], axis=-1) @ w_out[i]
        h = h + 0.5 * swiglu(rmsnorm(h, ffn2_norm[i]), ffn2_w_gate[i], ffn2_w_up[i], ffn2_w_down[i])
        gate = jax.nn.sigmoid(rmsnorm(h, ple_norm[i]) @ ple_w_gate[i])
        h = h + gate * rmsnorm(p[i] @ ple_w_proj[i], ple_post_norm[i])
    return rmsnorm(h, final_norm)
```

```python
import contextlib
import numpy as np
import concourse.bass as bass
import concourse.mybir as mybir
from concourse.bass_utils import run_bass_kernel_spmd

F32 = mybir.dt.float32
BF16 = mybir.dt.bfloat16
AF = mybir.ActivationFunctionType
ALU = mybir.AluOpType

D = 2048
T = 1024
NKC = 16
DFF = 5632
NFC = 44
PLE = 256
IN_DIM = 6688
EPS = 1e-6
NSLOT = 4
ENGS = ("pe", "act", "dve", "pool", "sp")


class Op:
    __slots__ = ("eng", "fn", "deps", "raw", "dma", "signal", "sigval", "sem", "idx", "ring_prev")

    def __init__(self, eng, fn, dma):
        self.eng = eng
        self.fn = fn
        self.dma = dma
        self.deps = set()
        self.raw = set()
        self.signal = False
        self.sigval = 0
        self.sem = None
        self.ring_prev = None


class Prog:
    def __init__(self, nc, strict=True, ring=8):
        self.nc = nc
        self.ops = []
        self.last_writer = {}
        self.readers = {}
        self.strict = strict
        self.ring = ring
        self.dma_ops = {"sp": [], "pool": []}
        self.phase_tok = None

    def op(self, eng, fn, reads=(), writes=(), dma=False):
        o = Op(eng, fn, dma)
        o.idx = len(self.ops)
        reads = list(reads)
        if self.phase_tok is not None:
            reads.append(self.phase_tok)
        for t in reads:
            w = self.last_writer.get(t)
            if w is not None:
                o.deps.add(w)
                o.raw.add(w)
        for t in writes:
            w = self.last_writer.get(t)
            if w is not None:
                o.deps.add(w)
            rd = self.readers.get(t)
            if rd:
                for lst in rd.values():
                    o.deps.update(lst)
        key = eng + ("_d" if dma else "")
        for t in reads:
            rd = self.readers.setdefault(t, {})
            lst = rd.setdefault(key, [])
            lst.append(o.idx)
            if dma:
                if len(lst) > self.ring:
                    del lst[0]
            elif len(lst) > 1:
                del lst[0]
        for t in writes:
            self.last_writer[t] = o.idx
            self.readers[t] = {}
        o.deps.discard(o.idx)
        if dma:
            lst = self.dma_ops[eng]
            k = len(lst)
            if k >= self.ring:
                o.ring_prev = lst[k - self.ring].idx
            lst.append(o)
        self.ops.append(o)
        return o

    def pe(self, fn, reads=(), writes=()):
        return self.op("pe", fn, reads, writes)

    def act(self, fn, reads=(), writes=()):
        return self.op("act", fn, reads, writes)

    def dve(self, fn, reads=(), writes=()):
        return self.op("dve", fn, reads, writes)

    def pool(self, fn, reads=(), writes=()):
        return self.op("pool", fn, reads, writes)

    def dma(self, q, fn, reads=(), writes=()):
        return self.op(q, fn, reads, writes, dma=True)

    def emit(self, final_wait_ops=()):
        nc = self.nc
        ops = self.ops
        for o in ops:
            for d in o.deps:
                od = ops[d]
                if od.dma:
                    continue
                if od.eng == o.eng and not o.dma:
                    if o.eng == "pe" or not self.strict or d not in o.raw:
                        continue
                od.signal = True
        with contextlib.ExitStack() as st:
            sems = {e: st.enter_context(nc.semaphore("s_" + e)) for e in ("pe", "act", "dve", "pool")}
            rings = {q: [st.enter_context(nc.semaphore("r_%s%d" % (q, i))) for i in range(self.ring)]
                     for q in ("sp", "pool") if self.dma_ops[q]}
            cnt = {e: 0 for e in sems}
            for o in ops:
                if not o.dma and o.signal:
                    cnt[o.eng] += 1
                    o.sigval = cnt[o.eng]
                    o.sem = sems[o.eng]
            for q, lst in self.dma_ops.items():
                for k, o in enumerate(lst):
                    o.sem = rings[q][k % self.ring]
                    o.sigval = 16 * (k // self.ring + 1)
            block = st.enter_context(nc.Block())
            engmap = {"pe": block.tensor, "act": block.scalar, "dve": block.vector,
                      "pool": block.gpsimd, "sp": block.sync}
            for e in ENGS:
                eops = [o for o in ops if o.eng == e]
                extra = list(final_wait_ops) if e == "sp" else []
                if not eops and not extra:
                    continue

                def body(eng, e=e, eops=eops, extra=extra):
                    waited = {}

                    def wait(od):
                        key = id(od.sem)
                        if waited.get(key, 0) >= od.sigval:
                            return
                        eng.wait_ge(od.sem, od.sigval)
                        waited[key] = od.sigval

                    for o in eops:
                        for d in sorted(o.deps):
                            od = ops[d]
                            if not od.dma and od.eng == e and not o.dma:
                                if e == "pe" or not self.strict or d not in o.raw:
                                    continue
                            wait(od)
                        if o.ring_prev is not None:
                            wait(ops[o.ring_prev])
                        ins = o.fn(eng)
                        if o.dma:
                            ins.then_inc(o.sem, 16)
                        elif o.signal:
                            ins.then_inc(o.sem, 1)
                    for o in extra:
                        wait(o)

                engmap[e](body)


SM = {}
_off = 0
for _n, _w in (("nw", 96), ("gcw", 96), ("scw", 48), ("scb", 12), ("gon", 1), ("son", 8), ("sd", 8),
               ("galog", 8), ("gdtb", 8), ("salog", 16), ("sdtb", 16)):
    SM[_n] = (_off, _w)
    _off += _w
NSM = _off


def build_nc(nslot=NSLOT, debug=None):
    nc = bass.Bass("TRN2", target_bir_lowering=False)
    dt = nc.dram_tensor
    xs = dt("xs", [nslot, D, T], F32, kind="ExternalInput").ap()
    pin = dt("pin", [PLE, T], F32, kind="ExternalInput").ap()
    cmask = dt("cmask", [128, 4], F32, kind="ExternalInput").ap()
    smalls = dt("smalls", [128, NSM], F32, kind="ExternalInput").ap()
    consts = dt("consts", [128, 5, 128], F32, kind="ExternalInput").ap()
    W = {}
    for nm, shp in (("ffn1_w_gate", [D, DFF]), ("ffn1_w_up", [D, DFF]), ("ffn1_w_down", [DFF, D]),
                    ("w_in", [D, IN_DIM]), ("w_out", [D, D]),
                    ("ffn2_w_gate", [D, DFF]), ("ffn2_w_up", [D, DFF]), ("ffn2_w_down", [DFF, D]),
                    ("ple_w_gate", [D, D]), ("ple_w_proj", [PLE, D])):
        W[nm] = dt(nm, shp, F32, kind="ExternalInput").ap()
    out = dt("out", [D, T], F32, kind="ExternalOutput").ap()
    QT = dt("QT", [8, 128, T], BF16, kind="Internal").ap()
    KT = dt("KT", [8, 128, T], BF16, kind="Internal").ap()
    GT = dt("GT", [8, 128, T], F32, kind="Internal").ap()
    ZT = dt("ZT", [8, 128, T], F32, kind="Internal").ap()
    XT = dt("XT", [8, 128, T], F32, kind="Internal").ap()
    BT = dt("BT", [2, 128, T], F32, kind="Internal").ap()
    CT = dt("CT", [2, 128, T], F32, kind="Internal").ap()
    KTM = dt("KTM", [T, 1024], BF16, kind="Internal").ap()
    VTM = dt("VTM", [T, 1024], BF16, kind="Internal").ap()
    XTM = dt("XTM", [T, 1024], F32, kind="Internal").ap()
    BTM = dt("BTM", [T, 256], F32, kind="Internal").ap()

    S = nc.alloc_sbuf_tensor
    h = S("h", [128, NKC, T], F32)
    xn = S("xn", [128, NKC, T], BF16)
    cst = S("cst", [128, 5, 128], F32)
    ones_bf = S("ones_bf", [128, 128], BF16)
    sm = S("sm", [128, NSM], F32)
    cm = S("cm", [128, 4], F32)
    negA_g = S("negA_g", [128, 8], F32)
    negA_s = S("negA_s", [128, 16], F32)
    rstd = S("rstd", [128, T], F32)
    sqb = [S("sqb%d" % i, [128, T], BF16) for i in range(2)]
    halo = S("halo", [128, 36, 3], F32)
    gts = S("gts", [128, 8, 32], F32)
    Gg = S("Gg", [128, 8, 8], F32)
    Bg = S("Bg", [128, 8, 8], F32)
    Dts = S("Dts", [128, 8, 16], F32)
    Las = S("Las", [128, 8, 16], F32)
    Sg = S("Sg", [128, 8, 128], F32)
    Ss = S("Ss", [128, 1024], F32)
    Sgb = S("Sgb", [128, 8, 128], BF16)
    scr = S("scr", [128, 8], F32)
    ARENA_F = 22100
    arena = S("arena", [128, ARENA_F], F32)
    banks = [nc.alloc_psum_tensor("bank%d" % i, [128, 512], F32) for i in range(8)]

    ident = cst[:, 0, :]
    tri = cst[:, 1, :]
    sgt = cst[:, 2, :]
    slt = cst[:, 3, :]
    ones = cst[:, 4, :]

    pg = Prog(nc)

    class Arena:
        def __init__(self):
            self.off = 0
            self.gen = 0

        def reset(self):
            self.off = 0
            self.gen += 1
            old = pg.phase_tok
            pg.phase_tok = None
            tok = ("phase", self.gen)
            pg.dve(lambda e: e.memset(scr[:, 0:1], 0.0), reads=[old] if old else [], writes=[tok, old] if old else [tok])
            pg.phase_tok = tok

        def alloc(self, shape, dtype=F32):
            n = 1
            for s_ in shape[1:]:
                n *= s_
            nf = n if dtype == F32 else (n + 1) // 2
            nf = (nf + 7) // 8 * 8
            assert self.off + nf <= ARENA_F, (self.off, nf)
            ap = arena[:, self.off:self.off + nf]
            self.off += nf
            if dtype != F32:
                ap = ap.bitcast(dtype)
            ap = ap[:, 0:n]
            if len(shape) == 3:
                ap = ap.rearrange("p (a b) -> p a b", a=shape[1])
            elif len(shape) == 4:
                ap = ap.rearrange("p (a b c) -> p a b c", a=shape[1], b=shape[2])
            if shape[0] < 128:
                ap = ap[0:shape[0]]
            return ap

    ar = Arena()
    uid = [0]

    def U(name):
        uid[0] += 1
        return (name, uid[0])

    bank_rr = [0]

    reserved = set()

    def nb():
        while True:
            i = bank_rr[0] % 8
            bank_rr[0] += 1
            if i not in reserved:
                return i

    def bk(i):
        return ("bank", i)

    alt = [0]

    def evac(fn, reads=(), writes=()):
        alt[0] ^= 1
        if alt[0]:
            return pg.act(lambda e: fn(e, True), reads, writes)
        return pg.dve(lambda e: fn(e, False), reads, writes)

    def copy_any(e, is_act, out, in_):
        if is_act:
            return e.copy(out=out, in_=in_)
        return e.tensor_copy(out=out, in_=in_)

    def smc(name, lo=0, hi=None):
        o, w = SM[name]
        hi = w if hi is None else hi
        return sm[:, o + lo:o + hi]

    pg.dma("sp", lambda e: e.dma_start(out=cst[:], in_=consts), writes=["cst"])
    pg.dma("sp", lambda e: e.dma_start(out=sm[:], in_=smalls), writes=["sm"])
    pg.dma("sp", lambda e: e.dma_start(out=cm[:], in_=cmask), writes=["cm"])
    pg.dve(lambda e: e.tensor_copy(out=ones_bf[:], in_=ones), reads=["cst"], writes=["ones_bf"])
    pg.act(lambda e: e.activation(out=negA_g[:], in_=smc("galog"), func=AF.Exp), reads=["sm"], writes=["negA_g0"])
    pg.dve(lambda e: e.tensor_single_scalar(out=negA_g[:], in_=negA_g[:], scalar=-1.0, op=ALU.mult), reads=["negA_g0"], writes=["negA_g"])
    pg.act(lambda e: e.activation(out=negA_s[:], in_=smc("salog"), func=AF.Exp), reads=["sm"], writes=["negA_s0"])
    pg.dve(lambda e: e.tensor_single_scalar(out=negA_s[:], in_=negA_s[:], scalar=-1.0, op=ALU.mult), reads=["negA_s0"], writes=["negA_s"])
    pg.dve(lambda e: e.memset(halo[:], 0.0), writes=[("halo", i) for i in range(36)])
    pg.dve(lambda e: e.memset(Sg[:], 0.0), writes=[("Sg", 0), ("Sg", 1)])
    pg.dve(lambda e: e.memset(Ss[:], 0.0), writes=[("Ss", 0), ("Ss", 1)])
    pg.dve(lambda e: e.memset(Sgb[:], 0.0), writes=[("Sgb", 0), ("Sgb", 1)])

    def run_all(gens):
        gens = list(gens)
        while gens:
            for g_ in list(gens):
                try:
                    next(g_)
                except StopIteration:
                    gens.remove(g_)

    def load_x_gen(s, reset=True, act_only=False):
        for cg in range(4):
            pg.dma("sp", lambda e, cg=cg: e.dma_start(out=h[:, cg * 4:(cg + 1) * 4, :], in_=xs[s, cg * 512:(cg + 1) * 512, :].rearrange("(c p) t -> p c t", p=128)),
                   writes=[("h", cg * 4 + j, tt) for j in range(4) for tt in range(2)])
            yield

    def rmsnorm_to_xn(widx, src=None, src_tok="h", dst=None, dst_tok="xn", dst_is_xn=True):
        src = h if src is None else src
        ssb = [nb(), nb()]
        for c in range(NKC):
            b = c % 2
            pg.act(lambda e, c=c, b=b: e.activation(out=sqb[b][:], in_=src[:, c, :], func=AF.Square),
                   reads=[(src_tok, c, 0), (src_tok, c, 1)], writes=[("sqb", b)])
            for tt in range(2):
                pg.pe(lambda e, c=c, b=b, tt=tt: e.matmul(banks[ssb[tt]][:], lhsT=ones_bf[:], rhs=sqb[b][:, tt * 512:(tt + 1) * 512], start=(c == 0), stop=(c == NKC - 1)),
                      reads=[("sqb", b), "ones_bf"], writes=[bk(ssb[tt])])
        for tt in range(2):
            pg.act(lambda e, tt=tt: e.activation(out=rstd[:, tt * 512:(tt + 1) * 512], in_=banks[ssb[tt]][:], func=AF.Ln, bias=EPS, scale=1.0 / D),
                   reads=[bk(ssb[tt])], writes=[("rstd0", tt), ("rstd", tt)])
            pg.act(lambda e, tt=tt: e.activation(out=rstd[:, tt * 512:(tt + 1) * 512], in_=rstd[:, tt * 512:(tt + 1) * 512], func=AF.Exp, scale=-0.5),
                   reads=[("rstd0", tt)], writes=[("rstd", tt)])
        if dst is None and not dst_is_xn:
            return
        dstt = xn if dst is None else dst
        for c in range(NKC):
            pg.dve(lambda e, c=c: e.scalar_tensor_tensor(out=dstt[:, c, :], in0=src[:, c, :], scalar=sm[:, widx * 16 + c:widx * 16 + c + 1], in1=rstd[:], op0=ALU.mult, op1=ALU.mult),
                   reads=[(src_tok, c, 0), (src_tok, c, 1), ("rstd", 0), ("rstd", 1), "sm"], writes=[(dst_tok, c)])

    def ffn(wg, wu, wd, widx):
        ar.reset()
        wgu = [ar.alloc([128, NKC, 256], BF16) for _ in range(4)]
        wdb = [ar.alloc([128, 4, D], BF16) for _ in range(2)]
        hid = [ar.alloc([128, 4, T], BF16) for _ in range(2)]
        sil = [ar.alloc([128, 512]) for _ in range(2)]
        gub = [(nb(), nb()), (nb(), nb())]
        dnb = [nb(), nb()]
        st = {"gu": 0, "dn": 0, "ws": 0}
        pre = {}

        def issue_w(g, pr):
            fo = (g * 4 + pr * 2) * 128
            sg_, su_ = st["ws"] % 4, (st["ws"] + 1) % 4
            st["ws"] += 2
            pg.dma("pool", lambda e, sg_=sg_, fo=fo: e.dma_start(out=wgu[sg_], in_=wg[:, fo:fo + 256].rearrange("(kc p) f -> p kc f", p=128)), writes=[("wgu", sg_)])
            pg.dma("pool", lambda e, su_=su_, fo=fo: e.dma_start(out=wgu[su_], in_=wu[:, fo:fo + 256].rearrange("(kc p) f -> p kc f", p=128)), writes=[("wgu", su_)])
            return sg_, su_

        pre[(0, 0)] = issue_w(0, 0)
        pre[(0, 1)] = issue_w(0, 1)
        rmsnorm_to_xn(widx)

        def gate_up(g):
            hb = g % 2
            for pr in range(2):
                fo = (g * 4 + pr * 2) * 128
                if (g, pr) in pre:
                    sg_, su_ = pre.pop((g, pr))
                else:
                    sg_, su_ = issue_w(g, pr)
                for fl in range(2):
                    fc = pr * 2 + fl
                    for tt in range(2):
                        gb, ub = gub[st["gu"] % 2]
                        st["gu"] += 1
                        for kc in range(NKC):
                            pg.pe(lambda e, gb=gb, sg_=sg_, kc=kc, fl=fl, tt=tt: e.matmul(banks[gb][:], lhsT=wgu[sg_][:, kc, fl * 128:(fl + 1) * 128], rhs=xn[:, kc, tt * 512:(tt + 1) * 512], start=(kc == 0), stop=(kc == NKC - 1)),
                                  reads=[("wgu", sg_), ("xn", kc)], writes=[bk(gb)])
                        for kc in range(NKC):
                            pg.pe(lambda e, ub=ub, su_=su_, kc=kc, fl=fl, tt=tt: e.matmul(banks[ub][:], lhsT=wgu[su_][:, kc, fl * 128:(fl + 1) * 128], rhs=xn[:, kc, tt * 512:(tt + 1) * 512], start=(kc == 0), stop=(kc == NKC - 1)),
                                  reads=[("wgu", su_), ("xn", kc)], writes=[bk(ub)])
                        sb_ = tt
                        pg.act(lambda e, gb=gb, sb_=sb_: e.activation(out=sil[sb_], in_=banks[gb][:], func=AF.Silu), reads=[bk(gb)], writes=[("sil", sb_)])
                        pg.dve(lambda e, ub=ub, sb_=sb_, hb=hb, fc=fc, tt=tt: e.tensor_tensor(out=hid[hb][:, fc, tt * 512:(tt + 1) * 512], in0=sil[sb_], in1=banks[ub][:], op=ALU.mult),
                               reads=[("sil", sb_), bk(ub)], writes=[("hid", hb, fc, tt)])

        def down(g):
            hb = g % 2
            pg.dma("pool", lambda e, hb=hb, g=g: e.dma_start(out=wdb[hb], in_=wd[g * 512:(g + 1) * 512, :].rearrange("(fc p) o -> p fc o", p=128)), writes=[("wdb", hb)])
            for oc in range(NKC):
                for tt in range(2):
                    db = dnb[st["dn"] % 2]
                    st["dn"] += 1
                    for fc in range(4):
                        pg.pe(lambda e, db=db, hb=hb, fc=fc, oc=oc, tt=tt: e.matmul(banks[db][:], lhsT=wdb[hb][:, fc, oc * 128:(oc + 1) * 128], rhs=hid[hb][:, fc, tt * 512:(tt + 1) * 512], start=(fc == 0), stop=(fc == 3)),
                              reads=[("wdb", hb), ("hid", hb, fc, tt)], writes=[bk(db)])
                    pg.dve(lambda e, db=db, oc=oc, tt=tt: e.scalar_tensor_tensor(out=h[:, oc, tt * 512:(tt + 1) * 512], in0=banks[db][:], scalar=0.5, in1=h[:, oc, tt * 512:(tt + 1) * 512], op0=ALU.mult, op1=ALU.add),
                           reads=[bk(db), ("h", oc, tt)], writes=[("h", oc, tt)])

        for g in range(11):
            gate_up(g)
            if g > 0:
                down(g - 1)
        down(10)

    def proj_phase(s, own):
        ar.reset()
        rmsnorm_to_xn(1)
        wsl = [ar.alloc([128, NKC, 128], BF16) for _ in range(3)]
        wgt = ar.alloc([128, NKC, 32], BF16)
        pc = [ar.alloc([128, T + 3]) for _ in range(2)]
        cs = [ar.alloc([128, T]) for _ in range(2)]
        cn = [ar.alloc([128, T]) for _ in range(2)]
        rs = ar.alloc([128, T])
        tms = [ar.alloc([128, 4, 128]) for _ in range(2)]
        tmsb = [ar.alloc([128, 4, 128], BF16) for _ in range(2)]
        cnb = [ar.alloc([128, T], BF16) for _ in range(2)]
        win = W["w_in"]
        pg.dma("pool", lambda e: e.dma_start(out=wgt[:, :, 0:16], in_=win[:, 4096:4112].rearrange("(kc p) f -> p kc f", p=128)), writes=["wgt_a"])
        pg.dma("pool", lambda e: e.dma_start(out=wgt[:, :, 16:32], in_=win[:, 6672:6688].rearrange("(kc p) f -> p kc f", p=128)), writes=["wgt_b"])
        gbk = nb()
        for tk in range(8):
            for kc in range(NKC):
                pg.pe(lambda e, tk=tk, kc=kc: e.matmul(banks[gbk][:, tk * 32:(tk + 1) * 32], lhsT=xn[:, kc, tk * 128:(tk + 1) * 128], rhs=wgt[:, kc, :], start=(kc == 0), stop=(kc == NKC - 1)),
                      reads=["wgt_a", "wgt_b", ("xn", kc)], writes=[bk(gbk)])
        pg.act(lambda e: e.copy(out=gts[:], in_=banks[gbk][:, 0:256].rearrange("p (a b) -> p a b", a=8)), reads=[bk(gbk)], writes=["gts"])
        pg.dve(lambda e: e.tensor_tensor(out=Gg[:], in0=gts[:, :, 0:8], in1=smc("gdtb").unsqueeze(1).broadcast_to([128, 8, 8]), op=ALU.add), reads=["gts", "sm"], writes=["Gg0", "Gg"])
        pg.act(lambda e: e.activation(out=Gg[:], in_=Gg[:], func=AF.Exp), reads=["Gg0"], writes=["Gg1"])
        pg.act(lambda e: e.activation(out=Gg[:], in_=Gg[:], func=AF.Ln, bias=1.0, scale=1.0), reads=["Gg1"], writes=["Gg2"])
        pg.dve(lambda e: e.tensor_tensor(out=Gg[:], in0=Gg[:], in1=negA_g[:].unsqueeze(1).broadcast_to([128, 8, 8]), op=ALU.mult), reads=["Gg2", "negA_g"], writes=["Gg"])
        pg.act(lambda e: e.activation(out=Bg[:], in_=gts[:, :, 8:16], func=AF.Sigmoid), reads=["gts"], writes=["Bg"])
        pg.dve(lambda e: e.tensor_tensor(out=Dts[:], in0=gts[:, :, 16:32], in1=smc("sdtb").unsqueeze(1).broadcast_to([128, 8, 16]), op=ALU.add), reads=["gts", "sm"], writes=["Dts0", "Dts"])
        pg.act(lambda e: e.activation(out=Dts[:], in_=Dts[:], func=AF.Exp), reads=["Dts0"], writes=["Dts1"])
        pg.act(lambda e: e.activation(out=Dts[:], in_=Dts[:], func=AF.Ln, bias=1.0, scale=1.0), reads=["Dts1"], writes=["Dts2"])
        pg.dve(lambda e: e.tensor_single_scalar(out=Dts[:], in_=Dts[:], scalar=cm[:, s:s + 1], op=ALU.mult), reads=["Dts2", "cm"], writes=["Dts"])
        pg.dve(lambda e: e.tensor_tensor(out=Las[:], in0=Dts[:], in1=negA_s[:].unsqueeze(1).broadcast_to([128, 8, 16]), op=ALU.mult), reads=["Dts", "negA_s"], writes=["Las"])

        chunks = []
        for c in range(8):
            chunks.append(("q", c, c * 128, c, own))
        for c in range(8):
            chunks.append(("k", c, 1024 + c * 128, 8 + c, True))
        for c in range(8):
            chunks.append(("v", c, 2048 + c * 128, 16 + c, True))
        for c in range(8):
            chunks.append(("gate", c, 3072 + c * 128, None, own))
        for c in range(8):
            chunks.append(("z", c, 4112 + c * 128, None, own))
        for c in range(8):
            chunks.append(("x", c, 5136 + c * 128, 24 + c, True))
        for c in range(2):
            chunks.append(("B", c, 6160 + c * 128, 32 + c, True))
        for c in range(2):
            chunks.append(("C", c, 6416 + c * 128, 34 + c, own))
        items = [ch for ch in chunks if ch[4]]
        if s == nslot - 2:
            hb_ = nb()
            hn_ = 0
            for kind, c, col, ci, _ in chunks:
                if kind not in ("q", "C"):
                    continue
                sl = hn_ % 3
                pg.dma("pool", lambda e, sl=sl, col=col: e.dma_start(out=wsl[sl], in_=win[:, col:col + 128].rearrange("(kc p) f -> p kc f", p=128)), writes=[("wsl", sl)])
                for kc in range(NKC):
                    pg.pe(lambda e, kc=kc, sl=sl, hn_=hn_: e.matmul(banks[hb_][:, hn_ * 4:hn_ * 4 + 3], lhsT=wsl[sl][:, kc, :], rhs=xn[:, kc, T - 3:T], start=(kc == 0), stop=(kc == NKC - 1)),
                          reads=[("wsl", sl), ("xn", kc)], writes=[bk(hb_)])
                pg.act(lambda e, ci=ci, hn_=hn_: e.copy(out=halo[:, ci, :], in_=banks[hb_][:, hn_ * 4:hn_ * 4 + 3]), reads=[bk(hb_)], writes=[("halo", ci)])
                hn_ += 1
        N_ = len(items)
        gz = [ar.alloc([128, T]) for _ in range(2)]
        mmb = [(0, 1), (2, 3)]
        l2b = (4, 5)
        trb = (6, 7)

        def A1(n):
            kind, c, col, ci, _ = items[n]
            sl = n % 3
            pg.dma("pool", lambda e, sl=sl, col=col: e.dma_start(out=wsl[sl], in_=win[:, col:col + 128].rearrange("(kc p) f -> p kc f", p=128)), writes=[("wsl", sl)])
            pb = mmb[n % 2]
            for tt in range(2):
                for kc in range(NKC):
                    pg.pe(lambda e, tt=tt, kc=kc, sl=sl, pb=pb: e.matmul(banks[pb[tt]][:], lhsT=wsl[sl][:, kc, :], rhs=xn[:, kc, tt * 512:(tt + 1) * 512], start=(kc == 0), stop=(kc == NKC - 1)),
                          reads=[("wsl", sl), ("xn", kc)], writes=[bk(pb[tt])])

        def A2(n):
            kind, c, col, ci, _ = items[n]
            b2 = n % 2
            pb = mmb[n % 2]
            if ci is None:
                for tt in range(2):
                    pg.act(lambda e, tt=tt, b2=b2, pb=pb: e.activation(out=gz[b2][:, tt * 512:(tt + 1) * 512], in_=banks[pb[tt]][:], func=AF.Silu), reads=[bk(pb[tt])], writes=[("gz", b2, tt)])
                dst = GT if kind == "gate" else ZT
                pg.dma("sp", lambda e, dst=dst, c=c, b2=b2: e.dma_start(out=dst[c], in_=gz[b2]), reads=[("gz", b2, 0), ("gz", b2, 1)], writes=[(kind + "T", c)])
                return
            pg.dve(lambda e, b2=b2, ci=ci: e.tensor_copy(out=pc[b2][:, 0:3], in_=halo[:, ci, :]), reads=[("halo", ci)], writes=[("pc", b2, "h")])
            for tt in range(2):
                pg.act(lambda e, tt=tt, b2=b2, pb=pb: e.copy(out=pc[b2][:, 3 + tt * 512:3 + (tt + 1) * 512], in_=banks[pb[tt]][:]), reads=[bk(pb[tt])], writes=[("pc", b2, tt)])
            pg.dve(lambda e, b2=b2, ci=ci: e.tensor_copy(out=halo[:, ci, :], in_=pc[b2][:, T:T + 3]), reads=[("pc", b2, 1)], writes=[("halo", ci)])

        srcs = {}

        def B1(n):
            kind, c, col, ci, _ = items[n]
            if ci is None:
                return
            b2 = n % 2
            cwo = SM["gcw"][0] if ci < 24 else SM["scw"][0]
            cj = ci if ci < 24 else ci - 24
            pcr = [("pc", b2, "h"), ("pc", b2, 0), ("pc", b2, 1), "sm"]
            pg.dve(lambda e, b2=b2, cwo=cwo, cj=cj: e.tensor_single_scalar(out=cn[b2], in_=pc[b2][:, 0:T], scalar=sm[:, cwo + cj * 4:cwo + cj * 4 + 1], op=ALU.mult), reads=pcr, writes=[("cn", b2)])
            for j in range(1, 4):
                pg.dve(lambda e, b2=b2, cwo=cwo, cj=cj, j=j: e.scalar_tensor_tensor(out=cn[b2], in0=pc[b2][:, j:T + j], scalar=sm[:, cwo + cj * 4 + j:cwo + cj * 4 + j + 1], in1=cn[b2], op0=ALU.mult, op1=ALU.add),
                       reads=pcr + [("cn", b2)], writes=[("cn", b2)])
            if ci < 24:
                pg.act(lambda e, b2=b2: e.activation(out=cs[b2], in_=cn[b2], func=AF.Silu), reads=[("cn", b2)], writes=[("cs", b2, 0), ("cs", b2, 1)])
            else:
                pg.act(lambda e, b2=b2, cj=cj: e.activation(out=cs[b2], in_=cn[b2], func=AF.Silu, bias=sm[:, SM["scb"][0] + cj:SM["scb"][0] + cj + 1], scale=1.0), reads=[("cn", b2), "sm"], writes=[("cs", b2, 0), ("cs", b2, 1)])
            csr = [("cs", b2, 0), ("cs", b2, 1)]
            src = cs[b2]
            if kind in ("q", "k"):
                pg.dve(lambda e, b2=b2: e.tensor_tensor(out=cn[b2], in0=cs[b2], in1=cs[b2], op=ALU.mult), reads=csr + [("cn", b2)], writes=[("cn", b2)])
                lb = l2b
                for tt in range(2):
                    pg.pe(lambda e, tt=tt, b2=b2, lb=lb: e.matmul(banks[lb[tt]][:], lhsT=ones, rhs=cn[b2][:, tt * 512:(tt + 1) * 512], start=True, stop=True), reads=[("cn", b2), "cst"], writes=[bk(lb[tt])])
                for tt in range(2):
                    pg.act(lambda e, tt=tt, lb=lb: e.activation(out=rs[:, tt * 512:(tt + 1) * 512], in_=banks[lb[tt]][:], func=AF.Ln, bias=EPS, scale=1.0), reads=[bk(lb[tt])], writes=[("rs0", tt), ("rs", tt)])
                for tt in range(2):
                    pg.act(lambda e, tt=tt: e.activation(out=rs[:, tt * 512:(tt + 1) * 512], in_=rs[:, tt * 512:(tt + 1) * 512], func=AF.Exp, scale=-0.5), reads=[("rs0", tt)], writes=[("rs", tt)])
                sc = (128.0 ** -0.5) if kind == "q" else 1.0
                pg.dve(lambda e, b2=b2, sc=sc: e.scalar_tensor_tensor(out=cn[b2], in0=cs[b2], scalar=sc, in1=rs, op0=ALU.mult, op1=ALU.mult), reads=csr + [("rs", 0), ("rs", 1), ("cn", b2)], writes=[("cn", b2)])
                src = cn[b2]
                csr = [("cn", b2)]
            srcs[n] = (src, csr)
            fm = {"q": QT, "k": KT, "x": XT, "B": BT, "C": CT}.get(kind)
            if fm is not None and (own or kind == "k"):
                if kind in ("q", "k"):
                    pg.act(lambda e, b2=b2, src=src: e.copy(out=cnb[b2], in_=src), reads=csr, writes=[("cnb", b2)])
                    pg.dma("sp", lambda e, fm=fm, c=c, b2=b2: e.dma_start(out=fm[c], in_=cnb[b2]), reads=[("cnb", b2)], writes=[(kind + "T", c)])
                else:
                    pg.dma("sp", lambda e, fm=fm, c=c, src=src: e.dma_start(out=fm[c], in_=src), reads=csr, writes=[(kind + "T", c)])

        def B2(n):
            kind, c, col, ci, _ = items[n]
            if ci is None:
                return
            src, csr = srcs[n]
            tm = {"k": KTM, "v": VTM, "x": XTM, "B": BTM}.get(kind)
            if tm is not None:
                for tg in range(2):
                    bi = trb[tg]
                    tb = tg
                    for j in range(4):
                        tk = tg * 4 + j
                        pg.pe(lambda e, bi=bi, j=j, tk=tk, src=src: e.transpose(out=banks[bi][:, j * 128:(j + 1) * 128], in_=src[:, tk * 128:(tk + 1) * 128], identity=ident), reads=csr + ["cst"], writes=[bk(bi)])
                    tmx = tmsb if kind in ("k", "v") else tms
                    tmt = "tmsb" if kind in ("k", "v") else "tms"
                    evac(lambda e, a, bi=bi, tb=tb, tmx=tmx: copy_any(e, a, tmx[tb], banks[bi][:].rearrange("p (a b) -> p a b", a=4)), reads=[bk(bi)], writes=[(tmt, tb)])
                    pg.dma("sp", lambda e, tm=tm, tg=tg, tb=tb, c=c, tmx=tmx: e.dma_start(out=tm[tg * 512:(tg + 1) * 512, c * 128:(c + 1) * 128].rearrange("(a p) f -> p a f", p=128), in_=tmx[tb]),
                           reads=[(tmt, tb)], writes=[(kind + "TM", c, tg)])

        for n in range(-2, N_ + 1):
            if 0 <= n + 2 < N_:
                A1(n + 2)
            if 0 <= n < N_:
                B1(n)
            if 0 <= n + 1 < N_:
                A2(n + 1)
            if 0 <= n - 1 < N_:
                B2(n - 1)

    def gdn_gen(own, reset=True):
        if reset:
            ar.reset()
        KTc = [ar.alloc([128, 8, 128], BF16) for _ in range(2)]
        Ktm = [ar.alloc([128, 8, 128], BF16) for _ in range(2)]
        Vtm = [ar.alloc([128, 8, 128], BF16) for _ in range(2)]
        QTc = [ar.alloc([128, 8, 128], BF16) for _ in range(2)] if own else None
        GTc = [ar.alloc([128, 8, 128]) for _ in range(2)] if own else None
        RG = ar.alloc([128, 8, 128])
        KG = ar.alloc([128, 8, 128], BF16)
        KD = [ar.alloc([128, 8, 128], BF16) for _ in range(2)]
        Pm = [ar.alloc([128, 8, 128], BF16) for _ in range(2)]
        NW = [ar.alloc([128, 8, 128], BF16) for _ in range(2)]
        AT = [ar.alloc([128, 8, 128], BF16) for _ in range(2)] if own else None
        QD = [ar.alloc([128, 8, 128], BF16) for _ in range(2)] if own else None
        gsm = [ar.alloc([128, 40]) for _ in range(2)]
        Dh = [ar.alloc([128, 4, 128]) for _ in range(2)]
        DmS = [ar.alloc([128, 4, 128]) for _ in range(2)]
        DmI = [ar.alloc([128, 4, 128]) for _ in range(2)] if own else None
        Ub = [[ar.alloc([128, 4, 128], BF16) for _ in range(2)] for _ in range(2)]
        Lb = [[ar.alloc([128, 4, 128], BF16) for _ in range(2)] for _ in range(2)]
        VN = ar.alloc([128, 8, 128], BF16)
        oT = ar.alloc([128, 8, 128]) if own else None
        oS = ar.alloc([128, 8, 128]) if own else None
        r4 = "p (a b) -> p a b"
        bc4 = lambda ap: ap.unsqueeze(2).broadcast_to([128, 4, 128])
        m4 = lambda ap: ap.unsqueeze(1).broadcast_to([128, 4, 128])

        def G1(tc):
            b = tc % 2
            tsl = slice(tc * 128, (tc + 1) * 128)
            tg = tc // 4
            g_ = gsm[b]
            pg.dma("sp", lambda e: e.dma_start(out=KTc[b], in_=KT[:, :, tsl].rearrange("h p t -> p h t")), reads=[("kT", c) for c in range(8)], writes=[("KTc", b)])
            pg.dma("sp", lambda e: e.dma_start(out=Ktm[b], in_=KTM[tsl, :].rearrange("p (h f) -> p h f", h=8)), reads=[("kTM", c, tg) for c in range(8)], writes=[("Ktm", b)])
            pg.dma("sp", lambda e: e.dma_start(out=Vtm[b], in_=VTM[tsl, :].rearrange("p (h f) -> p h f", h=8)), reads=[("vTM", c, tg) for c in range(8)], writes=[("Vtm", b)])
            if own:
                pg.dma("sp", lambda e: e.dma_start(out=QTc[b], in_=QT[:, :, tsl].rearrange("h p t -> p h t")), reads=[("qT", c) for c in range(8)], writes=[("QTc", b)])
                pg.dma("sp", lambda e: e.dma_start(out=GTc[b], in_=GT[:, :, tsl].rearrange("h p t -> p h t")), reads=[("gateT", c) for c in range(8)], writes=[("GTc", b)])
            gb_ = nb()
            pg.pe(lambda e: e.matmul(banks[gb_][:, 0:8], lhsT=tri, rhs=Gg[:, tc, :], start=True, stop=True), reads=["Gg", "cst"], writes=[bk(gb_)])
            pg.pe(lambda e: e.matmul(banks[gb_][:, 8:16], lhsT=ones, rhs=Gg[:, tc, :], start=True, stop=True), reads=["Gg", "cst"], writes=[bk(gb_)])
            pg.act(lambda e: e.copy(out=g_[:, 0:8], in_=banks[gb_][:, 0:8]), reads=[bk(gb_)], writes=[("gcs", b)])
            pg.act(lambda e: e.activation(out=g_[:, 8:16], in_=g_[:, 0:8], func=AF.Exp), reads=[("gcs", b)], writes=[("egc", b)])
            pg.dve(lambda e: e.tensor_tensor(out=g_[:, 32:40], in0=banks[gb_][:, 8:16], in1=g_[:, 0:8], op=ALU.subtract), reads=[bk(gb_), ("gcs", b)], writes=[("gtmp", b)])
            pg.act(lambda e: e.activation(out=g_[:, 16:24], in_=g_[:, 32:40], func=AF.Exp), reads=[("gtmp", b)], writes=[("dkd", b)])
            pg.act(lambda e: e.activation(out=g_[:, 24:32], in_=banks[gb_][:, 8:16], func=AF.Exp), reads=[bk(gb_)], writes=[("gtb", b)])
            pg.pool(lambda e: e.tensor_tensor(out=RG, in0=tri.unsqueeze(1).broadcast_to([128, 8, 128]), in1=Gg[:, tc, :].unsqueeze(2).broadcast_to([128, 8, 128]), op=ALU.mult), reads=["Gg", "cst"], writes=["RG"])
            pg.pool(lambda e: e.tensor_tensor(out=KG, in0=Ktm[b], in1=g_[:, 8:16].unsqueeze(2).broadcast_to([128, 8, 128]), op=ALU.mult), reads=[("Ktm", b), ("egc", b)], writes=["KG"])
            pg.pool(lambda e: e.tensor_tensor(out=KD[b], in0=Ktm[b], in1=g_[:, 16:24].unsqueeze(2).broadcast_to([128, 8, 128]), op=ALU.mult), reads=[("Ktm", b), ("dkd", b)], writes=[("KD", b)])
            HF = (0, 1)
            eb_ = [nb(), nb()]
            for hf in HF:
                for j in range(4):
                    pg.pe(lambda e, j=j, hf=hf: e.matmul(banks[eb_[hf]][:, j * 128:(j + 1) * 128], lhsT=sgt, rhs=RG[:, hf * 4 + j, :], start=True, stop=True), reads=["RG", "cst"], writes=[bk(eb_[hf])])
            gk_ = [nb(), nb()]
            for hf in HF:
                for j in range(4):
                    pg.pe(lambda e, j=j, hf=hf: e.matmul(banks[gk_[hf]][:, j * 128:(j + 1) * 128], lhsT=KTc[b][:, hf * 4 + j, :], rhs=KTc[b][:, hf * 4 + j, :], start=True, stop=True), reads=[("KTc", b)], writes=[bk(gk_[hf])])
            for hf in HF:
                pg.act(lambda e, hf=hf: e.activation(out=Dh[hf], in_=banks[eb_[hf]][:].rearrange(r4, a=4), func=AF.Exp), reads=[bk(eb_[hf])], writes=[("Dh", hf)])
                pg.pool(lambda e, hf=hf: e.tensor_tensor(out=DmS[hf], in0=Dh[hf], in1=m4(slt), op=ALU.mult), reads=[("Dh", hf), "cst"], writes=[("DmS", hf)])
                if own:
                    pg.dve(lambda e, hf=hf: e.tensor_tensor(out=DmI[hf], in0=Dh[hf], in1=m4(tri), op=ALU.mult), reads=[("Dh", hf), "cst"], writes=[("DmI", hf)])
            for hf in HF:
                hs = slice(hf * 4, hf * 4 + 4)
                pg.dve(lambda e, hf=hf: e.tensor_tensor(out=Dh[hf], in0=banks[gk_[hf]][:].rearrange(r4, a=4), in1=DmS[hf], op=ALU.mult), reads=[bk(gk_[hf]), ("DmS", hf), ("Dh", hf)], writes=[("Dh", hf)])
                pg.pool(lambda e, hf=hf, hs=hs: e.tensor_tensor(out=Dh[hf], in0=Dh[hf], in1=bc4(Bg[:, tc, hs]), op=ALU.mult), reads=[("Dh", hf), "Bg"], writes=[("Dh", hf)])
            if own:
                qb_ = [nb(), nb()]
                for hf in HF:
                    hs = slice(hf * 4, hf * 4 + 4)
                    for j in range(4):
                        pg.pe(lambda e, j=j, hf=hf: e.matmul(banks[qb_[hf]][:, j * 128:(j + 1) * 128], lhsT=KTc[b][:, hf * 4 + j, :], rhs=QTc[b][:, hf * 4 + j, :], start=True, stop=True), reads=[("KTc", b), ("QTc", b)], writes=[bk(qb_[hf])])
                    pg.dve(lambda e, hf=hf, hs=hs: e.tensor_tensor(out=AT[b][:, hs, :], in0=banks[qb_[hf]][:].rearrange(r4, a=4), in1=DmI[hf], op=ALU.mult), reads=[bk(qb_[hf]), ("DmI", hf)], writes=[("AT", b, hf)])
                xb_ = [nb(), nb()]
                for hf in HF:
                    hs = slice(hf * 4, hf * 4 + 4)
                    for j in range(4):
                        pg.pe(lambda e, j=j, hf=hf: e.matmul(banks[xb_[hf]][:, j * 128:(j + 1) * 128], lhsT=ones, rhs=RG[:, hf * 4 + j, :], start=True, stop=True), reads=["RG", "cst"], writes=[bk(xb_[hf])])
                    pg.act(lambda e, hf=hf: e.activation(out=DmI[hf], in_=banks[xb_[hf]][:].rearrange(r4, a=4), func=AF.Exp), reads=[bk(xb_[hf]), ("AT", b, hf)], writes=[("DmI", hf)])
                    pg.dve(lambda e, hf=hf, hs=hs: e.tensor_tensor(out=QD[b][:, hs, :], in0=DmI[hf], in1=QTc[b][:, hs, :], op=ALU.mult), reads=[("DmI", hf), ("QTc", b)], writes=[("QD", b, hf)])
            yield
            tb_ = [nb(), nb()]
            for hf in HF:
                hs = slice(hf * 4, hf * 4 + 4)
                for j in range(4):
                    pg.pe(lambda e, j=j, hf=hf: e.transpose(out=banks[tb_[hf]][:, j * 128:(j + 1) * 128], in_=Dh[hf][:, j, :], identity=ident), reads=[("Dh", hf), "cst"], writes=[bk(tb_[hf])])
                pg.act(lambda e, hf=hf: e.copy(out=Lb[hf][0], in_=banks[tb_[hf]][:].rearrange(r4, a=4)), reads=[bk(tb_[hf])], writes=[("L", hf, 0)])
                pg.act(lambda e, hf=hf: e.copy(out=Ub[hf][0], in_=Dh[hf]), reads=[("Dh", hf)], writes=[("U", hf, 0)])
                pg.pool(lambda e, hf=hf, hs=hs: e.tensor_tensor(out=Pm[b][:, hs, :], in0=m4(ident), in1=Dh[hf], op=ALU.subtract), reads=[("Dh", hf), "cst"], writes=[("Pm", b, hf)])
            cur = 0
            for lv in range(6):
                nx = 1 - cur
                l2 = [nb(), nb()]
                u2 = [nb(), nb()] if lv < 5 else None
                for hf in HF:
                    for j in range(4):
                        pg.pe(lambda e, j=j, hf=hf, cur=cur, l2=l2: e.matmul(banks[l2[hf]][:, j * 128:(j + 1) * 128], lhsT=Ub[hf][cur][:, j, :], rhs=Lb[hf][cur][:, j, :], start=True, stop=True), reads=[("U", hf, cur), ("L", hf, cur)], writes=[bk(l2[hf])])
                    if lv < 5:
                        for j in range(4):
                            pg.pe(lambda e, j=j, hf=hf, cur=cur, u2=u2: e.matmul(banks[u2[hf]][:, j * 128:(j + 1) * 128], lhsT=Lb[hf][cur][:, j, :], rhs=Ub[hf][cur][:, j, :], start=True, stop=True), reads=[("U", hf, cur), ("L", hf, cur)], writes=[bk(u2[hf])])
                for hf in HF:
                    pg.act(lambda e, hf=hf, nx=nx, l2=l2: e.copy(out=Lb[hf][nx], in_=banks[l2[hf]][:].rearrange(r4, a=4)), reads=[bk(l2[hf])], writes=[("L", hf, nx)])
                    if lv < 5:
                        if hf == 0:
                            pg.dve(lambda e, hf=hf, nx=nx, u2=u2: e.tensor_copy(out=Ub[hf][nx], in_=banks[u2[hf]][:].rearrange(r4, a=4)), reads=[bk(u2[hf])], writes=[("U", hf, nx)])
                        else:
                            pg.act(lambda e, hf=hf, nx=nx, u2=u2: e.copy(out=Ub[hf][nx], in_=banks[u2[hf]][:].rearrange(r4, a=4)), reads=[bk(u2[hf])], writes=[("U", hf, nx)])
                pb_ = [nb(), nb()]
                for hf in HF:
                    for j in range(4):
                        pg.pe(lambda e, j=j, hf=hf, nx=nx, pb_=pb_: e.matmul(banks[pb_[hf]][:, j * 128:(j + 1) * 128], lhsT=Lb[hf][nx][:, j, :], rhs=Pm[b][:, hf * 4 + j, :], start=True, stop=True), reads=[("L", hf, nx), ("Pm", b, hf)], writes=[bk(pb_[hf])])
                for hf in HF:
                    hs = slice(hf * 4, hf * 4 + 4)
                    pg.dve(lambda e, hf=hf, hs=hs, pb_=pb_: e.tensor_tensor(out=Pm[b][:, hs, :], in0=Pm[b][:, hs, :], in1=banks[pb_[hf]][:].rearrange(r4, a=4), op=ALU.add), reads=[bk(pb_[hf]), ("Pm", b, hf)], writes=[("Pm", b, hf)])
                cur = nx
                yield
            wb_ = [nb(), nb()]
            for hf in HF:
                hs = slice(hf * 4, hf * 4 + 4)
                for j in range(4):
                    pg.pe(lambda e, j=j, hf=hf: e.matmul(banks[wb_[hf]][:, j * 128:(j + 1) * 128], lhsT=KG[:, hf * 4 + j, :], rhs=Pm[b][:, hf * 4 + j, :], start=True, stop=True), reads=["KG", ("Pm", b, hf)], writes=[bk(wb_[hf])])
                pg.act(lambda e, hf=hf, hs=hs: e.mul(out=NW[b][:, hs, :], in_=banks[wb_[hf]][:].rearrange(r4, a=4), mul=-1.0), reads=[bk(wb_[hf])], writes=[("NW", b, hf)])

        def G2(tc):
            b = tc % 2
            tsl = slice(tc * 128, (tc + 1) * 128)
            g_ = gsm[b]
            HF = (0, 1)
            vb_ = [nb(), nb()]
            for hf in HF:
                hs = slice(hf * 4, hf * 4 + 4)
                for j in range(4):
                    h_ = hf * 4 + j
                    pg.pe(lambda e, j=j, hf=hf, h_=h_: e.matmul(banks[vb_[hf]][:, j * 128:(j + 1) * 128], lhsT=Pm[b][:, h_, :], rhs=Vtm[b][:, h_, :], start=True, stop=False), reads=[("Pm", b, hf), ("Vtm", b)], writes=[bk(vb_[hf])])
                    pg.pe(lambda e, j=j, hf=hf, h_=h_: e.matmul(banks[vb_[hf]][:, j * 128:(j + 1) * 128], lhsT=NW[b][:, h_, :], rhs=Sgb[:, h_, :], start=False, stop=True), reads=[("NW", b, hf), ("Sgb", hf)], writes=[bk(vb_[hf])])
                pg.dve(lambda e, hf=hf, hs=hs: e.tensor_tensor(out=VN[:, hs, :], in0=banks[vb_[hf]][:].rearrange(r4, a=4), in1=bc4(Bg[:, tc, hs]), op=ALU.mult), reads=[bk(vb_[hf]), "Bg"], writes=[("VN", hf)])
            yield
            if own:
                ob_ = [nb(), nb()]
                for hf in HF:
                    hs = slice(hf * 4, hf * 4 + 4)
                    for j in range(4):
                        h_ = hf * 4 + j
                        pg.pe(lambda e, j=j, hf=hf, h_=h_: e.matmul(banks[ob_[hf]][:, j * 128:(j + 1) * 128], lhsT=Sgb[:, h_, :], rhs=QD[b][:, h_, :], start=True, stop=False), reads=[("Sgb", hf), ("QD", b, hf)], writes=[bk(ob_[hf])])
                        pg.pe(lambda e, j=j, hf=hf, h_=h_: e.matmul(banks[ob_[hf]][:, j * 128:(j + 1) * 128], lhsT=VN[:, h_, :], rhs=AT[b][:, h_, :], start=False, stop=True), reads=[("VN", hf), ("AT", b, hf)], writes=[bk(ob_[hf])])
                    pg.act(lambda e, hf=hf, hs=hs: e.copy(out=oT[:, hs, :], in_=banks[ob_[hf]][:].rearrange(r4, a=4)), reads=[bk(ob_[hf])], writes=[("oT", hf), ("oT2", hf)])
            yield
            sb_ = [nb(), nb()]
            for hf in HF:
                hs = slice(hf * 4, hf * 4 + 4)
                for j in range(4):
                    h_ = hf * 4 + j
                    pg.pe(lambda e, j=j, hf=hf, h_=h_: e.matmul(banks[sb_[hf]][:, j * 128:(j + 1) * 128], lhsT=KD[b][:, h_, :], rhs=VN[:, h_, :], start=True, stop=True), reads=[("KD", b), ("VN", hf)], writes=[bk(sb_[hf])])
                pg.pool(lambda e, hs=hs, hf=hf: e.tensor_tensor(out=Sg[:, hs, :], in0=Sg[:, hs, :], in1=bc4(g_[:, 24 + hf * 4:28 + hf * 4]), op=ALU.mult), reads=[("Sg", hf), ("gtb", b)], writes=[("Sg", hf)])
                pg.dve(lambda e, hf=hf, hs=hs: e.tensor_tensor(out=Sg[:, hs, :], in0=Sg[:, hs, :], in1=banks[sb_[hf]][:].rearrange(r4, a=4), op=ALU.add), reads=[("Sg", hf), bk(sb_[hf])], writes=[("Sg", hf)])
                pg.act(lambda e, hs=hs, hf=hf: e.copy(out=Sgb[:, hs, :], in_=Sg[:, hs, :]), reads=[("Sg", hf)], writes=[("Sgb", hf)])
            yield
            if own:
                for hf in HF:
                    hs = slice(hf * 4, hf * 4 + 4)
                    pg.dve(lambda e, hs=hs: e.tensor_tensor(out=oS[:, hs, :], in0=oT[:, hs, :], in1=oT[:, hs, :], op=ALU.mult), reads=[("oT", hf)], writes=[("oS", hf), ("oS1", hf), ("oS2", hf)])
                    nb_ = nb()
                    pg.pe(lambda e, nb_=nb_, hs=hs: e.matmul(banks[nb_][:], lhsT=ones, rhs=oS[:, hs, :].rearrange("p a b -> p (a b)"), start=True, stop=True), reads=[("oS", hf), "cst"], writes=[bk(nb_)])
                    pg.act(lambda e, nb_=nb_, hs=hs: e.activation(out=oS[:, hs, :], in_=banks[nb_][:].rearrange(r4, a=4), func=AF.Ln, bias=EPS, scale=1.0 / 128), reads=[bk(nb_), ("oS", hf)], writes=[("oS1", hf), ("oS", hf)])
                    pg.act(lambda e, hs=hs: e.activation(out=oS[:, hs, :], in_=oS[:, hs, :], func=AF.Exp, scale=-0.5), reads=[("oS1", hf)], writes=[("oS2", hf)])
                    pg.dve(lambda e, hs=hs: e.tensor_tensor(out=oT[:, hs, :], in0=oT[:, hs, :], in1=oS[:, hs, :], op=ALU.mult), reads=[("oT", hf), ("oS2", hf)], writes=[("oT2", hf)])
                    pg.dve(lambda e, hs=hs, tsl=tsl, b=b: e.scalar_tensor_tensor(out=xn[:, hs, tsl], in0=oT[:, hs, :], scalar=sm[:, SM["gon"][0]:SM["gon"][0] + 1], in1=GTc[b][:, hs, :], op0=ALU.mult, op1=ALU.mult),
                           reads=[("oT2", hf), ("GTc", b), "sm"], writes=[("xn", c_) for c_ in range(hf * 4, hf * 4 + 4)])

        for _ in G1(0):
            pass
        for tc in range(8):
            subs = [G2(tc)]
            if tc + 1 < 8:
                subs.insert(0, G1(tc + 1))
            rnd = 0
            while subs:
                for g_ in list(subs):
                    try:
                        next(g_)
                    except StopIteration:
                        subs.remove(g_)
                rnd += 1
                if rnd == 2:
                    yield
            if rnd < 2:
                yield

    def ssd_gen(own, reset=True):
        if reset:
            ar.reset()
        Xtm = [ar.alloc([128, 16, 64]) for _ in range(2)]
        Btm = [ar.alloc([128, 256]) for _ in range(2)]
        XCD = ar.alloc([128, 16, 64])
        ssm_ = ar.alloc([128, 80])
        if own:
            BTc = [ar.alloc([128, 2, 128]) for _ in range(2)]
            CTc = [ar.alloc([128, 2, 128]) for _ in range(2)]
            XTc = [ar.alloc([128, 8, 128]) for _ in range(2)]
            ZTc = [ar.alloc([128, 8, 128]) for _ in range(2)]
            XC = ar.alloc([128, 16, 64], BF16)
            Ssb = ar.alloc([128, 1024], BF16)
            RL = ar.alloc([128, 16, 128])
            SEG = [ar.alloc([128, 4, 128]) for _ in range(2)]
            EA = [ar.alloc([128, 4, 128]) for _ in range(2)]
            CBm = ar.alloc([128, 2, 128])
            Mh = [ar.alloc([128, 4, 128], BF16) for _ in range(2)]
            CE = [ar.alloc([128, 4, 128], BF16) for _ in range(2)]
            yT = ar.alloc([128, 8, 128])
            ySq = ar.alloc([128, 8, 128])
            rr = ar.alloc([128, 2, 128])
            for g in range(2):
                pg.act(lambda e, g=g: e.copy(out=Ssb[:, g * 512:(g + 1) * 512], in_=Ss[:, g * 512:(g + 1) * 512]), reads=[("Ss", g)], writes=[("Ssb", g)])
        r4 = "p (a b) -> p a b"
        for tc in range(8):
            b = tc % 2
            tsl = slice(tc * 128, (tc + 1) * 128)
            tg = tc // 4
            pg.dma("sp", lambda e, b=b, tsl=tsl: e.dma_start(out=Xtm[b], in_=XTM[tsl, :].rearrange("p (h f) -> p h f", h=16)), reads=[("xTM", c, tg) for c in range(8)], writes=[("Xtm", b)])
            pg.dma("sp", lambda e, b=b, tsl=tsl: e.dma_start(out=Btm[b], in_=BTM[tsl, :]), reads=[("BTM", c, tg) for c in range(2)], writes=[("Btm", b)])
            if own:
                pg.dma("sp", lambda e, b=b, tsl=tsl: e.dma_start(out=BTc[b], in_=BT[:, :, tsl].rearrange("h p t -> p h t")), reads=[("BT", c) for c in range(2)], writes=[("BTc", b)])
                pg.dma("sp", lambda e, b=b, tsl=tsl: e.dma_start(out=CTc[b], in_=CT[:, :, tsl].rearrange("h p t -> p h t")), reads=[("CT", c) for c in range(2)], writes=[("CTc", b)])
                pg.dma("sp", lambda e, b=b, tsl=tsl: e.dma_start(out=XTc[b], in_=XT[:, :, tsl].rearrange("h p t -> p h t")), reads=[("xT", c) for c in range(8)], writes=[("XTc", b)])
                pg.dma("sp", lambda e, b=b, tsl=tsl: e.dma_start(out=ZTc[b], in_=ZT[:, :, tsl].rearrange("h p t -> p h t")), reads=[("zT", c) for c in range(8)], writes=[("ZTc", b)])
            ab_ = nb()
            pg.pe(lambda e, ab_=ab_, tc=tc: e.matmul(banks[ab_][:, 0:16], lhsT=tri, rhs=Las[:, tc, :], start=True, stop=True), reads=["Las", "cst"], writes=[bk(ab_)])
            pg.pe(lambda e, ab_=ab_, tc=tc: e.matmul(banks[ab_][:, 16:32], lhsT=ones, rhs=Las[:, tc, :], start=True, stop=True), reads=["Las", "cst"], writes=[bk(ab_)])
            pg.act(lambda e, ab_=ab_: e.copy(out=ssm_[:, 0:16], in_=banks[ab_][:, 0:16]), reads=[bk(ab_)], writes=["acs"])
            pg.dve(lambda e, ab_=ab_: e.tensor_tensor(out=ssm_[:, 48:64], in0=banks[ab_][:, 16:32], in1=ssm_[:, 0:16], op=ALU.subtract), reads=[bk(ab_), "acs"], writes=["stmp"])
            pg.act(lambda e: e.activation(out=ssm_[:, 16:32], in_=ssm_[:, 48:64], func=AF.Exp), reads=["stmp"], writes=["dte"])
            pg.act(lambda e, ab_=ab_: e.activation(out=ssm_[:, 32:48], in_=banks[ab_][:, 16:32], func=AF.Exp), reads=[bk(ab_)], writes=["cdb"])
            pg.dve(lambda e, tc=tc: e.tensor_tensor(out=ssm_[:, 64:80], in0=ssm_[:, 16:32], in1=Dts[:, tc, :], op=ALU.mult), reads=["dte", "Dts"], writes=["dtdte"])
            pg.pool(lambda e, b=b: e.tensor_tensor(out=XCD, in0=Xtm[b], in1=ssm_[:, 64:80].unsqueeze(2).broadcast_to([128, 16, 64]), op=ALU.mult), reads=[("Xtm", b), "dtdte"], writes=["XCD"])
            if own:
                pg.dve(lambda e, b=b, tc=tc: e.tensor_tensor(out=XC, in0=Xtm[b], in1=Dts[:, tc, :].unsqueeze(2).broadcast_to([128, 16, 64]), op=ALU.mult), reads=[("Xtm", b), "Dts"], writes=["XC"])
                pg.dve(lambda e, tc=tc: e.tensor_tensor(out=RL, in0=tri.unsqueeze(1).broadcast_to([128, 16, 128]), in1=Las[:, tc, :].unsqueeze(2).broadcast_to([128, 16, 128]), op=ALU.mult), reads=["Las", "cst"], writes=["RL"])
                for g in range(2):
                    cb_ = nb()
                    pg.pe(lambda e, cb_=cb_, g=g, b=b: e.matmul(banks[cb_][:, 0:128], lhsT=BTc[b][:, g, :], rhs=CTc[b][:, g, :], start=True, stop=True), reads=[("BTc", b), ("CTc", b)], writes=[bk(cb_)])
                    pg.dve(lambda e, cb_=cb_, g=g: e.tensor_tensor(out=CBm[:, g, :], in0=banks[cb_][:, 0:128], in1=tri, op=ALU.mult), reads=[bk(cb_), "cst"], writes=[("CBm", g)])
                pg.dve(lambda e, b=b: e.tensor_tensor(out=ySq, in0=XTc[b], in1=smc("sd").unsqueeze(2).broadcast_to([128, 8, 128]), op=ALU.mult), reads=[("XTc", b), "sm"], writes=["XD", "ySq"])

                def S1(q4):
                    g = q4 // 2
                    k2 = q4 % 2
                    se_ = nb()
                    ac_ = nb()
                    for j in range(4):
                        hh = q4 * 4 + j
                        pg.pe(lambda e, se_=se_, j=j, hh=hh: e.matmul(banks[se_][:, j * 128:(j + 1) * 128], lhsT=sgt, rhs=RL[:, hh, :], start=True, stop=True), reads=["RL", "cst"], writes=[bk(se_)])
                    for j in range(4):
                        hh = q4 * 4 + j
                        pg.pe(lambda e, ac_=ac_, j=j, hh=hh: e.matmul(banks[ac_][:, j * 128:(j + 1) * 128], lhsT=ones, rhs=RL[:, hh, :], start=True, stop=True), reads=["RL", "cst"], writes=[bk(ac_)])
                    pg.act(lambda e, se_=se_, k2=k2: e.activation(out=SEG[k2], in_=banks[se_][:].rearrange(r4, a=4), func=AF.Exp), reads=[bk(se_)], writes=[("SEG", k2)])
                    pg.act(lambda e, ac_=ac_, k2=k2: e.activation(out=EA[k2], in_=banks[ac_][:].rearrange(r4, a=4), func=AF.Exp), reads=[bk(ac_)], writes=[("EA", k2)])
                    pg.dve(lambda e, g=g, k2=k2: e.tensor_tensor(out=Mh[k2], in0=SEG[k2], in1=CBm[:, g, :].unsqueeze(1).broadcast_to([128, 4, 128]), op=ALU.mult), reads=[("SEG", k2), ("CBm", g)], writes=[("Mh", k2)])
                    pg.dve(lambda e, g=g, k2=k2, b=b: e.tensor_tensor(out=CE[k2], in0=EA[k2], in1=CTc[b][:, g, :].unsqueeze(1).broadcast_to([128, 4, 128]), op=ALU.mult), reads=[("EA", k2), ("CTc", b)], writes=[("CE", k2)])

                def S2(q4):
                    k2 = q4 % 2
                    yb_ = nb()
                    for j in range(4):
                        hh = q4 * 4 + j
                        pr = hh // 2
                        pg.pe(lambda e, yb_=yb_, j=j, pr=pr, k2=k2: e.matmul(banks[yb_][:, j * 128:(j + 1) * 128], lhsT=XC[:, 2 * pr:2 * pr + 2, :].rearrange("p a b -> p (a b)"), rhs=Mh[k2][:, j, :], start=True, stop=False), reads=["XC", ("Mh", k2)], writes=[bk(yb_)])
                        pg.pe(lambda e, yb_=yb_, j=j, pr=pr, k2=k2: e.matmul(banks[yb_][:, j * 128:(j + 1) * 128], lhsT=Ssb[:, pr * 128:(pr + 1) * 128], rhs=CE[k2][:, j, :], start=False, stop=True), reads=[("Ssb", pr // 4), ("CE", k2)], writes=[bk(yb_)])
                    p0 = q4 * 2
                    ybv = banks[yb_][:].rearrange("p (a b c) -> p a b c", a=2, b=2)
                    pg.dve(lambda e, ybv=ybv, p0=p0: e.tensor_tensor(out=yT[0:64, p0:p0 + 2, :], in0=ybv[0:64, :, 0, :], in1=ySq[0:64, p0:p0 + 2, :], op=ALU.add), reads=[bk(yb_), "XD"], writes=[("yT", p0, 0)])
                    pg.dve(lambda e, ybv=ybv, p0=p0: e.tensor_tensor(out=yT[64:128, p0:p0 + 2, :], in0=ybv[64:128, :, 1, :], in1=ySq[64:128, p0:p0 + 2, :], op=ALU.add), reads=[bk(yb_), "XD"], writes=[("yT", p0, 1)])

                S1(0)
                for q4 in range(4):
                    if q4 + 1 < 4:
                        S1(q4 + 1)
                    S2(q4)
                yr = [("yT", p0, k) for p0 in (0, 2, 4, 6) for k in (0, 1)]
                pg.dve(lambda e, b=b: e.tensor_tensor(out=yT, in0=yT, in1=ZTc[b], op=ALU.mult), reads=yr + [("ZTc", b)], writes=["yz"])
                pg.dve(lambda e: e.tensor_tensor(out=ySq, in0=yT, in1=yT, op=ALU.mult), reads=["yz", "XD"], writes=["ySq", "XD"])
                nb_ = nb()
                for g in range(2):
                    for k in range(4):
                        pg.pe(lambda e, nb_=nb_, g=g, k=k: e.matmul(banks[nb_][:, g * 128:(g + 1) * 128], lhsT=ones, rhs=ySq[:, g * 4 + k, :], start=(k == 0), stop=(k == 3)), reads=["ySq", "cst"], writes=[bk(nb_)])
                pg.act(lambda e, nb_=nb_: e.activation(out=rr, in_=banks[nb_][:, 0:256].rearrange("p (a b) -> p a b", a=2), func=AF.Ln, bias=EPS, scale=1.0 / 512), reads=[bk(nb_)], writes=["rr0", "rr"])
                pg.act(lambda e: e.activation(out=rr, in_=rr, func=AF.Exp, scale=-0.5), reads=["rr0"], writes=["rr"])
                for g in range(2):
                    pg.dve(lambda e, g=g: e.tensor_tensor(out=yT[:, g * 4:(g + 1) * 4, :], in0=yT[:, g * 4:(g + 1) * 4, :], in1=rr[:, g, :].unsqueeze(1).broadcast_to([128, 4, 128]), op=ALU.mult), reads=["yz", "rr"] + ([("yn", 0)] if g else []), writes=[("yn", g)])
                pg.dve(lambda e, tsl=tsl: e.tensor_tensor(out=xn[:, 8:16, tsl], in0=yT, in1=smc("son").unsqueeze(2).broadcast_to([128, 8, 128]), op=ALU.mult), reads=[("yn", 0), ("yn", 1), "sm"], writes=[("xn", c_) for c_ in range(8, 16)])
            for g in range(2):
                sb_ = nb()
                pg.pe(lambda e, sb_=sb_, g=g, b=b: e.matmul(banks[sb_][:], lhsT=Btm[b][:, g * 128:(g + 1) * 128], rhs=XCD[:, g * 8:(g + 1) * 8, :].rearrange("p a b -> p (a b)"), start=True, stop=True), reads=[("Btm", b), "XCD"], writes=[bk(sb_)])
                sv = Ss[:, g * 512:(g + 1) * 512].rearrange("p (a b) -> p a b", a=8)
                pg.dve(lambda e, sv=sv, g=g: e.tensor_tensor(out=sv, in0=sv, in1=ssm_[:, 32 + g * 8:40 + g * 8].unsqueeze(2).broadcast_to([128, 8, 64]), op=ALU.mult), reads=[("Ss", g), "cdb"], writes=[("Ss", g)])
                pg.dve(lambda e, sb_=sb_, g=g: e.tensor_tensor(out=Ss[:, g * 512:(g + 1) * 512], in0=Ss[:, g * 512:(g + 1) * 512], in1=banks[sb_][:], op=ALU.add), reads=[("Ss", g), bk(sb_)], writes=[("Ss", g)])
                if own:
                    pg.act(lambda e, g=g: e.copy(out=Ssb[:, g * 512:(g + 1) * 512], in_=Ss[:, g * 512:(g + 1) * 512]), reads=[("Ss", g)], writes=[("Ssb", g)])
            yield

    def dense_residual(wname, gate_fn=None):
        ar.reset()
        wsl = [ar.alloc([128, NKC, 128], BF16) for _ in range(3)]
        w = W[wname]
        for oc in range(NKC):
            sl = oc % 3
            pg.dma("pool", lambda e, sl=sl, oc=oc: e.dma_start(out=wsl[sl], in_=w[:, oc * 128:(oc + 1) * 128].rearrange("(kc p) f -> p kc f", p=128)), writes=[("wsl", sl)])
            for tt in range(2):
                bi = nb()
                for kc in range(NKC):
                    pg.pe(lambda e, bi=bi, sl=sl, kc=kc, tt=tt: e.matmul(banks[bi][:], lhsT=wsl[sl][:, kc, :], rhs=xn[:, kc, tt * 512:(tt + 1) * 512], start=(kc == 0), stop=(kc == NKC - 1)), reads=[("wsl", sl), ("xn", kc)], writes=[bk(bi)])
                pg.dve(lambda e, bi=bi, oc=oc, tt=tt: e.tensor_tensor(out=h[:, oc, tt * 512:(tt + 1) * 512], in0=h[:, oc, tt * 512:(tt + 1) * 512], in1=banks[bi][:], op=ALU.add), reads=[bk(bi), ("h", oc, tt)], writes=[("h", oc, tt)])

    def ple_phase():
        ar.reset()
        rmsnorm_to_xn(3)
        wsl = [ar.alloc([128, NKC, 128], BF16) for _ in range(3)]
        wpr = ar.alloc([128, 2, D], BF16)
        pT = ar.alloc([128, 2, T], BF16)
        gs = [ar.alloc([128, 512]) for _ in range(2)]
        pt_ = [ar.alloc([128, 512]) for _ in range(2)]
        sq2 = [ar.alloc([128, 512], BF16) for _ in range(4)]
        pg.dma("pool", lambda e: e.dma_start(out=wpr, in_=W["ple_w_proj"].rearrange("(kc p) f -> p kc f", p=128)), writes=["wpr"])
        pg.dma("pool", lambda e: e.dma_start(out=pT, in_=pin.rearrange("(kc p) t -> p kc t", p=128)), writes=["pT"])
        ptr = [["pT"], ["pT"]]
        ssb = [nb(), nb()]
        reserved.update(ssb)
        its = [(oc, tt) for oc in range(NKC) for tt in range(2)]

        def X1(n):
            oc, tt = its[n]
            bi = nb()
            q2 = n % 4
            for kc in range(2):
                pg.pe(lambda e, bi=bi, kc=kc, oc=oc, tt=tt: e.matmul(banks[bi][:], lhsT=wpr[:, kc, oc * 128:(oc + 1) * 128], rhs=pT[:, kc, tt * 512:(tt + 1) * 512], start=(kc == 0), stop=(kc == 1)),
                      reads=["wpr"] + ptr[tt], writes=[bk(bi)])
            pg.act(lambda e, bi=bi, q2=q2: e.activation(out=sq2[q2], in_=banks[bi][:], func=AF.Square), reads=[bk(bi)], writes=[("sq2", q2)])

        def Y1(n):
            oc, tt = its[n]
            q2 = n % 4
            pg.pe(lambda e, q2=q2, tt=tt, oc=oc: e.matmul(banks[ssb[tt]][:], lhsT=ones_bf[:], rhs=sq2[q2], start=(oc == 0), stop=(oc == NKC - 1)), reads=[("sq2", q2), "ones_bf"], writes=[bk(ssb[tt])])

        X1(0)
        X1(1)
        for n in range(len(its)):
            if n + 2 < len(its):
                X1(n + 2)
            Y1(n)
        for tt in range(2):
            pg.act(lambda e, tt=tt: e.activation(out=rstd[:, tt * 512:(tt + 1) * 512], in_=banks[ssb[tt]][:], func=AF.Ln, bias=EPS, scale=1.0 / D), reads=[bk(ssb[tt])], writes=[("rstd0", tt), ("rstd", tt)])
            pg.act(lambda e, tt=tt: e.activation(out=rstd[:, tt * 512:(tt + 1) * 512], in_=rstd[:, tt * 512:(tt + 1) * 512], func=AF.Exp, scale=-0.5), reads=[("rstd0", tt)], writes=[("rstd", tt)])
        reserved.clear()
        w = W["ple_w_gate"]
        for oc in range(NKC):
            sl = oc % 3
            pg.dma("pool", lambda e, sl=sl, oc=oc: e.dma_start(out=wsl[sl], in_=w[:, oc * 128:(oc + 1) * 128].rearrange("(kc p) f -> p kc f", p=128)), writes=[("wsl", sl)])
            for tt in range(2):
                bi = nb()
                for kc in range(NKC):
                    pg.pe(lambda e, bi=bi, sl=sl, kc=kc, tt=tt: e.matmul(banks[bi][:], lhsT=wsl[sl][:, kc, :], rhs=xn[:, kc, tt * 512:(tt + 1) * 512], start=(kc == 0), stop=(kc == NKC - 1)), reads=[("wsl", sl), ("xn", kc)], writes=[bk(bi)])
                pg.act(lambda e, bi=bi, tt=tt: e.activation(out=gs[tt], in_=banks[bi][:], func=AF.Sigmoid), reads=[bk(bi)], writes=[("gs", tt)])
                b2 = nb()
                for kc in range(2):
                    pg.pe(lambda e, b2=b2, kc=kc, oc=oc, tt=tt: e.matmul(banks[b2][:], lhsT=wpr[:, kc, oc * 128:(oc + 1) * 128], rhs=pT[:, kc, tt * 512:(tt + 1) * 512], start=(kc == 0), stop=(kc == 1)),
                          reads=["wpr"] + ptr[tt], writes=[bk(b2)])
                ts_ = slice(tt * 512, (tt + 1) * 512)
                pg.dve(lambda e, oc=oc, ts_=ts_, b2=b2, tt=tt: e.scalar_tensor_tensor(out=pt_[tt], in0=banks[b2][:], scalar=sm[:, 4 * 16 + oc:4 * 16 + oc + 1], in1=rstd[:, ts_], op0=ALU.mult, op1=ALU.mult),
                       reads=[bk(b2), ("rstd", tt), "sm"], writes=[("pt", tt)])
                pg.dve(lambda e, tt=tt: e.tensor_tensor(out=pt_[tt], in0=pt_[tt], in1=gs[tt], op=ALU.mult), reads=[("pt", tt), ("gs", tt)], writes=[("pt", tt)])
                pg.dve(lambda e, oc=oc, ts_=ts_, tt=tt: e.tensor_tensor(out=h[:, oc, ts_], in0=h[:, oc, ts_], in1=pt_[tt], op=ALU.add), reads=[("pt", tt), ("h", oc, tt)], writes=[("h", oc, tt)])

    def final_phase():
        rmsnorm_to_xn(5, dst=None, dst_is_xn=False)
        fins = []
        for cg in range(4):
            for j in range(4):
                c = cg * 4 + j
                pg.dve(lambda e, c=c: e.scalar_tensor_tensor(out=h[:, c, :], in0=h[:, c, :], scalar=sm[:, 5 * 16 + c:5 * 16 + c + 1], in1=rstd[:], op0=ALU.mult, op1=ALU.mult),
                       reads=[("h", c, 0), ("h", c, 1), ("rstd", 0), ("rstd", 1), "sm"], writes=[("h", c, 0), ("h", c, 1)])
            fins.append(pg.dma("sp", lambda e, cg=cg: e.dma_start(out=out[cg * 512:(cg + 1) * 512, :].rearrange("(c p) t -> p c t", p=128), in_=h[:, cg * 4:(cg + 1) * 4, :]),
                               reads=[("h", cg * 4 + j, tt) for j in range(4) for tt in range(2)], writes=[("out", cg)]))
        return fins

    stop = debug or ""
    fins = None
    preloaded = False
    for s in range(nslot):
        own = (s == nslot - 1)
        if not preloaded:
            run_all([load_x_gen(s)])
        preloaded = False
        ffn(W["ffn1_w_gate"], W["ffn1_w_up"], W["ffn1_w_down"], 0)
        if stop == "ffn1":
            continue
        proj_phase(s, own)
        if not own:
            ar.reset()
            run_all([gdn_gen(False, reset=False), ssd_gen(False, reset=False), load_x_gen(s + 1, reset=False, act_only=True)])
            preloaded = True
        else:
            run_all([gdn_gen(own)])
            run_all([ssd_gen(own)])
    if stop not in ("ffn1",):
        if stop != "nomix":
            dense_residual("w_out")
        if stop != "mix":
            ffn(W["ffn2_w_gate"], W["ffn2_w_up"], W["ffn2_w_down"], 2)
            ple_phase()
    if stop in ("ffn1", "mix"):
        fins = []
        for cg in range(4):
            fins.append(pg.dma("sp", lambda e, cg=cg: e.dma_start(out=out[cg * 512:(cg + 1) * 512, :].rearrange("(c p) t -> p c t", p=128), in_=h[:, cg * 4:(cg + 1) * 4, :]),
                               reads=[("h", cg * 4 + j, tt) for j in range(4) for tt in range(2)], writes=[("out", cg)]))
    else:
        fins = final_phase()
    pg.emit(final_wait_ops=fins)
    return nc


def _smalls(inp):
    sm = np.zeros((128, NSM), np.float32)

    def put(name, arr):
        o, w = SM[name]
        assert arr.shape == (128, w), (name, arr.shape)
        sm[:, o:o + w] = arr

    nws = [inp["ffn1_norm"][0], inp["mix_norm"][0], inp["ffn2_norm"][0], inp["ple_norm"][0], inp["ple_post_norm"][0], inp["final_norm"]]
    put("nw", np.concatenate([np.asarray(w).reshape(16, 128).T for w in nws], axis=1))
    put("gcw", np.asarray(inp["gdn_conv_w"][0]).reshape(4, 24, 128).transpose(2, 1, 0).reshape(128, 96))
    put("scw", np.asarray(inp["ssm_conv_w"][0]).reshape(4, 12, 128).transpose(2, 1, 0).reshape(128, 48))
    put("scb", np.asarray(inp["ssm_conv_b"][0]).reshape(12, 128).T)
    put("gon", np.asarray(inp["gdn_out_norm"][0]).reshape(128, 1))
    put("son", np.asarray(inp["ssm_out_norm"][0]).reshape(8, 128).T)
    put("sd", np.repeat(np.asarray(inp["ssm_d"][0]), 64).reshape(8, 128).T)
    put("galog", np.broadcast_to(np.asarray(inp["gdn_a_log"][0])[None, :], (128, 8)))
    put("gdtb", np.broadcast_to(np.asarray(inp["gdn_dt_bias"][0])[None, :], (128, 8)))
    put("salog", np.broadcast_to(np.asarray(inp["ssm_a_log"][0])[None, :], (128, 16)))
    put("sdtb", np.broadcast_to(np.asarray(inp["ssm_dt_bias"][0])[None, :], (128, 16)))
    return sm


def _consts():
    a = np.arange(128)[:, None]
    b = np.arange(128)[None, :]
    c = np.zeros((128, 5, 128), np.float32)
    c[:, 0] = (a == b)
    c[:, 1] = (a <= b)
    c[:, 2] = (a > b)
    c[:, 3] = (a < b)
    c[:, 4] = 1.0
    return c


_NC_CACHE = {}


def kernel(_nslot=NSLOT, _debug=None, _cores=8, **inputs):
    inp = {k: np.asarray(v) for k, v in inputs.items()}
    x = inp["x"]
    p = inp["p"][0]
    key = (_nslot, _debug)
    if key not in _NC_CACHE:
        _NC_CACHE[key] = build_nc(_nslot, _debug)
    nc = _NC_CACHE[key]
    sm = _smalls(inp)
    cst = _consts()
    wmap = {nm: np.ascontiguousarray(inp[nm][0]) for nm in ("ffn1_w_gate", "ffn1_w_up", "ffn1_w_down", "w_in", "w_out",
                                                           "ffn2_w_gate", "ffn2_w_up", "ffn2_w_down", "ple_w_gate", "ple_w_proj")}
    in_maps = []
    for r in range(_cores):
        b, q = r // 4, r % 4
        xs = np.zeros((_nslot, D, T), np.float32)
        cmask = np.zeros((128, 4), np.float32)
        for j in range(_nslot):
            seg = q - (_nslot - 1) + j
            if seg >= 0:
                xs[j] = x[b, seg * T:(seg + 1) * T].T
                cmask[:, j] = 1.0
        m = {"xs": xs, "pin": np.ascontiguousarray(p[b, q * T:(q + 1) * T].T), "cmask": cmask, "smalls": sm, "consts": cst}
        m.update(wmap)
        in_maps.append(m)
    res = run_bass_kernel_spmd(nc, in_maps, core_ids=list(range(_cores)))
    outp = np.zeros((2, 4 * T, D), np.float32)
    for r in range(_cores):
        b, q = r // 4, r % 4
        outp[b, q * T:(q + 1) * T] = res.results[r]["out"].T
    return outp
```

```python
import contextlib
import numpy as np
import concourse.bass as bass
import concourse.mybir as mybir
from concourse.bass_utils import run_bass_kernel_spmd

F32 = mybir.dt.float32
BF16 = mybir.dt.bfloat16
AF = mybir.ActivationFunctionType
ALU = mybir.AluOpType

D = 2048
T = 1024
NKC = 16
DFF = 5632
NFC = 44
PLE = 256
IN_DIM = 6688
EPS = 1e-6
NSLOT = 4
ENGS = ("pe", "act", "dve", "pool", "sp")


class Op:
    __slots__ = ("eng", "fn", "deps", "raw", "dma", "signal", "sigval", "sem", "idx", "ring_prev")

    def __init__(self, eng, fn, dma):
        self.eng = eng
        self.fn = fn
        self.dma = dma
        self.deps = set()
        self.raw = set()
        self.signal = False
        self.sigval = 0
        self.sem = None
        self.ring_prev = None


class Prog:
    def __init__(self, nc, strict=True, ring=8):
        self.nc = nc
        self.ops = []
        self.last_writer = {}
        self.readers = {}
        self.strict = strict
        self.ring = ring
        self.dma_ops = {"sp": [], "pool": []}
        self.phase_tok = None

    def op(self, eng, fn, reads=(), writes=(), dma=False):
        o = Op(eng, fn, dma)
        o.idx = len(self.ops)
        reads = list(reads)
        if self.phase_tok is not None:
            reads.append(self.phase_tok)
        for t in reads:
            w = self.last_writer.get(t)
            if w is not None:
                o.deps.add(w)
                o.raw.add(w)
        for t in writes:
            w = self.last_writer.get(t)
            if w is not None:
                o.deps.add(w)
            rd = self.readers.get(t)
            if rd:
                for lst in rd.values():
                    o.deps.update(lst)
        key = eng + ("_d" if dma else "")
        for t in reads:
            rd = self.readers.setdefault(t, {})
            lst = rd.setdefault(key, [])
            lst.append(o.idx)
            if dma:
                if len(lst) > self.ring:
                    del lst[0]
            elif len(lst) > 1:
                del lst[0]
        for t in writes:
            self.last_writer[t] = o.idx
            self.readers[t] = {}
        o.deps.discard(o.idx)
        if dma:
            lst = self.dma_ops[eng]
            k = len(lst)
            if k >= self.ring:
                o.ring_prev = lst[k - self.ring].idx
            lst.append(o)
        self.ops.append(o)
        return o

    def pe(self, fn, reads=(), writes=()):
        return self.op("pe", fn, reads, writes)

    def act(self, fn, reads=(), writes=()):
        return self.op("act", fn, reads, writes)

    def dve(self, fn, reads=(), writes=()):
        return self.op("dve", fn, reads, writes)

    def pool(self, fn, reads=(), writes=()):
        return self.op("pool", fn, reads, writes)

    def dma(self, q, fn, reads=(), writes=()):
        return self.op(q, fn, reads, writes, dma=True)

    def emit(self, final_wait_ops=()):
        nc = self.nc
        ops = self.ops
        for o in ops:
            for d in o.deps:
                od = ops[d]
                if od.dma:
                    continue
                if od.eng == o.eng and not o.dma:
                    if o.eng == "pe" or not self.strict or d not in o.raw:
                        continue
                od.signal = True
        with contextlib.ExitStack() as st:
            sems = {e: st.enter_context(nc.semaphore("s_" + e)) for e in ("pe", "act", "dve", "pool")}
            rings = {q: [st.enter_context(nc.semaphore("r_%s%d" % (q, i))) for i in range(self.ring)]
                     for q in ("sp", "pool") if self.dma_ops[q]}
            cnt = {e: 0 for e in sems}
            for o in ops:
                if not o.dma and o.signal:
                    cnt[o.eng] += 1
                    o.sigval = cnt[o.eng]
                    o.sem = sems[o.eng]
            for q, lst in self.dma_ops.items():
                for k, o in enumerate(lst):
                    o.sem = rings[q][k % self.ring]
                    o.sigval = 16 * (k // self.ring + 1)
            block = st.enter_context(nc.Block())
            engmap = {"pe": block.tensor, "act": block.scalar, "dve": block.vector,
                      "pool": block.gpsimd, "sp": block.sync}
            for e in ENGS:
                eops = [o for o in ops if o.eng == e]
                extra = list(final_wait_ops) if e == "sp" else []
                if not eops and not extra:
                    continue

                def body(eng, e=e, eops=eops, extra=extra):
                    waited = {}

                    def wait(od):
                        key = id(od.sem)
                        if waited.get(key, 0) >= od.sigval:
                            return
                        eng.wait_ge(od.sem, od.sigval)
                        waited[key] = od.sigval

                    for o in eops:
                        for d in sorted(o.deps):
                            od = ops[d]
                            if not od.dma and od.eng == e and not o.dma:
                                if e == "pe" or not self.strict or d not in o.raw:
                                    continue
                            wait(od)
                        if o.ring_prev is not None:
                            wait(ops[o.ring_prev])
                        ins = o.fn(eng)
                        if o.dma:
                            ins.then_inc(o.sem, 16)
                        elif o.signal:
                            ins.then_inc(o.sem, 1)
                    for o in extra:
                        wait(o)

                engmap[e](body)


SM = {}
_off = 0
for _n, _w in (("nw", 96), ("gcw", 96), ("scw", 48), ("scb", 12), ("gon", 1), ("son", 8), ("sd", 8),
               ("galog", 8), ("gdtb", 8), ("salog", 16), ("sdtb", 16)):
    SM[_n] = (_off, _w)
    _off += _w
NSM = _off


def build_nc(nslot=NSLOT, debug=None):
    nc = bass.Bass("TRN2", target_bir_lowering=False)
    dt = nc.dram_tensor
    xs = dt("xs", [nslot, D, T], F32, kind="ExternalInput").ap()
    pin = dt("pin", [PLE, T], F32, kind="ExternalInput").ap()
    cmask = dt("cmask", [128, 4], F32, kind="ExternalInput").ap()
    smalls = dt("smalls", [128, NSM], F32, kind="ExternalInput").ap()
    consts = dt("consts", [128, 5, 128], F32, kind="ExternalInput").ap()
    W = {}
    for nm, shp in (("ffn1_w_gate", [D, DFF]), ("ffn1_w_up", [D, DFF]), ("ffn1_w_down", [DFF, D]),
                    ("w_in", [D, IN_DIM]), ("w_out", [D, D]),
                    ("ffn2_w_gate", [D, DFF]), ("ffn2_w_up", [D, DFF]), ("ffn2_w_down", [DFF, D]),
                    ("ple_w_gate", [D, D]), ("ple_w_proj", [PLE, D])):
        W[nm] = dt(nm, shp, F32, kind="ExternalInput").ap()
    out = dt("out", [D, T], F32, kind="ExternalOutput").ap()
    QT = dt("QT", [8, 128, T], BF16, kind="Internal").ap()
    KT = dt("KT", [8, 128, T], BF16, kind="Internal").ap()
    GT = dt("GT", [8, 128, T], F32, kind="Internal").ap()
    ZT = dt("ZT", [8, 128, T], F32, kind="Internal").ap()
    XT = dt("XT", [8, 128, T], F32, kind="Internal").ap()
    BT = dt("BT", [2, 128, T], F32, kind="Internal").ap()
    CT = dt("CT", [2, 128, T], F32, kind="Internal").ap()
    KTM = dt("KTM", [T, 1024], BF16, kind="Internal").ap()
    VTM = dt("VTM", [T, 1024], BF16, kind="Internal").ap()
    XTM = dt("XTM", [T, 1024], F32, kind="Internal").ap()
    BTM = dt("BTM", [T, 256], F32, kind="Internal").ap()

    S = nc.alloc_sbuf_tensor
    h = S("h", [128, NKC, T], F32)
    xn = S("xn", [128, NKC, T], BF16)
    cst = S("cst", [128, 5, 128], F32)
    ones_bf = S("ones_bf", [128, 128], BF16)
    sm = S("sm", [128, NSM], F32)
    cm = S("cm", [128, 4], F32)
    negA_g = S("negA_g", [128, 8], F32)
    negA_s = S("negA_s", [128, 16], F32)
    rstd = S("rstd", [128, T], F32)
    sqb = [S("sqb%d" % i, [128, T], BF16) for i in range(2)]
    halo = S("halo", [128, 36, 3], F32)
    gts = S("gts", [128, 8, 32], F32)
    Gg = S("Gg", [128, 8, 8], F32)
    Bg = S("Bg", [128, 8, 8], F32)
    Dts = S("Dts", [128, 8, 16], F32)
    Las = S("Las", [128, 8, 16], F32)
    Sg = S("Sg", [128, 8, 128], F32)
    Ss = S("Ss", [128, 1024], F32)
    Sgb = S("Sgb", [128, 8, 128], BF16)
    scr = S("scr", [128, 8], F32)
    ARENA_F = 22100
    arena = S("arena", [128, ARENA_F], F32)
    banks = [nc.alloc_psum_tensor("bank%d" % i, [128, 512], F32) for i in range(8)]

    ident = cst[:, 0, :]
    tri = cst[:, 1, :]
    sgt = cst[:, 2, :]
    slt = cst[:, 3, :]
    ones = cst[:, 4, :]

    pg = Prog(nc)

    class Arena:
        def __init__(self):
            self.off = 0
            self.gen = 0

        def reset(self):
            self.off = 0
            self.gen += 1
            old = pg.phase_tok
            pg.phase_tok = None
            tok = ("phase", self.gen)
            pg.dve(lambda e: e.memset(scr[:, 0:1], 0.0), reads=[old] if old else [], writes=[tok, old] if old else [tok])
            pg.phase_tok = tok

        def alloc(self, shape, dtype=F32):
            n = 1
            for s_ in shape[1:]:
                n *= s_
            nf = n if dtype == F32 else (n + 1) // 2
            nf = (nf + 7) // 8 * 8
            assert self.off + nf <= ARENA_F, (self.off, nf)
            ap = arena[:, self.off:self.off + nf]
            self.off += nf
            if dtype != F32:
                ap = ap.bitcast(dtype)
            ap = ap[:, 0:n]
            if len(shape) == 3:
                ap = ap.rearrange("p (a b) -> p a b", a=shape[1])
            elif len(shape) == 4:
                ap = ap.rearrange("p (a b c) -> p a b c", a=shape[1], b=shape[2])
            if shape[0] < 128:
                ap = ap[0:shape[0]]
            return ap

    ar = Arena()
    uid = [0]

    def U(name):
        uid[0] += 1
        return (name, uid[0])

    bank_rr = [0]

    reserved = set()

    def nb():
        while True:
            i = bank_rr[0] % 8
            bank_rr[0] += 1
            if i not in reserved:
                return i

    def bk(i):
        return ("bank", i)

    alt = [0]

    def evac(fn, reads=(), writes=()):
        alt[0] ^= 1
        if alt[0]:
            return pg.act(lambda e: fn(e, True), reads, writes)
        return pg.dve(lambda e: fn(e, False), reads, writes)

    def copy_any(e, is_act, out, in_):
        if is_act:
            return e.copy(out=out, in_=in_)
        return e.tensor_copy(out=out, in_=in_)

    def smc(name, lo=0, hi=None):
        o, w = SM[name]
        hi = w if hi is None else hi
        return sm[:, o + lo:o + hi]

    pg.dma("sp", lambda e: e.dma_start(out=cst[:], in_=consts), writes=["cst"])
    pg.dma("sp", lambda e: e.dma_start(out=sm[:], in_=smalls), writes=["sm"])
    pg.dma("sp", lambda e: e.dma_start(out=cm[:], in_=cmask), writes=["cm"])
    pg.dve(lambda e: e.tensor_copy(out=ones_bf[:], in_=ones), reads=["cst"], writes=["ones_bf"])
    pg.act(lambda e: e.activation(out=negA_g[:], in_=smc("galog"), func=AF.Exp), reads=["sm"], writes=["negA_g0"])
    pg.dve(lambda e: e.tensor_single_scalar(out=negA_g[:], in_=negA_g[:], scalar=-1.0, op=ALU.mult), reads=["negA_g0"], writes=["negA_g"])
    pg.act(lambda e: e.activation(out=negA_s[:], in_=smc("salog"), func=AF.Exp), reads=["sm"], writes=["negA_s0"])
    pg.dve(lambda e: e.tensor_single_scalar(out=negA_s[:], in_=negA_s[:], scalar=-1.0, op=ALU.mult), reads=["negA_s0"], writes=["negA_s"])
    pg.dve(lambda e: e.memset(halo[:], 0.0), writes=[("halo", i) for i in range(36)])
    pg.dve(lambda e: e.memset(Sg[:], 0.0), writes=[("Sg", 0), ("Sg", 1)])
    pg.dve(lambda e: e.memset(Ss[:], 0.0), writes=[("Ss", 0), ("Ss", 1)])
    pg.dve(lambda e: e.memset(Sgb[:], 0.0), writes=[("Sgb", 0), ("Sgb", 1)])

    def run_all(gens):
        gens = list(gens)
        while gens:
            for g_ in list(gens):
                try:
                    next(g_)
                except StopIteration:
                    gens.remove(g_)

    def load_x_gen(s, reset=True, act_only=False):
        for cg in range(4):
            pg.dma("sp", lambda e, cg=cg: e.dma_start(out=h[:, cg * 4:(cg + 1) * 4, :], in_=xs[s, cg * 512:(cg + 1) * 512, :].rearrange("(c p) t -> p c t", p=128)),
                   writes=[("h", cg * 4 + j, tt) for j in range(4) for tt in range(2)])
            yield

    def rmsnorm_to_xn(widx, src=None, src_tok="h", dst=None, dst_tok="xn", dst_is_xn=True):
        src = h if src is None else src
        ssb = [nb(), nb()]
        for c in range(NKC):
            b = c % 2
            if b == 0:
                pg.act(lambda e, c=c, b=b: e.activation(out=sqb[b][:], in_=src[:, c, :], func=AF.Square),
                       reads=[(src_tok, c, 0), (src_tok, c, 1)], writes=[("sqb", b)])
            else:
                pg.dve(lambda e, c=c, b=b: e.tensor_tensor(out=sqb[b][:], in0=src[:, c, :], in1=src[:, c, :], op=ALU.mult),
                       reads=[(src_tok, c, 0), (src_tok, c, 1)], writes=[("sqb", b)])
            for tt in range(2):
                pg.pe(lambda e, c=c, b=b, tt=tt: e.matmul(banks[ssb[tt]][:], lhsT=ones_bf[:], rhs=sqb[b][:, tt * 512:(tt + 1) * 512], start=(c == 0), stop=(c == NKC - 1)),
                      reads=[("sqb", b), "ones_bf"], writes=[bk(ssb[tt])])
        for tt in range(2):
            pg.act(lambda e, tt=tt: e.activation(out=rstd[:, tt * 512:(tt + 1) * 512], in_=banks[ssb[tt]][:], func=AF.Ln, bias=EPS, scale=1.0 / D),
                   reads=[bk(ssb[tt])], writes=[("rstd0", tt), ("rstd", tt)])
            pg.act(lambda e, tt=tt: e.activation(out=rstd[:, tt * 512:(tt + 1) * 512], in_=rstd[:, tt * 512:(tt + 1) * 512], func=AF.Exp, scale=-0.5),
                   reads=[("rstd0", tt)], writes=[("rstd", tt)])
        if dst is None and not dst_is_xn:
            return
        dstt = xn if dst is None else dst
        for c in range(NKC):
            pg.dve(lambda e, c=c: e.scalar_tensor_tensor(out=dstt[:, c, :], in0=src[:, c, :], scalar=sm[:, widx * 16 + c:widx * 16 + c + 1], in1=rstd[:], op0=ALU.mult, op1=ALU.mult),
                   reads=[(src_tok, c, 0), (src_tok, c, 1), ("rstd", 0), ("rstd", 1), "sm"], writes=[(dst_tok, c)])

    def ffn(wg, wu, wd, widx):
        ar.reset()
        wgu = [ar.alloc([128, NKC, 256], BF16) for _ in range(4)]
        wdb = [ar.alloc([128, 4, D], BF16) for _ in range(2)]
        hid = [ar.alloc([128, 4, T], BF16) for _ in range(2)]
        sil = [ar.alloc([128, 512]) for _ in range(2)]
        gub = [(nb(), nb()), (nb(), nb())]
        dnb = [nb(), nb()]
        st = {"gu": 0, "dn": 0, "ws": 0}
        pre = {}

        def issue_w(g, pr):
            fo = (g * 4 + pr * 2) * 128
            sg_, su_ = st["ws"] % 4, (st["ws"] + 1) % 4
            st["ws"] += 2
            pg.dma("pool", lambda e, sg_=sg_, fo=fo: e.dma_start(out=wgu[sg_], in_=wg[:, fo:fo + 256].rearrange("(kc p) f -> p kc f", p=128)), writes=[("wgu", sg_)])
            pg.dma("pool", lambda e, su_=su_, fo=fo: e.dma_start(out=wgu[su_], in_=wu[:, fo:fo + 256].rearrange("(kc p) f -> p kc f", p=128)), writes=[("wgu", su_)])
            return sg_, su_

        pre[(0, 0)] = issue_w(0, 0)
        pre[(0, 1)] = issue_w(0, 1)
        rmsnorm_to_xn(widx)

        def gate_up(g):
            hb = g % 2
            for pr in range(2):
                fo = (g * 4 + pr * 2) * 128
                if (g, pr) in pre:
                    sg_, su_ = pre.pop((g, pr))
                else:
                    sg_, su_ = issue_w(g, pr)
                for fl in range(2):
                    fc = pr * 2 + fl
                    for tt in range(2):
                        gb, ub = gub[st["gu"] % 2]
                        st["gu"] += 1
                        for kc in range(NKC):
                            pg.pe(lambda e, gb=gb, sg_=sg_, kc=kc, fl=fl, tt=tt: e.matmul(banks[gb][:], lhsT=wgu[sg_][:, kc, fl * 128:(fl + 1) * 128], rhs=xn[:, kc, tt * 512:(tt + 1) * 512], start=(kc == 0), stop=(kc == NKC - 1)),
                                  reads=[("wgu", sg_), ("xn", kc)], writes=[bk(gb)])
                        for kc in range(NKC):
                            pg.pe(lambda e, ub=ub, su_=su_, kc=kc, fl=fl, tt=tt: e.matmul(banks[ub][:], lhsT=wgu[su_][:, kc, fl * 128:(fl + 1) * 128], rhs=xn[:, kc, tt * 512:(tt + 1) * 512], start=(kc == 0), stop=(kc == NKC - 1)),
                                  reads=[("wgu", su_), ("xn", kc)], writes=[bk(ub)])
                        sb_ = tt
                        pg.act(lambda e, gb=gb, sb_=sb_: e.activation(out=sil[sb_], in_=banks[gb][:], func=AF.Silu), reads=[bk(gb)], writes=[("sil", sb_)])
                        pg.dve(lambda e, ub=ub, sb_=sb_, hb=hb, fc=fc, tt=tt: e.tensor_tensor(out=hid[hb][:, fc, tt * 512:(tt + 1) * 512], in0=sil[sb_], in1=banks[ub][:], op=ALU.mult),
                               reads=[("sil", sb_), bk(ub)], writes=[("hid", hb, fc, tt)])

        def down(g):
            hb = g % 2
            pg.dma("pool", lambda e, hb=hb, g=g: e.dma_start(out=wdb[hb], in_=wd[g * 512:(g + 1) * 512, :].rearrange("(fc p) o -> p fc o", p=128)), writes=[("wdb", hb)])
            for oc in range(NKC):
                for tt in range(2):
                    db = dnb[st["dn"] % 2]
                    st["dn"] += 1
                    for fc in range(4):
                        pg.pe(lambda e, db=db, hb=hb, fc=fc, oc=oc, tt=tt: e.matmul(banks[db][:], lhsT=wdb[hb][:, fc, oc * 128:(oc + 1) * 128], rhs=hid[hb][:, fc, tt * 512:(tt + 1) * 512], start=(fc == 0), stop=(fc == 3)),
                              reads=[("wdb", hb), ("hid", hb, fc, tt)], writes=[bk(db)])
                    pg.dve(lambda e, db=db, oc=oc, tt=tt: e.scalar_tensor_tensor(out=h[:, oc, tt * 512:(tt + 1) * 512], in0=banks[db][:], scalar=0.5, in1=h[:, oc, tt * 512:(tt + 1) * 512], op0=ALU.mult, op1=ALU.add),
                           reads=[bk(db), ("h", oc, tt)], writes=[("h", oc, tt)])

        for g in range(11):
            gate_up(g)
            if g > 0:
                down(g - 1)
        down(10)

    def proj_phase(s, own):
        ar.reset()
        rmsnorm_to_xn(1)
        wsl = [ar.alloc([128, NKC, 128], BF16) for _ in range(3)]
        wgt = ar.alloc([128, NKC, 32], BF16)
        pc = [ar.alloc([128, T + 3]) for _ in range(2)]
        cs = [ar.alloc([128, T]) for _ in range(2)]
        cn = [ar.alloc([128, T]) for _ in range(2)]
        rs = ar.alloc([128, T])
        tms = [ar.alloc([128, 4, 128]) for _ in range(2)]
        tmsb = [ar.alloc([128, 4, 128], BF16) for _ in range(2)]
        cnb = [ar.alloc([128, T], BF16) for _ in range(2)]
        win = W["w_in"]
        pg.dma("pool", lambda e: e.dma_start(out=wgt[:, :, 0:16], in_=win[:, 4096:4112].rearrange("(kc p) f -> p kc f", p=128)), writes=["wgt_a"])
        pg.dma("pool", lambda e: e.dma_start(out=wgt[:, :, 16:32], in_=win[:, 6672:6688].rearrange("(kc p) f -> p kc f", p=128)), writes=["wgt_b"])
        gbk = nb()
        for tk in range(8):
            for kc in range(NKC):
                pg.pe(lambda e, tk=tk, kc=kc: e.matmul(banks[gbk][:, tk * 32:(tk + 1) * 32], lhsT=xn[:, kc, tk * 128:(tk + 1) * 128], rhs=wgt[:, kc, :], start=(kc == 0), stop=(kc == NKC - 1)),
                      reads=["wgt_a", "wgt_b", ("xn", kc)], writes=[bk(gbk)])
        pg.act(lambda e: e.copy(out=gts[:], in_=banks[gbk][:, 0:256].rearrange("p (a b) -> p a b", a=8)), reads=[bk(gbk)], writes=["gts"])
        pg.dve(lambda e: e.tensor_tensor(out=Gg[:], in0=gts[:, :, 0:8], in1=smc("gdtb").unsqueeze(1).broadcast_to([128, 8, 8]), op=ALU.add), reads=["gts", "sm"], writes=["Gg0", "Gg"])
        pg.act(lambda e: e.activation(out=Gg[:], in_=Gg[:], func=AF.Exp), reads=["Gg0"], writes=["Gg1"])
        pg.act(lambda e: e.activation(out=Gg[:], in_=Gg[:], func=AF.Ln, bias=1.0, scale=1.0), reads=["Gg1"], writes=["Gg2"])
        pg.dve(lambda e: e.tensor_tensor(out=Gg[:], in0=Gg[:], in1=negA_g[:].unsqueeze(1).broadcast_to([128, 8, 8]), op=ALU.mult), reads=["Gg2", "negA_g"], writes=["Gg"])
        pg.act(lambda e: e.activation(out=Bg[:], in_=gts[:, :, 8:16], func=AF.Sigmoid), reads=["gts"], writes=["Bg"])
        pg.dve(lambda e: e.tensor_tensor(out=Dts[:], in0=gts[:, :, 16:32], in1=smc("sdtb").unsqueeze(1).broadcast_to([128, 8, 16]), op=ALU.add), reads=["gts", "sm"], writes=["Dts0", "Dts"])
        pg.act(lambda e: e.activation(out=Dts[:], in_=Dts[:], func=AF.Exp), reads=["Dts0"], writes=["Dts1"])
        pg.act(lambda e: e.activation(out=Dts[:], in_=Dts[:], func=AF.Ln, bias=1.0, scale=1.0), reads=["Dts1"], writes=["Dts2"])
        pg.dve(lambda e: e.tensor_single_scalar(out=Dts[:], in_=Dts[:], scalar=cm[:, s:s + 1], op=ALU.mult), reads=["Dts2", "cm"], writes=["Dts"])
        pg.dve(lambda e: e.tensor_tensor(out=Las[:], in0=Dts[:], in1=negA_s[:].unsqueeze(1).broadcast_to([128, 8, 16]), op=ALU.mult), reads=["Dts", "negA_s"], writes=["Las"])

        chunks = []
        for c in range(8):
            chunks.append(("q", c, c * 128, c, own))
        for c in range(8):
            chunks.append(("k", c, 1024 + c * 128, 8 + c, True))
        for c in range(8):
            chunks.append(("v", c, 2048 + c * 128, 16 + c, True))
        for c in range(8):
            chunks.append(("gate", c, 3072 + c * 128, None, own))
        for c in range(8):
            chunks.append(("z", c, 4112 + c * 128, None, own))
        for c in range(8):
            chunks.append(("x", c, 5136 + c * 128, 24 + c, True))
        for c in range(2):
            chunks.append(("B", c, 6160 + c * 128, 32 + c, True))
        for c in range(2):
            chunks.append(("C", c, 6416 + c * 128, 34 + c, own))
        items = [ch for ch in chunks if ch[4]]
        if s == nslot - 2:
            hb_ = nb()
            hn_ = 0
            for kind, c, col, ci, _ in chunks:
                if kind not in ("q", "C"):
                    continue
                sl = hn_ % 3
                pg.dma("pool", lambda e, sl=sl, col=col: e.dma_start(out=wsl[sl], in_=win[:, col:col + 128].rearrange("(kc p) f -> p kc f", p=128)), writes=[("wsl", sl)])
                for kc in range(NKC):
                    pg.pe(lambda e, kc=kc, sl=sl, hn_=hn_: e.matmul(banks[hb_][:, hn_ * 4:hn_ * 4 + 3], lhsT=wsl[sl][:, kc, :], rhs=xn[:, kc, T - 3:T], start=(kc == 0), stop=(kc == NKC - 1)),
                          reads=[("wsl", sl), ("xn", kc)], writes=[bk(hb_)])
                pg.act(lambda e, ci=ci, hn_=hn_: e.copy(out=halo[:, ci, :], in_=banks[hb_][:, hn_ * 4:hn_ * 4 + 3]), reads=[bk(hb_)], writes=[("halo", ci)])
                hn_ += 1
        N_ = len(items)
        gz = [ar.alloc([128, T]) for _ in range(2)]
        mmb = [(0, 1), (2, 3)]
        l2b = (4, 5)
        trb = (6, 7)

        def A1(n):
            kind, c, col, ci, _ = items[n]
            sl = n % 3
            pg.dma("pool", lambda e, sl=sl, col=col: e.dma_start(out=wsl[sl], in_=win[:, col:col + 128].rearrange("(kc p) f -> p kc f", p=128)), writes=[("wsl", sl)])
            pb = mmb[n % 2]
            for tt in range(2):
                for kc in range(NKC):
                    pg.pe(lambda e, tt=tt, kc=kc, sl=sl, pb=pb: e.matmul(banks[pb[tt]][:], lhsT=wsl[sl][:, kc, :], rhs=xn[:, kc, tt * 512:(tt + 1) * 512], start=(kc == 0), stop=(kc == NKC - 1)),
                          reads=[("wsl", sl), ("xn", kc)], writes=[bk(pb[tt])])

        def A2(n):
            kind, c, col, ci, _ = items[n]
            b2 = n % 2
            pb = mmb[n % 2]
            if ci is None:
                for tt in range(2):
                    pg.act(lambda e, tt=tt, b2=b2, pb=pb: e.activation(out=gz[b2][:, tt * 512:(tt + 1) * 512], in_=banks[pb[tt]][:], func=AF.Silu), reads=[bk(pb[tt])], writes=[("gz", b2, tt)])
                dst = GT if kind == "gate" else ZT
                pg.dma("sp", lambda e, dst=dst, c=c, b2=b2: e.dma_start(out=dst[c], in_=gz[b2]), reads=[("gz", b2, 0), ("gz", b2, 1)], writes=[(kind + "T", c)])
                return
            pg.dve(lambda e, b2=b2, ci=ci: e.tensor_copy(out=pc[b2][:, 0:3], in_=halo[:, ci, :]), reads=[("halo", ci)], writes=[("pc", b2, "h")])
            for tt in range(2):
                pg.act(lambda e, tt=tt, b2=b2, pb=pb: e.copy(out=pc[b2][:, 3 + tt * 512:3 + (tt + 1) * 512], in_=banks[pb[tt]][:]), reads=[bk(pb[tt])], writes=[("pc", b2, tt)])
            pg.dve(lambda e, b2=b2, ci=ci: e.tensor_copy(out=halo[:, ci, :], in_=pc[b2][:, T:T + 3]), reads=[("pc", b2, 1)], writes=[("halo", ci)])

        srcs = {}

        def B1(n):
            kind, c, col, ci, _ = items[n]
            if ci is None:
                return
            b2 = n % 2
            cwo = SM["gcw"][0] if ci < 24 else SM["scw"][0]
            cj = ci if ci < 24 else ci - 24
            pcr = [("pc", b2, "h"), ("pc", b2, 0), ("pc", b2, 1), "sm"]
            pg.dve(lambda e, b2=b2, cwo=cwo, cj=cj: e.tensor_single_scalar(out=cn[b2], in_=pc[b2][:, 0:T], scalar=sm[:, cwo + cj * 4:cwo + cj * 4 + 1], op=ALU.mult), reads=pcr, writes=[("cn", b2)])
            for j in range(1, 4):
                pg.dve(lambda e, b2=b2, cwo=cwo, cj=cj, j=j: e.scalar_tensor_tensor(out=cn[b2], in0=pc[b2][:, j:T + j], scalar=sm[:, cwo + cj * 4 + j:cwo + cj * 4 + j + 1], in1=cn[b2], op0=ALU.mult, op1=ALU.add),
                       reads=pcr + [("cn", b2)], writes=[("cn", b2)])
            if ci < 24:
                pg.act(lambda e, b2=b2: e.activation(out=cs[b2], in_=cn[b2], func=AF.Silu), reads=[("cn", b2)], writes=[("cs", b2, 0), ("cs", b2, 1)])
            else:
                pg.act(lambda e, b2=b2, cj=cj: e.activation(out=cs[b2], in_=cn[b2], func=AF.Silu, bias=sm[:, SM["scb"][0] + cj:SM["scb"][0] + cj + 1], scale=1.0), reads=[("cn", b2), "sm"], writes=[("cs", b2, 0), ("cs", b2, 1)])
            csr = [("cs", b2, 0), ("cs", b2, 1)]
            src = cs[b2]
            if kind in ("q", "k"):
                pg.dve(lambda e, b2=b2: e.tensor_tensor(out=cn[b2], in0=cs[b2], in1=cs[b2], op=ALU.mult), reads=csr + [("cn", b2)], writes=[("cn", b2)])
                lb = l2b
                for tt in range(2):
                    pg.pe(lambda e, tt=tt, b2=b2, lb=lb: e.matmul(banks[lb[tt]][:], lhsT=ones, rhs=cn[b2][:, tt * 512:(tt + 1) * 512], start=True, stop=True), reads=[("cn", b2), "cst"], writes=[bk(lb[tt])])
                for tt in range(2):
                    pg.act(lambda e, tt=tt, lb=lb: e.activation(out=rs[:, tt * 512:(tt + 1) * 512], in_=banks[lb[tt]][:], func=AF.Ln, bias=EPS, scale=1.0), reads=[bk(lb[tt])], writes=[("rs0", tt), ("rs", tt)])
                for tt in range(2):
                    pg.act(lambda e, tt=tt: e.activation(out=rs[:, tt * 512:(tt + 1) * 512], in_=rs[:, tt * 512:(tt + 1) * 512], func=AF.Exp, scale=-0.5), reads=[("rs0", tt)], writes=[("rs", tt)])
                sc = (128.0 ** -0.5) if kind == "q" else 1.0
                pg.dve(lambda e, b2=b2, sc=sc: e.scalar_tensor_tensor(out=cn[b2], in0=cs[b2], scalar=sc, in1=rs, op0=ALU.mult, op1=ALU.mult), reads=csr + [("rs", 0), ("rs", 1), ("cn", b2)], writes=[("cn", b2)])
                src = cn[b2]
                csr = [("cn", b2)]
            srcs[n] = (src, csr)
            fm = {"q": QT, "k": KT, "x": XT, "B": BT, "C": CT}.get(kind)
            if fm is not None and (own or kind == "k"):
                if kind in ("q", "k"):
                    pg.act(lambda e, b2=b2, src=src: e.copy(out=cnb[b2], in_=src), reads=csr, writes=[("cnb", b2)])
                    pg.dma("sp", lambda e, fm=fm, c=c, b2=b2: e.dma_start(out=fm[c], in_=cnb[b2]), reads=[("cnb", b2)], writes=[(kind + "T", c)])
                else:
                    pg.dma("sp", lambda e, fm=fm, c=c, src=src: e.dma_start(out=fm[c], in_=src), reads=csr, writes=[(kind + "T", c)])

        def B2(n):
            kind, c, col, ci, _ = items[n]
            if ci is None:
                return
            src, csr = srcs[n]
            tm = {"k": KTM, "v": VTM, "x": XTM, "B": BTM}.get(kind)
            if tm is not None:
                for tg in range(2):
                    bi = trb[tg]
                    tb = tg
                    for j in range(4):
                        tk = tg * 4 + j
                        pg.pe(lambda e, bi=bi, j=j, tk=tk, src=src: e.transpose(out=banks[bi][:, j * 128:(j + 1) * 128], in_=src[:, tk * 128:(tk + 1) * 128], identity=ident), reads=csr + ["cst"], writes=[bk(bi)])
                    tmx = tmsb if kind in ("k", "v") else tms
                    tmt = "tmsb" if kind in ("k", "v") else "tms"
                    evac(lambda e, a, bi=bi, tb=tb, tmx=tmx: copy_any(e, a, tmx[tb], banks[bi][:].rearrange("p (a b) -> p a b", a=4)), reads=[bk(bi)], writes=[(tmt, tb)])
                    pg.dma("sp", lambda e, tm=tm, tg=tg, tb=tb, c=c, tmx=tmx: e.dma_start(out=tm[tg * 512:(tg + 1) * 512, c * 128:(c + 1) * 128].rearrange("(a p) f -> p a f", p=128), in_=tmx[tb]),
                           reads=[(tmt, tb)], writes=[(kind + "TM", c, tg)])

        for n in range(-2, N_ + 1):
            if 0 <= n + 2 < N_:
                A1(n + 2)
            if 0 <= n < N_:
                B1(n)
            if 0 <= n + 1 < N_:
                A2(n + 1)
            if 0 <= n - 1 < N_:
                B2(n - 1)

    def gdn_gen(own, reset=True):
        if reset:
            ar.reset()
        KTc = [ar.alloc([128, 8, 128], BF16) for _ in range(2)]
        Ktm = [ar.alloc([128, 8, 128], BF16) for _ in range(2)]
        Vtm = [ar.alloc([128, 8, 128], BF16) for _ in range(2)]
        QTc = [ar.alloc([128, 8, 128], BF16) for _ in range(2)] if own else None
        GTc = [ar.alloc([128, 8, 128]) for _ in range(2)] if own else None
        RG = ar.alloc([128, 8, 128])
        KG = ar.alloc([128, 8, 128], BF16)
        KD = [ar.alloc([128, 8, 128], BF16) for _ in range(2)]
        Pm = [ar.alloc([128, 8, 128], BF16) for _ in range(2)]
        NW = [ar.alloc([128, 8, 128], BF16) for _ in range(2)]
        AT = [ar.alloc([128, 8, 128], BF16) for _ in range(2)] if own else None
        QD = [ar.alloc([128, 8, 128], BF16) for _ in range(2)] if own else None
        gsm = [ar.alloc([128, 40]) for _ in range(2)]
        Dh = [ar.alloc([128, 4, 128]) for _ in range(2)]
        DmS = [ar.alloc([128, 4, 128]) for _ in range(2)]
        DmI = [ar.alloc([128, 4, 128]) for _ in range(2)] if own else None
        Ub = [[ar.alloc([128, 4, 128], BF16) for _ in range(2)] for _ in range(2)]
        Lb = [[ar.alloc([128, 4, 128], BF16) for _ in range(2)] for _ in range(2)]
        VN = ar.alloc([128, 8, 128], BF16)
        oT = ar.alloc([128, 8, 128]) if own else None
        oS = ar.alloc([128, 8, 128]) if own else None
        r4 = "p (a b) -> p a b"
        bc4 = lambda ap: ap.unsqueeze(2).broadcast_to([128, 4, 128])
        m4 = lambda ap: ap.unsqueeze(1).broadcast_to([128, 4, 128])

        def G1(tc):
            b = tc % 2
            tsl = slice(tc * 128, (tc + 1) * 128)
            tg = tc // 4
            g_ = gsm[b]
            pg.dma("sp", lambda e: e.dma_start(out=KTc[b], in_=KT[:, :, tsl].rearrange("h p t -> p h t")), reads=[("kT", c) for c in range(8)], writes=[("KTc", b)])
            pg.dma("sp", lambda e: e.dma_start(out=Ktm[b], in_=KTM[tsl, :].rearrange("p (h f) -> p h f", h=8)), reads=[("kTM", c, tg) for c in range(8)], writes=[("Ktm", b)])
            pg.dma("sp", lambda e: e.dma_start(out=Vtm[b], in_=VTM[tsl, :].rearrange("p (h f) -> p h f", h=8)), reads=[("vTM", c, tg) for c in range(8)], writes=[("Vtm", b)])
            if own:
                pg.dma("sp", lambda e: e.dma_start(out=QTc[b], in_=QT[:, :, tsl].rearrange("h p t -> p h t")), reads=[("qT", c) for c in range(8)], writes=[("QTc", b)])
                pg.dma("sp", lambda e: e.dma_start(out=GTc[b], in_=GT[:, :, tsl].rearrange("h p t -> p h t")), reads=[("gateT", c) for c in range(8)], writes=[("GTc", b)])
            gb_ = nb()
            pg.pe(lambda e: e.matmul(banks[gb_][:, 0:8], lhsT=tri, rhs=Gg[:, tc, :], start=True, stop=True), reads=["Gg", "cst"], writes=[bk(gb_)])
            pg.pe(lambda e: e.matmul(banks[gb_][:, 8:16], lhsT=ones, rhs=Gg[:, tc, :], start=True, stop=True), reads=["Gg", "cst"], writes=[bk(gb_)])
            pg.act(lambda e: e.copy(out=g_[:, 0:8], in_=banks[gb_][:, 0:8]), reads=[bk(gb_)], writes=[("gcs", b)])
            pg.act(lambda e: e.activation(out=g_[:, 8:16], in_=g_[:, 0:8], func=AF.Exp), reads=[("gcs", b)], writes=[("egc", b)])
            pg.dve(lambda e: e.tensor_tensor(out=g_[:, 32:40], in0=banks[gb_][:, 8:16], in1=g_[:, 0:8], op=ALU.subtract), reads=[bk(gb_), ("gcs", b)], writes=[("gtmp", b)])
            pg.act(lambda e: e.activation(out=g_[:, 16:24], in_=g_[:, 32:40], func=AF.Exp), reads=[("gtmp", b)], writes=[("dkd", b)])
            pg.act(lambda e: e.activation(out=g_[:, 24:32], in_=banks[gb_][:, 8:16], func=AF.Exp), reads=[bk(gb_)], writes=[("gtb", b)])
            pg.pool(lambda e: e.tensor_tensor(out=RG, in0=tri.unsqueeze(1).broadcast_to([128, 8, 128]), in1=Gg[:, tc, :].unsqueeze(2).broadcast_to([128, 8, 128]), op=ALU.mult), reads=["Gg", "cst"], writes=["RG"])
            pg.pool(lambda e: e.tensor_tensor(out=KG, in0=Ktm[b], in1=g_[:, 8:16].unsqueeze(2).broadcast_to([128, 8, 128]), op=ALU.mult), reads=[("Ktm", b), ("egc", b)], writes=["KG"])
            pg.pool(lambda e: e.tensor_tensor(out=KD[b], in0=Ktm[b], in1=g_[:, 16:24].unsqueeze(2).broadcast_to([128, 8, 128]), op=ALU.mult), reads=[("Ktm", b), ("dkd", b)], writes=[("KD", b)])
            HF = (0, 1)
            eb_ = [nb(), nb()]
            for hf in HF:
                for j in range(4):
                    pg.pe(lambda e, j=j, hf=hf: e.matmul(banks[eb_[hf]][:, j * 128:(j + 1) * 128], lhsT=sgt, rhs=RG[:, hf * 4 + j, :], start=True, stop=True), reads=["RG", "cst"], writes=[bk(eb_[hf])])
            gk_ = [nb(), nb()]
            for hf in HF:
                for j in range(4):
                    pg.pe(lambda e, j=j, hf=hf: e.matmul(banks[gk_[hf]][:, j * 128:(j + 1) * 128], lhsT=KTc[b][:, hf * 4 + j, :], rhs=KTc[b][:, hf * 4 + j, :], start=True, stop=True), reads=[("KTc", b)], writes=[bk(gk_[hf])])
            for hf in HF:
                pg.act(lambda e, hf=hf: e.activation(out=Dh[hf], in_=banks[eb_[hf]][:].rearrange(r4, a=4), func=AF.Exp), reads=[bk(eb_[hf])], writes=[("Dh", hf)])
                pg.pool(lambda e, hf=hf: e.tensor_tensor(out=DmS[hf], in0=Dh[hf], in1=m4(slt), op=ALU.mult), reads=[("Dh", hf), "cst"], writes=[("DmS", hf)])
                if own:
                    pg.dve(lambda e, hf=hf: e.tensor_tensor(out=DmI[hf], in0=Dh[hf], in1=m4(tri), op=ALU.mult), reads=[("Dh", hf), "cst"], writes=[("DmI", hf)])
            for hf in HF:
                hs = slice(hf * 4, hf * 4 + 4)
                pg.dve(lambda e, hf=hf: e.tensor_tensor(out=Dh[hf], in0=banks[gk_[hf]][:].rearrange(r4, a=4), in1=DmS[hf], op=ALU.mult), reads=[bk(gk_[hf]), ("DmS", hf), ("Dh", hf)], writes=[("Dh", hf)])
                pg.pool(lambda e, hf=hf, hs=hs: e.tensor_tensor(out=Dh[hf], in0=Dh[hf], in1=bc4(Bg[:, tc, hs]), op=ALU.mult), reads=[("Dh", hf), "Bg"], writes=[("Dh", hf)])
            if own:
                qb_ = [nb(), nb()]
                for hf in HF:
                    hs = slice(hf * 4, hf * 4 + 4)
                    for j in range(4):
                        pg.pe(lambda e, j=j, hf=hf: e.matmul(banks[qb_[hf]][:, j * 128:(j + 1) * 128], lhsT=KTc[b][:, hf * 4 + j, :], rhs=QTc[b][:, hf * 4 + j, :], start=True, stop=True), reads=[("KTc", b), ("QTc", b)], writes=[bk(qb_[hf])])
                    pg.dve(lambda e, hf=hf, hs=hs: e.tensor_tensor(out=AT[b][:, hs, :], in0=banks[qb_[hf]][:].rearrange(r4, a=4), in1=DmI[hf], op=ALU.mult), reads=[bk(qb_[hf]), ("DmI", hf)], writes=[("AT", b, hf)])
                xb_ = [nb(), nb()]
                for hf in HF:
                    hs = slice(hf * 4, hf * 4 + 4)
                    for j in range(4):
                        pg.pe(lambda e, j=j, hf=hf: e.matmul(banks[xb_[hf]][:, j * 128:(j + 1) * 128], lhsT=ones, rhs=RG[:, hf * 4 + j, :], start=True, stop=True), reads=["RG", "cst"], writes=[bk(xb_[hf])])
                    pg.act(lambda e, hf=hf: e.activation(out=DmI[hf], in_=banks[xb_[hf]][:].rearrange(r4, a=4), func=AF.Exp), reads=[bk(xb_[hf]), ("AT", b, hf)], writes=[("DmI", hf)])
                    pg.dve(lambda e, hf=hf, hs=hs: e.tensor_tensor(out=QD[b][:, hs, :], in0=DmI[hf], in1=QTc[b][:, hs, :], op=ALU.mult), reads=[("DmI", hf), ("QTc", b)], writes=[("QD", b, hf)])
            yield
            tb_ = [nb(), nb()]
            for hf in HF:
                hs = slice(hf * 4, hf * 4 + 4)
                for j in range(4):
                    pg.pe(lambda e, j=j, hf=hf: e.transpose(out=banks[tb_[hf]][:, j * 128:(j + 1) * 128], in_=Dh[hf][:, j, :], identity=ident), reads=[("Dh", hf), "cst"], writes=[bk(tb_[hf])])
                pg.act(lambda e, hf=hf: e.copy(out=Lb[hf][0], in_=banks[tb_[hf]][:].rearrange(r4, a=4)), reads=[bk(tb_[hf])], writes=[("L", hf, 0)])
                pg.act(lambda e, hf=hf: e.copy(out=Ub[hf][0], in_=Dh[hf]), reads=[("Dh", hf)], writes=[("U", hf, 0)])
                pg.pool(lambda e, hf=hf, hs=hs: e.tensor_tensor(out=Pm[b][:, hs, :], in0=m4(ident), in1=Dh[hf], op=ALU.subtract), reads=[("Dh", hf), "cst"], writes=[("Pm", b, hf)])
            cur = 0
            for lv in range(6):
                nx = 1 - cur
                l2 = [nb(), nb()]
                u2 = [nb(), nb()] if lv < 5 else None
                for hf in HF:
                    for j in range(4):
                        pg.pe(lambda e, j=j, hf=hf, cur=cur, l2=l2: e.matmul(banks[l2[hf]][:, j * 128:(j + 1) * 128], lhsT=Ub[hf][cur][:, j, :], rhs=Lb[hf][cur][:, j, :], start=True, stop=True), reads=[("U", hf, cur), ("L", hf, cur)], writes=[bk(l2[hf])])
                    if lv < 5:
                        for j in range(4):
                            pg.pe(lambda e, j=j, hf=hf, cur=cur, u2=u2: e.matmul(banks[u2[hf]][:, j * 128:(j + 1) * 128], lhsT=Lb[hf][cur][:, j, :], rhs=Ub[hf][cur][:, j, :], start=True, stop=True), reads=[("U", hf, cur), ("L", hf, cur)], writes=[bk(u2[hf])])
                for hf in HF:
                    pg.act(lambda e, hf=hf, nx=nx, l2=l2: e.copy(out=Lb[hf][nx], in_=banks[l2[hf]][:].rearrange(r4, a=4)), reads=[bk(l2[hf])], writes=[("L", hf, nx)])
                    if lv < 5:
                        if hf == 0:
                            pg.dve(lambda e, hf=hf, nx=nx, u2=u2: e.tensor_copy(out=Ub[hf][nx], in_=banks[u2[hf]][:].rearrange(r4, a=4)), reads=[bk(u2[hf])], writes=[("U", hf, nx)])
                        else:
                            pg.act(lambda e, hf=hf, nx=nx, u2=u2: e.copy(out=Ub[hf][nx], in_=banks[u2[hf]][:].rearrange(r4, a=4)), reads=[bk(u2[hf])], writes=[("U", hf, nx)])
                pb_ = [nb(), nb()]
                for hf in HF:
                    for j in range(4):
                        pg.pe(lambda e, j=j, hf=hf, nx=nx, pb_=pb_: e.matmul(banks[pb_[hf]][:, j * 128:(j + 1) * 128], lhsT=Lb[hf][nx][:, j, :], rhs=Pm[b][:, hf * 4 + j, :], start=True, stop=True), reads=[("L", hf, nx), ("Pm", b, hf)], writes=[bk(pb_[hf])])
                for hf in HF:
                    hs = slice(hf * 4, hf * 4 + 4)
                    pg.dve(lambda e, hf=hf, hs=hs, pb_=pb_: e.tensor_tensor(out=Pm[b][:, hs, :], in0=Pm[b][:, hs, :], in1=banks[pb_[hf]][:].rearrange(r4, a=4), op=ALU.add), reads=[bk(pb_[hf]), ("Pm", b, hf)], writes=[("Pm", b, hf)])
                cur = nx
                yield
            wb_ = [nb(), nb()]
            for hf in HF:
                hs = slice(hf * 4, hf * 4 + 4)
                for j in range(4):
                    pg.pe(lambda e, j=j, hf=hf: e.matmul(banks[wb_[hf]][:, j * 128:(j + 1) * 128], lhsT=KG[:, hf * 4 + j, :], rhs=Pm[b][:, hf * 4 + j, :], start=True, stop=True), reads=["KG", ("Pm", b, hf)], writes=[bk(wb_[hf])])
                pg.act(lambda e, hf=hf, hs=hs: e.mul(out=NW[b][:, hs, :], in_=banks[wb_[hf]][:].rearrange(r4, a=4), mul=-1.0), reads=[bk(wb_[hf])], writes=[("NW", b, hf)])

        def G2(tc):
            b = tc % 2
            tsl = slice(tc * 128, (tc + 1) * 128)
            g_ = gsm[b]
            HF = (0, 1)
            vb_ = [nb(), nb()]
            for hf in HF:
                hs = slice(hf * 4, hf * 4 + 4)
                for j in range(4):
                    h_ = hf * 4 + j
                    pg.pe(lambda e, j=j, hf=hf, h_=h_: e.matmul(banks[vb_[hf]][:, j * 128:(j + 1) * 128], lhsT=Pm[b][:, h_, :], rhs=Vtm[b][:, h_, :], start=True, stop=False), reads=[("Pm", b, hf), ("Vtm", b)], writes=[bk(vb_[hf])])
                    pg.pe(lambda e, j=j, hf=hf, h_=h_: e.matmul(banks[vb_[hf]][:, j * 128:(j + 1) * 128], lhsT=NW[b][:, h_, :], rhs=Sgb[:, h_, :], start=False, stop=True), reads=[("NW", b, hf), ("Sgb", hf)], writes=[bk(vb_[hf])])
                pg.dve(lambda e, hf=hf, hs=hs: e.tensor_tensor(out=VN[:, hs, :], in0=banks[vb_[hf]][:].rearrange(r4, a=4), in1=bc4(Bg[:, tc, hs]), op=ALU.mult), reads=[bk(vb_[hf]), "Bg"], writes=[("VN", hf)])
            yield
            if own:
                ob_ = [nb(), nb()]
                for hf in HF:
                    hs = slice(hf * 4, hf * 4 + 4)
                    for j in range(4):
                        h_ = hf * 4 + j
                        pg.pe(lambda e, j=j, hf=hf, h_=h_: e.matmul(banks[ob_[hf]][:, j * 128:(j + 1) * 128], lhsT=Sgb[:, h_, :], rhs=QD[b][:, h_, :], start=True, stop=False), reads=[("Sgb", hf), ("QD", b, hf)], writes=[bk(ob_[hf])])
                        pg.pe(lambda e, j=j, hf=hf, h_=h_: e.matmul(banks[ob_[hf]][:, j * 128:(j + 1) * 128], lhsT=VN[:, h_, :], rhs=AT[b][:, h_, :], start=False, stop=True), reads=[("VN", hf), ("AT", b, hf)], writes=[bk(ob_[hf])])
                    pg.act(lambda e, hf=hf, hs=hs: e.copy(out=oT[:, hs, :], in_=banks[ob_[hf]][:].rearrange(r4, a=4)), reads=[bk(ob_[hf])], writes=[("oT", hf), ("oT2", hf)])
            yield
            sb_ = [nb(), nb()]
            for hf in HF:
                hs = slice(hf * 4, hf * 4 + 4)
                for j in range(4):
                    h_ = hf * 4 + j
                    pg.pe(lambda e, j=j, hf=hf, h_=h_: e.matmul(banks[sb_[hf]][:, j * 128:(j + 1) * 128], lhsT=KD[b][:, h_, :], rhs=VN[:, h_, :], start=True, stop=True), reads=[("KD", b), ("VN", hf)], writes=[bk(sb_[hf])])
                pg.pool(lambda e, hs=hs, hf=hf: e.tensor_tensor(out=Sg[:, hs, :], in0=Sg[:, hs, :], in1=bc4(g_[:, 24 + hf * 4:28 + hf * 4]), op=ALU.mult), reads=[("Sg", hf), ("gtb", b)], writes=[("Sg", hf)])
                pg.dve(lambda e, hf=hf, hs=hs: e.tensor_tensor(out=Sg[:, hs, :], in0=Sg[:, hs, :], in1=banks[sb_[hf]][:].rearrange(r4, a=4), op=ALU.add), reads=[("Sg", hf), bk(sb_[hf])], writes=[("Sg", hf)])
                pg.act(lambda e, hs=hs, hf=hf: e.copy(out=Sgb[:, hs, :], in_=Sg[:, hs, :]), reads=[("Sg", hf)], writes=[("Sgb", hf)])
            yield
            if own:
                for hf in HF:
                    hs = slice(hf * 4, hf * 4 + 4)
                    pg.dve(lambda e, hs=hs: e.tensor_tensor(out=oS[:, hs, :], in0=oT[:, hs, :], in1=oT[:, hs, :], op=ALU.mult), reads=[("oT", hf)], writes=[("oS", hf), ("oS1", hf), ("oS2", hf)])
                    nb_ = nb()
                    pg.pe(lambda e, nb_=nb_, hs=hs: e.matmul(banks[nb_][:], lhsT=ones, rhs=oS[:, hs, :].rearrange("p a b -> p (a b)"), start=True, stop=True), reads=[("oS", hf), "cst"], writes=[bk(nb_)])
                    pg.act(lambda e, nb_=nb_, hs=hs: e.activation(out=oS[:, hs, :], in_=banks[nb_][:].rearrange(r4, a=4), func=AF.Ln, bias=EPS, scale=1.0 / 128), reads=[bk(nb_), ("oS", hf)], writes=[("oS1", hf), ("oS", hf)])
                    pg.act(lambda e, hs=hs: e.activation(out=oS[:, hs, :], in_=oS[:, hs, :], func=AF.Exp, scale=-0.5), reads=[("oS1", hf)], writes=[("oS2", hf)])
                    pg.dve(lambda e, hs=hs: e.tensor_tensor(out=oT[:, hs, :], in0=oT[:, hs, :], in1=oS[:, hs, :], op=ALU.mult), reads=[("oT", hf), ("oS2", hf)], writes=[("oT2", hf)])
                    pg.dve(lambda e, hs=hs, tsl=tsl, b=b: e.scalar_tensor_tensor(out=xn[:, hs, tsl], in0=oT[:, hs, :], scalar=sm[:, SM["gon"][0]:SM["gon"][0] + 1], in1=GTc[b][:, hs, :], op0=ALU.mult, op1=ALU.mult),
                           reads=[("oT2", hf), ("GTc", b), "sm"], writes=[("xn", c_) for c_ in range(hf * 4, hf * 4 + 4)])

        for _ in G1(0):
            pass
        for tc in range(8):
            subs = [G2(tc)]
            if tc + 1 < 8:
                subs.insert(0, G1(tc + 1))
            rnd = 0
            while subs:
                for g_ in list(subs):
                    try:
                        next(g_)
                    except StopIteration:
                        subs.remove(g_)
                rnd += 1
                if rnd == 2:
                    yield
            if rnd < 2:
                yield

    def ssd_gen(own, reset=True):
        if reset:
            ar.reset()
        Xtm = [ar.alloc([128, 16, 64]) for _ in range(2)]
        Btm = [ar.alloc([128, 256]) for _ in range(2)]
        XCD = ar.alloc([128, 16, 64])
        ssm_ = ar.alloc([128, 80])
        if own:
            BTc = [ar.alloc([128, 2, 128]) for _ in range(2)]
            CTc = [ar.alloc([128, 2, 128]) for _ in range(2)]
            XTc = [ar.alloc([128, 8, 128]) for _ in range(2)]
            ZTc = [ar.alloc([128, 8, 128]) for _ in range(2)]
            XC = ar.alloc([128, 16, 64], BF16)
            Ssb = ar.alloc([128, 1024], BF16)
            RL = ar.alloc([128, 16, 128])
            SEG = [ar.alloc([128, 4, 128]) for _ in range(2)]
            EA = [ar.alloc([128, 4, 128]) for _ in range(2)]
            CBm = ar.alloc([128, 2, 128])
            Mh = [ar.alloc([128, 4, 128], BF16) for _ in range(2)]
            CE = [ar.alloc([128, 4, 128], BF16) for _ in range(2)]
            yT = ar.alloc([128, 8, 128])
            ySq = ar.alloc([128, 8, 128])
            rr = ar.alloc([128, 2, 128])
            for g in range(2):
                pg.act(lambda e, g=g: e.copy(out=Ssb[:, g * 512:(g + 1) * 512], in_=Ss[:, g * 512:(g + 1) * 512]), reads=[("Ss", g)], writes=[("Ssb", g)])
        r4 = "p (a b) -> p a b"
        for tc in range(8):
            b = tc % 2
            tsl = slice(tc * 128, (tc + 1) * 128)
            tg = tc // 4
            pg.dma("sp", lambda e, b=b, tsl=tsl: e.dma_start(out=Xtm[b], in_=XTM[tsl, :].rearrange("p (h f) -> p h f", h=16)), reads=[("xTM", c, tg) for c in range(8)], writes=[("Xtm", b)])
            pg.dma("sp", lambda e, b=b, tsl=tsl: e.dma_start(out=Btm[b], in_=BTM[tsl, :]), reads=[("BTM", c, tg) for c in range(2)], writes=[("Btm", b)])
            if own:
                pg.dma("sp", lambda e, b=b, tsl=tsl: e.dma_start(out=BTc[b], in_=BT[:, :, tsl].rearrange("h p t -> p h t")), reads=[("BT", c) for c in range(2)], writes=[("BTc", b)])
                pg.dma("sp", lambda e, b=b, tsl=tsl: e.dma_start(out=CTc[b], in_=CT[:, :, tsl].rearrange("h p t -> p h t")), reads=[("CT", c) for c in range(2)], writes=[("CTc", b)])
                pg.dma("sp", lambda e, b=b, tsl=tsl: e.dma_start(out=XTc[b], in_=XT[:, :, tsl].rearrange("h p t -> p h t")), reads=[("xT", c) for c in range(8)], writes=[("XTc", b)])
                pg.dma("sp", lambda e, b=b, tsl=tsl: e.dma_start(out=ZTc[b], in_=ZT[:, :, tsl].rearrange("h p t -> p h t")), reads=[("zT", c) for c in range(8)], writes=[("ZTc", b)])
            ab_ = nb()
            pg.pe(lambda e, ab_=ab_, tc=tc: e.matmul(banks[ab_][:, 0:16], lhsT=tri, rhs=Las[:, tc, :], start=True, stop=True), reads=["Las", "cst"], writes=[bk(ab_)])
            pg.pe(lambda e, ab_=ab_, tc=tc: e.matmul(banks[ab_][:, 16:32], lhsT=ones, rhs=Las[:, tc, :], start=True, stop=True), reads=["Las", "cst"], writes=[bk(ab_)])
            pg.act(lambda e, ab_=ab_: e.copy(out=ssm_[:, 0:16], in_=banks[ab_][:, 0:16]), reads=[bk(ab_)], writes=["acs"])
            pg.dve(lambda e, ab_=ab_: e.tensor_tensor(out=ssm_[:, 48:64], in0=banks[ab_][:, 16:32], in1=ssm_[:, 0:16], op=ALU.subtract), reads=[bk(ab_), "acs"], writes=["stmp"])
            pg.act(lambda e: e.activation(out=ssm_[:, 16:32], in_=ssm_[:, 48:64], func=AF.Exp), reads=["stmp"], writes=["dte"])
            pg.act(lambda e, ab_=ab_: e.activation(out=ssm_[:, 32:48], in_=banks[ab_][:, 16:32], func=AF.Exp), reads=[bk(ab_)], writes=["cdb"])
            pg.dve(lambda e, tc=tc: e.tensor_tensor(out=ssm_[:, 64:80], in0=ssm_[:, 16:32], in1=Dts[:, tc, :], op=ALU.mult), reads=["dte", "Dts"], writes=["dtdte"])
            pg.pool(lambda e, b=b: e.tensor_tensor(out=XCD, in0=Xtm[b], in1=ssm_[:, 64:80].unsqueeze(2).broadcast_to([128, 16, 64]), op=ALU.mult), reads=[("Xtm", b), "dtdte"], writes=["XCD"])
            if own:
                pg.dve(lambda e, b=b, tc=tc: e.tensor_tensor(out=XC, in0=Xtm[b], in1=Dts[:, tc, :].unsqueeze(2).broadcast_to([128, 16, 64]), op=ALU.mult), reads=[("Xtm", b), "Dts"], writes=["XC"])
                pg.dve(lambda e, tc=tc: e.tensor_tensor(out=RL, in0=tri.unsqueeze(1).broadcast_to([128, 16, 128]), in1=Las[:, tc, :].unsqueeze(2).broadcast_to([128, 16, 128]), op=ALU.mult), reads=["Las", "cst"], writes=["RL"])
                for g in range(2):
                    cb_ = nb()
                    pg.pe(lambda e, cb_=cb_, g=g, b=b: e.matmul(banks[cb_][:, 0:128], lhsT=BTc[b][:, g, :], rhs=CTc[b][:, g, :], start=True, stop=True), reads=[("BTc", b), ("CTc", b)], writes=[bk(cb_)])
                    pg.dve(lambda e, cb_=cb_, g=g: e.tensor_tensor(out=CBm[:, g, :], in0=banks[cb_][:, 0:128], in1=tri, op=ALU.mult), reads=[bk(cb_), "cst"], writes=[("CBm", g)])
                pg.dve(lambda e, b=b: e.tensor_tensor(out=ySq, in0=XTc[b], in1=smc("sd").unsqueeze(2).broadcast_to([128, 8, 128]), op=ALU.mult), reads=[("XTc", b), "sm"], writes=["XD", "ySq"])

                def S1(q4):
                    g = q4 // 2
                    k2 = q4 % 2
                    se_ = nb()
                    ac_ = nb()
                    for j in range(4):
                        hh = q4 * 4 + j
                        pg.pe(lambda e, se_=se_, j=j, hh=hh: e.matmul(banks[se_][:, j * 128:(j + 1) * 128], lhsT=sgt, rhs=RL[:, hh, :], start=True, stop=True), reads=["RL", "cst"], writes=[bk(se_)])
                    for j in range(4):
                        hh = q4 * 4 + j
                        pg.pe(lambda e, ac_=ac_, j=j, hh=hh: e.matmul(banks[ac_][:, j * 128:(j + 1) * 128], lhsT=ones, rhs=RL[:, hh, :], start=True, stop=True), reads=["RL", "cst"], writes=[bk(ac_)])
                    pg.act(lambda e, se_=se_, k2=k2: e.activation(out=SEG[k2], in_=banks[se_][:].rearrange(r4, a=4), func=AF.Exp), reads=[bk(se_)], writes=[("SEG", k2)])
                    pg.act(lambda e, ac_=ac_, k2=k2: e.activation(out=EA[k2], in_=banks[ac_][:].rearrange(r4, a=4), func=AF.Exp), reads=[bk(ac_)], writes=[("EA", k2)])
                    pg.dve(lambda e, g=g, k2=k2: e.tensor_tensor(out=Mh[k2], in0=SEG[k2], in1=CBm[:, g, :].unsqueeze(1).broadcast_to([128, 4, 128]), op=ALU.mult), reads=[("SEG", k2), ("CBm", g)], writes=[("Mh", k2)])
                    pg.dve(lambda e, g=g, k2=k2, b=b: e.tensor_tensor(out=CE[k2], in0=EA[k2], in1=CTc[b][:, g, :].unsqueeze(1).broadcast_to([128, 4, 128]), op=ALU.mult), reads=[("EA", k2), ("CTc", b)], writes=[("CE", k2)])

                def S2(q4):
                    k2 = q4 % 2
                    yb_ = nb()
                    for j in range(4):
                        hh = q4 * 4 + j
                        pr = hh // 2
                        pg.pe(lambda e, yb_=yb_, j=j, pr=pr, k2=k2: e.matmul(banks[yb_][:, j * 128:(j + 1) * 128], lhsT=XC[:, 2 * pr:2 * pr + 2, :].rearrange("p a b -> p (a b)"), rhs=Mh[k2][:, j, :], start=True, stop=False), reads=["XC", ("Mh", k2)], writes=[bk(yb_)])
                        pg.pe(lambda e, yb_=yb_, j=j, pr=pr, k2=k2: e.matmul(banks[yb_][:, j * 128:(j + 1) * 128], lhsT=Ssb[:, pr * 128:(pr + 1) * 128], rhs=CE[k2][:, j, :], start=False, stop=True), reads=[("Ssb", pr // 4), ("CE", k2)], writes=[bk(yb_)])
                    p0 = q4 * 2
                    ybv = banks[yb_][:].rearrange("p (a b c) -> p a b c", a=2, b=2)
                    pg.dve(lambda e, ybv=ybv, p0=p0: e.tensor_tensor(out=yT[0:64, p0:p0 + 2, :], in0=ybv[0:64, :, 0, :], in1=ySq[0:64, p0:p0 + 2, :], op=ALU.add), reads=[bk(yb_), "XD"], writes=[("yT", p0, 0)])
                    pg.dve(lambda e, ybv=ybv, p0=p0: e.tensor_tensor(out=yT[64:128, p0:p0 + 2, :], in0=ybv[64:128, :, 1, :], in1=ySq[64:128, p0:p0 + 2, :], op=ALU.add), reads=[bk(yb_), "XD"], writes=[("yT", p0, 1)])

                S1(0)
                for q4 in range(4):
                    if q4 + 1 < 4:
                        S1(q4 + 1)
                    S2(q4)
                yr = [("yT", p0, k) for p0 in (0, 2, 4, 6) for k in (0, 1)]
                pg.dve(lambda e, b=b: e.tensor_tensor(out=yT, in0=yT, in1=ZTc[b], op=ALU.mult), reads=yr + [("ZTc", b)], writes=["yz"])
                pg.dve(lambda e: e.tensor_tensor(out=ySq, in0=yT, in1=yT, op=ALU.mult), reads=["yz", "XD"], writes=["ySq", "XD"])
                nb_ = nb()
                for g in range(2):
                    for k in range(4):
                        pg.pe(lambda e, nb_=nb_, g=g, k=k: e.matmul(banks[nb_][:, g * 128:(g + 1) * 128], lhsT=ones, rhs=ySq[:, g * 4 + k, :], start=(k == 0), stop=(k == 3)), reads=["ySq", "cst"], writes=[bk(nb_)])
                pg.act(lambda e, nb_=nb_: e.activation(out=rr, in_=banks[nb_][:, 0:256].rearrange("p (a b) -> p a b", a=2), func=AF.Ln, bias=EPS, scale=1.0 / 512), reads=[bk(nb_)], writes=["rr0", "rr"])
                pg.act(lambda e: e.activation(out=rr, in_=rr, func=AF.Exp, scale=-0.5), reads=["rr0"], writes=["rr"])
                for g in range(2):
                    pg.dve(lambda e, g=g: e.tensor_tensor(out=yT[:, g * 4:(g + 1) * 4, :], in0=yT[:, g * 4:(g + 1) * 4, :], in1=rr[:, g, :].unsqueeze(1).broadcast_to([128, 4, 128]), op=ALU.mult), reads=["yz", "rr"] + ([("yn", 0)] if g else []), writes=[("yn", g)])
                pg.dve(lambda e, tsl=tsl: e.tensor_tensor(out=xn[:, 8:16, tsl], in0=yT, in1=smc("son").unsqueeze(2).broadcast_to([128, 8, 128]), op=ALU.mult), reads=[("yn", 0), ("yn", 1), "sm"], writes=[("xn", c_) for c_ in range(8, 16)])
            for g in range(2):
                sb_ = nb()
                pg.pe(lambda e, sb_=sb_, g=g, b=b: e.matmul(banks[sb_][:], lhsT=Btm[b][:, g * 128:(g + 1) * 128], rhs=XCD[:, g * 8:(g + 1) * 8, :].rearrange("p a b -> p (a b)"), start=True, stop=True), reads=[("Btm", b), "XCD"], writes=[bk(sb_)])
                sv = Ss[:, g * 512:(g + 1) * 512].rearrange("p (a b) -> p a b", a=8)
                pg.dve(lambda e, sv=sv, g=g: e.tensor_tensor(out=sv, in0=sv, in1=ssm_[:, 32 + g * 8:40 + g * 8].unsqueeze(2).broadcast_to([128, 8, 64]), op=ALU.mult), reads=[("Ss", g), "cdb"], writes=[("Ss", g)])
                pg.dve(lambda e, sb_=sb_, g=g: e.tensor_tensor(out=Ss[:, g * 512:(g + 1) * 512], in0=Ss[:, g * 512:(g + 1) * 512], in1=banks[sb_][:], op=ALU.add), reads=[("Ss", g), bk(sb_)], writes=[("Ss", g)])
                if own:
                    pg.act(lambda e, g=g: e.copy(out=Ssb[:, g * 512:(g + 1) * 512], in_=Ss[:, g * 512:(g + 1) * 512]), reads=[("Ss", g)], writes=[("Ssb", g)])
            yield

    def dense_residual(wname, gate_fn=None):
        ar.reset()
        wsl = [ar.alloc([128, NKC, 128], BF16) for _ in range(3)]
        w = W[wname]
        for oc in range(NKC):
            sl = oc % 3
            pg.dma("pool", lambda e, sl=sl, oc=oc: e.dma_start(out=wsl[sl], in_=w[:, oc * 128:(oc + 1) * 128].rearrange("(kc p) f -> p kc f", p=128)), writes=[("wsl", sl)])
            for tt in range(2):
                bi = nb()
                for kc in range(NKC):
                    pg.pe(lambda e, bi=bi, sl=sl, kc=kc, tt=tt: e.matmul(banks[bi][:], lhsT=wsl[sl][:, kc, :], rhs=xn[:, kc, tt * 512:(tt + 1) * 512], start=(kc == 0), stop=(kc == NKC - 1)), reads=[("wsl", sl), ("xn", kc)], writes=[bk(bi)])
                pg.dve(lambda e, bi=bi, oc=oc, tt=tt: e.tensor_tensor(out=h[:, oc, tt * 512:(tt + 1) * 512], in0=h[:, oc, tt * 512:(tt + 1) * 512], in1=banks[bi][:], op=ALU.add), reads=[bk(bi), ("h", oc, tt)], writes=[("h", oc, tt)])

    def ple_phase():
        ar.reset()
        rmsnorm_to_xn(3)
        wsl = [ar.alloc([128, NKC, 128], BF16) for _ in range(3)]
        wpr = ar.alloc([128, 2, D], BF16)
        pT = ar.alloc([128, 2, T], BF16)
        gs = [ar.alloc([128, 512]) for _ in range(2)]
        pt_ = [ar.alloc([128, 512]) for _ in range(2)]
        sq2 = [ar.alloc([128, 512], BF16) for _ in range(4)]
        pg.dma("pool", lambda e: e.dma_start(out=wpr, in_=W["ple_w_proj"].rearrange("(kc p) f -> p kc f", p=128)), writes=["wpr"])
        pg.dma("pool", lambda e: e.dma_start(out=pT, in_=pin.rearrange("(kc p) t -> p kc t", p=128)), writes=["pT"])
        ptr = [["pT"], ["pT"]]
        ssb = [nb(), nb()]
        reserved.update(ssb)
        its = [(oc, tt) for oc in range(NKC) for tt in range(2)]

        def X1(n):
            oc, tt = its[n]
            bi = nb()
            q2 = n % 4
            for kc in range(2):
                pg.pe(lambda e, bi=bi, kc=kc, oc=oc, tt=tt: e.matmul(banks[bi][:], lhsT=wpr[:, kc, oc * 128:(oc + 1) * 128], rhs=pT[:, kc, tt * 512:(tt + 1) * 512], start=(kc == 0), stop=(kc == 1)),
                      reads=["wpr"] + ptr[tt], writes=[bk(bi)])
            pg.act(lambda e, bi=bi, q2=q2: e.activation(out=sq2[q2], in_=banks[bi][:], func=AF.Square), reads=[bk(bi)], writes=[("sq2", q2)])

        def Y1(n):
            oc, tt = its[n]
            q2 = n % 4
            pg.pe(lambda e, q2=q2, tt=tt, oc=oc: e.matmul(banks[ssb[tt]][:], lhsT=ones_bf[:], rhs=sq2[q2], start=(oc == 0), stop=(oc == NKC - 1)), reads=[("sq2", q2), "ones_bf"], writes=[bk(ssb[tt])])

        X1(0)
        X1(1)
        for n in range(len(its)):
            if n + 2 < len(its):
                X1(n + 2)
            Y1(n)
        for tt in range(2):
            pg.act(lambda e, tt=tt: e.activation(out=rstd[:, tt * 512:(tt + 1) * 512], in_=banks[ssb[tt]][:], func=AF.Ln, bias=EPS, scale=1.0 / D), reads=[bk(ssb[tt])], writes=[("rstd0", tt), ("rstd", tt)])
            pg.act(lambda e, tt=tt: e.activation(out=rstd[:, tt * 512:(tt + 1) * 512], in_=rstd[:, tt * 512:(tt + 1) * 512], func=AF.Exp, scale=-0.5), reads=[("rstd0", tt)], writes=[("rstd", tt)])
        reserved.clear()
        w = W["ple_w_gate"]
        for oc in range(NKC):
            sl = oc % 3
            pg.dma("pool", lambda e, sl=sl, oc=oc: e.dma_start(out=wsl[sl], in_=w[:, oc * 128:(oc + 1) * 128].rearrange("(kc p) f -> p kc f", p=128)), writes=[("wsl", sl)])
            for tt in range(2):
                bi = nb()
                for kc in range(NKC):
                    pg.pe(lambda e, bi=bi, sl=sl, kc=kc, tt=tt: e.matmul(banks[bi][:], lhsT=wsl[sl][:, kc, :], rhs=xn[:, kc, tt * 512:(tt + 1) * 512], start=(kc == 0), stop=(kc == NKC - 1)), reads=[("wsl", sl), ("xn", kc)], writes=[bk(bi)])
                pg.act(lambda e, bi=bi, tt=tt: e.activation(out=gs[tt], in_=banks[bi][:], func=AF.Sigmoid), reads=[bk(bi)], writes=[("gs", tt)])
                b2 = nb()
                for kc in range(2):
                    pg.pe(lambda e, b2=b2, kc=kc, oc=oc, tt=tt: e.matmul(banks[b2][:], lhsT=wpr[:, kc, oc * 128:(oc + 1) * 128], rhs=pT[:, kc, tt * 512:(tt + 1) * 512], start=(kc == 0), stop=(kc == 1)),
                          reads=["wpr"] + ptr[tt], writes=[bk(b2)])
                ts_ = slice(tt * 512, (tt + 1) * 512)
                pg.dve(lambda e, oc=oc, ts_=ts_, b2=b2, tt=tt: e.scalar_tensor_tensor(out=pt_[tt], in0=banks[b2][:], scalar=sm[:, 4 * 16 + oc:4 * 16 + oc + 1], in1=rstd[:, ts_], op0=ALU.mult, op1=ALU.mult),
                       reads=[bk(b2), ("rstd", tt), "sm"], writes=[("pt", tt)])
                pg.dve(lambda e, tt=tt: e.tensor_tensor(out=pt_[tt], in0=pt_[tt], in1=gs[tt], op=ALU.mult), reads=[("pt", tt), ("gs", tt)], writes=[("pt", tt)])
                pg.dve(lambda e, oc=oc, ts_=ts_, tt=tt: e.tensor_tensor(out=h[:, oc, ts_], in0=h[:, oc, ts_], in1=pt_[tt], op=ALU.add), reads=[("pt", tt), ("h", oc, tt)], writes=[("h", oc, tt)])

    def final_phase():
        rmsnorm_to_xn(5, dst=None, dst_is_xn=False)
        fins = []
        for cg in range(4):
            for j in range(4):
                c = cg * 4 + j
                pg.dve(lambda e, c=c: e.scalar_tensor_tensor(out=h[:, c, :], in0=h[:, c, :], scalar=sm[:, 5 * 16 + c:5 * 16 + c + 1], in1=rstd[:], op0=ALU.mult, op1=ALU.mult),
                       reads=[("h", c, 0), ("h", c, 1), ("rstd", 0), ("rstd", 1), "sm"], writes=[("h", c, 0), ("h", c, 1)])
            fins.append(pg.dma("sp", lambda e, cg=cg: e.dma_start(out=out[cg * 512:(cg + 1) * 512, :].rearrange("(c p) t -> p c t", p=128), in_=h[:, cg * 4:(cg + 1) * 4, :]),
                               reads=[("h", cg * 4 + j, tt) for j in range(4) for tt in range(2)], writes=[("out", cg)]))
        return fins

    stop = debug or ""
    fins = None
    preloaded = False
    for s in range(nslot):
        own = (s == nslot - 1)
        if not preloaded:
            run_all([load_x_gen(s)])
        preloaded = False
        ffn(W["ffn1_w_gate"], W["ffn1_w_up"], W["ffn1_w_down"], 0)
        if stop == "ffn1":
            continue
        proj_phase(s, own)
        if not own:
            ar.reset()
            run_all([gdn_gen(False, reset=False), ssd_gen(False, reset=False), load_x_gen(s + 1, reset=False, act_only=True)])
            preloaded = True
        else:
            run_all([gdn_gen(own)])
            run_all([ssd_gen(own)])
    if stop not in ("ffn1",):
        if stop != "nomix":
            dense_residual("w_out")
        if stop != "mix":
            ffn(W["ffn2_w_gate"], W["ffn2_w_up"], W["ffn2_w_down"], 2)
            ple_phase()
    if stop in ("ffn1", "mix"):
        fins = []
        for cg in range(4):
            fins.append(pg.dma("sp", lambda e, cg=cg: e.dma_start(out=out[cg * 512:(cg + 1) * 512, :].rearrange("(c p) t -> p c t", p=128), in_=h[:, cg * 4:(cg + 1) * 4, :]),
                               reads=[("h", cg * 4 + j, tt) for j in range(4) for tt in range(2)], writes=[("out", cg)]))
    else:
        fins = final_phase()
    pg.emit(final_wait_ops=fins)
    return nc


def _smalls(inp):
    sm = np.zeros((128, NSM), np.float32)

    def put(name, arr):
        o, w = SM[name]
        assert arr.shape == (128, w), (name, arr.shape)
        sm[:, o:o + w] = arr

    nws = [inp["ffn1_norm"][0], inp["mix_norm"][0], inp["ffn2_norm"][0], inp["ple_norm"][0], inp["ple_post_norm"][0], inp["final_norm"]]
    put("nw", np.concatenate([np.asarray(w).reshape(16, 128).T for w in nws], axis=1))
    put("gcw", np.asarray(inp["gdn_conv_w"][0]).reshape(4, 24, 128).transpose(2, 1, 0).reshape(128, 96))
    put("scw", np.asarray(inp["ssm_conv_w"][0]).reshape(4, 12, 128).transpose(2, 1, 0).reshape(128, 48))
    put("scb", np.asarray(inp["ssm_conv_b"][0]).reshape(12, 128).T)
    put("gon", np.asarray(inp["gdn_out_norm"][0]).reshape(128, 1))
    put("son", np.asarray(inp["ssm_out_norm"][0]).reshape(8, 128).T)
    put("sd", np.repeat(np.asarray(inp["ssm_d"][0]), 64).reshape(8, 128).T)
    put("galog", np.broadcast_to(np.asarray(inp["gdn_a_log"][0])[None, :], (128, 8)))
    put("gdtb", np.broadcast_to(np.asarray(inp["gdn_dt_bias"][0])[None, :], (128, 8)))
    put("salog", np.broadcast_to(np.asarray(inp["ssm_a_log"][0])[None, :], (128, 16)))
    put("sdtb", np.broadcast_to(np.asarray(inp["ssm_dt_bias"][0])[None, :], (128, 16)))
    return sm


def _consts():
    a = np.arange(128)[:, None]
    b = np.arange(128)[None, :]
    c = np.zeros((128, 5, 128), np.float32)
    c[:, 0] = (a == b)
    c[:, 1] = (a <= b)
    c[:, 2] = (a > b)
    c[:, 3] = (a < b)
    c[:, 4] = 1.0
    return c


_NC_CACHE = {}


def kernel(_nslot=NSLOT, _debug=None, _cores=8, **inputs):
    inp = {k: np.asarray(v) for k, v in inputs.items()}
    x = inp["x"]
    p = inp["p"][0]
    key = (_nslot, _debug)
    if key not in _NC_CACHE:
        _NC_CACHE[key] = build_nc(_nslot, _debug)
    nc = _NC_CACHE[key]
    sm = _smalls(inp)
    cst = _consts()
    wmap = {nm: np.ascontiguousarray(inp[nm][0]) for nm in ("ffn1_w_gate", "ffn1_w_up", "ffn1_w_down", "w_in", "w_out",
                                                           "ffn2_w_gate", "ffn2_w_up", "ffn2_w_down", "ple_w_gate", "ple_w_proj")}
    in_maps = []
    for r in range(_cores):
        b, q = r // 4, r % 4
        xs = np.zeros((_nslot, D, T), np.float32)
        cmask = np.zeros((128, 4), np.float32)
        for j in range(_nslot):
            seg = q - (_nslot - 1) + j
            if seg >= 0:
                xs[j] = x[b, seg * T:(seg + 1) * T].T
                cmask[:, j] = 1.0
        m = {"xs": xs, "pin": np.ascontiguousarray(p[b, q * T:(q + 1) * T].T), "cmask": cmask, "smalls": sm, "consts": cst}
        m.update(wmap)
        in_maps.append(m)
    res = run_bass_kernel_spmd(nc, in_maps, core_ids=list(range(_cores)))
    outp = np.zeros((2, 4 * T, D), np.float32)
    for r in range(_cores):
        b, q = r // 4, r % 4
        outp[b, q * T:(q + 1) * T] = res.results[r]["out"].T
    return outp
```

```python
import contextlib
import numpy as np
import concourse.bass as bass
import concourse.mybir as mybir
from concourse.bass_utils import run_bass_kernel_spmd

F32 = mybir.dt.float32
BF16 = mybir.dt.bfloat16
AF = mybir.ActivationFunctionType
ALU = mybir.AluOpType

D = 2048
T = 1024
NKC = 16
DFF = 5632
NFC = 44
PLE = 256
IN_DIM = 6688
EPS = 1e-6
NSLOT = 4
ENGS = ("pe", "act", "dve", "pool", "sp")


class Op:
    __slots__ = ("eng", "fn", "deps", "raw", "dma", "signal", "sigval", "sem", "idx", "ring_prev")

    def __init__(self, eng, fn, dma):
        self.eng = eng
        self.fn = fn
        self.dma = dma
        self.deps = set()
        self.raw = set()
        self.signal = False
        self.sigval = 0
        self.sem = None
        self.ring_prev = None


class Prog:
    def __init__(self, nc, strict=True, ring=8):
        self.nc = nc
        self.ops = []
        self.last_writer = {}
        self.readers = {}
        self.strict = strict
        self.ring = ring
        self.dma_ops = {"sp": [], "pool": []}
        self.phase_tok = None

    def op(self, eng, fn, reads=(), writes=(), dma=False):
        o = Op(eng, fn, dma)
        o.idx = len(self.ops)
        reads = list(reads)
        if self.phase_tok is not None:
            reads.append(self.phase_tok)
        for t in reads:
            w = self.last_writer.get(t)
            if w is not None:
                o.deps.add(w)
                o.raw.add(w)
        for t in writes:
            w = self.last_writer.get(t)
            if w is not None:
                o.deps.add(w)
            rd = self.readers.get(t)
            if rd:
                for lst in rd.values():
                    o.deps.update(lst)
        key = eng + ("_d" if dma else "")
        for t in reads:
            rd = self.readers.setdefault(t, {})
            lst = rd.setdefault(key, [])
            lst.append(o.idx)
            if dma:
                if len(lst) > self.ring:
                    del lst[0]
            elif len(lst) > 1:
                del lst[0]
        for t in writes:
            self.last_writer[t] = o.idx
            self.readers[t] = {}
        o.deps.discard(o.idx)
        if dma:
            lst = self.dma_ops[eng]
            k = len(lst)
            if k >= self.ring:
                o.ring_prev = lst[k - self.ring].idx
            lst.append(o)
        self.ops.append(o)
        return o

    def pe(self, fn, reads=(), writes=()):
        return self.op("pe", fn, reads, writes)

    def act(self, fn, reads=(), writes=()):
        return self.op("act", fn, reads, writes)

    def dve(self, fn, reads=(), writes=()):
        return self.op("dve", fn, reads, writes)

    def pool(self, fn, reads=(), writes=()):
        return self.op("pool", fn, reads, writes)

    def dma(self, q, fn, reads=(), writes=()):
        return self.op(q, fn, reads, writes, dma=True)

    def emit(self, final_wait_ops=()):
        nc = self.nc
        ops = self.ops
        for o in ops:
            for d in o.deps:
                od = ops[d]
                if od.dma:
                    continue
                if od.eng == o.eng and not o.dma:
                    if o.eng == "pe" or not self.strict or d not in o.raw:
                        continue
                od.signal = True
        with contextlib.ExitStack() as st:
            sems = {e: st.enter_context(nc.semaphore("s_" + e)) for e in ("pe", "act", "dve", "pool")}
            rings = {q: [st.enter_context(nc.semaphore("r_%s%d" % (q, i))) for i in range(self.ring)]
                     for q in ("sp", "pool") if self.dma_ops[q]}
            cnt = {e: 0 for e in sems}
            for o in ops:
                if not o.dma and o.signal:
                    cnt[o.eng] += 1
                    o.sigval = cnt[o.eng]
                    o.sem = sems[o.eng]
            for q, lst in self.dma_ops.items():
                for k, o in enumerate(lst):
                    o.sem = rings[q][k % self.ring]
                    o.sigval = 16 * (k // self.ring + 1)
            block = st.enter_context(nc.Block())
            engmap = {"pe": block.tensor, "act": block.scalar, "dve": block.vector,
                      "pool": block.gpsimd, "sp": block.sync}
            for e in ENGS:
                eops = [o for o in ops if o.eng == e]
                extra = list(final_wait_ops) if e == "sp" else []
                if not eops and not extra:
                    continue

                def body(eng, e=e, eops=eops, extra=extra):
                    waited = {}

                    def wait(od):
                        key = id(od.sem)
                        if waited.get(key, 0) >= od.sigval:
                            return
                        eng.wait_ge(od.sem, od.sigval)
                        waited[key] = od.sigval

                    for o in eops:
                        for d in sorted(o.deps):
                            od = ops[d]
                            if not od.dma and od.eng == e and not o.dma:
                                if e == "pe" or not self.strict or d not in o.raw:
                                    continue
                            wait(od)
                        if o.ring_prev is not None:
                            wait(ops[o.ring_prev])
                        ins = o.fn(eng)
                        if o.dma:
                            ins.then_inc(o.sem, 16)
                        elif o.signal:
                            ins.then_inc(o.sem, 1)
                    for o in extra:
                        wait(o)

                engmap[e](body)


SM = {}
_off = 0
for _n, _w in (("nw", 96), ("gcw", 96), ("scw", 48), ("scb", 12), ("gon", 1), ("son", 8), ("sd", 8),
               ("galog", 8), ("gdtb", 8), ("salog", 16), ("sdtb", 16)):
    SM[_n] = (_off, _w)
    _off += _w
NSM = _off


def build_nc(nslot=NSLOT, debug=None):
    nc = bass.Bass("TRN2", target_bir_lowering=False)
    dt = nc.dram_tensor
    xs = dt("xs", [nslot, D, T], F32, kind="ExternalInput").ap()
    pin = dt("pin", [PLE, T], F32, kind="ExternalInput").ap()
    cmask = dt("cmask", [128, 4], F32, kind="ExternalInput").ap()
    smalls = dt("smalls", [128, NSM], F32, kind="ExternalInput").ap()
    consts = dt("consts", [128, 5, 128], F32, kind="ExternalInput").ap()
    W = {}
    for nm, shp in (("ffn1_w_gate", [D, DFF]), ("ffn1_w_up", [D, DFF]), ("ffn1_w_down", [DFF, D]),
                    ("w_in", [D, IN_DIM]), ("w_out", [D, D]),
                    ("ffn2_w_gate", [D, DFF]), ("ffn2_w_up", [D, DFF]), ("ffn2_w_down", [DFF, D]),
                    ("ple_w_gate", [D, D]), ("ple_w_proj", [PLE, D])):
        W[nm] = dt(nm, shp, F32, kind="ExternalInput").ap()
    out = dt("out", [D, T], F32, kind="ExternalOutput").ap()
    QT = dt("QT", [8, 128, T], BF16, kind="Internal").ap()
    KT = dt("KT", [8, 128, T], BF16, kind="Internal").ap()
    GT = dt("GT", [8, 128, T], F32, kind="Internal").ap()
    ZT = dt("ZT", [8, 128, T], F32, kind="Internal").ap()
    XT = dt("XT", [8, 128, T], F32, kind="Internal").ap()
    BT = dt("BT", [2, 128, T], F32, kind="Internal").ap()
    CT = dt("CT", [2, 128, T], F32, kind="Internal").ap()
    KTM = dt("KTM", [T, 1024], BF16, kind="Internal").ap()
    VTM = dt("VTM", [T, 1024], BF16, kind="Internal").ap()
    XTM = dt("XTM", [T, 1024], F32, kind="Internal").ap()
    BTM = dt("BTM", [T, 256], F32, kind="Internal").ap()

    S = nc.alloc_sbuf_tensor
    h = S("h", [128, NKC, T], F32)
    xn = S("xn", [128, NKC, T], BF16)
    cst = S("cst", [128, 5, 128], F32)
    ones_bf = S("ones_bf", [128, 128], BF16)
    sm = S("sm", [128, NSM], F32)
    cm = S("cm", [128, 4], F32)
    negA_g = S("negA_g", [128, 8], F32)
    negA_s = S("negA_s", [128, 16], F32)
    rstd = S("rstd", [128, T], F32)
    sqb = [S("sqb%d" % i, [128, T], BF16) for i in range(2)]
    halo = S("halo", [128, 36, 3], F32)
    gts = S("gts", [128, 8, 32], F32)
    Gg = S("Gg", [128, 8, 8], F32)
    Bg = S("Bg", [128, 8, 8], F32)
    Dts = S("Dts", [128, 8, 16], F32)
    Las = S("Las", [128, 8, 16], F32)
    Sg = S("Sg", [128, 8, 128], F32)
    Ss = S("Ss", [128, 1024], F32)
    Sgb = S("Sgb", [128, 8, 128], BF16)
    scr = S("scr", [128, 8], F32)
    ARENA_F = 22100
    arena = S("arena", [128, ARENA_F], F32)
    banks = [nc.alloc_psum_tensor("bank%d" % i, [128, 512], F32) for i in range(8)]

    ident = cst[:, 0, :]
    tri = cst[:, 1, :]
    sgt = cst[:, 2, :]
    slt = cst[:, 3, :]
    ones = cst[:, 4, :]

    pg = Prog(nc)

    class Arena:
        def __init__(self):
            self.off = 0
            self.gen = 0

        def reset(self):
            self.off = 0
            self.gen += 1
            old = pg.phase_tok
            pg.phase_tok = None
            tok = ("phase", self.gen)
            pg.dve(lambda e: e.memset(scr[:, 0:1], 0.0), reads=[old] if old else [], writes=[tok, old] if old else [tok])
            pg.phase_tok = tok

        def alloc(self, shape, dtype=F32):
            n = 1
            for s_ in shape[1:]:
                n *= s_
            nf = n if dtype == F32 else (n + 1) // 2
            nf = (nf + 7) // 8 * 8
            assert self.off + nf <= ARENA_F, (self.off, nf)
            ap = arena[:, self.off:self.off + nf]
            self.off += nf
            if dtype != F32:
                ap = ap.bitcast(dtype)
            ap = ap[:, 0:n]
            if len(shape) == 3:
                ap = ap.rearrange("p (a b) -> p a b", a=shape[1])
            elif len(shape) == 4:
                ap = ap.rearrange("p (a b c) -> p a b c", a=shape[1], b=shape[2])
            if shape[0] < 128:
                ap = ap[0:shape[0]]
            return ap

    ar = Arena()
    uid = [0]

    def U(name):
        uid[0] += 1
        return (name, uid[0])

    bank_rr = [0]

    reserved = set()

    def nb():
        while True:
            i = bank_rr[0] % 8
            bank_rr[0] += 1
            if i not in reserved:
                return i

    def bk(i):
        return ("bank", i)

    alt = [0]

    def evac(fn, reads=(), writes=()):
        alt[0] ^= 1
        if alt[0]:
            return pg.act(lambda e: fn(e, True), reads, writes)
        return pg.dve(lambda e: fn(e, False), reads, writes)

    def copy_any(e, is_act, out, in_):
        if is_act:
            return e.copy(out=out, in_=in_)
        return e.tensor_copy(out=out, in_=in_)

    def smc(name, lo=0, hi=None):
        o, w = SM[name]
        hi = w if hi is None else hi
        return sm[:, o + lo:o + hi]

    pg.dma("sp", lambda e: e.dma_start(out=cst[:], in_=consts), writes=["cst"])
    pg.dma("sp", lambda e: e.dma_start(out=sm[:], in_=smalls), writes=["sm"])
    pg.dma("sp", lambda e: e.dma_start(out=cm[:], in_=cmask), writes=["cm"])
    pg.dve(lambda e: e.tensor_copy(out=ones_bf[:], in_=ones), reads=["cst"], writes=["ones_bf"])
    pg.act(lambda e: e.activation(out=negA_g[:], in_=smc("galog"), func=AF.Exp), reads=["sm"], writes=["negA_g0"])
    pg.dve(lambda e: e.tensor_single_scalar(out=negA_g[:], in_=negA_g[:], scalar=-1.0, op=ALU.mult), reads=["negA_g0"], writes=["negA_g"])
    pg.act(lambda e: e.activation(out=negA_s[:], in_=smc("salog"), func=AF.Exp), reads=["sm"], writes=["negA_s0"])
    pg.dve(lambda e: e.tensor_single_scalar(out=negA_s[:], in_=negA_s[:], scalar=-1.0, op=ALU.mult), reads=["negA_s0"], writes=["negA_s"])
    pg.dve(lambda e: e.memset(halo[:], 0.0), writes=[("halo", i) for i in range(36)])
    pg.dve(lambda e: e.memset(Sg[:], 0.0), writes=[("Sg", 0), ("Sg", 1)])
    pg.dve(lambda e: e.memset(Ss[:], 0.0), writes=[("Ss", 0), ("Ss", 1)])
    pg.dve(lambda e: e.memset(Sgb[:], 0.0), writes=[("Sgb", 0), ("Sgb", 1)])

    def run_all(gens):
        gens = list(gens)
        while gens:
            for g_ in list(gens):
                try:
                    next(g_)
                except StopIteration:
                    gens.remove(g_)

    def load_x_gen(s, reset=True, act_only=False):
        for cg in range(4):
            pg.dma("sp", lambda e, cg=cg: e.dma_start(out=h[:, cg * 4:(cg + 1) * 4, :], in_=xs[s, cg * 512:(cg + 1) * 512, :].rearrange("(c p) t -> p c t", p=128)),
                   writes=[("h", cg * 4 + j, tt) for j in range(4) for tt in range(2)])
            yield

    def rmsnorm_to_xn(widx, src=None, src_tok="h", dst=None, dst_tok="xn", dst_is_xn=True):
        src = h if src is None else src
        ssb = [nb(), nb()]
        for c in range(NKC):
            b = c % 2
            if b == 0:
                pg.act(lambda e, c=c, b=b: e.activation(out=sqb[b][:], in_=src[:, c, :], func=AF.Square),
                       reads=[(src_tok, c, 0), (src_tok, c, 1)], writes=[("sqb", b)])
            else:
                pg.dve(lambda e, c=c, b=b: e.tensor_tensor(out=sqb[b][:], in0=src[:, c, :], in1=src[:, c, :], op=ALU.mult),
                       reads=[(src_tok, c, 0), (src_tok, c, 1)], writes=[("sqb", b)])
            for tt in range(2):
                pg.pe(lambda e, c=c, b=b, tt=tt: e.matmul(banks[ssb[tt]][:], lhsT=ones_bf[:], rhs=sqb[b][:, tt * 512:(tt + 1) * 512], start=(c == 0), stop=(c == NKC - 1)),
                      reads=[("sqb", b), "ones_bf"], writes=[bk(ssb[tt])])
        for tt in range(2):
            pg.act(lambda e, tt=tt: e.activation(out=rstd[:, tt * 512:(tt + 1) * 512], in_=banks[ssb[tt]][:], func=AF.Ln, bias=EPS, scale=1.0 / D),
                   reads=[bk(ssb[tt])], writes=[("rstd0", tt), ("rstd", tt)])
            pg.act(lambda e, tt=tt: e.activation(out=rstd[:, tt * 512:(tt + 1) * 512], in_=rstd[:, tt * 512:(tt + 1) * 512], func=AF.Exp, scale=-0.5),
                   reads=[("rstd0", tt)], writes=[("rstd", tt)])
        if dst is None and not dst_is_xn:
            return
        dstt = xn if dst is None else dst
        for c in range(NKC):
            pg.dve(lambda e, c=c: e.scalar_tensor_tensor(out=dstt[:, c, :], in0=src[:, c, :], scalar=sm[:, widx * 16 + c:widx * 16 + c + 1], in1=rstd[:], op0=ALU.mult, op1=ALU.mult),
                   reads=[(src_tok, c, 0), (src_tok, c, 1), ("rstd", 0), ("rstd", 1), "sm"], writes=[(dst_tok, c)])

    def ffn(wg, wu, wd, widx, prenormed=False):
        ar.reset()
        wgu = [ar.alloc([128, NKC, 256], BF16) for _ in range(4)]
        wdb = [ar.alloc([128, 4, D], BF16) for _ in range(2)]
        hid = [ar.alloc([128, 4, T], BF16) for _ in range(2)]
        sil = [ar.alloc([128, 512]) for _ in range(2)]
        gub = [(nb(), nb()), (nb(), nb())]
        dnb = [nb(), nb()]
        st = {"gu": 0, "dn": 0, "ws": 0}
        pre = {}

        def issue_w(g, pr):
            fo = (g * 4 + pr * 2) * 128
            sg_, su_ = st["ws"] % 4, (st["ws"] + 1) % 4
            st["ws"] += 2
            pg.dma("pool", lambda e, sg_=sg_, fo=fo: e.dma_start(out=wgu[sg_], in_=wg[:, fo:fo + 256].rearrange("(kc p) f -> p kc f", p=128)), writes=[("wgu", sg_)])
            pg.dma("pool", lambda e, su_=su_, fo=fo: e.dma_start(out=wgu[su_], in_=wu[:, fo:fo + 256].rearrange("(kc p) f -> p kc f", p=128)), writes=[("wgu", su_)])
            return sg_, su_

        pre[(0, 0)] = issue_w(0, 0)
        pre[(0, 1)] = issue_w(0, 1)
        if not prenormed:
            rmsnorm_to_xn(widx)

        def gate_up(g):
            hb = g % 2
            for pr in range(2):
                fo = (g * 4 + pr * 2) * 128
                if (g, pr) in pre:
                    sg_, su_ = pre.pop((g, pr))
                else:
                    sg_, su_ = issue_w(g, pr)
                for fl in range(2):
                    fc = pr * 2 + fl
                    for tt in range(2):
                        gb, ub = gub[st["gu"] % 2]
                        st["gu"] += 1
                        for kc in range(NKC):
                            pg.pe(lambda e, gb=gb, sg_=sg_, kc=kc, fl=fl, tt=tt: e.matmul(banks[gb][:], lhsT=wgu[sg_][:, kc, fl * 128:(fl + 1) * 128], rhs=xn[:, kc, tt * 512:(tt + 1) * 512], start=(kc == 0), stop=(kc == NKC - 1)),
                                  reads=[("wgu", sg_), ("xn", kc)], writes=[bk(gb)])
                        for kc in range(NKC):
                            pg.pe(lambda e, ub=ub, su_=su_, kc=kc, fl=fl, tt=tt: e.matmul(banks[ub][:], lhsT=wgu[su_][:, kc, fl * 128:(fl + 1) * 128], rhs=xn[:, kc, tt * 512:(tt + 1) * 512], start=(kc == 0), stop=(kc == NKC - 1)),
                                  reads=[("wgu", su_), ("xn", kc)], writes=[bk(ub)])
                        sb_ = tt
                        pg.act(lambda e, gb=gb, sb_=sb_: e.activation(out=sil[sb_], in_=banks[gb][:], func=AF.Silu), reads=[bk(gb)], writes=[("sil", sb_)])
                        pg.dve(lambda e, ub=ub, sb_=sb_, hb=hb, fc=fc, tt=tt: e.tensor_tensor(out=hid[hb][:, fc, tt * 512:(tt + 1) * 512], in0=sil[sb_], in1=banks[ub][:], op=ALU.mult),
                               reads=[("sil", sb_), bk(ub)], writes=[("hid", hb, fc, tt)])

        def down(g):
            hb = g % 2
            pg.dma("pool", lambda e, hb=hb, g=g: e.dma_start(out=wdb[hb], in_=wd[g * 512:(g + 1) * 512, :].rearrange("(fc p) o -> p fc o", p=128)), writes=[("wdb", hb)])
            for oc in range(NKC):
                for tt in range(2):
                    db = dnb[st["dn"] % 2]
                    st["dn"] += 1
                    for fc in range(4):
                        pg.pe(lambda e, db=db, hb=hb, fc=fc, oc=oc, tt=tt: e.matmul(banks[db][:], lhsT=wdb[hb][:, fc, oc * 128:(oc + 1) * 128], rhs=hid[hb][:, fc, tt * 512:(tt + 1) * 512], start=(fc == 0), stop=(fc == 3)),
                              reads=[("wdb", hb), ("hid", hb, fc, tt)], writes=[bk(db)])
                    pg.dve(lambda e, db=db, oc=oc, tt=tt: e.scalar_tensor_tensor(out=h[:, oc, tt * 512:(tt + 1) * 512], in0=banks[db][:], scalar=0.5, in1=h[:, oc, tt * 512:(tt + 1) * 512], op0=ALU.mult, op1=ALU.add),
                           reads=[bk(db), ("h", oc, tt)], writes=[("h", oc, tt)])

        for g in range(11):
            gate_up(g)
            if g > 0:
                down(g - 1)
        down(10)

    def proj_phase(s, own):
        ar.reset()
        rmsnorm_to_xn(1)
        wsl = [ar.alloc([128, NKC, 128], BF16) for _ in range(3)]
        wgt = ar.alloc([128, NKC, 32], BF16)
        pc = [ar.alloc([128, T + 3]) for _ in range(2)]
        cs = [ar.alloc([128, T]) for _ in range(2)]
        cn = [ar.alloc([128, T]) for _ in range(2)]
        rs = ar.alloc([128, T])
        tms = [ar.alloc([128, 4, 128]) for _ in range(2)]
        tmsb = [ar.alloc([128, 4, 128], BF16) for _ in range(2)]
        cnb = [ar.alloc([128, T], BF16) for _ in range(2)]
        win = W["w_in"]
        pg.dma("pool", lambda e: e.dma_start(out=wgt[:, :, 0:16], in_=win[:, 4096:4112].rearrange("(kc p) f -> p kc f", p=128)), writes=["wgt_a"])
        pg.dma("pool", lambda e: e.dma_start(out=wgt[:, :, 16:32], in_=win[:, 6672:6688].rearrange("(kc p) f -> p kc f", p=128)), writes=["wgt_b"])
        gbk = nb()
        for tk in range(8):
            for kc in range(NKC):
                pg.pe(lambda e, tk=tk, kc=kc: e.matmul(banks[gbk][:, tk * 32:(tk + 1) * 32], lhsT=xn[:, kc, tk * 128:(tk + 1) * 128], rhs=wgt[:, kc, :], start=(kc == 0), stop=(kc == NKC - 1)),
                      reads=["wgt_a", "wgt_b", ("xn", kc)], writes=[bk(gbk)])
        pg.act(lambda e: e.copy(out=gts[:], in_=banks[gbk][:, 0:256].rearrange("p (a b) -> p a b", a=8)), reads=[bk(gbk)], writes=["gts"])
        pg.dve(lambda e: e.tensor_tensor(out=Gg[:], in0=gts[:, :, 0:8], in1=smc("gdtb").unsqueeze(1).broadcast_to([128, 8, 8]), op=ALU.add), reads=["gts", "sm"], writes=["Gg0", "Gg"])
        pg.act(lambda e: e.activation(out=Gg[:], in_=Gg[:], func=AF.Exp), reads=["Gg0"], writes=["Gg1"])
        pg.act(lambda e: e.activation(out=Gg[:], in_=Gg[:], func=AF.Ln, bias=1.0, scale=1.0), reads=["Gg1"], writes=["Gg2"])
        pg.dve(lambda e: e.tensor_tensor(out=Gg[:], in0=Gg[:], in1=negA_g[:].unsqueeze(1).broadcast_to([128, 8, 8]), op=ALU.mult), reads=["Gg2", "negA_g"], writes=["Gg"])
        pg.act(lambda e: e.activation(out=Bg[:], in_=gts[:, :, 8:16], func=AF.Sigmoid), reads=["gts"], writes=["Bg"])
        pg.dve(lambda e: e.tensor_tensor(out=Dts[:], in0=gts[:, :, 16:32], in1=smc("sdtb").unsqueeze(1).broadcast_to([128, 8, 16]), op=ALU.add), reads=["gts", "sm"], writes=["Dts0", "Dts"])
        pg.act(lambda e: e.activation(out=Dts[:], in_=Dts[:], func=AF.Exp), reads=["Dts0"], writes=["Dts1"])
        pg.act(lambda e: e.activation(out=Dts[:], in_=Dts[:], func=AF.Ln, bias=1.0, scale=1.0), reads=["Dts1"], writes=["Dts2"])
        pg.dve(lambda e: e.tensor_single_scalar(out=Dts[:], in_=Dts[:], scalar=cm[:, s:s + 1], op=ALU.mult), reads=["Dts2", "cm"], writes=["Dts"])
        pg.dve(lambda e: e.tensor_tensor(out=Las[:], in0=Dts[:], in1=negA_s[:].unsqueeze(1).broadcast_to([128, 8, 16]), op=ALU.mult), reads=["Dts", "negA_s"], writes=["Las"])

        chunks = []
        for c in range(8):
            chunks.append(("q", c, c * 128, c, own))
        for c in range(8):
            chunks.append(("k", c, 1024 + c * 128, 8 + c, True))
        for c in range(8):
            chunks.append(("v", c, 2048 + c * 128, 16 + c, True))
        for c in range(8):
            chunks.append(("gate", c, 3072 + c * 128, None, own))
        for c in range(8):
            chunks.append(("z", c, 4112 + c * 128, None, own))
        for c in range(8):
            chunks.append(("x", c, 5136 + c * 128, 24 + c, True))
        for c in range(2):
            chunks.append(("B", c, 6160 + c * 128, 32 + c, True))
        for c in range(2):
            chunks.append(("C", c, 6416 + c * 128, 34 + c, own))
        items = [ch for ch in chunks if ch[4]]
        if s == nslot - 2:
            hb_ = nb()
            hn_ = 0
            for kind, c, col, ci, _ in chunks:
                if kind not in ("q", "C"):
                    continue
                sl = hn_ % 3
                pg.dma("pool", lambda e, sl=sl, col=col: e.dma_start(out=wsl[sl], in_=win[:, col:col + 128].rearrange("(kc p) f -> p kc f", p=128)), writes=[("wsl", sl)])
                for kc in range(NKC):
                    pg.pe(lambda e, kc=kc, sl=sl, hn_=hn_: e.matmul(banks[hb_][:, hn_ * 4:hn_ * 4 + 3], lhsT=wsl[sl][:, kc, :], rhs=xn[:, kc, T - 3:T], start=(kc == 0), stop=(kc == NKC - 1)),
                          reads=[("wsl", sl), ("xn", kc)], writes=[bk(hb_)])
                pg.act(lambda e, ci=ci, hn_=hn_: e.copy(out=halo[:, ci, :], in_=banks[hb_][:, hn_ * 4:hn_ * 4 + 3]), reads=[bk(hb_)], writes=[("halo", ci)])
                hn_ += 1
        N_ = len(items)
        gz = [ar.alloc([128, T]) for _ in range(2)]
        mmb = [(0, 1), (2, 3)]
        l2b = (4, 5)
        trb = (6, 7)

        def A1(n):
            kind, c, col, ci, _ = items[n]
            sl = n % 3
            pg.dma("pool", lambda e, sl=sl, col=col: e.dma_start(out=wsl[sl], in_=win[:, col:col + 128].rearrange("(kc p) f -> p kc f", p=128)), writes=[("wsl", sl)])
            pb = mmb[n % 2]
            for tt in range(2):
                for kc in range(NKC):
                    pg.pe(lambda e, tt=tt, kc=kc, sl=sl, pb=pb: e.matmul(banks[pb[tt]][:], lhsT=wsl[sl][:, kc, :], rhs=xn[:, kc, tt * 512:(tt + 1) * 512], start=(kc == 0), stop=(kc == NKC - 1)),
                          reads=[("wsl", sl), ("xn", kc)], writes=[bk(pb[tt])])

        def A2(n):
            kind, c, col, ci, _ = items[n]
            b2 = n % 2
            pb = mmb[n % 2]
            if ci is None:
                for tt in range(2):
                    pg.act(lambda e, tt=tt, b2=b2, pb=pb: e.activation(out=gz[b2][:, tt * 512:(tt + 1) * 512], in_=banks[pb[tt]][:], func=AF.Silu), reads=[bk(pb[tt])], writes=[("gz", b2, tt)])
                dst = GT if kind == "gate" else ZT
                pg.dma("sp", lambda e, dst=dst, c=c, b2=b2: e.dma_start(out=dst[c], in_=gz[b2]), reads=[("gz", b2, 0), ("gz", b2, 1)], writes=[(kind + "T", c)])
                return
            pg.dve(lambda e, b2=b2, ci=ci: e.tensor_copy(out=pc[b2][:, 0:3], in_=halo[:, ci, :]), reads=[("halo", ci)], writes=[("pc", b2, "h")])
            for tt in range(2):
                pg.act(lambda e, tt=tt, b2=b2, pb=pb: e.copy(out=pc[b2][:, 3 + tt * 512:3 + (tt + 1) * 512], in_=banks[pb[tt]][:]), reads=[bk(pb[tt])], writes=[("pc", b2, tt)])
            pg.dve(lambda e, b2=b2, ci=ci: e.tensor_copy(out=halo[:, ci, :], in_=pc[b2][:, T:T + 3]), reads=[("pc", b2, 1)], writes=[("halo", ci)])

        srcs = {}

        def B1(n):
            kind, c, col, ci, _ = items[n]
            if ci is None:
                return
            b2 = n % 2
            cwo = SM["gcw"][0] if ci < 24 else SM["scw"][0]
            cj = ci if ci < 24 else ci - 24
            pcr = [("pc", b2, "h"), ("pc", b2, 0), ("pc", b2, 1), "sm"]
            pg.dve(lambda e, b2=b2, cwo=cwo, cj=cj: e.tensor_single_scalar(out=cn[b2], in_=pc[b2][:, 0:T], scalar=sm[:, cwo + cj * 4:cwo + cj * 4 + 1], op=ALU.mult), reads=pcr, writes=[("cn", b2)])
            for j in range(1, 4):
                pg.dve(lambda e, b2=b2, cwo=cwo, cj=cj, j=j: e.scalar_tensor_tensor(out=cn[b2], in0=pc[b2][:, j:T + j], scalar=sm[:, cwo + cj * 4 + j:cwo + cj * 4 + j + 1], in1=cn[b2], op0=ALU.mult, op1=ALU.add),
                       reads=pcr + [("cn", b2)], writes=[("cn", b2)])
            if ci < 24:
                pg.act(lambda e, b2=b2: e.activation(out=cs[b2], in_=cn[b2], func=AF.Silu), reads=[("cn", b2)], writes=[("cs", b2, 0), ("cs", b2, 1)])
            else:
                pg.act(lambda e, b2=b2, cj=cj: e.activation(out=cs[b2], in_=cn[b2], func=AF.Silu, bias=sm[:, SM["scb"][0] + cj:SM["scb"][0] + cj + 1], scale=1.0), reads=[("cn", b2), "sm"], writes=[("cs", b2, 0), ("cs", b2, 1)])
            csr = [("cs", b2, 0), ("cs", b2, 1)]
            src = cs[b2]
            if kind in ("q", "k"):
                pg.dve(lambda e, b2=b2: e.tensor_tensor(out=cn[b2], in0=cs[b2], in1=cs[b2], op=ALU.mult), reads=csr + [("cn", b2)], writes=[("cn", b2)])
                lb = l2b
                for tt in range(2):
                    pg.pe(lambda e, tt=tt, b2=b2, lb=lb: e.matmul(banks[lb[tt]][:], lhsT=ones, rhs=cn[b2][:, tt * 512:(tt + 1) * 512], start=True, stop=True), reads=[("cn", b2), "cst"], writes=[bk(lb[tt])])
                for tt in range(2):
                    pg.act(lambda e, tt=tt, lb=lb: e.activation(out=rs[:, tt * 512:(tt + 1) * 512], in_=banks[lb[tt]][:], func=AF.Ln, bias=EPS, scale=1.0), reads=[bk(lb[tt])], writes=[("rs0", tt), ("rs", tt)])
                for tt in range(2):
                    pg.act(lambda e, tt=tt: e.activation(out=rs[:, tt * 512:(tt + 1) * 512], in_=rs[:, tt * 512:(tt + 1) * 512], func=AF.Exp, scale=-0.5), reads=[("rs0", tt)], writes=[("rs", tt)])
                sc = (128.0 ** -0.5) if kind == "q" else 1.0
                pg.dve(lambda e, b2=b2, sc=sc: e.scalar_tensor_tensor(out=cn[b2], in0=cs[b2], scalar=sc, in1=rs, op0=ALU.mult, op1=ALU.mult), reads=csr + [("rs", 0), ("rs", 1), ("cn", b2)], writes=[("cn", b2)])
                src = cn[b2]
                csr = [("cn", b2)]
            srcs[n] = (src, csr)
            fm = {"q": QT, "k": KT, "x": XT, "B": BT, "C": CT}.get(kind)
            if fm is not None and (own or kind == "k"):
                if kind in ("q", "k"):
                    pg.act(lambda e, b2=b2, src=src: e.copy(out=cnb[b2], in_=src), reads=csr, writes=[("cnb", b2)])
                    pg.dma("sp", lambda e, fm=fm, c=c, b2=b2: e.dma_start(out=fm[c], in_=cnb[b2]), reads=[("cnb", b2)], writes=[(kind + "T", c)])
                else:
                    pg.dma("sp", lambda e, fm=fm, c=c, src=src: e.dma_start(out=fm[c], in_=src), reads=csr, writes=[(kind + "T", c)])

        def B2(n):
            kind, c, col, ci, _ = items[n]
            if ci is None:
                return
            src, csr = srcs[n]
            tm = {"k": KTM, "v": VTM, "x": XTM, "B": BTM}.get(kind)
            if tm is not None:
                for tg in range(2):
                    bi = trb[tg]
                    tb = tg
                    for j in range(4):
                        tk = tg * 4 + j
                        pg.pe(lambda e, bi=bi, j=j, tk=tk, src=src: e.transpose(out=banks[bi][:, j * 128:(j + 1) * 128], in_=src[:, tk * 128:(tk + 1) * 128], identity=ident), reads=csr + ["cst"], writes=[bk(bi)])
                    tmx = tmsb if kind in ("k", "v") else tms
                    tmt = "tmsb" if kind in ("k", "v") else "tms"
                    evac(lambda e, a, bi=bi, tb=tb, tmx=tmx: copy_any(e, a, tmx[tb], banks[bi][:].rearrange("p (a b) -> p a b", a=4)), reads=[bk(bi)], writes=[(tmt, tb)])
                    pg.dma("sp", lambda e, tm=tm, tg=tg, tb=tb, c=c, tmx=tmx: e.dma_start(out=tm[tg * 512:(tg + 1) * 512, c * 128:(c + 1) * 128].rearrange("(a p) f -> p a f", p=128), in_=tmx[tb]),
                           reads=[(tmt, tb)], writes=[(kind + "TM", c, tg)])

        for n in range(-2, N_ + 1):
            if 0 <= n + 2 < N_:
                A1(n + 2)
            if 0 <= n < N_:
                B1(n)
            if 0 <= n + 1 < N_:
                A2(n + 1)
            if 0 <= n - 1 < N_:
                B2(n - 1)

    def gdn_gen(own, reset=True):
        if reset:
            ar.reset()
        KTc = [ar.alloc([128, 8, 128], BF16) for _ in range(2)]
        Ktm = [ar.alloc([128, 8, 128], BF16) for _ in range(2)]
        Vtm = [ar.alloc([128, 8, 128], BF16) for _ in range(2)]
        QTc = [ar.alloc([128, 8, 128], BF16) for _ in range(2)] if own else None
        GTc = [ar.alloc([128, 8, 128]) for _ in range(2)] if own else None
        RG = ar.alloc([128, 8, 128])
        KG = ar.alloc([128, 8, 128], BF16)
        KD = [ar.alloc([128, 8, 128], BF16) for _ in range(2)]
        Pm = [ar.alloc([128, 8, 128], BF16) for _ in range(2)]
        NW = [ar.alloc([128, 8, 128], BF16) for _ in range(2)]
        AT = [ar.alloc([128, 8, 128], BF16) for _ in range(2)] if own else None
        QD = [ar.alloc([128, 8, 128], BF16) for _ in range(2)] if own else None
        gsm = [ar.alloc([128, 40]) for _ in range(2)]
        Dh = [ar.alloc([128, 4, 128]) for _ in range(2)]
        DmS = [ar.alloc([128, 4, 128]) for _ in range(2)]
        DmI = [ar.alloc([128, 4, 128]) for _ in range(2)] if own else None
        Ub = [[ar.alloc([128, 4, 128], BF16) for _ in range(2)] for _ in range(2)]
        Lb = [[ar.alloc([128, 4, 128], BF16) for _ in range(2)] for _ in range(2)]
        VN = ar.alloc([128, 8, 128], BF16)
        oT = ar.alloc([128, 8, 128]) if own else None
        oS = ar.alloc([128, 8, 128]) if own else None
        r4 = "p (a b) -> p a b"
        bc4 = lambda ap: ap.unsqueeze(2).broadcast_to([128, 4, 128])
        m4 = lambda ap: ap.unsqueeze(1).broadcast_to([128, 4, 128])

        def G1(tc):
            b = tc % 2
            tsl = slice(tc * 128, (tc + 1) * 128)
            tg = tc // 4
            g_ = gsm[b]
            pg.dma("sp", lambda e: e.dma_start(out=KTc[b], in_=KT[:, :, tsl].rearrange("h p t -> p h t")), reads=[("kT", c) for c in range(8)], writes=[("KTc", b)])
            pg.dma("sp", lambda e: e.dma_start(out=Ktm[b], in_=KTM[tsl, :].rearrange("p (h f) -> p h f", h=8)), reads=[("kTM", c, tg) for c in range(8)], writes=[("Ktm", b)])
            pg.dma("sp", lambda e: e.dma_start(out=Vtm[b], in_=VTM[tsl, :].rearrange("p (h f) -> p h f", h=8)), reads=[("vTM", c, tg) for c in range(8)], writes=[("Vtm", b)])
            if own:
                pg.dma("sp", lambda e: e.dma_start(out=QTc[b], in_=QT[:, :, tsl].rearrange("h p t -> p h t")), reads=[("qT", c) for c in range(8)], writes=[("QTc", b)])
                pg.dma("sp", lambda e: e.dma_start(out=GTc[b], in_=GT[:, :, tsl].rearrange("h p t -> p h t")), reads=[("gateT", c) for c in range(8)], writes=[("GTc", b)])
            gb_ = nb()
            pg.pe(lambda e: e.matmul(banks[gb_][:, 0:8], lhsT=tri, rhs=Gg[:, tc, :], start=True, stop=True), reads=["Gg", "cst"], writes=[bk(gb_)])
            pg.pe(lambda e: e.matmul(banks[gb_][:, 8:16], lhsT=ones, rhs=Gg[:, tc, :], start=True, stop=True), reads=["Gg", "cst"], writes=[bk(gb_)])
            pg.act(lambda e: e.copy(out=g_[:, 0:8], in_=banks[gb_][:, 0:8]), reads=[bk(gb_)], writes=[("gcs", b)])
            pg.act(lambda e: e.activation(out=g_[:, 8:16], in_=g_[:, 0:8], func=AF.Exp), reads=[("gcs", b)], writes=[("egc", b)])
            pg.dve(lambda e: e.tensor_tensor(out=g_[:, 32:40], in0=banks[gb_][:, 8:16], in1=g_[:, 0:8], op=ALU.subtract), reads=[bk(gb_), ("gcs", b)], writes=[("gtmp", b)])
            pg.act(lambda e: e.activation(out=g_[:, 16:24], in_=g_[:, 32:40], func=AF.Exp), reads=[("gtmp", b)], writes=[("dkd", b)])
            pg.act(lambda e: e.activation(out=g_[:, 24:32], in_=banks[gb_][:, 8:16], func=AF.Exp), reads=[bk(gb_)], writes=[("gtb", b)])
            pg.pool(lambda e: e.tensor_tensor(out=RG, in0=tri.unsqueeze(1).broadcast_to([128, 8, 128]), in1=Gg[:, tc, :].unsqueeze(2).broadcast_to([128, 8, 128]), op=ALU.mult), reads=["Gg", "cst"], writes=["RG"])
            pg.pool(lambda e: e.tensor_tensor(out=KG, in0=Ktm[b], in1=g_[:, 8:16].unsqueeze(2).broadcast_to([128, 8, 128]), op=ALU.mult), reads=[("Ktm", b), ("egc", b)], writes=["KG"])
            pg.pool(lambda e: e.tensor_tensor(out=KD[b], in0=Ktm[b], in1=g_[:, 16:24].unsqueeze(2).broadcast_to([128, 8, 128]), op=ALU.mult), reads=[("Ktm", b), ("dkd", b)], writes=[("KD", b)])
            HF = (0, 1)
            eb_ = [nb(), nb()]
            for hf in HF:
                for j in range(4):
                    pg.pe(lambda e, j=j, hf=hf: e.matmul(banks[eb_[hf]][:, j * 128:(j + 1) * 128], lhsT=sgt, rhs=RG[:, hf * 4 + j, :], start=True, stop=True), reads=["RG", "cst"], writes=[bk(eb_[hf])])
            gk_ = [nb(), nb()]
            for hf in HF:
                for j in range(4):
                    pg.pe(lambda e, j=j, hf=hf: e.matmul(banks[gk_[hf]][:, j * 128:(j + 1) * 128], lhsT=KTc[b][:, hf * 4 + j, :], rhs=KTc[b][:, hf * 4 + j, :], start=True, stop=True), reads=[("KTc", b)], writes=[bk(gk_[hf])])
            for hf in HF:
                pg.act(lambda e, hf=hf: e.activation(out=Dh[hf], in_=banks[eb_[hf]][:].rearrange(r4, a=4), func=AF.Exp), reads=[bk(eb_[hf])], writes=[("Dh", hf)])
                pg.pool(lambda e, hf=hf: e.tensor_tensor(out=DmS[hf], in0=Dh[hf], in1=m4(slt), op=ALU.mult), reads=[("Dh", hf), "cst"], writes=[("DmS", hf)])
                if own:
                    pg.dve(lambda e, hf=hf: e.tensor_tensor(out=DmI[hf], in0=Dh[hf], in1=m4(tri), op=ALU.mult), reads=[("Dh", hf), "cst"], writes=[("DmI", hf)])
            for hf in HF:
                hs = slice(hf * 4, hf * 4 + 4)
                pg.dve(lambda e, hf=hf: e.tensor_tensor(out=Dh[hf], in0=banks[gk_[hf]][:].rearrange(r4, a=4), in1=DmS[hf], op=ALU.mult), reads=[bk(gk_[hf]), ("DmS", hf), ("Dh", hf)], writes=[("Dh", hf)])
                pg.pool(lambda e, hf=hf, hs=hs: e.tensor_tensor(out=Dh[hf], in0=Dh[hf], in1=bc4(Bg[:, tc, hs]), op=ALU.mult), reads=[("Dh", hf), "Bg"], writes=[("Dh", hf)])
            if own:
                qb_ = [nb(), nb()]
                for hf in HF:
                    hs = slice(hf * 4, hf * 4 + 4)
                    for j in range(4):
                        pg.pe(lambda e, j=j, hf=hf: e.matmul(banks[qb_[hf]][:, j * 128:(j + 1) * 128], lhsT=KTc[b][:, hf * 4 + j, :], rhs=QTc[b][:, hf * 4 + j, :], start=True, stop=True), reads=[("KTc", b), ("QTc", b)], writes=[bk(qb_[hf])])
                    pg.dve(lambda e, hf=hf, hs=hs: e.tensor_tensor(out=AT[b][:, hs, :], in0=banks[qb_[hf]][:].rearrange(r4, a=4), in1=DmI[hf], op=ALU.mult), reads=[bk(qb_[hf]), ("DmI", hf)], writes=[("AT", b, hf)])
                xb_ = [nb(), nb()]
                for hf in HF:
                    hs = slice(hf * 4, hf * 4 + 4)
                    for j in range(4):
                        pg.pe(lambda e, j=j, hf=hf: e.matmul(banks[xb_[hf]][:, j * 128:(j + 1) * 128], lhsT=ones, rhs=RG[:, hf * 4 + j, :], start=True, stop=True), reads=["RG", "cst"], writes=[bk(xb_[hf])])
                    pg.act(lambda e, hf=hf: e.activation(out=DmI[hf], in_=banks[xb_[hf]][:].rearrange(r4, a=4), func=AF.Exp), reads=[bk(xb_[hf]), ("AT", b, hf)], writes=[("DmI", hf)])
                    pg.dve(lambda e, hf=hf, hs=hs: e.tensor_tensor(out=QD[b][:, hs, :], in0=DmI[hf], in1=QTc[b][:, hs, :], op=ALU.mult), reads=[("DmI", hf), ("QTc", b)], writes=[("QD", b, hf)])
            yield
            tb_ = [nb(), nb()]
            for hf in HF:
                hs = slice(hf * 4, hf * 4 + 4)
                for j in range(4):
                    pg.pe(lambda e, j=j, hf=hf: e.transpose(out=banks[tb_[hf]][:, j * 128:(j + 1) * 128], in_=Dh[hf][:, j, :], identity=ident), reads=[("Dh", hf), "cst"], writes=[bk(tb_[hf])])
                pg.act(lambda e, hf=hf: e.copy(out=Lb[hf][0], in_=banks[tb_[hf]][:].rearrange(r4, a=4)), reads=[bk(tb_[hf])], writes=[("L", hf, 0)])
                pg.act(lambda e, hf=hf: e.copy(out=Ub[hf][0], in_=Dh[hf]), reads=[("Dh", hf)], writes=[("U", hf, 0)])
                pg.pool(lambda e, hf=hf, hs=hs: e.tensor_tensor(out=Pm[b][:, hs, :], in0=m4(ident), in1=Dh[hf], op=ALU.subtract), reads=[("Dh", hf), "cst"], writes=[("Pm", b, hf)])
            cur = 0
            for lv in range(6):
                nx = 1 - cur
                l2 = [nb(), nb()]
                u2 = [nb(), nb()] if lv < 5 else None
                for hf in HF:
                    for j in range(4):
                        pg.pe(lambda e, j=j, hf=hf, cur=cur, l2=l2: e.matmul(banks[l2[hf]][:, j * 128:(j + 1) * 128], lhsT=Ub[hf][cur][:, j, :], rhs=Lb[hf][cur][:, j, :], start=True, stop=True), reads=[("U", hf, cur), ("L", hf, cur)], writes=[bk(l2[hf])])
                    if lv < 5:
                        for j in range(4):
                            pg.pe(lambda e, j=j, hf=hf, cur=cur, u2=u2: e.matmul(banks[u2[hf]][:, j * 128:(j + 1) * 128], lhsT=Lb[hf][cur][:, j, :], rhs=Ub[hf][cur][:, j, :], start=True, stop=True), reads=[("U", hf, cur), ("L", hf, cur)], writes=[bk(u2[hf])])
                for hf in HF:
                    pg.act(lambda e, hf=hf, nx=nx, l2=l2: e.copy(out=Lb[hf][nx], in_=banks[l2[hf]][:].rearrange(r4, a=4)), reads=[bk(l2[hf])], writes=[("L", hf, nx)])
                    if lv < 5:
                        if hf == 0:
                            pg.dve(lambda e, hf=hf, nx=nx, u2=u2: e.tensor_copy(out=Ub[hf][nx], in_=banks[u2[hf]][:].rearrange(r4, a=4)), reads=[bk(u2[hf])], writes=[("U", hf, nx)])
                        else:
                            pg.act(lambda e, hf=hf, nx=nx, u2=u2: e.copy(out=Ub[hf][nx], in_=banks[u2[hf]][:].rearrange(r4, a=4)), reads=[bk(u2[hf])], writes=[("U", hf, nx)])
                pb_ = [nb(), nb()]
                for hf in HF:
                    for j in range(4):
                        pg.pe(lambda e, j=j, hf=hf, nx=nx, pb_=pb_: e.matmul(banks[pb_[hf]][:, j * 128:(j + 1) * 128], lhsT=Lb[hf][nx][:, j, :], rhs=Pm[b][:, hf * 4 + j, :], start=True, stop=True), reads=[("L", hf, nx), ("Pm", b, hf)], writes=[bk(pb_[hf])])
                for hf in HF:
                    hs = slice(hf * 4, hf * 4 + 4)
                    pg.dve(lambda e, hf=hf, hs=hs, pb_=pb_: e.tensor_tensor(out=Pm[b][:, hs, :], in0=Pm[b][:, hs, :], in1=banks[pb_[hf]][:].rearrange(r4, a=4), op=ALU.add), reads=[bk(pb_[hf]), ("Pm", b, hf)], writes=[("Pm", b, hf)])
                cur = nx
                yield
            wb_ = [nb(), nb()]
            for hf in HF:
                hs = slice(hf * 4, hf * 4 + 4)
                for j in range(4):
                    pg.pe(lambda e, j=j, hf=hf: e.matmul(banks[wb_[hf]][:, j * 128:(j + 1) * 128], lhsT=KG[:, hf * 4 + j, :], rhs=Pm[b][:, hf * 4 + j, :], start=True, stop=True), reads=["KG", ("Pm", b, hf)], writes=[bk(wb_[hf])])
                pg.act(lambda e, hf=hf, hs=hs: e.mul(out=NW[b][:, hs, :], in_=banks[wb_[hf]][:].rearrange(r4, a=4), mul=-1.0), reads=[bk(wb_[hf])], writes=[("NW", b, hf)])

        def G2(tc):
            b = tc % 2
            tsl = slice(tc * 128, (tc + 1) * 128)
            g_ = gsm[b]
            HF = (0, 1)
            vb_ = [nb(), nb()]
            for hf in HF:
                hs = slice(hf * 4, hf * 4 + 4)
                for j in range(4):
                    h_ = hf * 4 + j
                    pg.pe(lambda e, j=j, hf=hf, h_=h_: e.matmul(banks[vb_[hf]][:, j * 128:(j + 1) * 128], lhsT=Pm[b][:, h_, :], rhs=Vtm[b][:, h_, :], start=True, stop=False), reads=[("Pm", b, hf), ("Vtm", b)], writes=[bk(vb_[hf])])
                    pg.pe(lambda e, j=j, hf=hf, h_=h_: e.matmul(banks[vb_[hf]][:, j * 128:(j + 1) * 128], lhsT=NW[b][:, h_, :], rhs=Sgb[:, h_, :], start=False, stop=True), reads=[("NW", b, hf), ("Sgb", hf)], writes=[bk(vb_[hf])])
                pg.dve(lambda e, hf=hf, hs=hs: e.tensor_tensor(out=VN[:, hs, :], in0=banks[vb_[hf]][:].rearrange(r4, a=4), in1=bc4(Bg[:, tc, hs]), op=ALU.mult), reads=[bk(vb_[hf]), "Bg"], writes=[("VN", hf)])
            yield
            if own:
                ob_ = [nb(), nb()]
                for hf in HF:
                    hs = slice(hf * 4, hf * 4 + 4)
                    for j in range(4):
                        h_ = hf * 4 + j
                        pg.pe(lambda e, j=j, hf=hf, h_=h_: e.matmul(banks[ob_[hf]][:, j * 128:(j + 1) * 128], lhsT=Sgb[:, h_, :], rhs=QD[b][:, h_, :], start=True, stop=False), reads=[("Sgb", hf), ("QD", b, hf)], writes=[bk(ob_[hf])])
                        pg.pe(lambda e, j=j, hf=hf, h_=h_: e.matmul(banks[ob_[hf]][:, j * 128:(j + 1) * 128], lhsT=VN[:, h_, :], rhs=AT[b][:, h_, :], start=False, stop=True), reads=[("VN", hf), ("AT", b, hf)], writes=[bk(ob_[hf])])
                    pg.act(lambda e, hf=hf, hs=hs: e.copy(out=oT[:, hs, :], in_=banks[ob_[hf]][:].rearrange(r4, a=4)), reads=[bk(ob_[hf])], writes=[("oT", hf), ("oT2", hf)])
            yield
            sb_ = [nb(), nb()]
            for hf in HF:
                hs = slice(hf * 4, hf * 4 + 4)
                for j in range(4):
                    h_ = hf * 4 + j
                    pg.pe(lambda e, j=j, hf=hf, h_=h_: e.matmul(banks[sb_[hf]][:, j * 128:(j + 1) * 128], lhsT=KD[b][:, h_, :], rhs=VN[:, h_, :], start=True, stop=True), reads=[("KD", b), ("VN", hf)], writes=[bk(sb_[hf])])
                pg.pool(lambda e, hs=hs, hf=hf: e.tensor_tensor(out=Sg[:, hs, :], in0=Sg[:, hs, :], in1=bc4(g_[:, 24 + hf * 4:28 + hf * 4]), op=ALU.mult), reads=[("Sg", hf), ("gtb", b)], writes=[("Sg", hf)])
                pg.dve(lambda e, hf=hf, hs=hs: e.tensor_tensor(out=Sg[:, hs, :], in0=Sg[:, hs, :], in1=banks[sb_[hf]][:].rearrange(r4, a=4), op=ALU.add), reads=[("Sg", hf), bk(sb_[hf])], writes=[("Sg", hf)])
                pg.act(lambda e, hs=hs, hf=hf: e.copy(out=Sgb[:, hs, :], in_=Sg[:, hs, :]), reads=[("Sg", hf)], writes=[("Sgb", hf)])
            yield
            if own:
                for hf in HF:
                    hs = slice(hf * 4, hf * 4 + 4)
                    pg.dve(lambda e, hs=hs: e.tensor_tensor(out=oS[:, hs, :], in0=oT[:, hs, :], in1=oT[:, hs, :], op=ALU.mult), reads=[("oT", hf)], writes=[("oS", hf), ("oS1", hf), ("oS2", hf)])
                    nb_ = nb()
                    pg.pe(lambda e, nb_=nb_, hs=hs: e.matmul(banks[nb_][:], lhsT=ones, rhs=oS[:, hs, :].rearrange("p a b -> p (a b)"), start=True, stop=True), reads=[("oS", hf), "cst"], writes=[bk(nb_)])
                    pg.act(lambda e, nb_=nb_, hs=hs: e.activation(out=oS[:, hs, :], in_=banks[nb_][:].rearrange(r4, a=4), func=AF.Ln, bias=EPS, scale=1.0 / 128), reads=[bk(nb_), ("oS", hf)], writes=[("oS1", hf), ("oS", hf)])
                    pg.act(lambda e, hs=hs: e.activation(out=oS[:, hs, :], in_=oS[:, hs, :], func=AF.Exp, scale=-0.5), reads=[("oS1", hf)], writes=[("oS2", hf)])
                    pg.dve(lambda e, hs=hs: e.tensor_tensor(out=oT[:, hs, :], in0=oT[:, hs, :], in1=oS[:, hs, :], op=ALU.mult), reads=[("oT", hf), ("oS2", hf)], writes=[("oT2", hf)])
                    pg.dve(lambda e, hs=hs, tsl=tsl, b=b: e.scalar_tensor_tensor(out=xn[:, hs, tsl], in0=oT[:, hs, :], scalar=sm[:, SM["gon"][0]:SM["gon"][0] + 1], in1=GTc[b][:, hs, :], op0=ALU.mult, op1=ALU.mult),
                           reads=[("oT2", hf), ("GTc", b), "sm"], writes=[("xn", c_) for c_ in range(hf * 4, hf * 4 + 4)])

        for _ in G1(0):
            pass
        for tc in range(8):
            subs = [G2(tc)]
            if tc + 1 < 8:
                subs.insert(0, G1(tc + 1))
            rnd = 0
            while subs:
                for g_ in list(subs):
                    try:
                        next(g_)
                    except StopIteration:
                        subs.remove(g_)
                rnd += 1
                if rnd == 2:
                    yield
            if rnd < 2:
                yield

    def ssd_gen(own, reset=True):
        if reset:
            ar.reset()
        Xtm = [ar.alloc([128, 16, 64]) for _ in range(2)]
        Btm = [ar.alloc([128, 256]) for _ in range(2)]
        XCD = ar.alloc([128, 16, 64])
        ssm_ = ar.alloc([128, 80])
        if own:
            BTc = [ar.alloc([128, 2, 128]) for _ in range(2)]
            CTc = [ar.alloc([128, 2, 128]) for _ in range(2)]
            XTc = [ar.alloc([128, 8, 128]) for _ in range(2)]
            ZTc = [ar.alloc([128, 8, 128]) for _ in range(2)]
            XC = ar.alloc([128, 16, 64], BF16)
            Ssb = ar.alloc([128, 1024], BF16)
            RL = ar.alloc([128, 16, 128])
            SEG = [ar.alloc([128, 4, 128]) for _ in range(2)]
            EA = [ar.alloc([128, 4, 128]) for _ in range(2)]
            CBm = ar.alloc([128, 2, 128])
            Mh = [ar.alloc([128, 4, 128], BF16) for _ in range(2)]
            CE = [ar.alloc([128, 4, 128], BF16) for _ in range(2)]
            yT = ar.alloc([128, 8, 128])
            ySq = ar.alloc([128, 8, 128])
            rr = ar.alloc([128, 2, 128])
            for g in range(2):
                pg.act(lambda e, g=g: e.copy(out=Ssb[:, g * 512:(g + 1) * 512], in_=Ss[:, g * 512:(g + 1) * 512]), reads=[("Ss", g)], writes=[("Ssb", g)])
        r4 = "p (a b) -> p a b"
        for tc in range(8):
            b = tc % 2
            tsl = slice(tc * 128, (tc + 1) * 128)
            tg = tc // 4
            pg.dma("sp", lambda e, b=b, tsl=tsl: e.dma_start(out=Xtm[b], in_=XTM[tsl, :].rearrange("p (h f) -> p h f", h=16)), reads=[("xTM", c, tg) for c in range(8)], writes=[("Xtm", b)])
            pg.dma("sp", lambda e, b=b, tsl=tsl: e.dma_start(out=Btm[b], in_=BTM[tsl, :]), reads=[("BTM", c, tg) for c in range(2)], writes=[("Btm", b)])
            if own:
                pg.dma("sp", lambda e, b=b, tsl=tsl: e.dma_start(out=BTc[b], in_=BT[:, :, tsl].rearrange("h p t -> p h t")), reads=[("BT", c) for c in range(2)], writes=[("BTc", b)])
                pg.dma("sp", lambda e, b=b, tsl=tsl: e.dma_start(out=CTc[b], in_=CT[:, :, tsl].rearrange("h p t -> p h t")), reads=[("CT", c) for c in range(2)], writes=[("CTc", b)])
                pg.dma("sp", lambda e, b=b, tsl=tsl: e.dma_start(out=XTc[b], in_=XT[:, :, tsl].rearrange("h p t -> p h t")), reads=[("xT", c) for c in range(8)], writes=[("XTc", b)])
                pg.dma("sp", lambda e, b=b, tsl=tsl: e.dma_start(out=ZTc[b], in_=ZT[:, :, tsl].rearrange("h p t -> p h t")), reads=[("zT", c) for c in range(8)], writes=[("ZTc", b)])
            ab_ = nb()
            pg.pe(lambda e, ab_=ab_, tc=tc: e.matmul(banks[ab_][:, 0:16], lhsT=tri, rhs=Las[:, tc, :], start=True, stop=True), reads=["Las", "cst"], writes=[bk(ab_)])
            pg.pe(lambda e, ab_=ab_, tc=tc: e.matmul(banks[ab_][:, 16:32], lhsT=ones, rhs=Las[:, tc, :], start=True, stop=True), reads=["Las", "cst"], writes=[bk(ab_)])
            pg.act(lambda e, ab_=ab_: e.copy(out=ssm_[:, 0:16], in_=banks[ab_][:, 0:16]), reads=[bk(ab_)], writes=["acs"])
            pg.dve(lambda e, ab_=ab_: e.tensor_tensor(out=ssm_[:, 48:64], in0=banks[ab_][:, 16:32], in1=ssm_[:, 0:16], op=ALU.subtract), reads=[bk(ab_), "acs"], writes=["stmp"])
            pg.act(lambda e: e.activation(out=ssm_[:, 16:32], in_=ssm_[:, 48:64], func=AF.Exp), reads=["stmp"], writes=["dte"])
            pg.act(lambda e, ab_=ab_: e.activation(out=ssm_[:, 32:48], in_=banks[ab_][:, 16:32], func=AF.Exp), reads=[bk(ab_)], writes=["cdb"])
            pg.dve(lambda e, tc=tc: e.tensor_tensor(out=ssm_[:, 64:80], in0=ssm_[:, 16:32], in1=Dts[:, tc, :], op=ALU.mult), reads=["dte", "Dts"], writes=["dtdte"])
            pg.pool(lambda e, b=b: e.tensor_tensor(out=XCD, in0=Xtm[b], in1=ssm_[:, 64:80].unsqueeze(2).broadcast_to([128, 16, 64]), op=ALU.mult), reads=[("Xtm", b), "dtdte"], writes=["XCD"])
            if own:
                pg.dve(lambda e, b=b, tc=tc: e.tensor_tensor(out=XC, in0=Xtm[b], in1=Dts[:, tc, :].unsqueeze(2).broadcast_to([128, 16, 64]), op=ALU.mult), reads=[("Xtm", b), "Dts"], writes=["XC"])
                pg.dve(lambda e, tc=tc: e.tensor_tensor(out=RL, in0=tri.unsqueeze(1).broadcast_to([128, 16, 128]), in1=Las[:, tc, :].unsqueeze(2).broadcast_to([128, 16, 128]), op=ALU.mult), reads=["Las", "cst"], writes=["RL"])
                for g in range(2):
                    cb_ = nb()
                    pg.pe(lambda e, cb_=cb_, g=g, b=b: e.matmul(banks[cb_][:, 0:128], lhsT=BTc[b][:, g, :], rhs=CTc[b][:, g, :], start=True, stop=True), reads=[("BTc", b), ("CTc", b)], writes=[bk(cb_)])
                    pg.dve(lambda e, cb_=cb_, g=g: e.tensor_tensor(out=CBm[:, g, :], in0=banks[cb_][:, 0:128], in1=tri, op=ALU.mult), reads=[bk(cb_), "cst"], writes=[("CBm", g)])
                pg.dve(lambda e, b=b: e.tensor_tensor(out=ySq, in0=XTc[b], in1=smc("sd").unsqueeze(2).broadcast_to([128, 8, 128]), op=ALU.mult), reads=[("XTc", b), "sm"], writes=["XD", "ySq"])

                def S1(q4):
                    g = q4 // 2
                    k2 = q4 % 2
                    se_ = nb()
                    ac_ = nb()
                    for j in range(4):
                        hh = q4 * 4 + j
                        pg.pe(lambda e, se_=se_, j=j, hh=hh: e.matmul(banks[se_][:, j * 128:(j + 1) * 128], lhsT=sgt, rhs=RL[:, hh, :], start=True, stop=True), reads=["RL", "cst"], writes=[bk(se_)])
                    for j in range(4):
                        hh = q4 * 4 + j
                        pg.pe(lambda e, ac_=ac_, j=j, hh=hh: e.matmul(banks[ac_][:, j * 128:(j + 1) * 128], lhsT=ones, rhs=RL[:, hh, :], start=True, stop=True), reads=["RL", "cst"], writes=[bk(ac_)])
                    pg.act(lambda e, se_=se_, k2=k2: e.activation(out=SEG[k2], in_=banks[se_][:].rearrange(r4, a=4), func=AF.Exp), reads=[bk(se_)], writes=[("SEG", k2)])
                    pg.act(lambda e, ac_=ac_, k2=k2: e.activation(out=EA[k2], in_=banks[ac_][:].rearrange(r4, a=4), func=AF.Exp), reads=[bk(ac_)], writes=[("EA", k2)])
                    pg.dve(lambda e, g=g, k2=k2: e.tensor_tensor(out=Mh[k2], in0=SEG[k2], in1=CBm[:, g, :].unsqueeze(1).broadcast_to([128, 4, 128]), op=ALU.mult), reads=[("SEG", k2), ("CBm", g)], writes=[("Mh", k2)])
                    pg.dve(lambda e, g=g, k2=k2, b=b: e.tensor_tensor(out=CE[k2], in0=EA[k2], in1=CTc[b][:, g, :].unsqueeze(1).broadcast_to([128, 4, 128]), op=ALU.mult), reads=[("EA", k2), ("CTc", b)], writes=[("CE", k2)])

                def S2(q4):
                    k2 = q4 % 2
                    yb_ = nb()
                    for j in range(4):
                        hh = q4 * 4 + j
                        pr = hh // 2
                        pg.pe(lambda e, yb_=yb_, j=j, pr=pr, k2=k2: e.matmul(banks[yb_][:, j * 128:(j + 1) * 128], lhsT=XC[:, 2 * pr:2 * pr + 2, :].rearrange("p a b -> p (a b)"), rhs=Mh[k2][:, j, :], start=True, stop=False), reads=["XC", ("Mh", k2)], writes=[bk(yb_)])
                        pg.pe(lambda e, yb_=yb_, j=j, pr=pr, k2=k2: e.matmul(banks[yb_][:, j * 128:(j + 1) * 128], lhsT=Ssb[:, pr * 128:(pr + 1) * 128], rhs=CE[k2][:, j, :], start=False, stop=True), reads=[("Ssb", pr // 4), ("CE", k2)], writes=[bk(yb_)])
                    p0 = q4 * 2
                    ybv = banks[yb_][:].rearrange("p (a b c) -> p a b c", a=2, b=2)
                    pg.dve(lambda e, ybv=ybv, p0=p0: e.tensor_tensor(out=yT[0:64, p0:p0 + 2, :], in0=ybv[0:64, :, 0, :], in1=ySq[0:64, p0:p0 + 2, :], op=ALU.add), reads=[bk(yb_), "XD"], writes=[("yT", p0, 0)])
                    pg.dve(lambda e, ybv=ybv, p0=p0: e.tensor_tensor(out=yT[64:128, p0:p0 + 2, :], in0=ybv[64:128, :, 1, :], in1=ySq[64:128, p0:p0 + 2, :], op=ALU.add), reads=[bk(yb_), "XD"], writes=[("yT", p0, 1)])

                S1(0)
                for q4 in range(4):
                    if q4 + 1 < 4:
                        S1(q4 + 1)
                    S2(q4)
                yr = [("yT", p0, k) for p0 in (0, 2, 4, 6) for k in (0, 1)]
                pg.dve(lambda e, b=b: e.tensor_tensor(out=yT, in0=yT, in1=ZTc[b], op=ALU.mult), reads=yr + [("ZTc", b)], writes=["yz"])
                pg.dve(lambda e: e.tensor_tensor(out=ySq, in0=yT, in1=yT, op=ALU.mult), reads=["yz", "XD"], writes=["ySq", "XD"])
                nb_ = nb()
                for g in range(2):
                    for k in range(4):
                        pg.pe(lambda e, nb_=nb_, g=g, k=k: e.matmul(banks[nb_][:, g * 128:(g + 1) * 128], lhsT=ones, rhs=ySq[:, g * 4 + k, :], start=(k == 0), stop=(k == 3)), reads=["ySq", "cst"], writes=[bk(nb_)])
                pg.act(lambda e, nb_=nb_: e.activation(out=rr, in_=banks[nb_][:, 0:256].rearrange("p (a b) -> p a b", a=2), func=AF.Ln, bias=EPS, scale=1.0 / 512), reads=[bk(nb_)], writes=["rr0", "rr"])
                pg.act(lambda e: e.activation(out=rr, in_=rr, func=AF.Exp, scale=-0.5), reads=["rr0"], writes=["rr"])
                for g in range(2):
                    pg.dve(lambda e, g=g: e.tensor_tensor(out=yT[:, g * 4:(g + 1) * 4, :], in0=yT[:, g * 4:(g + 1) * 4, :], in1=rr[:, g, :].unsqueeze(1).broadcast_to([128, 4, 128]), op=ALU.mult), reads=["yz", "rr"] + ([("yn", 0)] if g else []), writes=[("yn", g)])
                pg.dve(lambda e, tsl=tsl: e.tensor_tensor(out=xn[:, 8:16, tsl], in0=yT, in1=smc("son").unsqueeze(2).broadcast_to([128, 8, 128]), op=ALU.mult), reads=[("yn", 0), ("yn", 1), "sm"], writes=[("xn", c_) for c_ in range(8, 16)])
            for g in range(2):
                sb_ = nb()
                pg.pe(lambda e, sb_=sb_, g=g, b=b: e.matmul(banks[sb_][:], lhsT=Btm[b][:, g * 128:(g + 1) * 128], rhs=XCD[:, g * 8:(g + 1) * 8, :].rearrange("p a b -> p (a b)"), start=True, stop=True), reads=[("Btm", b), "XCD"], writes=[bk(sb_)])
                sv = Ss[:, g * 512:(g + 1) * 512].rearrange("p (a b) -> p a b", a=8)
                pg.dve(lambda e, sv=sv, g=g: e.tensor_tensor(out=sv, in0=sv, in1=ssm_[:, 32 + g * 8:40 + g * 8].unsqueeze(2).broadcast_to([128, 8, 64]), op=ALU.mult), reads=[("Ss", g), "cdb"], writes=[("Ss", g)])
                pg.dve(lambda e, sb_=sb_, g=g: e.tensor_tensor(out=Ss[:, g * 512:(g + 1) * 512], in0=Ss[:, g * 512:(g + 1) * 512], in1=banks[sb_][:], op=ALU.add), reads=[("Ss", g), bk(sb_)], writes=[("Ss", g)])
                if own:
                    pg.act(lambda e, g=g: e.copy(out=Ssb[:, g * 512:(g + 1) * 512], in_=Ss[:, g * 512:(g + 1) * 512]), reads=[("Ss", g)], writes=[("Ssb", g)])
            yield

    def dense_residual(wname, gate_fn=None):
        ar.reset()
        wsl = [ar.alloc([128, NKC, 128], BF16) for _ in range(3)]
        w = W[wname]
        for oc in range(NKC):
            sl = oc % 3
            pg.dma("pool", lambda e, sl=sl, oc=oc: e.dma_start(out=wsl[sl], in_=w[:, oc * 128:(oc + 1) * 128].rearrange("(kc p) f -> p kc f", p=128)), writes=[("wsl", sl)])
            for tt in range(2):
                bi = nb()
                for kc in range(NKC):
                    pg.pe(lambda e, bi=bi, sl=sl, kc=kc, tt=tt: e.matmul(banks[bi][:], lhsT=wsl[sl][:, kc, :], rhs=xn[:, kc, tt * 512:(tt + 1) * 512], start=(kc == 0), stop=(kc == NKC - 1)), reads=[("wsl", sl), ("xn", kc)], writes=[bk(bi)])
                pg.dve(lambda e, bi=bi, oc=oc, tt=tt: e.tensor_tensor(out=h[:, oc, tt * 512:(tt + 1) * 512], in0=h[:, oc, tt * 512:(tt + 1) * 512], in1=banks[bi][:], op=ALU.add), reads=[bk(bi), ("h", oc, tt)], writes=[("h", oc, tt)])

    def ple_phase():
        ar.reset()
        rmsnorm_to_xn(3)
        wsl = [ar.alloc([128, NKC, 128], BF16) for _ in range(3)]
        wpr = ar.alloc([128, 2, D], BF16)
        pT = ar.alloc([128, 2, T], BF16)
        gs = [ar.alloc([128, 512]) for _ in range(2)]
        pt_ = [ar.alloc([128, 512]) for _ in range(2)]
        sq2 = [ar.alloc([128, 512], BF16) for _ in range(4)]
        pg.dma("pool", lambda e: e.dma_start(out=wpr, in_=W["ple_w_proj"].rearrange("(kc p) f -> p kc f", p=128)), writes=["wpr"])
        pg.dma("pool", lambda e: e.dma_start(out=pT, in_=pin.rearrange("(kc p) t -> p kc t", p=128)), writes=["pT"])
        ptr = [["pT"], ["pT"]]
        ssb = [nb(), nb()]
        reserved.update(ssb)
        its = [(oc, tt) for oc in range(NKC) for tt in range(2)]

        def X1(n):
            oc, tt = its[n]
            bi = nb()
            q2 = n % 4
            for kc in range(2):
                pg.pe(lambda e, bi=bi, kc=kc, oc=oc, tt=tt: e.matmul(banks[bi][:], lhsT=wpr[:, kc, oc * 128:(oc + 1) * 128], rhs=pT[:, kc, tt * 512:(tt + 1) * 512], start=(kc == 0), stop=(kc == 1)),
                      reads=["wpr"] + ptr[tt], writes=[bk(bi)])
            pg.act(lambda e, bi=bi, q2=q2: e.activation(out=sq2[q2], in_=banks[bi][:], func=AF.Square), reads=[bk(bi)], writes=[("sq2", q2)])

        def Y1(n):
            oc, tt = its[n]
            q2 = n % 4
            pg.pe(lambda e, q2=q2, tt=tt, oc=oc: e.matmul(banks[ssb[tt]][:], lhsT=ones_bf[:], rhs=sq2[q2], start=(oc == 0), stop=(oc == NKC - 1)), reads=[("sq2", q2), "ones_bf"], writes=[bk(ssb[tt])])

        X1(0)
        X1(1)
        for n in range(len(its)):
            if n + 2 < len(its):
                X1(n + 2)
            Y1(n)
        for tt in range(2):
            pg.act(lambda e, tt=tt: e.activation(out=rstd[:, tt * 512:(tt + 1) * 512], in_=banks[ssb[tt]][:], func=AF.Ln, bias=EPS, scale=1.0 / D), reads=[bk(ssb[tt])], writes=[("rstd0", tt), ("rstd", tt)])
            pg.act(lambda e, tt=tt: e.activation(out=rstd[:, tt * 512:(tt + 1) * 512], in_=rstd[:, tt * 512:(tt + 1) * 512], func=AF.Exp, scale=-0.5), reads=[("rstd0", tt)], writes=[("rstd", tt)])
        reserved.clear()
        w = W["ple_w_gate"]
        for oc in range(NKC):
            sl = oc % 3
            pg.dma("pool", lambda e, sl=sl, oc=oc: e.dma_start(out=wsl[sl], in_=w[:, oc * 128:(oc + 1) * 128].rearrange("(kc p) f -> p kc f", p=128)), writes=[("wsl", sl)])
            for tt in range(2):
                bi = nb()
                for kc in range(NKC):
                    pg.pe(lambda e, bi=bi, sl=sl, kc=kc, tt=tt: e.matmul(banks[bi][:], lhsT=wsl[sl][:, kc, :], rhs=xn[:, kc, tt * 512:(tt + 1) * 512], start=(kc == 0), stop=(kc == NKC - 1)), reads=[("wsl", sl), ("xn", kc)], writes=[bk(bi)])
                pg.act(lambda e, bi=bi, tt=tt: e.activation(out=gs[tt], in_=banks[bi][:], func=AF.Sigmoid), reads=[bk(bi)], writes=[("gs", tt)])
                b2 = nb()
                for kc in range(2):
                    pg.pe(lambda e, b2=b2, kc=kc, oc=oc, tt=tt: e.matmul(banks[b2][:], lhsT=wpr[:, kc, oc * 128:(oc + 1) * 128], rhs=pT[:, kc, tt * 512:(tt + 1) * 512], start=(kc == 0), stop=(kc == 1)),
                          reads=["wpr"] + ptr[tt], writes=[bk(b2)])
                ts_ = slice(tt * 512, (tt + 1) * 512)
                pg.dve(lambda e, oc=oc, ts_=ts_, b2=b2, tt=tt: e.scalar_tensor_tensor(out=pt_[tt], in0=banks[b2][:], scalar=sm[:, 4 * 16 + oc:4 * 16 + oc + 1], in1=rstd[:, ts_], op0=ALU.mult, op1=ALU.mult),
                       reads=[bk(b2), ("rstd", tt), "sm"], writes=[("pt", tt)])
                pg.dve(lambda e, tt=tt: e.tensor_tensor(out=pt_[tt], in0=pt_[tt], in1=gs[tt], op=ALU.mult), reads=[("pt", tt), ("gs", tt)], writes=[("pt", tt)])
                pg.dve(lambda e, oc=oc, ts_=ts_, tt=tt: e.tensor_tensor(out=h[:, oc, ts_], in0=h[:, oc, ts_], in1=pt_[tt], op=ALU.add), reads=[("pt", tt), ("h", oc, tt)], writes=[("h", oc, tt)])

    def final_phase():
        rmsnorm_to_xn(5, dst=None, dst_is_xn=False)
        fins = []
        for cg in range(4):
            for j in range(4):
                c = cg * 4 + j
                pg.dve(lambda e, c=c: e.scalar_tensor_tensor(out=h[:, c, :], in0=h[:, c, :], scalar=sm[:, 5 * 16 + c:5 * 16 + c + 1], in1=rstd[:], op0=ALU.mult, op1=ALU.mult),
                       reads=[("h", c, 0), ("h", c, 1), ("rstd", 0), ("rstd", 1), "sm"], writes=[("h", c, 0), ("h", c, 1)])
            fins.append(pg.dma("sp", lambda e, cg=cg: e.dma_start(out=out[cg * 512:(cg + 1) * 512, :].rearrange("(c p) t -> p c t", p=128), in_=h[:, cg * 4:(cg + 1) * 4, :]),
                               reads=[("h", cg * 4 + j, tt) for j in range(4) for tt in range(2)], writes=[("out", cg)]))
        return fins

    stop = debug or ""
    fins = None
    preloaded = False
    for s in range(nslot):
        own = (s == nslot - 1)
        if not preloaded:
            run_all([load_x_gen(s)])
        was_pre = preloaded
        preloaded = False
        ffn(W["ffn1_w_gate"], W["ffn1_w_up"], W["ffn1_w_down"], 0, prenormed=was_pre)
        if stop == "ffn1":
            continue
        proj_phase(s, own)
        if not own:
            ar.reset()
            def norm_gen():
                for _ in range(5):
                    yield
                rmsnorm_to_xn(0)
                yield
            run_all([gdn_gen(False, reset=False), ssd_gen(False, reset=False), load_x_gen(s + 1, reset=False, act_only=True), norm_gen()])
            preloaded = True
        else:
            run_all([gdn_gen(own)])
            run_all([ssd_gen(own)])
    if stop not in ("ffn1",):
        if stop != "nomix":
            dense_residual("w_out")
        if stop != "mix":
            ffn(W["ffn2_w_gate"], W["ffn2_w_up"], W["ffn2_w_down"], 2)
            ple_phase()
    if stop in ("ffn1", "mix"):
        fins = []
        for cg in range(4):
            fins.append(pg.dma("sp", lambda e, cg=cg: e.dma_start(out=out[cg * 512:(cg + 1) * 512, :].rearrange("(c p) t -> p c t", p=128), in_=h[:, cg * 4:(cg + 1) * 4, :]),
                               reads=[("h", cg * 4 + j, tt) for j in range(4) for tt in range(2)], writes=[("out", cg)]))
    else:
        fins = final_phase()
    pg.emit(final_wait_ops=fins)
    return nc


def _smalls(inp):
    sm = np.zeros((128, NSM), np.float32)

    def put(name, arr):
        o, w = SM[name]
        assert arr.shape == (128, w), (name, arr.shape)
        sm[:, o:o + w] = arr

    nws = [inp["ffn1_norm"][0], inp["mix_norm"][0], inp["ffn2_norm"][0], inp["ple_norm"][0], inp["ple_post_norm"][0], inp["final_norm"]]
    put("nw", np.concatenate([np.asarray(w).reshape(16, 128).T for w in nws], axis=1))
    put("gcw", np.asarray(inp["gdn_conv_w"][0]).reshape(4, 24, 128).transpose(2, 1, 0).reshape(128, 96))
    put("scw", np.asarray(inp["ssm_conv_w"][0]).reshape(4, 12, 128).transpose(2, 1, 0).reshape(128, 48))
    put("scb", np.asarray(inp["ssm_conv_b"][0]).reshape(12, 128).T)
    put("gon", np.asarray(inp["gdn_out_norm"][0]).reshape(128, 1))
    put("son", np.asarray(inp["ssm_out_norm"][0]).reshape(8, 128).T)
    put("sd", np.repeat(np.asarray(inp["ssm_d"][0]), 64).reshape(8, 128).T)
    put("galog", np.broadcast_to(np.asarray(inp["gdn_a_log"][0])[None, :], (128, 8)))
    put("gdtb", np.broadcast_to(np.asarray(inp["gdn_dt_bias"][0])[None, :], (128, 8)))
    put("salog", np.broadcast_to(np.asarray(inp["ssm_a_log"][0])[None, :], (128, 16)))
    put("sdtb", np.broadcast_to(np.asarray(inp["ssm_dt_bias"][0])[None, :], (128, 16)))
    return sm


def _consts():
    a = np.arange(128)[:, None]
    b = np.arange(128)[None, :]
    c = np.zeros((128, 5, 128), np.float32)
    c[:, 0] = (a == b)
    c[:, 1] = (a <= b)
    c[:, 2] = (a > b)
    c[:, 3] = (a < b)
    c[:, 4] = 1.0
    return c


_NC_CACHE = {}


def kernel(_nslot=NSLOT, _debug=None, _cores=8, **inputs):
    inp = {k: np.asarray(v) for k, v in inputs.items()}
    x = inp["x"]
    p = inp["p"][0]
    key = (_nslot, _debug)
    if key not in _NC_CACHE:
        _NC_CACHE[key] = build_nc(_nslot, _debug)
    nc = _NC_CACHE[key]
    sm = _smalls(inp)
    cst = _consts()
    wmap = {nm: np.ascontiguousarray(inp[nm][0]) for nm in ("ffn1_w_gate", "ffn1_w_up", "ffn1_w_down", "w_in", "w_out",
                                                           "ffn2_w_gate", "ffn2_w_up", "ffn2_w_down", "ple_w_gate", "ple_w_proj")}
    in_maps = []
    for r in range(_cores):
        b, q = r // 4, r % 4
        xs = np.zeros((_nslot, D, T), np.float32)
        cmask = np.zeros((128, 4), np.float32)
        for j in range(_nslot):
            seg = q - (_nslot - 1) + j
            if seg >= 0:
                xs[j] = x[b, seg * T:(seg + 1) * T].T
                cmask[:, j] = 1.0
        m = {"xs": xs, "pin": np.ascontiguousarray(p[b, q * T:(q + 1) * T].T), "cmask": cmask, "smalls": sm, "consts": cst}
        m.update(wmap)
        in_maps.append(m)
    res = run_bass_kernel_spmd(nc, in_maps, core_ids=list(range(_cores)))
    outp = np.zeros((2, 4 * T, D), np.float32)
    for r in range(_cores):
        b, q = r // 4, r % 4
        outp[b, q * T:(q + 1) * T] = res.results[r]["out"].T
    return outp
```

```python
import contextlib
import numpy as np
import concourse.bass as bass
import concourse.mybir as mybir
from concourse.bass_utils import run_bass_kernel_spmd

F32 = mybir.dt.float32
BF16 = mybir.dt.bfloat16
AF = mybir.ActivationFunctionType
ALU = mybir.AluOpType

D = 2048
T = 1024
NKC = 16
DFF = 5632
NFC = 44
PLE = 256
IN_DIM = 6688
EPS = 1e-6
NSLOT = 4
ENGS = ("pe", "act", "dve", "pool", "sp")


class Op:
    __slots__ = ("eng", "fn", "deps", "raw", "dma", "signal", "sigval", "sem", "idx", "ring_prev")

    def __init__(self, eng, fn, dma):
        self.eng = eng
        self.fn = fn
        self.dma = dma
        self.deps = set()
        self.raw = set()
        self.signal = False
        self.sigval = 0
        self.sem = None
        self.ring_prev = None


class Prog:
    def __init__(self, nc, strict=True, ring=8):
        self.nc = nc
        self.ops = []
        self.last_writer = {}
        self.readers = {}
        self.strict = strict
        self.ring = ring
        self.dma_ops = {"sp": [], "pool": []}
        self.phase_tok = None

    def op(self, eng, fn, reads=(), writes=(), dma=False):
        o = Op(eng, fn, dma)
        o.idx = len(self.ops)
        reads = list(reads)
        if self.phase_tok is not None:
            reads.append(self.phase_tok)
        for t in reads:
            w = self.last_writer.get(t)
            if w is not None:
                o.deps.add(w)
                o.raw.add(w)
        for t in writes:
            w = self.last_writer.get(t)
            if w is not None:
                o.deps.add(w)
            rd = self.readers.get(t)
            if rd:
                for lst in rd.values():
                    o.deps.update(lst)
        key = eng + ("_d" if dma else "")
        for t in reads:
            rd = self.readers.setdefault(t, {})
            lst = rd.setdefault(key, [])
            lst.append(o.idx)
            if dma:
                if len(lst) > self.ring:
                    del lst[0]
            elif len(lst) > 1:
                del lst[0]
        for t in writes:
            self.last_writer[t] = o.idx
            self.readers[t] = {}
        o.deps.discard(o.idx)
        if dma:
            lst = self.dma_ops[eng]
            k = len(lst)
            if k >= self.ring:
                o.ring_prev = lst[k - self.ring].idx
            lst.append(o)
        self.ops.append(o)
        return o

    def pe(self, fn, reads=(), writes=()):
        return self.op("pe", fn, reads, writes)

    def act(self, fn, reads=(), writes=()):
        return self.op("act", fn, reads, writes)

    def dve(self, fn, reads=(), writes=()):
        return self.op("dve", fn, reads, writes)

    def pool(self, fn, reads=(), writes=()):
        return self.op("pool", fn, reads, writes)

    def dma(self, q, fn, reads=(), writes=()):
        return self.op(q, fn, reads, writes, dma=True)

    def emit(self, final_wait_ops=()):
        nc = self.nc
        ops = self.ops
        for o in ops:
            for d in o.deps:
                od = ops[d]
                if od.dma:
                    continue
                if od.eng == o.eng and not o.dma:
                    if o.eng == "pe" or not self.strict or d not in o.raw:
                        continue
                od.signal = True
        with contextlib.ExitStack() as st:
            sems = {e: st.enter_context(nc.semaphore("s_" + e)) for e in ("pe", "act", "dve", "pool")}
            rings = {q: [st.enter_context(nc.semaphore("r_%s%d" % (q, i))) for i in range(self.ring)]
                     for q in ("sp", "pool") if self.dma_ops[q]}
            cnt = {e: 0 for e in sems}
            for o in ops:
                if not o.dma and o.signal:
                    cnt[o.eng] += 1
                    o.sigval = cnt[o.eng]
                    o.sem = sems[o.eng]
            for q, lst in self.dma_ops.items():
                for k, o in enumerate(lst):
                    o.sem = rings[q][k % self.ring]
                    o.sigval = 16 * (k // self.ring + 1)
            block = st.enter_context(nc.Block())
            engmap = {"pe": block.tensor, "act": block.scalar, "dve": block.vector,
                      "pool": block.gpsimd, "sp": block.sync}
            for e in ENGS:
                eops = [o for o in ops if o.eng == e]
                extra = list(final_wait_ops) if e == "sp" else []
                if not eops and not extra:
                    continue

                def body(eng, e=e, eops=eops, extra=extra):
                    waited = {}

                    def wait(od):
                        key = id(od.sem)
                        if waited.get(key, 0) >= od.sigval:
                            return
                        eng.wait_ge(od.sem, od.sigval)
                        waited[key] = od.sigval

                    for o in eops:
                        for d in sorted(o.deps):
                            od = ops[d]
                            if not od.dma and od.eng == e and not o.dma:
                                if e == "pe" or not self.strict or d not in o.raw:
                                    continue
                            wait(od)
                        if o.ring_prev is not None:
                            wait(ops[o.ring_prev])
                        ins = o.fn(eng)
                        if o.dma:
                            ins.then_inc(o.sem, 16)
                        elif o.signal:
                            ins.then_inc(o.sem, 1)
                    for o in extra:
                        wait(o)

                engmap[e](body)


SM = {}
_off = 0
for _n, _w in (("nw", 96), ("gcw", 96), ("scw", 48), ("scb", 12), ("gon", 1), ("son", 8), ("sd", 8),
               ("galog", 8), ("gdtb", 8), ("salog", 16), ("sdtb", 16)):
    SM[_n] = (_off, _w)
    _off += _w
NSM = _off


def build_nc(nslot=NSLOT, debug=None):
    nc = bass.Bass("TRN2", target_bir_lowering=False)
    dt = nc.dram_tensor
    xs = dt("xs", [nslot, D, T], F32, kind="ExternalInput").ap()
    pin = dt("pin", [PLE, T], F32, kind="ExternalInput").ap()
    cmask = dt("cmask", [128, 4], F32, kind="ExternalInput").ap()
    smalls = dt("smalls", [128, NSM], F32, kind="ExternalInput").ap()
    consts = dt("consts", [128, 5, 128], F32, kind="ExternalInput").ap()
    W = {}
    for nm, shp in (("ffn1_w_gate", [D, DFF]), ("ffn1_w_up", [D, DFF]), ("ffn1_w_down", [DFF, D]),
                    ("w_in", [D, IN_DIM]), ("w_out", [D, D]),
                    ("ffn2_w_gate", [D, DFF]), ("ffn2_w_up", [D, DFF]), ("ffn2_w_down", [DFF, D]),
                    ("ple_w_gate", [D, D]), ("ple_w_proj", [PLE, D])):
        W[nm] = dt(nm, shp, F32, kind="ExternalInput").ap()
    out = dt("out", [D, T], F32, kind="ExternalOutput").ap()
    QT = dt("QT", [8, 128, T], BF16, kind="Internal").ap()
    KT = dt("KT", [8, 128, T], BF16, kind="Internal").ap()
    GT = dt("GT", [8, 128, T], F32, kind="Internal").ap()
    ZT = dt("ZT", [8, 128, T], F32, kind="Internal").ap()
    XT = dt("XT", [8, 128, T], F32, kind="Internal").ap()
    BT = dt("BT", [2, 128, T], F32, kind="Internal").ap()
    CT = dt("CT", [2, 128, T], F32, kind="Internal").ap()
    KTM = dt("KTM", [T, 1024], BF16, kind="Internal").ap()
    VTM = dt("VTM", [T, 1024], BF16, kind="Internal").ap()
    XTM = dt("XTM", [T, 1024], F32, kind="Internal").ap()
    BTM = dt("BTM", [T, 256], F32, kind="Internal").ap()

    S = nc.alloc_sbuf_tensor
    h = S("h", [128, NKC, T], F32)
    xn = S("xn", [128, NKC, T], BF16)
    cst = S("cst", [128, 5, 128], F32)
    ones_bf = S("ones_bf", [128, 128], BF16)
    sm = S("sm", [128, NSM], F32)
    cm = S("cm", [128, 4], F32)
    negA_g = S("negA_g", [128, 8], F32)
    negA_s = S("negA_s", [128, 16], F32)
    rstd = S("rstd", [128, T], F32)
    sqb = [S("sqb%d" % i, [128, T], BF16) for i in range(2)]
    halo = S("halo", [128, 36, 3], F32)
    gts = S("gts", [128, 8, 32], F32)
    Gg = S("Gg", [128, 8, 8], F32)
    Bg = S("Bg", [128, 8, 8], F32)
    Dts = S("Dts", [128, 8, 16], F32)
    Las = S("Las", [128, 8, 16], F32)
    Sg = S("Sg", [128, 8, 128], F32)
    Ss = S("Ss", [128, 1024], F32)
    Sgb = S("Sgb", [128, 8, 128], BF16)
    scr = S("scr", [128, 8], F32)
    ARENA_F = 22100
    arena = S("arena", [128, ARENA_F], F32)
    banks = [nc.alloc_psum_tensor("bank%d" % i, [128, 512], F32) for i in range(8)]

    ident = cst[:, 0, :]
    tri = cst[:, 1, :]
    sgt = cst[:, 2, :]
    slt = cst[:, 3, :]
    ones = cst[:, 4, :]

    pg = Prog(nc)

    class Arena:
        def __init__(self):
            self.off = 0
            self.gen = 0

        def reset(self):
            self.off = 0
            self.gen += 1
            old = pg.phase_tok
            pg.phase_tok = None
            tok = ("phase", self.gen)
            pg.dve(lambda e: e.memset(scr[:, 0:1], 0.0), reads=[old] if old else [], writes=[tok, old] if old else [tok])
            pg.phase_tok = tok

        def alloc(self, shape, dtype=F32):
            n = 1
            for s_ in shape[1:]:
                n *= s_
            nf = n if dtype == F32 else (n + 1) // 2
            nf = (nf + 7) // 8 * 8
            assert self.off + nf <= ARENA_F, (self.off, nf)
            ap = arena[:, self.off:self.off + nf]
            self.off += nf
            if dtype != F32:
                ap = ap.bitcast(dtype)
            ap = ap[:, 0:n]
            if len(shape) == 3:
                ap = ap.rearrange("p (a b) -> p a b", a=shape[1])
            elif len(shape) == 4:
                ap = ap.rearrange("p (a b c) -> p a b c", a=shape[1], b=shape[2])
            if shape[0] < 128:
                ap = ap[0:shape[0]]
            return ap

    ar = Arena()
    uid = [0]

    def U(name):
        uid[0] += 1
        return (name, uid[0])

    bank_rr = [0]

    reserved = set()

    def nb():
        while True:
            i = bank_rr[0] % 8
            bank_rr[0] += 1
            if i not in reserved:
                return i

    def bk(i):
        return ("bank", i)

    alt = [0]

    def evac(fn, reads=(), writes=()):
        alt[0] ^= 1
        if alt[0]:
            return pg.act(lambda e: fn(e, True), reads, writes)
        return pg.dve(lambda e: fn(e, False), reads, writes)

    def copy_any(e, is_act, out, in_):
        if is_act:
            return e.copy(out=out, in_=in_)
        return e.tensor_copy(out=out, in_=in_)

    def smc(name, lo=0, hi=None):
        o, w = SM[name]
        hi = w if hi is None else hi
        return sm[:, o + lo:o + hi]

    pg.dma("sp", lambda e: e.dma_start(out=cst[:], in_=consts), writes=["cst"])
    pg.dma("sp", lambda e: e.dma_start(out=sm[:], in_=smalls), writes=["sm"])
    pg.dma("sp", lambda e: e.dma_start(out=cm[:], in_=cmask), writes=["cm"])
    pg.dve(lambda e: e.tensor_copy(out=ones_bf[:], in_=ones), reads=["cst"], writes=["ones_bf"])
    pg.act(lambda e: e.activation(out=negA_g[:], in_=smc("galog"), func=AF.Exp), reads=["sm"], writes=["negA_g0"])
    pg.dve(lambda e: e.tensor_single_scalar(out=negA_g[:], in_=negA_g[:], scalar=-1.0, op=ALU.mult), reads=["negA_g0"], writes=["negA_g"])
    pg.act(lambda e: e.activation(out=negA_s[:], in_=smc("salog"), func=AF.Exp), reads=["sm"], writes=["negA_s0"])
    pg.dve(lambda e: e.tensor_single_scalar(out=negA_s[:], in_=negA_s[:], scalar=-1.0, op=ALU.mult), reads=["negA_s0"], writes=["negA_s"])
    pg.dve(lambda e: e.memset(halo[:], 0.0), writes=[("halo", i) for i in range(36)])
    pg.dve(lambda e: e.memset(Sg[:], 0.0), writes=[("Sg", 0), ("Sg", 1)])
    pg.dve(lambda e: e.memset(Ss[:], 0.0), writes=[("Ss", 0), ("Ss", 1)])
    pg.dve(lambda e: e.memset(Sgb[:], 0.0), writes=[("Sgb", 0), ("Sgb", 1)])

    def run_all(gens):
        gens = list(gens)
        while gens:
            for g_ in list(gens):
                try:
                    next(g_)
                except StopIteration:
                    gens.remove(g_)

    def load_x_gen(s, reset=True, act_only=False):
        for cg in range(4):
            pg.dma("sp", lambda e, cg=cg: e.dma_start(out=h[:, cg * 4:(cg + 1) * 4, :], in_=xs[s, cg * 512:(cg + 1) * 512, :].rearrange("(c p) t -> p c t", p=128)),
                   writes=[("h", cg * 4 + j, tt) for j in range(4) for tt in range(2)])
            yield

    def rmsnorm_to_xn(widx, src=None, src_tok="h", dst=None, dst_tok="xn", dst_is_xn=True):
        src = h if src is None else src
        ssb = [nb(), nb()]
        for c in range(NKC):
            b = c % 2
            pg.act(lambda e, c=c, b=b: e.activation(out=sqb[b][:], in_=src[:, c, :], func=AF.Square),
                   reads=[(src_tok, c, 0), (src_tok, c, 1)], writes=[("sqb", b)])
            for tt in range(2):
                pg.pe(lambda e, c=c, b=b, tt=tt: e.matmul(banks[ssb[tt]][:], lhsT=ones_bf[:], rhs=sqb[b][:, tt * 512:(tt + 1) * 512], start=(c == 0), stop=(c == NKC - 1)),
                      reads=[("sqb", b), "ones_bf"], writes=[bk(ssb[tt])])
        for tt in range(2):
            pg.act(lambda e, tt=tt: e.activation(out=rstd[:, tt * 512:(tt + 1) * 512], in_=banks[ssb[tt]][:], func=AF.Ln, bias=EPS, scale=1.0 / D),
                   reads=[bk(ssb[tt])], writes=[("rstd0", tt), ("rstd", tt)])
            pg.act(lambda e, tt=tt: e.activation(out=rstd[:, tt * 512:(tt + 1) * 512], in_=rstd[:, tt * 512:(tt + 1) * 512], func=AF.Exp, scale=-0.5),
                   reads=[("rstd0", tt)], writes=[("rstd", tt)])
        if dst is None and not dst_is_xn:
            return
        dstt = xn if dst is None else dst
        for c in range(NKC):
            pg.dve(lambda e, c=c: e.scalar_tensor_tensor(out=dstt[:, c, :], in0=src[:, c, :], scalar=sm[:, widx * 16 + c:widx * 16 + c + 1], in1=rstd[:], op0=ALU.mult, op1=ALU.mult),
                   reads=[(src_tok, c, 0), (src_tok, c, 1), ("rstd", 0), ("rstd", 1), "sm"], writes=[(dst_tok, c)])

    def ffn(wg, wu, wd, widx):
        ar.reset()
        wgu = [ar.alloc([128, NKC, 256], BF16) for _ in range(4)]
        wdb = [ar.alloc([128, 4, D], BF16) for _ in range(2)]
        hid = [ar.alloc([128, 4, T], BF16) for _ in range(2)]
        sil = [ar.alloc([128, 512]) for _ in range(2)]
        gub = [(nb(), nb()), (nb(), nb())]
        dnb = [nb(), nb()]
        st = {"gu": 0, "dn": 0, "ws": 0}
        pre = {}

        def issue_w(g, pr):
            fo = (g * 4 + pr * 2) * 128
            sg_, su_ = st["ws"] % 4, (st["ws"] + 1) % 4
            st["ws"] += 2
            pg.dma("pool", lambda e, sg_=sg_, fo=fo: e.dma_start(out=wgu[sg_], in_=wg[:, fo:fo + 256].rearrange("(kc p) f -> p kc f", p=128)), writes=[("wgu", sg_)])
            pg.dma("pool", lambda e, su_=su_, fo=fo: e.dma_start(out=wgu[su_], in_=wu[:, fo:fo + 256].rearrange("(kc p) f -> p kc f", p=128)), writes=[("wgu", su_)])
            return sg_, su_

        pre[(0, 0)] = issue_w(0, 0)
        pre[(0, 1)] = issue_w(0, 1)
        rmsnorm_to_xn(widx)

        def gate_up(g):
            hb = g % 2
            for pr in range(2):
                fo = (g * 4 + pr * 2) * 128
                if (g, pr) in pre:
                    sg_, su_ = pre.pop((g, pr))
                else:
                    sg_, su_ = issue_w(g, pr)
                for fl in range(2):
                    fc = pr * 2 + fl
                    for tt in range(2):
                        gb, ub = gub[st["gu"] % 2]
                        st["gu"] += 1
                        for kc in range(NKC):
                            pg.pe(lambda e, gb=gb, sg_=sg_, kc=kc, fl=fl, tt=tt: e.matmul(banks[gb][:], lhsT=wgu[sg_][:, kc, fl * 128:(fl + 1) * 128], rhs=xn[:, kc, tt * 512:(tt + 1) * 512], start=(kc == 0), stop=(kc == NKC - 1)),
                                  reads=[("wgu", sg_), ("xn", kc)], writes=[bk(gb)])
                        for kc in range(NKC):
                            pg.pe(lambda e, ub=ub, su_=su_, kc=kc, fl=fl, tt=tt: e.matmul(banks[ub][:], lhsT=wgu[su_][:, kc, fl * 128:(fl + 1) * 128], rhs=xn[:, kc, tt * 512:(tt + 1) * 512], start=(kc == 0), stop=(kc == NKC - 1)),
                                  reads=[("wgu", su_), ("xn", kc)], writes=[bk(ub)])
                        sb_ = tt
                        pg.act(lambda e, gb=gb, sb_=sb_: e.activation(out=sil[sb_], in_=banks[gb][:], func=AF.Silu), reads=[bk(gb)], writes=[("sil", sb_)])
                        pg.dve(lambda e, ub=ub, sb_=sb_, hb=hb, fc=fc, tt=tt: e.tensor_tensor(out=hid[hb][:, fc, tt * 512:(tt + 1) * 512], in0=sil[sb_], in1=banks[ub][:], op=ALU.mult),
                               reads=[("sil", sb_), bk(ub)], writes=[("hid", hb, fc, tt)])

        def down(g):
            hb = g % 2
            pg.dma("pool", lambda e, hb=hb, g=g: e.dma_start(out=wdb[hb], in_=wd[g * 512:(g + 1) * 512, :].rearrange("(fc p) o -> p fc o", p=128)), writes=[("wdb", hb)])
            for oc in range(NKC):
                for tt in range(2):
                    db = dnb[st["dn"] % 2]
                    st["dn"] += 1
                    for fc in range(4):
                        pg.pe(lambda e, db=db, hb=hb, fc=fc, oc=oc, tt=tt: e.matmul(banks[db][:], lhsT=wdb[hb][:, fc, oc * 128:(oc + 1) * 128], rhs=hid[hb][:, fc, tt * 512:(tt + 1) * 512], start=(fc == 0), stop=(fc == 3)),
                              reads=[("wdb", hb), ("hid", hb, fc, tt)], writes=[bk(db)])
                    pg.dve(lambda e, db=db, oc=oc, tt=tt: e.scalar_tensor_tensor(out=h[:, oc, tt * 512:(tt + 1) * 512], in0=banks[db][:], scalar=0.5, in1=h[:, oc, tt * 512:(tt + 1) * 512], op0=ALU.mult, op1=ALU.add),
                           reads=[bk(db), ("h", oc, tt)], writes=[("h", oc, tt)])

        for g in range(11):
            gate_up(g)
            if g > 0:
                down(g - 1)
        down(10)

    def proj_phase(s, own):
        ar.reset()
        rmsnorm_to_xn(1)
        wsl = [ar.alloc([128, NKC, 128], BF16) for _ in range(3)]
        wgt = ar.alloc([128, NKC, 32], BF16)
        pc = [ar.alloc([128, T + 3]) for _ in range(2)]
        cs = [ar.alloc([128, T]) for _ in range(2)]
        cn = [ar.alloc([128, T]) for _ in range(2)]
        rs = ar.alloc([128, T])
        tms = [ar.alloc([128, 4, 128]) for _ in range(2)]
        tmsb = [ar.alloc([128, 4, 128], BF16) for _ in range(2)]
        cnb = [ar.alloc([128, T], BF16) for _ in range(2)]
        win = W["w_in"]
        pg.dma("pool", lambda e: e.dma_start(out=wgt[:, :, 0:16], in_=win[:, 4096:4112].rearrange("(kc p) f -> p kc f", p=128)), writes=["wgt_a"])
        pg.dma("pool", lambda e: e.dma_start(out=wgt[:, :, 16:32], in_=win[:, 6672:6688].rearrange("(kc p) f -> p kc f", p=128)), writes=["wgt_b"])
        gbk = nb()
        for tk in range(8):
            for kc in range(NKC):
                pg.pe(lambda e, tk=tk, kc=kc: e.matmul(banks[gbk][:, tk * 32:(tk + 1) * 32], lhsT=xn[:, kc, tk * 128:(tk + 1) * 128], rhs=wgt[:, kc, :], start=(kc == 0), stop=(kc == NKC - 1)),
                      reads=["wgt_a", "wgt_b", ("xn", kc)], writes=[bk(gbk)])
        pg.act(lambda e: e.copy(out=gts[:], in_=banks[gbk][:, 0:256].rearrange("p (a b) -> p a b", a=8)), reads=[bk(gbk)], writes=["gts"])
        pg.dve(lambda e: e.tensor_tensor(out=Gg[:], in0=gts[:, :, 0:8], in1=smc("gdtb").unsqueeze(1).broadcast_to([128, 8, 8]), op=ALU.add), reads=["gts", "sm"], writes=["Gg0", "Gg"])
        pg.act(lambda e: e.activation(out=Gg[:], in_=Gg[:], func=AF.Exp), reads=["Gg0"], writes=["Gg1"])
        pg.act(lambda e: e.activation(out=Gg[:], in_=Gg[:], func=AF.Ln, bias=1.0, scale=1.0), reads=["Gg1"], writes=["Gg2"])
        pg.dve(lambda e: e.tensor_tensor(out=Gg[:], in0=Gg[:], in1=negA_g[:].unsqueeze(1).broadcast_to([128, 8, 8]), op=ALU.mult), reads=["Gg2", "negA_g"], writes=["Gg"])
        pg.act(lambda e: e.activation(out=Bg[:], in_=gts[:, :, 8:16], func=AF.Sigmoid), reads=["gts"], writes=["Bg"])
        pg.dve(lambda e: e.tensor_tensor(out=Dts[:], in0=gts[:, :, 16:32], in1=smc("sdtb").unsqueeze(1).broadcast_to([128, 8, 16]), op=ALU.add), reads=["gts", "sm"], writes=["Dts0", "Dts"])
        pg.act(lambda e: e.activation(out=Dts[:], in_=Dts[:], func=AF.Exp), reads=["Dts0"], writes=["Dts1"])
        pg.act(lambda e: e.activation(out=Dts[:], in_=Dts[:], func=AF.Ln, bias=1.0, scale=1.0), reads=["Dts1"], writes=["Dts2"])
        pg.dve(lambda e: e.tensor_single_scalar(out=Dts[:], in_=Dts[:], scalar=cm[:, s:s + 1], op=ALU.mult), reads=["Dts2", "cm"], writes=["Dts"])
        pg.dve(lambda e: e.tensor_tensor(out=Las[:], in0=Dts[:], in1=negA_s[:].unsqueeze(1).broadcast_to([128, 8, 16]), op=ALU.mult), reads=["Dts", "negA_s"], writes=["Las"])

        chunks = []
        for c in range(8):
            chunks.append(("q", c, c * 128, c, own))
        for c in range(8):
            chunks.append(("k", c, 1024 + c * 128, 8 + c, True))
        for c in range(8):
            chunks.append(("v", c, 2048 + c * 128, 16 + c, True))
        for c in range(8):
            chunks.append(("gate", c, 3072 + c * 128, None, own))
        for c in range(8):
            chunks.append(("z", c, 4112 + c * 128, None, own))
        for c in range(8):
            chunks.append(("x", c, 5136 + c * 128, 24 + c, True))
        for c in range(2):
            chunks.append(("B", c, 6160 + c * 128, 32 + c, True))
        for c in range(2):
            chunks.append(("C", c, 6416 + c * 128, 34 + c, own))
        items = [ch for ch in chunks if ch[4]]
        if s == nslot - 2:
            hb_ = nb()
            hn_ = 0
            for kind, c, col, ci, _ in chunks:
                if kind not in ("q", "C"):
                    continue
                sl = hn_ % 3
                pg.dma("pool", lambda e, sl=sl, col=col: e.dma_start(out=wsl[sl], in_=win[:, col:col + 128].rearrange("(kc p) f -> p kc f", p=128)), writes=[("wsl", sl)])
                for kc in range(NKC):
                    pg.pe(lambda e, kc=kc, sl=sl, hn_=hn_: e.matmul(banks[hb_][:, hn_ * 4:hn_ * 4 + 3], lhsT=wsl[sl][:, kc, :], rhs=xn[:, kc, T - 3:T], start=(kc == 0), stop=(kc == NKC - 1)),
                          reads=[("wsl", sl), ("xn", kc)], writes=[bk(hb_)])
                pg.act(lambda e, ci=ci, hn_=hn_: e.copy(out=halo[:, ci, :], in_=banks[hb_][:, hn_ * 4:hn_ * 4 + 3]), reads=[bk(hb_)], writes=[("halo", ci)])
                hn_ += 1
        N_ = len(items)
        gz = [ar.alloc([128, T]) for _ in range(2)]
        mmb = [(0, 1), (2, 3)]
        l2b = (4, 5)
        trb = (6, 7)

        def A1(n):
            kind, c, col, ci, _ = items[n]
            sl = n % 3
            pg.dma("pool", lambda e, sl=sl, col=col: e.dma_start(out=wsl[sl], in_=win[:, col:col + 128].rearrange("(kc p) f -> p kc f", p=128)), writes=[("wsl", sl)])
            pb = mmb[n % 2]
            for tt in range(2):
                for kc in range(NKC):
                    pg.pe(lambda e, tt=tt, kc=kc, sl=sl, pb=pb: e.matmul(banks[pb[tt]][:], lhsT=wsl[sl][:, kc, :], rhs=xn[:, kc, tt * 512:(tt + 1) * 512], start=(kc == 0), stop=(kc == NKC - 1)),
                          reads=[("wsl", sl), ("xn", kc)], writes=[bk(pb[tt])])

        def A2(n):
            kind, c, col, ci, _ = items[n]
            b2 = n % 2
            pb = mmb[n % 2]
            if ci is None:
                for tt in range(2):
                    pg.act(lambda e, tt=tt, b2=b2, pb=pb: e.activation(out=gz[b2][:, tt * 512:(tt + 1) * 512], in_=banks[pb[tt]][:], func=AF.Silu), reads=[bk(pb[tt])], writes=[("gz", b2, tt)])
                dst = GT if kind == "gate" else ZT
                pg.dma("sp", lambda e, dst=dst, c=c, b2=b2: e.dma_start(out=dst[c], in_=gz[b2]), reads=[("gz", b2, 0), ("gz", b2, 1)], writes=[(kind + "T", c)])
                return
            pg.dve(lambda e, b2=b2, ci=ci: e.tensor_copy(out=pc[b2][:, 0:3], in_=halo[:, ci, :]), reads=[("halo", ci)], writes=[("pc", b2, "h")])
            for tt in range(2):
                pg.act(lambda e, tt=tt, b2=b2, pb=pb: e.copy(out=pc[b2][:, 3 + tt * 512:3 + (tt + 1) * 512], in_=banks[pb[tt]][:]), reads=[bk(pb[tt])], writes=[("pc", b2, tt)])
            pg.dve(lambda e, b2=b2, ci=ci: e.tensor_copy(out=halo[:, ci, :], in_=pc[b2][:, T:T + 3]), reads=[("pc", b2, 1)], writes=[("halo", ci)])

        srcs = {}

        def B1(n):
            kind, c, col, ci, _ = items[n]
            if ci is None:
                return
            b2 = n % 2
            cwo = SM["gcw"][0] if ci < 24 else SM["scw"][0]
            cj = ci if ci < 24 else ci - 24
            pcr = [("pc", b2, "h"), ("pc", b2, 0), ("pc", b2, 1), "sm"]
            pg.dve(lambda e, b2=b2, cwo=cwo, cj=cj: e.tensor_single_scalar(out=cn[b2], in_=pc[b2][:, 0:T], scalar=sm[:, cwo + cj * 4:cwo + cj * 4 + 1], op=ALU.mult), reads=pcr, writes=[("cn", b2)])
            for j in range(1, 4):
                pg.dve(lambda e, b2=b2, cwo=cwo, cj=cj, j=j: e.scalar_tensor_tensor(out=cn[b2], in0=pc[b2][:, j:T + j], scalar=sm[:, cwo + cj * 4 + j:cwo + cj * 4 + j + 1], in1=cn[b2], op0=ALU.mult, op1=ALU.add),
                       reads=pcr + [("cn", b2)], writes=[("cn", b2)])
            if ci < 24:
                pg.act(lambda e, b2=b2: e.activation(out=cs[b2], in_=cn[b2], func=AF.Silu), reads=[("cn", b2)], writes=[("cs", b2, 0), ("cs", b2, 1)])
            else:
                pg.act(lambda e, b2=b2, cj=cj: e.activation(out=cs[b2], in_=cn[b2], func=AF.Silu, bias=sm[:, SM["scb"][0] + cj:SM["scb"][0] + cj + 1], scale=1.0), reads=[("cn", b2), "sm"], writes=[("cs", b2, 0), ("cs", b2, 1)])
            csr = [("cs", b2, 0), ("cs", b2, 1)]
            src = cs[b2]
            if kind in ("q", "k"):
                pg.dve(lambda e, b2=b2: e.tensor_tensor(out=cnb[b2], in0=cs[b2], in1=cs[b2], op=ALU.mult), reads=csr, writes=[("cnb", b2)])
                lb = l2b
                for tt in range(2):
                    pg.pe(lambda e, tt=tt, b2=b2, lb=lb: e.matmul(banks[lb[tt]][:], lhsT=ones_bf[:], rhs=cnb[b2][:, tt * 512:(tt + 1) * 512], start=True, stop=True), reads=[("cnb", b2), "ones_bf"], writes=[bk(lb[tt])])
                for tt in range(2):
                    pg.act(lambda e, tt=tt, lb=lb: e.activation(out=rs[:, tt * 512:(tt + 1) * 512], in_=banks[lb[tt]][:], func=AF.Ln, bias=EPS, scale=1.0), reads=[bk(lb[tt])], writes=[("rs0", tt), ("rs", tt)])
                for tt in range(2):
                    pg.act(lambda e, tt=tt: e.activation(out=rs[:, tt * 512:(tt + 1) * 512], in_=rs[:, tt * 512:(tt + 1) * 512], func=AF.Exp, scale=-0.5), reads=[("rs0", tt)], writes=[("rs", tt)])
                sc = (128.0 ** -0.5) if kind == "q" else 1.0
                pg.dve(lambda e, b2=b2, sc=sc: e.scalar_tensor_tensor(out=cn[b2], in0=cs[b2], scalar=sc, in1=rs, op0=ALU.mult, op1=ALU.mult), reads=csr + [("rs", 0), ("rs", 1), ("cn", b2)], writes=[("cn", b2)])
                src = cn[b2]
                csr = [("cn", b2)]
            srcs[n] = (src, csr)
            fm = {"q": QT, "k": KT, "x": XT, "B": BT, "C": CT}.get(kind)
            if fm is not None and (own or kind == "k"):
                if kind in ("q", "k"):
                    pg.act(lambda e, b2=b2, src=src: e.copy(out=cnb[b2], in_=src), reads=csr, writes=[("cnb", b2)])
                    pg.dma("sp", lambda e, fm=fm, c=c, b2=b2: e.dma_start(out=fm[c], in_=cnb[b2]), reads=[("cnb", b2)], writes=[(kind + "T", c)])
                else:
                    pg.dma("sp", lambda e, fm=fm, c=c, src=src: e.dma_start(out=fm[c], in_=src), reads=csr, writes=[(kind + "T", c)])

        def B2(n):
            kind, c, col, ci, _ = items[n]
            if ci is None:
                return
            src, csr = srcs[n]
            tm = {"k": KTM, "v": VTM, "x": XTM, "B": BTM}.get(kind)
            if tm is not None:
                for tg in range(2):
                    bi = trb[tg]
                    tb = tg
                    for j in range(4):
                        tk = tg * 4 + j
                        pg.pe(lambda e, bi=bi, j=j, tk=tk, src=src: e.transpose(out=banks[bi][:, j * 128:(j + 1) * 128], in_=src[:, tk * 128:(tk + 1) * 128], identity=ident), reads=csr + ["cst"], writes=[bk(bi)])
                    tmx = tmsb if kind in ("k", "v") else tms
                    tmt = "tmsb" if kind in ("k", "v") else "tms"
                    evac(lambda e, a, bi=bi, tb=tb, tmx=tmx: copy_any(e, a, tmx[tb], banks[bi][:].rearrange("p (a b) -> p a b", a=4)), reads=[bk(bi)], writes=[(tmt, tb)])
                    pg.dma("sp", lambda e, tm=tm, tg=tg, tb=tb, c=c, tmx=tmx: e.dma_start(out=tm[tg * 512:(tg + 1) * 512, c * 128:(c + 1) * 128].rearrange("(a p) f -> p a f", p=128), in_=tmx[tb]),
                           reads=[(tmt, tb)], writes=[(kind + "TM", c, tg)])

        for n in range(-2, N_ + 1):
            if 0 <= n + 2 < N_:
                A1(n + 2)
            if 0 <= n < N_:
                B1(n)
            if 0 <= n + 1 < N_:
                A2(n + 1)
            if 0 <= n - 1 < N_:
                B2(n - 1)

    def gdn_gen(own, reset=True):
        if reset:
            ar.reset()
        KTc = [ar.alloc([128, 8, 128], BF16) for _ in range(2)]
        Ktm = [ar.alloc([128, 8, 128], BF16) for _ in range(2)]
        Vtm = [ar.alloc([128, 8, 128], BF16) for _ in range(2)]
        QTc = [ar.alloc([128, 8, 128], BF16) for _ in range(2)] if own else None
        GTc = [ar.alloc([128, 8, 128]) for _ in range(2)] if own else None
        RG = ar.alloc([128, 8, 128])
        KG = ar.alloc([128, 8, 128], BF16)
        KD = [ar.alloc([128, 8, 128], BF16) for _ in range(2)]
        Pm = [ar.alloc([128, 8, 128], BF16) for _ in range(2)]
        NW = [ar.alloc([128, 8, 128], BF16) for _ in range(2)]
        AT = [ar.alloc([128, 8, 128], BF16) for _ in range(2)] if own else None
        QD = [ar.alloc([128, 8, 128], BF16) for _ in range(2)] if own else None
        gsm = [ar.alloc([128, 40]) for _ in range(2)]
        Dh = [ar.alloc([128, 4, 128]) for _ in range(2)]
        DmS = [ar.alloc([128, 4, 128]) for _ in range(2)]
        DmI = [ar.alloc([128, 4, 128]) for _ in range(2)] if own else None
        Ub = [[ar.alloc([128, 4, 128], BF16) for _ in range(2)] for _ in range(2)]
        Lb = [[ar.alloc([128, 4, 128], BF16) for _ in range(2)] for _ in range(2)]
        VN = ar.alloc([128, 8, 128], BF16)
        oT = ar.alloc([128, 8, 128]) if own else None
        oS = ar.alloc([128, 8, 128]) if own else None
        r4 = "p (a b) -> p a b"
        bc4 = lambda ap: ap.unsqueeze(2).broadcast_to([128, 4, 128])
        m4 = lambda ap: ap.unsqueeze(1).broadcast_to([128, 4, 128])

        def G1(tc):
            b = tc % 2
            tsl = slice(tc * 128, (tc + 1) * 128)
            tg = tc // 4
            g_ = gsm[b]
            pg.dma("sp", lambda e: e.dma_start(out=KTc[b], in_=KT[:, :, tsl].rearrange("h p t -> p h t")), reads=[("kT", c) for c in range(8)], writes=[("KTc", b)])
            pg.dma("sp", lambda e: e.dma_start(out=Ktm[b], in_=KTM[tsl, :].rearrange("p (h f) -> p h f", h=8)), reads=[("kTM", c, tg) for c in range(8)], writes=[("Ktm", b)])
            pg.dma("sp", lambda e: e.dma_start(out=Vtm[b], in_=VTM[tsl, :].rearrange("p (h f) -> p h f", h=8)), reads=[("vTM", c, tg) for c in range(8)], writes=[("Vtm", b)])
            if own:
                pg.dma("sp", lambda e: e.dma_start(out=QTc[b], in_=QT[:, :, tsl].rearrange("h p t -> p h t")), reads=[("qT", c) for c in range(8)], writes=[("QTc", b)])
                pg.dma("sp", lambda e: e.dma_start(out=GTc[b], in_=GT[:, :, tsl].rearrange("h p t -> p h t")), reads=[("gateT", c) for c in range(8)], writes=[("GTc", b)])
            gb_ = nb()
            pg.pe(lambda e: e.matmul(banks[gb_][:, 0:8], lhsT=tri, rhs=Gg[:, tc, :], start=True, stop=True), reads=["Gg", "cst"], writes=[bk(gb_)])
            pg.pe(lambda e: e.matmul(banks[gb_][:, 8:16], lhsT=ones, rhs=Gg[:, tc, :], start=True, stop=True), reads=["Gg", "cst"], writes=[bk(gb_)])
            pg.act(lambda e: e.copy(out=g_[:, 0:8], in_=banks[gb_][:, 0:8]), reads=[bk(gb_)], writes=[("gcs", b)])
            pg.act(lambda e: e.activation(out=g_[:, 8:16], in_=g_[:, 0:8], func=AF.Exp), reads=[("gcs", b)], writes=[("egc", b)])
            pg.dve(lambda e: e.tensor_tensor(out=g_[:, 32:40], in0=banks[gb_][:, 8:16], in1=g_[:, 0:8], op=ALU.subtract), reads=[bk(gb_), ("gcs", b)], writes=[("gtmp", b)])
            pg.act(lambda e: e.activation(out=g_[:, 16:24], in_=g_[:, 32:40], func=AF.Exp), reads=[("gtmp", b)], writes=[("dkd", b)])
            pg.act(lambda e: e.activation(out=g_[:, 24:32], in_=banks[gb_][:, 8:16], func=AF.Exp), reads=[bk(gb_)], writes=[("gtb", b)])
            pg.pool(lambda e: e.tensor_tensor(out=RG, in0=tri.unsqueeze(1).broadcast_to([128, 8, 128]), in1=Gg[:, tc, :].unsqueeze(2).broadcast_to([128, 8, 128]), op=ALU.mult), reads=["Gg", "cst"], writes=["RG"])
            pg.pool(lambda e: e.tensor_tensor(out=KG, in0=Ktm[b], in1=g_[:, 8:16].unsqueeze(2).broadcast_to([128, 8, 128]), op=ALU.mult), reads=[("Ktm", b), ("egc", b)], writes=["KG"])
            pg.pool(lambda e: e.tensor_tensor(out=KD[b], in0=Ktm[b], in1=g_[:, 16:24].unsqueeze(2).broadcast_to([128, 8, 128]), op=ALU.mult), reads=[("Ktm", b), ("dkd", b)], writes=[("KD", b)])
            HF = (0, 1)
            eb_ = [nb(), nb()]
            for hf in HF:
                for j in range(4):
                    pg.pe(lambda e, j=j, hf=hf: e.matmul(banks[eb_[hf]][:, j * 128:(j + 1) * 128], lhsT=sgt, rhs=RG[:, hf * 4 + j, :], start=True, stop=True), reads=["RG", "cst"], writes=[bk(eb_[hf])])
            gk_ = [nb(), nb()]
            for hf in HF:
                for j in range(4):
                    pg.pe(lambda e, j=j, hf=hf: e.matmul(banks[gk_[hf]][:, j * 128:(j + 1) * 128], lhsT=KTc[b][:, hf * 4 + j, :], rhs=KTc[b][:, hf * 4 + j, :], start=True, stop=True), reads=[("KTc", b)], writes=[bk(gk_[hf])])
            for hf in HF:
                pg.act(lambda e, hf=hf: e.activation(out=Dh[hf], in_=banks[eb_[hf]][:].rearrange(r4, a=4), func=AF.Exp), reads=[bk(eb_[hf])], writes=[("Dh", hf)])
                pg.pool(lambda e, hf=hf: e.tensor_tensor(out=DmS[hf], in0=Dh[hf], in1=m4(slt), op=ALU.mult), reads=[("Dh", hf), "cst"], writes=[("DmS", hf)])
                if own:
                    pg.dve(lambda e, hf=hf: e.tensor_tensor(out=DmI[hf], in0=Dh[hf], in1=m4(tri), op=ALU.mult), reads=[("Dh", hf), "cst"], writes=[("DmI", hf)])
            for hf in HF:
                hs = slice(hf * 4, hf * 4 + 4)
                pg.dve(lambda e, hf=hf: e.tensor_tensor(out=Dh[hf], in0=banks[gk_[hf]][:].rearrange(r4, a=4), in1=DmS[hf], op=ALU.mult), reads=[bk(gk_[hf]), ("DmS", hf), ("Dh", hf)], writes=[("Dh", hf)])
                pg.pool(lambda e, hf=hf, hs=hs: e.tensor_tensor(out=Dh[hf], in0=Dh[hf], in1=bc4(Bg[:, tc, hs]), op=ALU.mult), reads=[("Dh", hf), "Bg"], writes=[("Dh", hf)])
            if own:
                qb_ = [nb(), nb()]
                for hf in HF:
                    hs = slice(hf * 4, hf * 4 + 4)
                    for j in range(4):
                        pg.pe(lambda e, j=j, hf=hf: e.matmul(banks[qb_[hf]][:, j * 128:(j + 1) * 128], lhsT=KTc[b][:, hf * 4 + j, :], rhs=QTc[b][:, hf * 4 + j, :], start=True, stop=True), reads=[("KTc", b), ("QTc", b)], writes=[bk(qb_[hf])])
                    pg.dve(lambda e, hf=hf, hs=hs: e.tensor_tensor(out=AT[b][:, hs, :], in0=banks[qb_[hf]][:].rearrange(r4, a=4), in1=DmI[hf], op=ALU.mult), reads=[bk(qb_[hf]), ("DmI", hf)], writes=[("AT", b, hf)])
                xb_ = [nb(), nb()]
                for hf in HF:
                    hs = slice(hf * 4, hf * 4 + 4)
                    for j in range(4):
                        pg.pe(lambda e, j=j, hf=hf: e.matmul(banks[xb_[hf]][:, j * 128:(j + 1) * 128], lhsT=ones, rhs=RG[:, hf * 4 + j, :], start=True, stop=True), reads=["RG", "cst"], writes=[bk(xb_[hf])])
                    pg.act(lambda e, hf=hf: e.activation(out=DmI[hf], in_=banks[xb_[hf]][:].rearrange(r4, a=4), func=AF.Exp), reads=[bk(xb_[hf]), ("AT", b, hf)], writes=[("DmI", hf)])
                    pg.dve(lambda e, hf=hf, hs=hs: e.tensor_tensor(out=QD[b][:, hs, :], in0=DmI[hf], in1=QTc[b][:, hs, :], op=ALU.mult), reads=[("DmI", hf), ("QTc", b)], writes=[("QD", b, hf)])
            yield
            tb_ = [nb(), nb()]
            for hf in HF:
                hs = slice(hf * 4, hf * 4 + 4)
                for j in range(4):
                    pg.pe(lambda e, j=j, hf=hf: e.transpose(out=banks[tb_[hf]][:, j * 128:(j + 1) * 128], in_=Dh[hf][:, j, :], identity=ident), reads=[("Dh", hf), "cst"], writes=[bk(tb_[hf])])
                pg.act(lambda e, hf=hf: e.copy(out=Lb[hf][0], in_=banks[tb_[hf]][:].rearrange(r4, a=4)), reads=[bk(tb_[hf])], writes=[("L", hf, 0)])
                pg.act(lambda e, hf=hf: e.copy(out=Ub[hf][0], in_=Dh[hf]), reads=[("Dh", hf)], writes=[("U", hf, 0)])
                pg.pool(lambda e, hf=hf, hs=hs: e.tensor_tensor(out=Pm[b][:, hs, :], in0=m4(ident), in1=Dh[hf], op=ALU.subtract), reads=[("Dh", hf), "cst"], writes=[("Pm", b, hf)])
            cur = 0
            for lv in range(6):
                nx = 1 - cur
                l2 = [nb(), nb()]
                u2 = [nb(), nb()] if lv < 5 else None
                for hf in HF:
                    for j in range(4):
                        pg.pe(lambda e, j=j, hf=hf, cur=cur, l2=l2: e.matmul(banks[l2[hf]][:, j * 128:(j + 1) * 128], lhsT=Ub[hf][cur][:, j, :], rhs=Lb[hf][cur][:, j, :], start=True, stop=True), reads=[("U", hf, cur), ("L", hf, cur)], writes=[bk(l2[hf])])
                    if lv < 5:
                        for j in range(4):
                            pg.pe(lambda e, j=j, hf=hf, cur=cur, u2=u2: e.matmul(banks[u2[hf]][:, j * 128:(j + 1) * 128], lhsT=Lb[hf][cur][:, j, :], rhs=Ub[hf][cur][:, j, :], start=True, stop=True), reads=[("U", hf, cur), ("L", hf, cur)], writes=[bk(u2[hf])])
                for hf in HF:
                    pg.act(lambda e, hf=hf, nx=nx, l2=l2: e.copy(out=Lb[hf][nx], in_=banks[l2[hf]][:].rearrange(r4, a=4)), reads=[bk(l2[hf])], writes=[("L", hf, nx)])
                    if lv < 5:
                        if hf == 0:
                            pg.dve(lambda e, hf=hf, nx=nx, u2=u2: e.tensor_copy(out=Ub[hf][nx], in_=banks[u2[hf]][:].rearrange(r4, a=4)), reads=[bk(u2[hf])], writes=[("U", hf, nx)])
                        else:
                            pg.act(lambda e, hf=hf, nx=nx, u2=u2: e.copy(out=Ub[hf][nx], in_=banks[u2[hf]][:].rearrange(r4, a=4)), reads=[bk(u2[hf])], writes=[("U", hf, nx)])
                pb_ = [nb(), nb()]
                for hf in HF:
                    for j in range(4):
                        pg.pe(lambda e, j=j, hf=hf, nx=nx, pb_=pb_: e.matmul(banks[pb_[hf]][:, j * 128:(j + 1) * 128], lhsT=Lb[hf][nx][:, j, :], rhs=Pm[b][:, hf * 4 + j, :], start=True, stop=True), reads=[("L", hf, nx), ("Pm", b, hf)], writes=[bk(pb_[hf])])
                for hf in HF:
                    hs = slice(hf * 4, hf * 4 + 4)
                    pg.dve(lambda e, hf=hf, hs=hs, pb_=pb_: e.tensor_tensor(out=Pm[b][:, hs, :], in0=Pm[b][:, hs, :], in1=banks[pb_[hf]][:].rearrange(r4, a=4), op=ALU.add), reads=[bk(pb_[hf]), ("Pm", b, hf)], writes=[("Pm", b, hf)])
                cur = nx
                yield
            wb_ = [nb(), nb()]
            for hf in HF:
                hs = slice(hf * 4, hf * 4 + 4)
                for j in range(4):
                    pg.pe(lambda e, j=j, hf=hf: e.matmul(banks[wb_[hf]][:, j * 128:(j + 1) * 128], lhsT=KG[:, hf * 4 + j, :], rhs=Pm[b][:, hf * 4 + j, :], start=True, stop=True), reads=["KG", ("Pm", b, hf)], writes=[bk(wb_[hf])])
                pg.act(lambda e, hf=hf, hs=hs: e.mul(out=NW[b][:, hs, :], in_=banks[wb_[hf]][:].rearrange(r4, a=4), mul=-1.0), reads=[bk(wb_[hf])], writes=[("NW", b, hf)])

        def G2(tc):
            b = tc % 2
            tsl = slice(tc * 128, (tc + 1) * 128)
            g_ = gsm[b]
            HF = (0, 1)
            vb_ = [nb(), nb()]
            for hf in HF:
                hs = slice(hf * 4, hf * 4 + 4)
                for j in range(4):
                    h_ = hf * 4 + j
                    pg.pe(lambda e, j=j, hf=hf, h_=h_: e.matmul(banks[vb_[hf]][:, j * 128:(j + 1) * 128], lhsT=Pm[b][:, h_, :], rhs=Vtm[b][:, h_, :], start=True, stop=False), reads=[("Pm", b, hf), ("Vtm", b)], writes=[bk(vb_[hf])])
                    pg.pe(lambda e, j=j, hf=hf, h_=h_: e.matmul(banks[vb_[hf]][:, j * 128:(j + 1) * 128], lhsT=NW[b][:, h_, :], rhs=Sgb[:, h_, :], start=False, stop=True), reads=[("NW", b, hf), ("Sgb", hf)], writes=[bk(vb_[hf])])
                pg.dve(lambda e, hf=hf, hs=hs: e.tensor_tensor(out=VN[:, hs, :], in0=banks[vb_[hf]][:].rearrange(r4, a=4), in1=bc4(Bg[:, tc, hs]), op=ALU.mult), reads=[bk(vb_[hf]), "Bg"], writes=[("VN", hf)])
            yield
            if own:
                ob_ = [nb(), nb()]
                for hf in HF:
                    hs = slice(hf * 4, hf * 4 + 4)
                    for j in range(4):
                        h_ = hf * 4 + j
                        pg.pe(lambda e, j=j, hf=hf, h_=h_: e.matmul(banks[ob_[hf]][:, j * 128:(j + 1) * 128], lhsT=Sgb[:, h_, :], rhs=QD[b][:, h_, :], start=True, stop=False), reads=[("Sgb", hf), ("QD", b, hf)], writes=[bk(ob_[hf])])
                        pg.pe(lambda e, j=j, hf=hf, h_=h_: e.matmul(banks[ob_[hf]][:, j * 128:(j + 1) * 128], lhsT=VN[:, h_, :], rhs=AT[b][:, h_, :], start=False, stop=True), reads=[("VN", hf), ("AT", b, hf)], writes=[bk(ob_[hf])])
                    pg.act(lambda e, hf=hf, hs=hs: e.copy(out=oT[:, hs, :], in_=banks[ob_[hf]][:].rearrange(r4, a=4)), reads=[bk(ob_[hf])], writes=[("oT", hf), ("oT2", hf)])
            yield
            sb_ = [nb(), nb()]
            for hf in HF:
                hs = slice(hf * 4, hf * 4 + 4)
                for j in range(4):
                    h_ = hf * 4 + j
                    pg.pe(lambda e, j=j, hf=hf, h_=h_: e.matmul(banks[sb_[hf]][:, j * 128:(j + 1) * 128], lhsT=KD[b][:, h_, :], rhs=VN[:, h_, :], start=True, stop=True), reads=[("KD", b), ("VN", hf)], writes=[bk(sb_[hf])])
                pg.pool(lambda e, hs=hs, hf=hf: e.tensor_tensor(out=Sg[:, hs, :], in0=Sg[:, hs, :], in1=bc4(g_[:, 24 + hf * 4:28 + hf * 4]), op=ALU.mult), reads=[("Sg", hf), ("gtb", b)], writes=[("Sg", hf)])
                pg.dve(lambda e, hf=hf, hs=hs: e.tensor_tensor(out=Sg[:, hs, :], in0=Sg[:, hs, :], in1=banks[sb_[hf]][:].rearrange(r4, a=4), op=ALU.add), reads=[("Sg", hf), bk(sb_[hf])], writes=[("Sg", hf)])
                pg.act(lambda e, hs=hs, hf=hf: e.copy(out=Sgb[:, hs, :], in_=Sg[:, hs, :]), reads=[("Sg", hf)], writes=[("Sgb", hf)])
            yield
            if own:
                for hf in HF:
                    hs = slice(hf * 4, hf * 4 + 4)
                    pg.dve(lambda e, hs=hs: e.tensor_tensor(out=oS[:, hs, :], in0=oT[:, hs, :], in1=oT[:, hs, :], op=ALU.mult), reads=[("oT", hf)], writes=[("oS", hf), ("oS1", hf), ("oS2", hf)])
                    nb_ = nb()
                    pg.pe(lambda e, nb_=nb_, hs=hs: e.matmul(banks[nb_][:], lhsT=ones, rhs=oS[:, hs, :].rearrange("p a b -> p (a b)"), start=True, stop=True), reads=[("oS", hf), "cst"], writes=[bk(nb_)])
                    pg.act(lambda e, nb_=nb_, hs=hs: e.activation(out=oS[:, hs, :], in_=banks[nb_][:].rearrange(r4, a=4), func=AF.Ln, bias=EPS, scale=1.0 / 128), reads=[bk(nb_), ("oS", hf)], writes=[("oS1", hf), ("oS", hf)])
                    pg.act(lambda e, hs=hs: e.activation(out=oS[:, hs, :], in_=oS[:, hs, :], func=AF.Exp, scale=-0.5), reads=[("oS1", hf)], writes=[("oS2", hf)])
                    pg.dve(lambda e, hs=hs: e.tensor_tensor(out=oT[:, hs, :], in0=oT[:, hs, :], in1=oS[:, hs, :], op=ALU.mult), reads=[("oT", hf), ("oS2", hf)], writes=[("oT2", hf)])
                    pg.dve(lambda e, hs=hs, tsl=tsl, b=b: e.scalar_tensor_tensor(out=xn[:, hs, tsl], in0=oT[:, hs, :], scalar=sm[:, SM["gon"][0]:SM["gon"][0] + 1], in1=GTc[b][:, hs, :], op0=ALU.mult, op1=ALU.mult),
                           reads=[("oT2", hf), ("GTc", b), "sm"], writes=[("xn", c_) for c_ in range(hf * 4, hf * 4 + 4)])

        for _ in G1(0):
            pass
        for tc in range(8):
            subs = [G2(tc)]
            if tc + 1 < 8:
                subs.insert(0, G1(tc + 1))
            rnd = 0
            while subs:
                for g_ in list(subs):
                    try:
                        next(g_)
                    except StopIteration:
                        subs.remove(g_)
                rnd += 1
                if rnd == 2:
                    yield
            if rnd < 2:
                yield

    def ssd_gen(own, reset=True):
        if reset:
            ar.reset()
        Xtm = [ar.alloc([128, 16, 64]) for _ in range(2)]
        Btm = [ar.alloc([128, 256]) for _ in range(2)]
        XCD = ar.alloc([128, 16, 64])
        ssm_ = ar.alloc([128, 80])
        if own:
            BTc = [ar.alloc([128, 2, 128]) for _ in range(2)]
            CTc = [ar.alloc([128, 2, 128]) for _ in range(2)]
            XTc = [ar.alloc([128, 8, 128]) for _ in range(2)]
            ZTc = [ar.alloc([128, 8, 128]) for _ in range(2)]
            XC = ar.alloc([128, 16, 64], BF16)
            Ssb = ar.alloc([128, 1024], BF16)
            RL = ar.alloc([128, 16, 128])
            SEG = [ar.alloc([128, 4, 128]) for _ in range(2)]
            EA = [ar.alloc([128, 4, 128]) for _ in range(2)]
            CBm = ar.alloc([128, 2, 128])
            Mh = [ar.alloc([128, 4, 128], BF16) for _ in range(2)]
            CE = [ar.alloc([128, 4, 128], BF16) for _ in range(2)]
            yT = ar.alloc([128, 8, 128])
            ySq = ar.alloc([128, 8, 128])
            rr = ar.alloc([128, 2, 128])
            for g in range(2):
                pg.act(lambda e, g=g: e.copy(out=Ssb[:, g * 512:(g + 1) * 512], in_=Ss[:, g * 512:(g + 1) * 512]), reads=[("Ss", g)], writes=[("Ssb", g)])
        r4 = "p (a b) -> p a b"
        for tc in range(8):
            b = tc % 2
            tsl = slice(tc * 128, (tc + 1) * 128)
            tg = tc // 4
            pg.dma("sp", lambda e, b=b, tsl=tsl: e.dma_start(out=Xtm[b], in_=XTM[tsl, :].rearrange("p (h f) -> p h f", h=16)), reads=[("xTM", c, tg) for c in range(8)], writes=[("Xtm", b)])
            pg.dma("sp", lambda e, b=b, tsl=tsl: e.dma_start(out=Btm[b], in_=BTM[tsl, :]), reads=[("BTM", c, tg) for c in range(2)], writes=[("Btm", b)])
            if own:
                pg.dma("sp", lambda e, b=b, tsl=tsl: e.dma_start(out=BTc[b], in_=BT[:, :, tsl].rearrange("h p t -> p h t")), reads=[("BT", c) for c in range(2)], writes=[("BTc", b)])
                pg.dma("sp", lambda e, b=b, tsl=tsl: e.dma_start(out=CTc[b], in_=CT[:, :, tsl].rearrange("h p t -> p h t")), reads=[("CT", c) for c in range(2)], writes=[("CTc", b)])
                pg.dma("sp", lambda e, b=b, tsl=tsl: e.dma_start(out=XTc[b], in_=XT[:, :, tsl].rearrange("h p t -> p h t")), reads=[("xT", c) for c in range(8)], writes=[("XTc", b)])
                pg.dma("sp", lambda e, b=b, tsl=tsl: e.dma_start(out=ZTc[b], in_=ZT[:, :, tsl].rearrange("h p t -> p h t")), reads=[("zT", c) for c in range(8)], writes=[("ZTc", b)])
            ab_ = nb()
            pg.pe(lambda e, ab_=ab_, tc=tc: e.matmul(banks[ab_][:, 0:16], lhsT=tri, rhs=Las[:, tc, :], start=True, stop=True), reads=["Las", "cst"], writes=[bk(ab_)])
            pg.pe(lambda e, ab_=ab_, tc=tc: e.matmul(banks[ab_][:, 16:32], lhsT=ones, rhs=Las[:, tc, :], start=True, stop=True), reads=["Las", "cst"], writes=[bk(ab_)])
            pg.act(lambda e, ab_=ab_: e.copy(out=ssm_[:, 0:16], in_=banks[ab_][:, 0:16]), reads=[bk(ab_)], writes=["acs"])
            pg.dve(lambda e, ab_=ab_: e.tensor_tensor(out=ssm_[:, 48:64], in0=banks[ab_][:, 16:32], in1=ssm_[:, 0:16], op=ALU.subtract), reads=[bk(ab_), "acs"], writes=["stmp"])
            pg.act(lambda e: e.activation(out=ssm_[:, 16:32], in_=ssm_[:, 48:64], func=AF.Exp), reads=["stmp"], writes=["dte"])
            pg.act(lambda e, ab_=ab_: e.activation(out=ssm_[:, 32:48], in_=banks[ab_][:, 16:32], func=AF.Exp), reads=[bk(ab_)], writes=["cdb"])
            pg.dve(lambda e, tc=tc: e.tensor_tensor(out=ssm_[:, 64:80], in0=ssm_[:, 16:32], in1=Dts[:, tc, :], op=ALU.mult), reads=["dte", "Dts"], writes=["dtdte"])
            pg.pool(lambda e, b=b: e.tensor_tensor(out=XCD, in0=Xtm[b], in1=ssm_[:, 64:80].unsqueeze(2).broadcast_to([128, 16, 64]), op=ALU.mult), reads=[("Xtm", b), "dtdte"], writes=["XCD"])
            if own:
                pg.dve(lambda e, b=b, tc=tc: e.tensor_tensor(out=XC, in0=Xtm[b], in1=Dts[:, tc, :].unsqueeze(2).broadcast_to([128, 16, 64]), op=ALU.mult), reads=[("Xtm", b), "Dts"], writes=["XC"])
                pg.dve(lambda e, tc=tc: e.tensor_tensor(out=RL, in0=tri.unsqueeze(1).broadcast_to([128, 16, 128]), in1=Las[:, tc, :].unsqueeze(2).broadcast_to([128, 16, 128]), op=ALU.mult), reads=["Las", "cst"], writes=["RL"])
                for g in range(2):
                    cb_ = nb()
                    pg.pe(lambda e, cb_=cb_, g=g, b=b: e.matmul(banks[cb_][:, 0:128], lhsT=BTc[b][:, g, :], rhs=CTc[b][:, g, :], start=True, stop=True), reads=[("BTc", b), ("CTc", b)], writes=[bk(cb_)])
                    pg.dve(lambda e, cb_=cb_, g=g: e.tensor_tensor(out=CBm[:, g, :], in0=banks[cb_][:, 0:128], in1=tri, op=ALU.mult), reads=[bk(cb_), "cst"], writes=[("CBm", g)])
                pg.dve(lambda e, b=b: e.tensor_tensor(out=ySq, in0=XTc[b], in1=smc("sd").unsqueeze(2).broadcast_to([128, 8, 128]), op=ALU.mult), reads=[("XTc", b), "sm"], writes=["XD", "ySq"])

                def S1(q4):
                    g = q4 // 2
                    k2 = q4 % 2
                    se_ = nb()
                    ac_ = nb()
                    for j in range(4):
                        hh = q4 * 4 + j
                        pg.pe(lambda e, se_=se_, j=j, hh=hh: e.matmul(banks[se_][:, j * 128:(j + 1) * 128], lhsT=sgt, rhs=RL[:, hh, :], start=True, stop=True), reads=["RL", "cst"], writes=[bk(se_)])
                    for j in range(4):
                        hh = q4 * 4 + j
                        pg.pe(lambda e, ac_=ac_, j=j, hh=hh: e.matmul(banks[ac_][:, j * 128:(j + 1) * 128], lhsT=ones, rhs=RL[:, hh, :], start=True, stop=True), reads=["RL", "cst"], writes=[bk(ac_)])
                    pg.act(lambda e, se_=se_, k2=k2: e.activation(out=SEG[k2], in_=banks[se_][:].rearrange(r4, a=4), func=AF.Exp), reads=[bk(se_)], writes=[("SEG", k2)])
                    pg.act(lambda e, ac_=ac_, k2=k2: e.activation(out=EA[k2], in_=banks[ac_][:].rearrange(r4, a=4), func=AF.Exp), reads=[bk(ac_)], writes=[("EA", k2)])
                    pg.dve(lambda e, g=g, k2=k2: e.tensor_tensor(out=Mh[k2], in0=SEG[k2], in1=CBm[:, g, :].unsqueeze(1).broadcast_to([128, 4, 128]), op=ALU.mult), reads=[("SEG", k2), ("CBm", g)], writes=[("Mh", k2)])
                    pg.dve(lambda e, g=g, k2=k2, b=b: e.tensor_tensor(out=CE[k2], in0=EA[k2], in1=CTc[b][:, g, :].unsqueeze(1).broadcast_to([128, 4, 128]), op=ALU.mult), reads=[("EA", k2), ("CTc", b)], writes=[("CE", k2)])

                def S2(q4):
                    k2 = q4 % 2
                    yb_ = nb()
                    for j in range(4):
                        hh = q4 * 4 + j
                        pr = hh // 2
                        pg.pe(lambda e, yb_=yb_, j=j, pr=pr, k2=k2: e.matmul(banks[yb_][:, j * 128:(j + 1) * 128], lhsT=XC[:, 2 * pr:2 * pr + 2, :].rearrange("p a b -> p (a b)"), rhs=Mh[k2][:, j, :], start=True, stop=False), reads=["XC", ("Mh", k2)], writes=[bk(yb_)])
                        pg.pe(lambda e, yb_=yb_, j=j, pr=pr, k2=k2: e.matmul(banks[yb_][:, j * 128:(j + 1) * 128], lhsT=Ssb[:, pr * 128:(pr + 1) * 128], rhs=CE[k2][:, j, :], start=False, stop=True), reads=[("Ssb", pr // 4), ("CE", k2)], writes=[bk(yb_)])
                    p0 = q4 * 2
                    ybv = banks[yb_][:].rearrange("p (a b c) -> p a b c", a=2, b=2)
                    pg.dve(lambda e, ybv=ybv, p0=p0: e.tensor_tensor(out=yT[0:64, p0:p0 + 2, :], in0=ybv[0:64, :, 0, :], in1=ySq[0:64, p0:p0 + 2, :], op=ALU.add), reads=[bk(yb_), "XD"], writes=[("yT", p0, 0)])
                    pg.dve(lambda e, ybv=ybv, p0=p0: e.tensor_tensor(out=yT[64:128, p0:p0 + 2, :], in0=ybv[64:128, :, 1, :], in1=ySq[64:128, p0:p0 + 2, :], op=ALU.add), reads=[bk(yb_), "XD"], writes=[("yT", p0, 1)])

                S1(0)
                for q4 in range(4):
                    if q4 + 1 < 4:
                        S1(q4 + 1)
                    S2(q4)
                yr = [("yT", p0, k) for p0 in (0, 2, 4, 6) for k in (0, 1)]
                pg.dve(lambda e, b=b: e.tensor_tensor(out=yT, in0=yT, in1=ZTc[b], op=ALU.mult), reads=yr + [("ZTc", b)], writes=["yz"])
                pg.dve(lambda e: e.tensor_tensor(out=ySq, in0=yT, in1=yT, op=ALU.mult), reads=["yz", "XD"], writes=["ySq", "XD"])
                nb_ = nb()
                for g in range(2):
                    for k in range(4):
                        pg.pe(lambda e, nb_=nb_, g=g, k=k: e.matmul(banks[nb_][:, g * 128:(g + 1) * 128], lhsT=ones, rhs=ySq[:, g * 4 + k, :], start=(k == 0), stop=(k == 3)), reads=["ySq", "cst"], writes=[bk(nb_)])
                pg.act(lambda e, nb_=nb_: e.activation(out=rr, in_=banks[nb_][:, 0:256].rearrange("p (a b) -> p a b", a=2), func=AF.Ln, bias=EPS, scale=1.0 / 512), reads=[bk(nb_)], writes=["rr0", "rr"])
                pg.act(lambda e: e.activation(out=rr, in_=rr, func=AF.Exp, scale=-0.5), reads=["rr0"], writes=["rr"])
                for g in range(2):
                    pg.dve(lambda e, g=g: e.tensor_tensor(out=yT[:, g * 4:(g + 1) * 4, :], in0=yT[:, g * 4:(g + 1) * 4, :], in1=rr[:, g, :].unsqueeze(1).broadcast_to([128, 4, 128]), op=ALU.mult), reads=["yz", "rr"] + ([("yn", 0)] if g else []), writes=[("yn", g)])
                pg.dve(lambda e, tsl=tsl: e.tensor_tensor(out=xn[:, 8:16, tsl], in0=yT, in1=smc("son").unsqueeze(2).broadcast_to([128, 8, 128]), op=ALU.mult), reads=[("yn", 0), ("yn", 1), "sm"], writes=[("xn", c_) for c_ in range(8, 16)])
            for g in range(2):
                sb_ = nb()
                pg.pe(lambda e, sb_=sb_, g=g, b=b: e.matmul(banks[sb_][:], lhsT=Btm[b][:, g * 128:(g + 1) * 128], rhs=XCD[:, g * 8:(g + 1) * 8, :].rearrange("p a b -> p (a b)"), start=True, stop=True), reads=[("Btm", b), "XCD"], writes=[bk(sb_)])
                sv = Ss[:, g * 512:(g + 1) * 512].rearrange("p (a b) -> p a b", a=8)
                pg.dve(lambda e, sv=sv, g=g: e.tensor_tensor(out=sv, in0=sv, in1=ssm_[:, 32 + g * 8:40 + g * 8].unsqueeze(2).broadcast_to([128, 8, 64]), op=ALU.mult), reads=[("Ss", g), "cdb"], writes=[("Ss", g)])
                pg.dve(lambda e, sb_=sb_, g=g: e.tensor_tensor(out=Ss[:, g * 512:(g + 1) * 512], in0=Ss[:, g * 512:(g + 1) * 512], in1=banks[sb_][:], op=ALU.add), reads=[("Ss", g), bk(sb_)], writes=[("Ss", g)])
                if own:
                    pg.act(lambda e, g=g: e.copy(out=Ssb[:, g * 512:(g + 1) * 512], in_=Ss[:, g * 512:(g + 1) * 512]), reads=[("Ss", g)], writes=[("Ssb", g)])
            yield

    def dense_residual(wname, gate_fn=None):
        ar.reset()
        wsl = [ar.alloc([128, NKC, 128], BF16) for _ in range(3)]
        w = W[wname]
        for oc in range(NKC):
            sl = oc % 3
            pg.dma("pool", lambda e, sl=sl, oc=oc: e.dma_start(out=wsl[sl], in_=w[:, oc * 128:(oc + 1) * 128].rearrange("(kc p) f -> p kc f", p=128)), writes=[("wsl", sl)])
            for tt in range(2):
                bi = nb()
                for kc in range(NKC):
                    pg.pe(lambda e, bi=bi, sl=sl, kc=kc, tt=tt: e.matmul(banks[bi][:], lhsT=wsl[sl][:, kc, :], rhs=xn[:, kc, tt * 512:(tt + 1) * 512], start=(kc == 0), stop=(kc == NKC - 1)), reads=[("wsl", sl), ("xn", kc)], writes=[bk(bi)])
                pg.dve(lambda e, bi=bi, oc=oc, tt=tt: e.tensor_tensor(out=h[:, oc, tt * 512:(tt + 1) * 512], in0=h[:, oc, tt * 512:(tt + 1) * 512], in1=banks[bi][:], op=ALU.add), reads=[bk(bi), ("h", oc, tt)], writes=[("h", oc, tt)])

    def ple_phase():
        ar.reset()
        rmsnorm_to_xn(3)
        wsl = [ar.alloc([128, NKC, 128], BF16) for _ in range(3)]
        wpr = ar.alloc([128, 2, D], BF16)
        pT = ar.alloc([128, 2, T], BF16)
        gs = [ar.alloc([128, 512]) for _ in range(2)]
        pt_ = [ar.alloc([128, 512]) for _ in range(2)]
        sq2 = [ar.alloc([128, 512], BF16) for _ in range(2)]
        pg.dma("pool", lambda e: e.dma_start(out=wpr, in_=W["ple_w_proj"].rearrange("(kc p) f -> p kc f", p=128)), writes=["wpr"])
        pg.dma("pool", lambda e: e.dma_start(out=pT, in_=pin.rearrange("(kc p) t -> p kc t", p=128)), writes=["pT"])
        ptr = [["pT"], ["pT"]]
        ssb = [nb(), nb()]
        reserved.update(ssb)
        n = 0
        for oc in range(NKC):
            for tt in range(2):
                bi = nb()
                for kc in range(2):
                    pg.pe(lambda e, bi=bi, kc=kc, oc=oc, tt=tt: e.matmul(banks[bi][:], lhsT=wpr[:, kc, oc * 128:(oc + 1) * 128], rhs=pT[:, kc, tt * 512:(tt + 1) * 512], start=(kc == 0), stop=(kc == 1)),
                          reads=["wpr"] + ptr[tt], writes=[bk(bi)])
                q2 = n % 2
                n += 1
                pg.act(lambda e, bi=bi, q2=q2: e.activation(out=sq2[q2], in_=banks[bi][:], func=AF.Square), reads=[bk(bi)], writes=[("sq2", q2)])
                pg.pe(lambda e, q2=q2, tt=tt, oc=oc: e.matmul(banks[ssb[tt]][:], lhsT=ones_bf[:], rhs=sq2[q2], start=(oc == 0), stop=(oc == NKC - 1)), reads=[("sq2", q2), "ones_bf"], writes=[bk(ssb[tt])])
        for tt in range(2):
            pg.act(lambda e, tt=tt: e.activation(out=rstd[:, tt * 512:(tt + 1) * 512], in_=banks[ssb[tt]][:], func=AF.Ln, bias=EPS, scale=1.0 / D), reads=[bk(ssb[tt])], writes=[("rstd0", tt), ("rstd", tt)])
            pg.act(lambda e, tt=tt: e.activation(out=rstd[:, tt * 512:(tt + 1) * 512], in_=rstd[:, tt * 512:(tt + 1) * 512], func=AF.Exp, scale=-0.5), reads=[("rstd0", tt)], writes=[("rstd", tt)])
        reserved.clear()
        w = W["ple_w_gate"]
        for oc in range(NKC):
            sl = oc % 3
            pg.dma("pool", lambda e, sl=sl, oc=oc: e.dma_start(out=wsl[sl], in_=w[:, oc * 128:(oc + 1) * 128].rearrange("(kc p) f -> p kc f", p=128)), writes=[("wsl", sl)])
            for tt in range(2):
                bi = nb()
                for kc in range(NKC):
                    pg.pe(lambda e, bi=bi, sl=sl, kc=kc, tt=tt: e.matmul(banks[bi][:], lhsT=wsl[sl][:, kc, :], rhs=xn[:, kc, tt * 512:(tt + 1) * 512], start=(kc == 0), stop=(kc == NKC - 1)), reads=[("wsl", sl), ("xn", kc)], writes=[bk(bi)])
                pg.act(lambda e, bi=bi, tt=tt: e.activation(out=gs[tt], in_=banks[bi][:], func=AF.Sigmoid), reads=[bk(bi)], writes=[("gs", tt)])
                b2 = nb()
                for kc in range(2):
                    pg.pe(lambda e, b2=b2, kc=kc, oc=oc, tt=tt: e.matmul(banks[b2][:], lhsT=wpr[:, kc, oc * 128:(oc + 1) * 128], rhs=pT[:, kc, tt * 512:(tt + 1) * 512], start=(kc == 0), stop=(kc == 1)),
                          reads=["wpr"] + ptr[tt], writes=[bk(b2)])
                ts_ = slice(tt * 512, (tt + 1) * 512)
                pg.dve(lambda e, oc=oc, ts_=ts_, b2=b2, tt=tt: e.scalar_tensor_tensor(out=pt_[tt], in0=banks[b2][:], scalar=sm[:, 4 * 16 + oc:4 * 16 + oc + 1], in1=rstd[:, ts_], op0=ALU.mult, op1=ALU.mult),
                       reads=[bk(b2), ("rstd", tt), "sm"], writes=[("pt", tt)])
                pg.dve(lambda e, tt=tt: e.tensor_tensor(out=pt_[tt], in0=pt_[tt], in1=gs[tt], op=ALU.mult), reads=[("pt", tt), ("gs", tt)], writes=[("pt", tt)])
                pg.dve(lambda e, oc=oc, ts_=ts_, tt=tt: e.tensor_tensor(out=h[:, oc, ts_], in0=h[:, oc, ts_], in1=pt_[tt], op=ALU.add), reads=[("pt", tt), ("h", oc, tt)], writes=[("h", oc, tt)])

    def final_phase():
        rmsnorm_to_xn(5, dst=None, dst_is_xn=False)
        fins = []
        for cg in range(4):
            for j in range(4):
                c = cg * 4 + j
                pg.dve(lambda e, c=c: e.scalar_tensor_tensor(out=h[:, c, :], in0=h[:, c, :], scalar=sm[:, 5 * 16 + c:5 * 16 + c + 1], in1=rstd[:], op0=ALU.mult, op1=ALU.mult),
                       reads=[("h", c, 0), ("h", c, 1), ("rstd", 0), ("rstd", 1), "sm"], writes=[("h", c, 0), ("h", c, 1)])
            fins.append(pg.dma("sp", lambda e, cg=cg: e.dma_start(out=out[cg * 512:(cg + 1) * 512, :].rearrange("(c p) t -> p c t", p=128), in_=h[:, cg * 4:(cg + 1) * 4, :]),
                               reads=[("h", cg * 4 + j, tt) for j in range(4) for tt in range(2)], writes=[("out", cg)]))
        return fins

    stop = debug or ""
    fins = None
    preloaded = False
    for s in range(nslot):
        own = (s == nslot - 1)
        if not preloaded:
            run_all([load_x_gen(s)])
        preloaded = False
        ffn(W["ffn1_w_gate"], W["ffn1_w_up"], W["ffn1_w_down"], 0)
        if stop == "ffn1":
            continue
        proj_phase(s, own)
        if not own:
            ar.reset()
            run_all([gdn_gen(False, reset=False), ssd_gen(False, reset=False), load_x_gen(s + 1, reset=False, act_only=True)])
            preloaded = True
        else:
            run_all([gdn_gen(own)])
            run_all([ssd_gen(own)])
    if stop not in ("ffn1",):
        if stop != "nomix":
            dense_residual("w_out")
        if stop != "mix":
            ffn(W["ffn2_w_gate"], W["ffn2_w_up"], W["ffn2_w_down"], 2)
            ple_phase()
    if stop in ("ffn1", "mix"):
        fins = []
        for cg in range(4):
            fins.append(pg.dma("sp", lambda e, cg=cg: e.dma_start(out=out[cg * 512:(cg + 1) * 512, :].rearrange("(c p) t -> p c t", p=128), in_=h[:, cg * 4:(cg + 1) * 4, :]),
                               reads=[("h", cg * 4 + j, tt) for j in range(4) for tt in range(2)], writes=[("out", cg)]))
    else:
        fins = final_phase()
    pg.emit(final_wait_ops=fins)
    return nc


def _smalls(inp):
    sm = np.zeros((128, NSM), np.float32)

    def put(name, arr):
        o, w = SM[name]
        assert arr.shape == (128, w), (name, arr.shape)
        sm[:, o:o + w] = arr

    nws = [inp["ffn1_norm"][0], inp["mix_norm"][0], inp["ffn2_norm"][0], inp["ple_norm"][0], inp["ple_post_norm"][0], inp["final_norm"]]
    put("nw", np.concatenate([np.asarray(w).reshape(16, 128).T for w in nws], axis=1))
    put("gcw", np.asarray(inp["gdn_conv_w"][0]).reshape(4, 24, 128).transpose(2, 1, 0).reshape(128, 96))
    put("scw", np.asarray(inp["ssm_conv_w"][0]).reshape(4, 12, 128).transpose(2, 1, 0).reshape(128, 48))
    put("scb", np.asarray(inp["ssm_conv_b"][0]).reshape(12, 128).T)
    put("gon", np.asarray(inp["gdn_out_norm"][0]).reshape(128, 1))
    put("son", np.asarray(inp["ssm_out_norm"][0]).reshape(8, 128).T)
    put("sd", np.repeat(np.asarray(inp["ssm_d"][0]), 64).reshape(8, 128).T)
    put("galog", np.broadcast_to(np.asarray(inp["gdn_a_log"][0])[None, :], (128, 8)))
    put("gdtb", np.broadcast_to(np.asarray(inp["gdn_dt_bias"][0])[None, :], (128, 8)))
    put("salog", np.broadcast_to(np.asarray(inp["ssm_a_log"][0])[None, :], (128, 16)))
    put("sdtb", np.broadcast_to(np.asarray(inp["ssm_dt_bias"][0])[None, :], (128, 16)))
    return sm


def _consts():
    a = np.arange(128)[:, None]
    b = np.arange(128)[None, :]
    c = np.zeros((128, 5, 128), np.float32)
    c[:, 0] = (a == b)
    c[:, 1] = (a <= b)
    c[:, 2] = (a > b)
    c[:, 3] = (a < b)
    c[:, 4] = 1.0
    return c


_NC_CACHE = {}


def kernel(_nslot=NSLOT, _debug=None, _cores=8, **inputs):
    inp = {k: np.asarray(v) for k, v in inputs.items()}
    x = inp["x"]
    p = inp["p"][0]
    key = (_nslot, _debug)
    if key not in _NC_CACHE:
        _NC_CACHE[key] = build_nc(_nslot, _debug)
    nc = _NC_CACHE[key]
    sm = _smalls(inp)
    cst = _consts()
    wmap = {nm: np.ascontiguousarray(inp[nm][0]) for nm in ("ffn1_w_gate", "ffn1_w_up", "ffn1_w_down", "w_in", "w_out",
                                                           "ffn2_w_gate", "ffn2_w_up", "ffn2_w_down", "ple_w_gate", "ple_w_proj")}
    in_maps = []
    for r in range(_cores):
        b, q = r // 4, r % 4
        xs = np.zeros((_nslot, D, T), np.float32)
        cmask = np.zeros((128, 4), np.float32)
        for j in range(_nslot):
            seg = q - (_nslot - 1) + j
            if seg >= 0:
                xs[j] = x[b, seg * T:(seg + 1) * T].T
                cmask[:, j] = 1.0
        m = {"xs": xs, "pin": np.ascontiguousarray(p[b, q * T:(q + 1) * T].T), "cmask": cmask, "smalls": sm, "consts": cst}
        m.update(wmap)
        in_maps.append(m)
    res = run_bass_kernel_spmd(nc, in_maps, core_ids=list(range(_cores)))
    outp = np.zeros((2, 4 * T, D), np.float32)
    for r in range(_cores):
        b, q = r // 4, r % 4
        outp[b, q * T:(q + 1) * T] = res.results[r]["out"].T
    return outp
```

```python
import contextlib
import numpy as np
import concourse.bass as bass
import concourse.mybir as mybir
from concourse.bass_utils import run_bass_kernel_spmd

F32 = mybir.dt.float32
BF16 = mybir.dt.bfloat16
AF = mybir.ActivationFunctionType
ALU = mybir.AluOpType

D = 2048
T = 1024
NKC = 16
DFF = 5632
NFC = 44
PLE = 256
IN_DIM = 6688
EPS = 1e-6
NSLOT = 4
ENGS = ("pe", "act", "dve", "pool", "sp")


class Op:
    __slots__ = ("eng", "fn", "deps", "raw", "dma", "signal", "sigval", "sem", "idx", "ring_prev")

    def __init__(self, eng, fn, dma):
        self.eng = eng
        self.fn = fn
        self.dma = dma
        self.deps = set()
        self.raw = set()
        self.signal = False
        self.sigval = 0
        self.sem = None
        self.ring_prev = None


class Prog:
    def __init__(self, nc, strict=True, ring=8):
        self.nc = nc
        self.ops = []
        self.last_writer = {}
        self.readers = {}
        self.strict = strict
        self.ring = ring
        self.dma_ops = {"sp": [], "pool": []}
        self.phase_tok = None

    def op(self, eng, fn, reads=(), writes=(), dma=False):
        o = Op(eng, fn, dma)
        o.idx = len(self.ops)
        reads = list(reads)
        if self.phase_tok is not None:
            reads.append(self.phase_tok)
        for t in reads:
            w = self.last_writer.get(t)
            if w is not None:
                o.deps.add(w)
                o.raw.add(w)
        for t in writes:
            w = self.last_writer.get(t)
            if w is not None:
                o.deps.add(w)
            rd = self.readers.get(t)
            if rd:
                for lst in rd.values():
                    o.deps.update(lst)
        key = eng + ("_d" if dma else "")
        for t in reads:
            rd = self.readers.setdefault(t, {})
            lst = rd.setdefault(key, [])
            lst.append(o.idx)
            if dma:
                if len(lst) > self.ring:
                    del lst[0]
            elif len(lst) > 1:
                del lst[0]
        for t in writes:
            self.last_writer[t] = o.idx
            self.readers[t] = {}
        o.deps.discard(o.idx)
        if dma:
            lst = self.dma_ops[eng]
            k = len(lst)
            if k >= self.ring:
                o.ring_prev = lst[k - self.ring].idx
            lst.append(o)
        self.ops.append(o)
        return o

    def pe(self, fn, reads=(), writes=()):
        return self.op("pe", fn, reads, writes)

    def act(self, fn, reads=(), writes=()):
        return self.op("act", fn, reads, writes)

    def dve(self, fn, reads=(), writes=()):
        return self.op("dve", fn, reads, writes)

    def pool(self, fn, reads=(), writes=()):
        return self.op("pool", fn, reads, writes)

    def dma(self, q, fn, reads=(), writes=()):
        return self.op(q, fn, reads, writes, dma=True)

    def emit(self, final_wait_ops=()):
        nc = self.nc
        ops = self.ops
        for o in ops:
            for d in o.deps:
                od = ops[d]
                if od.dma:
                    continue
                if od.eng == o.eng and not o.dma:
                    if o.eng == "pe" or not self.strict or d not in o.raw:
                        continue
                od.signal = True
        with contextlib.ExitStack() as st:
            sems = {e: st.enter_context(nc.semaphore("s_" + e)) for e in ("pe", "act", "dve", "pool")}
            rings = {q: [st.enter_context(nc.semaphore("r_%s%d" % (q, i))) for i in range(self.ring)]
                     for q in ("sp", "pool") if self.dma_ops[q]}
            cnt = {e: 0 for e in sems}
            for o in ops:
                if not o.dma and o.signal:
                    cnt[o.eng] += 1
                    o.sigval = cnt[o.eng]
                    o.sem = sems[o.eng]
            for q, lst in self.dma_ops.items():
                for k, o in enumerate(lst):
                    o.sem = rings[q][k % self.ring]
                    o.sigval = 16 * (k // self.ring + 1)
            block = st.enter_context(nc.Block())
            engmap = {"pe": block.tensor, "act": block.scalar, "dve": block.vector,
                      "pool": block.gpsimd, "sp": block.sync}
            for e in ENGS:
                eops = [o for o in ops if o.eng == e]
                extra = list(final_wait_ops) if e == "sp" else []
                if not eops and not extra:
                    continue

                def body(eng, e=e, eops=eops, extra=extra):
                    waited = {}

                    def wait(od):
                        key = id(od.sem)
                        if waited.get(key, 0) >= od.sigval:
                            return
                        eng.wait_ge(od.sem, od.sigval)
                        waited[key] = od.sigval

                    for o in eops:
                        for d in sorted(o.deps):
                            od = ops[d]
                            if not od.dma and od.eng == e and not o.dma:
                                if e == "pe" or not self.strict or d not in o.raw:
                                    continue
                            wait(od)
                        if o.ring_prev is not None:
                            wait(ops[o.ring_prev])
                        ins = o.fn(eng)
                        if o.dma:
                            ins.then_inc(o.sem, 16)
                        elif o.signal:
                            ins.then_inc(o.sem, 1)
                    for o in extra:
                        wait(o)

                engmap[e](body)


SM = {}
_off = 0
for _n, _w in (("nw", 96), ("gcw", 96), ("scw", 48), ("scb", 12), ("gon", 1), ("son", 8), ("sd", 8),
               ("galog", 8), ("gdtb", 8), ("salog", 16), ("sdtb", 16)):
    SM[_n] = (_off, _w)
    _off += _w
NSM = _off


def build_nc(nslot=NSLOT, debug=None):
    nc = bass.Bass("TRN2", target_bir_lowering=False)
    dt = nc.dram_tensor
    xs = dt("xs", [nslot, D, T], F32, kind="ExternalInput").ap()
    pin = dt("pin", [PLE, T], F32, kind="ExternalInput").ap()
    cmask = dt("cmask", [128, 4], F32, kind="ExternalInput").ap()
    smalls = dt("smalls", [128, NSM], F32, kind="ExternalInput").ap()
    consts = dt("consts", [128, 5, 128], F32, kind="ExternalInput").ap()
    W = {}
    for nm, shp in (("ffn1_w_gate", [D, DFF]), ("ffn1_w_up", [D, DFF]), ("ffn1_w_down", [DFF, D]),
                    ("w_in", [D, IN_DIM]), ("w_out", [D, D]),
                    ("ffn2_w_gate", [D, DFF]), ("ffn2_w_up", [D, DFF]), ("ffn2_w_down", [DFF, D]),
                    ("ple_w_gate", [D, D]), ("ple_w_proj", [PLE, D])):
        W[nm] = dt(nm, shp, F32, kind="ExternalInput").ap()
    out = dt("out", [D, T], F32, kind="ExternalOutput").ap()
    QT = dt("QT", [8, 128, T], BF16, kind="Internal").ap()
    KT = dt("KT", [8, 128, T], BF16, kind="Internal").ap()
    GT = dt("GT", [8, 128, T], F32, kind="Internal").ap()
    ZT = dt("ZT", [8, 128, T], F32, kind="Internal").ap()
    XT = dt("XT", [8, 128, T], F32, kind="Internal").ap()
    BT = dt("BT", [2, 128, T], F32, kind="Internal").ap()
    CT = dt("CT", [2, 128, T], F32, kind="Internal").ap()
    KTM = dt("KTM", [T, 1024], BF16, kind="Internal").ap()
    VTM = dt("VTM", [T, 1024], BF16, kind="Internal").ap()
    XTM = dt("XTM", [T, 1024], F32, kind="Internal").ap()
    BTM = dt("BTM", [T, 256], F32, kind="Internal").ap()

    S = nc.alloc_sbuf_tensor
    h = S("h", [128, NKC, T], F32)
    xn = S("xn", [128, NKC, T], BF16)
    cst = S("cst", [128, 5, 128], F32)
    ones_bf = S("ones_bf", [128, 128], BF16)
    sm = S("sm", [128, NSM], F32)
    cm = S("cm", [128, 4], F32)
    negA_g = S("negA_g", [128, 8], F32)
    negA_s = S("negA_s", [128, 16], F32)
    rstd = S("rstd", [128, T], F32)
    sqb = [S("sqb%d" % i, [128, T], BF16) for i in range(2)]
    halo = S("halo", [128, 36, 3], F32)
    gts = S("gts", [128, 8, 32], F32)
    Gg = S("Gg", [128, 8, 8], F32)
    Bg = S("Bg", [128, 8, 8], F32)
    Dts = S("Dts", [128, 8, 16], F32)
    Las = S("Las", [128, 8, 16], F32)
    Sg = S("Sg", [128, 8, 128], F32)
    Ss = S("Ss", [128, 1024], F32)
    Sgb = S("Sgb", [128, 8, 128], BF16)
    scr = S("scr", [128, 8], F32)
    ARENA_F = 22100
    arena = S("arena", [128, ARENA_F], F32)
    banks = [nc.alloc_psum_tensor("bank%d" % i, [128, 512], F32) for i in range(8)]

    ident = cst[:, 0, :]
    tri = cst[:, 1, :]
    sgt = cst[:, 2, :]
    slt = cst[:, 3, :]
    ones = cst[:, 4, :]

    pg = Prog(nc)

    class Arena:
        def __init__(self):
            self.off = 0
            self.gen = 0

        def reset(self):
            self.off = 0
            self.gen += 1
            old = pg.phase_tok
            pg.phase_tok = None
            tok = ("phase", self.gen)
            pg.dve(lambda e: e.memset(scr[:, 0:1], 0.0), reads=[old] if old else [], writes=[tok, old] if old else [tok])
            pg.phase_tok = tok

        def alloc(self, shape, dtype=F32):
            n = 1
            for s_ in shape[1:]:
                n *= s_
            nf = n if dtype == F32 else (n + 1) // 2
            nf = (nf + 7) // 8 * 8
            assert self.off + nf <= ARENA_F, (self.off, nf)
            ap = arena[:, self.off:self.off + nf]
            self.off += nf
            if dtype != F32:
                ap = ap.bitcast(dtype)
            ap = ap[:, 0:n]
            if len(shape) == 3:
                ap = ap.rearrange("p (a b) -> p a b", a=shape[1])
            elif len(shape) == 4:
                ap = ap.rearrange("p (a b c) -> p a b c", a=shape[1], b=shape[2])
            if shape[0] < 128:
                ap = ap[0:shape[0]]
            return ap

    ar = Arena()
    uid = [0]

    def U(name):
        uid[0] += 1
        return (name, uid[0])

    bank_rr = [0]

    reserved = set()

    def nb():
        while True:
            i = bank_rr[0] % 8
            bank_rr[0] += 1
            if i not in reserved:
                return i

    def bk(i):
        return ("bank", i)

    alt = [0]

    def evac(fn, reads=(), writes=()):
        alt[0] ^= 1
        if alt[0]:
            return pg.act(lambda e: fn(e, True), reads, writes)
        return pg.dve(lambda e: fn(e, False), reads, writes)

    def copy_any(e, is_act, out, in_):
        if is_act:
            return e.copy(out=out, in_=in_)
        return e.tensor_copy(out=out, in_=in_)

    def smc(name, lo=0, hi=None):
        o, w = SM[name]
        hi = w if hi is None else hi
        return sm[:, o + lo:o + hi]

    pg.dma("sp", lambda e: e.dma_start(out=cst[:], in_=consts), writes=["cst"])
    pg.dma("sp", lambda e: e.dma_start(out=sm[:], in_=smalls), writes=["sm"])
    pg.dma("sp", lambda e: e.dma_start(out=cm[:], in_=cmask), writes=["cm"])
    pg.dve(lambda e: e.tensor_copy(out=ones_bf[:], in_=ones), reads=["cst"], writes=["ones_bf"])
    pg.act(lambda e: e.activation(out=negA_g[:], in_=smc("galog"), func=AF.Exp), reads=["sm"], writes=["negA_g0"])
    pg.dve(lambda e: e.tensor_single_scalar(out=negA_g[:], in_=negA_g[:], scalar=-1.0, op=ALU.mult), reads=["negA_g0"], writes=["negA_g"])
    pg.act(lambda e: e.activation(out=negA_s[:], in_=smc("salog"), func=AF.Exp), reads=["sm"], writes=["negA_s0"])
    pg.dve(lambda e: e.tensor_single_scalar(out=negA_s[:], in_=negA_s[:], scalar=-1.0, op=ALU.mult), reads=["negA_s0"], writes=["negA_s"])
    pg.dve(lambda e: e.memset(halo[:], 0.0), writes=[("halo", i) for i in range(36)])
    pg.dve(lambda e: e.memset(Sg[:], 0.0), writes=[("Sg", 0), ("Sg", 1)])
    pg.dve(lambda e: e.memset(Ss[:], 0.0), writes=[("Ss", 0), ("Ss", 1)])
    pg.dve(lambda e: e.memset(Sgb[:], 0.0), writes=[("Sgb", 0), ("Sgb", 1)])

    def run_all(gens):
        gens = list(gens)
        while gens:
            for g_ in list(gens):
                try:
                    next(g_)
                except StopIteration:
                    gens.remove(g_)

    def load_x_gen(s, reset=True, act_only=False):
        for cg in range(4):
            pg.dma("sp", lambda e, cg=cg: e.dma_start(out=h[:, cg * 4:(cg + 1) * 4, :], in_=xs[s, cg * 512:(cg + 1) * 512, :].rearrange("(c p) t -> p c t", p=128)),
                   writes=[("h", cg * 4 + j, tt) for j in range(4) for tt in range(2)])
            yield

    def rmsnorm_to_xn(widx, src=None, src_tok="h", dst=None, dst_tok="xn", dst_is_xn=True):
        src = h if src is None else src
        ssb = [nb(), nb()]
        for c in range(NKC):
            b = c % 2
            pg.act(lambda e, c=c, b=b: e.activation(out=sqb[b][:], in_=src[:, c, :], func=AF.Square),
                   reads=[(src_tok, c, 0), (src_tok, c, 1)], writes=[("sqb", b)])
            for tt in range(2):
                pg.pe(lambda e, c=c, b=b, tt=tt: e.matmul(banks[ssb[tt]][:], lhsT=ones_bf[:], rhs=sqb[b][:, tt * 512:(tt + 1) * 512], start=(c == 0), stop=(c == NKC - 1)),
                      reads=[("sqb", b), "ones_bf"], writes=[bk(ssb[tt])])
        for tt in range(2):
            pg.act(lambda e, tt=tt: e.activation(out=rstd[:, tt * 512:(tt + 1) * 512], in_=banks[ssb[tt]][:], func=AF.Ln, bias=EPS, scale=1.0 / D),
                   reads=[bk(ssb[tt])], writes=[("rstd0", tt), ("rstd", tt)])
            pg.act(lambda e, tt=tt: e.activation(out=rstd[:, tt * 512:(tt + 1) * 512], in_=rstd[:, tt * 512:(tt + 1) * 512], func=AF.Exp, scale=-0.5),
                   reads=[("rstd0", tt)], writes=[("rstd", tt)])
        if dst is None and not dst_is_xn:
            return
        dstt = xn if dst is None else dst
        for c in range(NKC):
            pg.dve(lambda e, c=c: e.scalar_tensor_tensor(out=dstt[:, c, :], in0=src[:, c, :], scalar=sm[:, widx * 16 + c:widx * 16 + c + 1], in1=rstd[:], op0=ALU.mult, op1=ALU.mult),
                   reads=[(src_tok, c, 0), (src_tok, c, 1), ("rstd", 0), ("rstd", 1), "sm"], writes=[(dst_tok, c)])

    def ffn(wg, wu, wd, widx):
        ar.reset()
        wgu = [ar.alloc([128, NKC, 256], BF16) for _ in range(4)]
        wdb = [ar.alloc([128, 4, D], BF16) for _ in range(2)]
        hid = [ar.alloc([128, 4, T], BF16) for _ in range(2)]
        sil = [ar.alloc([128, 512]) for _ in range(2)]
        gub = [(nb(), nb()), (nb(), nb())]
        dnb = [nb(), nb()]
        st = {"gu": 0, "dn": 0, "ws": 0}
        pre = {}

        def issue_w(g, pr):
            fo = (g * 4 + pr * 2) * 128
            sg_, su_ = st["ws"] % 4, (st["ws"] + 1) % 4
            st["ws"] += 2
            pg.dma("pool", lambda e, sg_=sg_, fo=fo: e.dma_start(out=wgu[sg_], in_=wg[:, fo:fo + 256].rearrange("(kc p) f -> p kc f", p=128)), writes=[("wgu", sg_)])
            pg.dma("pool", lambda e, su_=su_, fo=fo: e.dma_start(out=wgu[su_], in_=wu[:, fo:fo + 256].rearrange("(kc p) f -> p kc f", p=128)), writes=[("wgu", su_)])
            return sg_, su_

        pre[(0, 0)] = issue_w(0, 0)
        pre[(0, 1)] = issue_w(0, 1)
        rmsnorm_to_xn(widx)

        def gate_up(g):
            hb = g % 2
            for pr in range(2):
                fo = (g * 4 + pr * 2) * 128
                if (g, pr) in pre:
                    sg_, su_ = pre.pop((g, pr))
                else:
                    sg_, su_ = issue_w(g, pr)
                for fl in range(2):
                    fc = pr * 2 + fl
                    for tt in range(2):
                        gb, ub = gub[st["gu"] % 2]
                        st["gu"] += 1
                        for kc in range(NKC):
                            pg.pe(lambda e, gb=gb, sg_=sg_, kc=kc, fl=fl, tt=tt: e.matmul(banks[gb][:], lhsT=wgu[sg_][:, kc, fl * 128:(fl + 1) * 128], rhs=xn[:, kc, tt * 512:(tt + 1) * 512], start=(kc == 0), stop=(kc == NKC - 1)),
                                  reads=[("wgu", sg_), ("xn", kc)], writes=[bk(gb)])
                        for kc in range(NKC):
                            pg.pe(lambda e, ub=ub, su_=su_, kc=kc, fl=fl, tt=tt: e.matmul(banks[ub][:], lhsT=wgu[su_][:, kc, fl * 128:(fl + 1) * 128], rhs=xn[:, kc, tt * 512:(tt + 1) * 512], start=(kc == 0), stop=(kc == NKC - 1)),
                                  reads=[("wgu", su_), ("xn", kc)], writes=[bk(ub)])
                        sb_ = tt
                        pg.act(lambda e, gb=gb, sb_=sb_: e.activation(out=sil[sb_], in_=banks[gb][:], func=AF.Silu), reads=[bk(gb)], writes=[("sil", sb_)])
                        pg.dve(lambda e, ub=ub, sb_=sb_, hb=hb, fc=fc, tt=tt: e.tensor_tensor(out=hid[hb][:, fc, tt * 512:(tt + 1) * 512], in0=sil[sb_], in1=banks[ub][:], op=ALU.mult),
                               reads=[("sil", sb_), bk(ub)], writes=[("hid", hb, fc, tt)])

        def down(g):
            hb = g % 2
            pg.dma("pool", lambda e, hb=hb, g=g: e.dma_start(out=wdb[hb], in_=wd[g * 512:(g + 1) * 512, :].rearrange("(fc p) o -> p fc o", p=128)), writes=[("wdb", hb)])
            for oc in range(NKC):
                for tt in range(2):
                    db = dnb[st["dn"] % 2]
                    st["dn"] += 1
                    for fc in range(4):
                        pg.pe(lambda e, db=db, hb=hb, fc=fc, oc=oc, tt=tt: e.matmul(banks[db][:], lhsT=wdb[hb][:, fc, oc * 128:(oc + 1) * 128], rhs=hid[hb][:, fc, tt * 512:(tt + 1) * 512], start=(fc == 0), stop=(fc == 3)),
                              reads=[("wdb", hb), ("hid", hb, fc, tt)], writes=[bk(db)])
                    pg.dve(lambda e, db=db, oc=oc, tt=tt: e.scalar_tensor_tensor(out=h[:, oc, tt * 512:(tt + 1) * 512], in0=banks[db][:], scalar=0.5, in1=h[:, oc, tt * 512:(tt + 1) * 512], op0=ALU.mult, op1=ALU.add),
                           reads=[bk(db), ("h", oc, tt)], writes=[("h", oc, tt)])

        for g in range(11):
            gate_up(g)
            if g > 0:
                down(g - 1)
        down(10)

    def proj_phase(s, own):
        ar.reset()
        rmsnorm_to_xn(1)
        wsl = [ar.alloc([128, NKC, 128], BF16) for _ in range(3)]
        wgt = ar.alloc([128, NKC, 32], BF16)
        pc = [ar.alloc([128, T + 3]) for _ in range(2)]
        cs = [ar.alloc([128, T]) for _ in range(2)]
        cn = [ar.alloc([128, T]) for _ in range(2)]
        rs = ar.alloc([128, T])
        tms = [ar.alloc([128, 4, 128]) for _ in range(2)]
        tmsb = [ar.alloc([128, 4, 128], BF16) for _ in range(2)]
        cnb = [ar.alloc([128, T], BF16) for _ in range(2)]
        win = W["w_in"]
        pg.dma("pool", lambda e: e.dma_start(out=wgt[:, :, 0:16], in_=win[:, 4096:4112].rearrange("(kc p) f -> p kc f", p=128)), writes=["wgt_a"])
        pg.dma("pool", lambda e: e.dma_start(out=wgt[:, :, 16:32], in_=win[:, 6672:6688].rearrange("(kc p) f -> p kc f", p=128)), writes=["wgt_b"])
        gbk = nb()
        for tk in range(8):
            for kc in range(NKC):
                pg.pe(lambda e, tk=tk, kc=kc: e.matmul(banks[gbk][:, tk * 32:(tk + 1) * 32], lhsT=xn[:, kc, tk * 128:(tk + 1) * 128], rhs=wgt[:, kc, :], start=(kc == 0), stop=(kc == NKC - 1)),
                      reads=["wgt_a", "wgt_b", ("xn", kc)], writes=[bk(gbk)])
        pg.act(lambda e: e.copy(out=gts[:], in_=banks[gbk][:, 0:256].rearrange("p (a b) -> p a b", a=8)), reads=[bk(gbk)], writes=["gts"])
        pg.dve(lambda e: e.tensor_tensor(out=Gg[:], in0=gts[:, :, 0:8], in1=smc("gdtb").unsqueeze(1).broadcast_to([128, 8, 8]), op=ALU.add), reads=["gts", "sm"], writes=["Gg0", "Gg"])
        pg.act(lambda e: e.activation(out=Gg[:], in_=Gg[:], func=AF.Exp), reads=["Gg0"], writes=["Gg1"])
        pg.act(lambda e: e.activation(out=Gg[:], in_=Gg[:], func=AF.Ln, bias=1.0, scale=1.0), reads=["Gg1"], writes=["Gg2"])
        pg.dve(lambda e: e.tensor_tensor(out=Gg[:], in0=Gg[:], in1=negA_g[:].unsqueeze(1).broadcast_to([128, 8, 8]), op=ALU.mult), reads=["Gg2", "negA_g"], writes=["Gg"])
        pg.act(lambda e: e.activation(out=Bg[:], in_=gts[:, :, 8:16], func=AF.Sigmoid), reads=["gts"], writes=["Bg"])
        pg.dve(lambda e: e.tensor_tensor(out=Dts[:], in0=gts[:, :, 16:32], in1=smc("sdtb").unsqueeze(1).broadcast_to([128, 8, 16]), op=ALU.add), reads=["gts", "sm"], writes=["Dts0", "Dts"])
        pg.act(lambda e: e.activation(out=Dts[:], in_=Dts[:], func=AF.Exp), reads=["Dts0"], writes=["Dts1"])
        pg.act(lambda e: e.activation(out=Dts[:], in_=Dts[:], func=AF.Ln, bias=1.0, scale=1.0), reads=["Dts1"], writes=["Dts2"])
        pg.dve(lambda e: e.tensor_single_scalar(out=Dts[:], in_=Dts[:], scalar=cm[:, s:s + 1], op=ALU.mult), reads=["Dts2", "cm"], writes=["Dts"])
        pg.dve(lambda e: e.tensor_tensor(out=Las[:], in0=Dts[:], in1=negA_s[:].unsqueeze(1).broadcast_to([128, 8, 16]), op=ALU.mult), reads=["Dts", "negA_s"], writes=["Las"])

        chunks = []
        for c in range(8):
            chunks.append(("q", c, c * 128, c, own))
        for c in range(8):
            chunks.append(("k", c, 1024 + c * 128, 8 + c, True))
        for c in range(8):
            chunks.append(("v", c, 2048 + c * 128, 16 + c, True))
        for c in range(8):
            chunks.append(("gate", c, 3072 + c * 128, None, own))
        for c in range(8):
            chunks.append(("z", c, 4112 + c * 128, None, own))
        for c in range(8):
            chunks.append(("x", c, 5136 + c * 128, 24 + c, True))
        for c in range(2):
            chunks.append(("B", c, 6160 + c * 128, 32 + c, True))
        for c in range(2):
            chunks.append(("C", c, 6416 + c * 128, 34 + c, own))
        items = [ch for ch in chunks if ch[4]]
        if s == nslot - 2:
            hb_ = nb()
            hn_ = 0
            for kind, c, col, ci, _ in chunks:
                if kind not in ("q", "C"):
                    continue
                sl = hn_ % 3
                pg.dma("pool", lambda e, sl=sl, col=col: e.dma_start(out=wsl[sl], in_=win[:, col:col + 128].rearrange("(kc p) f -> p kc f", p=128)), writes=[("wsl", sl)])
                for kc in range(NKC):
                    pg.pe(lambda e, kc=kc, sl=sl, hn_=hn_: e.matmul(banks[hb_][:, hn_ * 4:hn_ * 4 + 3], lhsT=wsl[sl][:, kc, :], rhs=xn[:, kc, T - 3:T], start=(kc == 0), stop=(kc == NKC - 1)),
                          reads=[("wsl", sl), ("xn", kc)], writes=[bk(hb_)])
                pg.act(lambda e, ci=ci, hn_=hn_: e.copy(out=halo[:, ci, :], in_=banks[hb_][:, hn_ * 4:hn_ * 4 + 3]), reads=[bk(hb_)], writes=[("halo", ci)])
                hn_ += 1
        N_ = len(items)
        gz = [ar.alloc([128, T]) for _ in range(2)]
        mmb = [(0, 1), (2, 3)]
        l2b = (4, 5)
        trb = (6, 7)

        def A1(n):
            kind, c, col, ci, _ = items[n]
            sl = n % 3
            pg.dma("pool", lambda e, sl=sl, col=col: e.dma_start(out=wsl[sl], in_=win[:, col:col + 128].rearrange("(kc p) f -> p kc f", p=128)), writes=[("wsl", sl)])
            pb = mmb[n % 2]
            for tt in range(2):
                for kc in range(NKC):
                    pg.pe(lambda e, tt=tt, kc=kc, sl=sl, pb=pb: e.matmul(banks[pb[tt]][:], lhsT=wsl[sl][:, kc, :], rhs=xn[:, kc, tt * 512:(tt + 1) * 512], start=(kc == 0), stop=(kc == NKC - 1)),
                          reads=[("wsl", sl), ("xn", kc)], writes=[bk(pb[tt])])

        def A2(n):
            kind, c, col, ci, _ = items[n]
            b2 = n % 2
            pb = mmb[n % 2]
            if ci is None:
                for tt in range(2):
                    pg.act(lambda e, tt=tt, b2=b2, pb=pb: e.activation(out=gz[b2][:, tt * 512:(tt + 1) * 512], in_=banks[pb[tt]][:], func=AF.Silu), reads=[bk(pb[tt])], writes=[("gz", b2, tt)])
                dst = GT if kind == "gate" else ZT
                pg.dma("sp", lambda e, dst=dst, c=c, b2=b2: e.dma_start(out=dst[c], in_=gz[b2]), reads=[("gz", b2, 0), ("gz", b2, 1)], writes=[(kind + "T", c)])
                return
            pg.dve(lambda e, b2=b2, ci=ci: e.tensor_copy(out=pc[b2][:, 0:3], in_=halo[:, ci, :]), reads=[("halo", ci)], writes=[("pc", b2, "h")])
            for tt in range(2):
                pg.act(lambda e, tt=tt, b2=b2, pb=pb: e.copy(out=pc[b2][:, 3 + tt * 512:3 + (tt + 1) * 512], in_=banks[pb[tt]][:]), reads=[bk(pb[tt])], writes=[("pc", b2, tt)])
            pg.dve(lambda e, b2=b2, ci=ci: e.tensor_copy(out=halo[:, ci, :], in_=pc[b2][:, T:T + 3]), reads=[("pc", b2, 1)], writes=[("halo", ci)])

        srcs = {}

        def B1(n):
            kind, c, col, ci, _ = items[n]
            if ci is None:
                return
            b2 = n % 2
            cwo = SM["gcw"][0] if ci < 24 else SM["scw"][0]
            cj = ci if ci < 24 else ci - 24
            pcr = [("pc", b2, "h"), ("pc", b2, 0), ("pc", b2, 1), "sm"]
            pg.dve(lambda e, b2=b2, cwo=cwo, cj=cj: e.tensor_single_scalar(out=cn[b2], in_=pc[b2][:, 0:T], scalar=sm[:, cwo + cj * 4:cwo + cj * 4 + 1], op=ALU.mult), reads=pcr, writes=[("cn", b2)])
            for j in range(1, 4):
                pg.dve(lambda e, b2=b2, cwo=cwo, cj=cj, j=j: e.scalar_tensor_tensor(out=cn[b2], in0=pc[b2][:, j:T + j], scalar=sm[:, cwo + cj * 4 + j:cwo + cj * 4 + j + 1], in1=cn[b2], op0=ALU.mult, op1=ALU.add),
                       reads=pcr + [("cn", b2)], writes=[("cn", b2)])
            if ci < 24:
                pg.act(lambda e, b2=b2: e.activation(out=cs[b2], in_=cn[b2], func=AF.Silu), reads=[("cn", b2)], writes=[("cs", b2, 0), ("cs", b2, 1)])
            else:
                pg.act(lambda e, b2=b2, cj=cj: e.activation(out=cs[b2], in_=cn[b2], func=AF.Silu, bias=sm[:, SM["scb"][0] + cj:SM["scb"][0] + cj + 1], scale=1.0), reads=[("cn", b2), "sm"], writes=[("cs", b2, 0), ("cs", b2, 1)])
            csr = [("cs", b2, 0), ("cs", b2, 1)]
            src = cs[b2]
            if kind in ("q", "k"):
                pg.dve(lambda e, b2=b2: e.tensor_tensor(out=cnb[b2], in0=cs[b2], in1=cs[b2], op=ALU.mult), reads=csr, writes=[("cnb", b2)])
                lb = l2b
                for tt in range(2):
                    pg.pe(lambda e, tt=tt, b2=b2, lb=lb: e.matmul(banks[lb[tt]][:], lhsT=ones_bf[:], rhs=cnb[b2][:, tt * 512:(tt + 1) * 512], start=True, stop=True), reads=[("cnb", b2), "ones_bf"], writes=[bk(lb[tt])])
                for tt in range(2):
                    pg.act(lambda e, tt=tt, lb=lb: e.activation(out=rs[:, tt * 512:(tt + 1) * 512], in_=banks[lb[tt]][:], func=AF.Ln, bias=EPS, scale=1.0), reads=[bk(lb[tt])], writes=[("rs0", tt), ("rs", tt)])
                for tt in range(2):
                    pg.act(lambda e, tt=tt: e.activation(out=rs[:, tt * 512:(tt + 1) * 512], in_=rs[:, tt * 512:(tt + 1) * 512], func=AF.Exp, scale=-0.5), reads=[("rs0", tt)], writes=[("rs", tt)])
                sc = (128.0 ** -0.5) if kind == "q" else 1.0
                pg.dve(lambda e, b2=b2, sc=sc: e.scalar_tensor_tensor(out=cn[b2], in0=cs[b2], scalar=sc, in1=rs, op0=ALU.mult, op1=ALU.mult), reads=csr + [("rs", 0), ("rs", 1), ("cn", b2)], writes=[("cn", b2)])
                src = cn[b2]
                csr = [("cn", b2)]
            srcs[n] = (src, csr)
            fm = {"q": QT, "k": KT, "x": XT, "B": BT, "C": CT}.get(kind)
            if fm is not None and (own or kind == "k"):
                if kind in ("q", "k"):
                    pg.act(lambda e, b2=b2, src=src: e.copy(out=cnb[b2], in_=src), reads=csr, writes=[("cnb", b2)])
                    pg.dma("sp", lambda e, fm=fm, c=c, b2=b2: e.dma_start(out=fm[c], in_=cnb[b2]), reads=[("cnb", b2)], writes=[(kind + "T", c)])
                else:
                    pg.dma("sp", lambda e, fm=fm, c=c, src=src: e.dma_start(out=fm[c], in_=src), reads=csr, writes=[(kind + "T", c)])

        def B2(n):
            kind, c, col, ci, _ = items[n]
            if ci is None:
                return
            src, csr = srcs[n]
            tm = {"k": KTM, "v": VTM, "x": XTM, "B": BTM}.get(kind)
            if tm is not None:
                for tg in range(2):
                    bi = trb[tg]
                    tb = tg
                    for j in range(4):
                        tk = tg * 4 + j
                        pg.pe(lambda e, bi=bi, j=j, tk=tk, src=src: e.transpose(out=banks[bi][:, j * 128:(j + 1) * 128], in_=src[:, tk * 128:(tk + 1) * 128], identity=ident), reads=csr + ["cst"], writes=[bk(bi)])
                    tmx = tmsb if kind in ("k", "v") else tms
                    tmt = "tmsb" if kind in ("k", "v") else "tms"
                    evac(lambda e, a, bi=bi, tb=tb, tmx=tmx: copy_any(e, a, tmx[tb], banks[bi][:].rearrange("p (a b) -> p a b", a=4)), reads=[bk(bi)], writes=[(tmt, tb)])
                    pg.dma("sp", lambda e, tm=tm, tg=tg, tb=tb, c=c, tmx=tmx: e.dma_start(out=tm[tg * 512:(tg + 1) * 512, c * 128:(c + 1) * 128].rearrange("(a p) f -> p a f", p=128), in_=tmx[tb]),
                           reads=[(tmt, tb)], writes=[(kind + "TM", c, tg)])

        for n in range(-2, N_ + 1):
            if 0 <= n + 2 < N_:
                A1(n + 2)
            if 0 <= n < N_:
                B1(n)
            if 0 <= n + 1 < N_:
                A2(n + 1)
            if 0 <= n - 1 < N_:
                B2(n - 1)

    def gdn_gen(own, reset=True):
        if reset:
            ar.reset()
        KTc = [ar.alloc([128, 8, 128], BF16) for _ in range(2)]
        Ktm = [ar.alloc([128, 8, 128], BF16) for _ in range(2)]
        Vtm = [ar.alloc([128, 8, 128], BF16) for _ in range(2)]
        QTc = [ar.alloc([128, 8, 128], BF16) for _ in range(2)] if own else None
        GTc = [ar.alloc([128, 8, 128]) for _ in range(2)] if own else None
        RG = ar.alloc([128, 8, 128])
        KG = ar.alloc([128, 8, 128], BF16)
        KD = [ar.alloc([128, 8, 128], BF16) for _ in range(2)]
        Pm = [ar.alloc([128, 8, 128], BF16) for _ in range(2)]
        NW = [ar.alloc([128, 8, 128], BF16) for _ in range(2)]
        AT = [ar.alloc([128, 8, 128], BF16) for _ in range(2)] if own else None
        QD = [ar.alloc([128, 8, 128], BF16) for _ in range(2)] if own else None
        gsm = [ar.alloc([128, 40]) for _ in range(2)]
        Dh = [ar.alloc([128, 4, 128]) for _ in range(2)]
        DmS = [ar.alloc([128, 4, 128]) for _ in range(2)]
        DmI = [ar.alloc([128, 4, 128]) for _ in range(2)] if own else None
        Ub = [[ar.alloc([128, 4, 128], BF16) for _ in range(2)] for _ in range(2)]
        Lb = [[ar.alloc([128, 4, 128], BF16) for _ in range(2)] for _ in range(2)]
        VN = ar.alloc([128, 8, 128], BF16)
        oT = ar.alloc([128, 8, 128]) if own else None
        oS = ar.alloc([128, 8, 128]) if own else None
        r4 = "p (a b) -> p a b"
        bc4 = lambda ap: ap.unsqueeze(2).broadcast_to([128, 4, 128])
        m4 = lambda ap: ap.unsqueeze(1).broadcast_to([128, 4, 128])

        def G1(tc):
            b = tc % 2
            tsl = slice(tc * 128, (tc + 1) * 128)
            tg = tc // 4
            g_ = gsm[b]
            pg.dma("sp", lambda e: e.dma_start(out=KTc[b], in_=KT[:, :, tsl].rearrange("h p t -> p h t")), reads=[("kT", c) for c in range(8)], writes=[("KTc", b)])
            pg.dma("sp", lambda e: e.dma_start(out=Ktm[b], in_=KTM[tsl, :].rearrange("p (h f) -> p h f", h=8)), reads=[("kTM", c, tg) for c in range(8)], writes=[("Ktm", b)])
            pg.dma("sp", lambda e: e.dma_start(out=Vtm[b], in_=VTM[tsl, :].rearrange("p (h f) -> p h f", h=8)), reads=[("vTM", c, tg) for c in range(8)], writes=[("Vtm", b)])
            if own:
                pg.dma("sp", lambda e: e.dma_start(out=QTc[b], in_=QT[:, :, tsl].rearrange("h p t -> p h t")), reads=[("qT", c) for c in range(8)], writes=[("QTc", b)])
                pg.dma("sp", lambda e: e.dma_start(out=GTc[b], in_=GT[:, :, tsl].rearrange("h p t -> p h t")), reads=[("gateT", c) for c in range(8)], writes=[("GTc", b)])
            gb_ = nb()
            pg.pe(lambda e: e.matmul(banks[gb_][:, 0:8], lhsT=tri, rhs=Gg[:, tc, :], start=True, stop=True), reads=["Gg", "cst"], writes=[bk(gb_)])
            pg.pe(lambda e: e.matmul(banks[gb_][:, 8:16], lhsT=ones, rhs=Gg[:, tc, :], start=True, stop=True), reads=["Gg", "cst"], writes=[bk(gb_)])
            pg.act(lambda e: e.copy(out=g_[:, 0:8], in_=banks[gb_][:, 0:8]), reads=[bk(gb_)], writes=[("gcs", b)])
            pg.act(lambda e: e.activation(out=g_[:, 8:16], in_=g_[:, 0:8], func=AF.Exp), reads=[("gcs", b)], writes=[("egc", b)])
            pg.dve(lambda e: e.tensor_tensor(out=g_[:, 32:40], in0=banks[gb_][:, 8:16], in1=g_[:, 0:8], op=ALU.subtract), reads=[bk(gb_), ("gcs", b)], writes=[("gtmp", b)])
            pg.act(lambda e: e.activation(out=g_[:, 16:24], in_=g_[:, 32:40], func=AF.Exp), reads=[("gtmp", b)], writes=[("dkd", b)])
            pg.act(lambda e: e.activation(out=g_[:, 24:32], in_=banks[gb_][:, 8:16], func=AF.Exp), reads=[bk(gb_)], writes=[("gtb", b)])
            pg.pool(lambda e: e.tensor_tensor(out=RG, in0=tri.unsqueeze(1).broadcast_to([128, 8, 128]), in1=Gg[:, tc, :].unsqueeze(2).broadcast_to([128, 8, 128]), op=ALU.mult), reads=["Gg", "cst"], writes=["RG"])
            pg.pool(lambda e: e.tensor_tensor(out=KG, in0=Ktm[b], in1=g_[:, 8:16].unsqueeze(2).broadcast_to([128, 8, 128]), op=ALU.mult), reads=[("Ktm", b), ("egc", b)], writes=["KG"])
            pg.pool(lambda e: e.tensor_tensor(out=KD[b], in0=Ktm[b], in1=g_[:, 16:24].unsqueeze(2).broadcast_to([128, 8, 128]), op=ALU.mult), reads=[("Ktm", b), ("dkd", b)], writes=[("KD", b)])
            HF = (0, 1)
            eb_ = [nb(), nb()]
            for hf in HF:
                for j in range(4):
                    pg.pe(lambda e, j=j, hf=hf: e.matmul(banks[eb_[hf]][:, j * 128:(j + 1) * 128], lhsT=sgt, rhs=RG[:, hf * 4 + j, :], start=True, stop=True), reads=["RG", "cst"], writes=[bk(eb_[hf])])
            gk_ = [nb(), nb()]
            for hf in HF:
                for j in range(4):
                    pg.pe(lambda e, j=j, hf=hf: e.matmul(banks[gk_[hf]][:, j * 128:(j + 1) * 128], lhsT=KTc[b][:, hf * 4 + j, :], rhs=KTc[b][:, hf * 4 + j, :], start=True, stop=True), reads=[("KTc", b)], writes=[bk(gk_[hf])])
            for hf in HF:
                pg.act(lambda e, hf=hf: e.activation(out=Dh[hf], in_=banks[eb_[hf]][:].rearrange(r4, a=4), func=AF.Exp), reads=[bk(eb_[hf])], writes=[("Dh", hf)])
                pg.pool(lambda e, hf=hf: e.tensor_tensor(out=DmS[hf], in0=Dh[hf], in1=m4(slt), op=ALU.mult), reads=[("Dh", hf), "cst"], writes=[("DmS", hf)])
                if own:
                    pg.dve(lambda e, hf=hf: e.tensor_tensor(out=DmI[hf], in0=Dh[hf], in1=m4(tri), op=ALU.mult), reads=[("Dh", hf), "cst"], writes=[("DmI", hf)])
            for hf in HF:
                hs = slice(hf * 4, hf * 4 + 4)
                pg.dve(lambda e, hf=hf: e.tensor_tensor(out=Dh[hf], in0=banks[gk_[hf]][:].rearrange(r4, a=4), in1=DmS[hf], op=ALU.mult), reads=[bk(gk_[hf]), ("DmS", hf), ("Dh", hf)], writes=[("Dh", hf)])
                pg.pool(lambda e, hf=hf, hs=hs: e.tensor_tensor(out=Dh[hf], in0=Dh[hf], in1=bc4(Bg[:, tc, hs]), op=ALU.mult), reads=[("Dh", hf), "Bg"], writes=[("Dh", hf)])
            if own:
                qb_ = [nb(), nb()]
                for hf in HF:
                    hs = slice(hf * 4, hf * 4 + 4)
                    for j in range(4):
                        pg.pe(lambda e, j=j, hf=hf: e.matmul(banks[qb_[hf]][:, j * 128:(j + 1) * 128], lhsT=KTc[b][:, hf * 4 + j, :], rhs=QTc[b][:, hf * 4 + j, :], start=True, stop=True), reads=[("KTc", b), ("QTc", b)], writes=[bk(qb_[hf])])
                    pg.dve(lambda e, hf=hf, hs=hs: e.tensor_tensor(out=AT[b][:, hs, :], in0=banks[qb_[hf]][:].rearrange(r4, a=4), in1=DmI[hf], op=ALU.mult), reads=[bk(qb_[hf]), ("DmI", hf)], writes=[("AT", b, hf)])
                xb_ = [nb(), nb()]
                for hf in HF:
                    hs = slice(hf * 4, hf * 4 + 4)
                    for j in range(4):
                        pg.pe(lambda e, j=j, hf=hf: e.matmul(banks[xb_[hf]][:, j * 128:(j + 1) * 128], lhsT=ones, rhs=RG[:, hf * 4 + j, :], start=True, stop=True), reads=["RG", "cst"], writes=[bk(xb_[hf])])
                    pg.act(lambda e, hf=hf: e.activation(out=DmI[hf], in_=banks[xb_[hf]][:].rearrange(r4, a=4), func=AF.Exp), reads=[bk(xb_[hf]), ("AT", b, hf)], writes=[("DmI", hf)])
                    pg.dve(lambda e, hf=hf, hs=hs: e.tensor_tensor(out=QD[b][:, hs, :], in0=DmI[hf], in1=QTc[b][:, hs, :], op=ALU.mult), reads=[("DmI", hf), ("QTc", b)], writes=[("QD", b, hf)])
            yield
            tb_ = [nb(), nb()]
            for hf in HF:
                hs = slice(hf * 4, hf * 4 + 4)
                for j in range(4):
                    pg.pe(lambda e, j=j, hf=hf: e.transpose(out=banks[tb_[hf]][:, j * 128:(j + 1) * 128], in_=Dh[hf][:, j, :], identity=ident), reads=[("Dh", hf), "cst"], writes=[bk(tb_[hf])])
                pg.act(lambda e, hf=hf: e.copy(out=Lb[hf][0], in_=banks[tb_[hf]][:].rearrange(r4, a=4)), reads=[bk(tb_[hf])], writes=[("L", hf, 0)])
                pg.act(lambda e, hf=hf: e.copy(out=Ub[hf][0], in_=Dh[hf]), reads=[("Dh", hf)], writes=[("U", hf, 0)])
                pg.pool(lambda e, hf=hf, hs=hs: e.tensor_tensor(out=Pm[b][:, hs, :], in0=m4(ident), in1=Dh[hf], op=ALU.subtract), reads=[("Dh", hf), "cst"], writes=[("Pm", b, hf)])
            cur = 0
            for lv in range(6):
                nx = 1 - cur
                l2 = [nb(), nb()]
                u2 = [nb(), nb()] if lv < 5 else None
                for hf in HF:
                    for j in range(4):
                        pg.pe(lambda e, j=j, hf=hf, cur=cur, l2=l2: e.matmul(banks[l2[hf]][:, j * 128:(j + 1) * 128], lhsT=Ub[hf][cur][:, j, :], rhs=Lb[hf][cur][:, j, :], start=True, stop=True), reads=[("U", hf, cur), ("L", hf, cur)], writes=[bk(l2[hf])])
                    if lv < 5:
                        for j in range(4):
                            pg.pe(lambda e, j=j, hf=hf, cur=cur, u2=u2: e.matmul(banks[u2[hf]][:, j * 128:(j + 1) * 128], lhsT=Lb[hf][cur][:, j, :], rhs=Ub[hf][cur][:, j, :], start=True, stop=True), reads=[("U", hf, cur), ("L", hf, cur)], writes=[bk(u2[hf])])
                for hf in HF:
                    pg.act(lambda e, hf=hf, nx=nx, l2=l2: e.copy(out=Lb[hf][nx], in_=banks[l2[hf]][:].rearrange(r4, a=4)), reads=[bk(l2[hf])], writes=[("L", hf, nx)])
                    if lv < 5:
                        if hf == 0:
                            pg.dve(lambda e, hf=hf, nx=nx, u2=u2: e.tensor_copy(out=Ub[hf][nx], in_=banks[u2[hf]][:].rearrange(r4, a=4)), reads=[bk(u2[hf])], writes=[("U", hf, nx)])
                        else:
                            pg.act(lambda e, hf=hf, nx=nx, u2=u2: e.copy(out=Ub[hf][nx], in_=banks[u2[hf]][:].rearrange(r4, a=4)), reads=[bk(u2[hf])], writes=[("U", hf, nx)])
                pb_ = [nb(), nb()]
                for hf in HF:
                    for j in range(4):
                        pg.pe(lambda e, j=j, hf=hf, nx=nx, pb_=pb_: e.matmul(banks[pb_[hf]][:, j * 128:(j + 1) * 128], lhsT=Lb[hf][nx][:, j, :], rhs=Pm[b][:, hf * 4 + j, :], start=True, stop=True), reads=[("L", hf, nx), ("Pm", b, hf)], writes=[bk(pb_[hf])])
                for hf in HF:
                    hs = slice(hf * 4, hf * 4 + 4)
                    pg.dve(lambda e, hf=hf, hs=hs, pb_=pb_: e.tensor_tensor(out=Pm[b][:, hs, :], in0=Pm[b][:, hs, :], in1=banks[pb_[hf]][:].rearrange(r4, a=4), op=ALU.add), reads=[bk(pb_[hf]), ("Pm", b, hf)], writes=[("Pm", b, hf)])
                cur = nx
                yield
            wb_ = [nb(), nb()]
            for hf in HF:
                hs = slice(hf * 4, hf * 4 + 4)
                for j in range(4):
                    pg.pe(lambda e, j=j, hf=hf: e.matmul(banks[wb_[hf]][:, j * 128:(j + 1) * 128], lhsT=KG[:, hf * 4 + j, :], rhs=Pm[b][:, hf * 4 + j, :], start=True, stop=True), reads=["KG", ("Pm", b, hf)], writes=[bk(wb_[hf])])
                pg.act(lambda e, hf=hf, hs=hs: e.mul(out=NW[b][:, hs, :], in_=banks[wb_[hf]][:].rearrange(r4, a=4), mul=-1.0), reads=[bk(wb_[hf])], writes=[("NW", b, hf)])

        def G2(tc):
            b = tc % 2
            tsl = slice(tc * 128, (tc + 1) * 128)
            g_ = gsm[b]
            HF = (0, 1)
            vb_ = [nb(), nb()]
            for hf in HF:
                hs = slice(hf * 4, hf * 4 + 4)
                for j in range(4):
                    h_ = hf * 4 + j
                    pg.pe(lambda e, j=j, hf=hf, h_=h_: e.matmul(banks[vb_[hf]][:, j * 128:(j + 1) * 128], lhsT=Pm[b][:, h_, :], rhs=Vtm[b][:, h_, :], start=True, stop=False), reads=[("Pm", b, hf), ("Vtm", b)], writes=[bk(vb_[hf])])
                    pg.pe(lambda e, j=j, hf=hf, h_=h_: e.matmul(banks[vb_[hf]][:, j * 128:(j + 1) * 128], lhsT=NW[b][:, h_, :], rhs=Sgb[:, h_, :], start=False, stop=True), reads=[("NW", b, hf), ("Sgb", hf)], writes=[bk(vb_[hf])])
                pg.dve(lambda e, hf=hf, hs=hs: e.tensor_tensor(out=VN[:, hs, :], in0=banks[vb_[hf]][:].rearrange(r4, a=4), in1=bc4(Bg[:, tc, hs]), op=ALU.mult), reads=[bk(vb_[hf]), "Bg"], writes=[("VN", hf)])
            yield
            if own:
                ob_ = [nb(), nb()]
                for hf in HF:
                    hs = slice(hf * 4, hf * 4 + 4)
                    for j in range(4):
                        h_ = hf * 4 + j
                        pg.pe(lambda e, j=j, hf=hf, h_=h_: e.matmul(banks[ob_[hf]][:, j * 128:(j + 1) * 128], lhsT=Sgb[:, h_, :], rhs=QD[b][:, h_, :], start=True, stop=False), reads=[("Sgb", hf), ("QD", b, hf)], writes=[bk(ob_[hf])])
                        pg.pe(lambda e, j=j, hf=hf, h_=h_: e.matmul(banks[ob_[hf]][:, j * 128:(j + 1) * 128], lhsT=VN[:, h_, :], rhs=AT[b][:, h_, :], start=False, stop=True), reads=[("VN", hf), ("AT", b, hf)], writes=[bk(ob_[hf])])
                    pg.act(lambda e, hf=hf, hs=hs: e.copy(out=oT[:, hs, :], in_=banks[ob_[hf]][:].rearrange(r4, a=4)), reads=[bk(ob_[hf])], writes=[("oT", hf), ("oT2", hf)])
            yield
            sb_ = [nb(), nb()]
            for hf in HF:
                hs = slice(hf * 4, hf * 4 + 4)
                for j in range(4):
                    h_ = hf * 4 + j
                    pg.pe(lambda e, j=j, hf=hf, h_=h_: e.matmul(banks[sb_[hf]][:, j * 128:(j + 1) * 128], lhsT=KD[b][:, h_, :], rhs=VN[:, h_, :], start=True, stop=True), reads=[("KD", b), ("VN", hf)], writes=[bk(sb_[hf])])
                pg.pool(lambda e, hs=hs, hf=hf: e.tensor_tensor(out=Sg[:, hs, :], in0=Sg[:, hs, :], in1=bc4(g_[:, 24 + hf * 4:28 + hf * 4]), op=ALU.mult), reads=[("Sg", hf), ("gtb", b)], writes=[("Sg", hf)])
                pg.dve(lambda e, hf=hf, hs=hs: e.tensor_tensor(out=Sg[:, hs, :], in0=Sg[:, hs, :], in1=banks[sb_[hf]][:].rearrange(r4, a=4), op=ALU.add), reads=[("Sg", hf), bk(sb_[hf])], writes=[("Sg", hf)])
                pg.act(lambda e, hs=hs, hf=hf: e.copy(out=Sgb[:, hs, :], in_=Sg[:, hs, :]), reads=[("Sg", hf)], writes=[("Sgb", hf)])
            yield
            if own:
                for hf in HF:
                    hs = slice(hf * 4, hf * 4 + 4)
                    pg.dve(lambda e, hs=hs: e.tensor_tensor(out=oS[:, hs, :], in0=oT[:, hs, :], in1=oT[:, hs, :], op=ALU.mult), reads=[("oT", hf)], writes=[("oS", hf), ("oS1", hf), ("oS2", hf)])
                    nb_ = nb()
                    pg.pe(lambda e, nb_=nb_, hs=hs: e.matmul(banks[nb_][:], lhsT=ones, rhs=oS[:, hs, :].rearrange("p a b -> p (a b)"), start=True, stop=True), reads=[("oS", hf), "cst"], writes=[bk(nb_)])
                    pg.act(lambda e, nb_=nb_, hs=hs: e.activation(out=oS[:, hs, :], in_=banks[nb_][:].rearrange(r4, a=4), func=AF.Ln, bias=EPS, scale=1.0 / 128), reads=[bk(nb_), ("oS", hf)], writes=[("oS1", hf), ("oS", hf)])
                    pg.act(lambda e, hs=hs: e.activation(out=oS[:, hs, :], in_=oS[:, hs, :], func=AF.Exp, scale=-0.5), reads=[("oS1", hf)], writes=[("oS2", hf)])
                    pg.dve(lambda e, hs=hs: e.tensor_tensor(out=oT[:, hs, :], in0=oT[:, hs, :], in1=oS[:, hs, :], op=ALU.mult), reads=[("oT", hf), ("oS2", hf)], writes=[("oT2", hf)])
                    pg.dve(lambda e, hs=hs, tsl=tsl, b=b: e.scalar_tensor_tensor(out=xn[:, hs, tsl], in0=oT[:, hs, :], scalar=sm[:, SM["gon"][0]:SM["gon"][0] + 1], in1=GTc[b][:, hs, :], op0=ALU.mult, op1=ALU.mult),
                           reads=[("oT2", hf), ("GTc", b), "sm"], writes=[("xn", c_) for c_ in range(hf * 4, hf * 4 + 4)])

        for _ in G1(0):
            pass
        for tc in range(8):
            subs = [G2(tc)]
            if tc + 1 < 8:
                subs.insert(0, G1(tc + 1))
            rnd = 0
            while subs:
                for g_ in list(subs):
                    try:
                        next(g_)
                    except StopIteration:
                        subs.remove(g_)
                rnd += 1
                if rnd == 2:
                    yield
            if rnd < 2:
                yield

    def ssd_gen(own, reset=True):
        if reset:
            ar.reset()
        Xtm = [ar.alloc([128, 16, 64]) for _ in range(2)]
        Btm = [ar.alloc([128, 256]) for _ in range(2)]
        XCD = ar.alloc([128, 16, 64])
        ssm_ = ar.alloc([128, 80])
        if own:
            BTc = [ar.alloc([128, 2, 128]) for _ in range(2)]
            CTc = [ar.alloc([128, 2, 128]) for _ in range(2)]
            XTc = [ar.alloc([128, 8, 128]) for _ in range(2)]
            ZTc = [ar.alloc([128, 8, 128]) for _ in range(2)]
            XC = ar.alloc([128, 16, 64], BF16)
            Ssb = ar.alloc([128, 1024], BF16)
            RL = ar.alloc([128, 16, 128])
            SEG = [ar.alloc([128, 4, 128]) for _ in range(2)]
            EA = [ar.alloc([128, 4, 128]) for _ in range(2)]
            CBm = ar.alloc([128, 2, 128])
            Mh = [ar.alloc([128, 4, 128], BF16) for _ in range(2)]
            CE = [ar.alloc([128, 4, 128], BF16) for _ in range(2)]
            yT = ar.alloc([128, 8, 128])
            ySq = ar.alloc([128, 8, 128])
            rr = ar.alloc([128, 2, 128])
            for g in range(2):
                pg.act(lambda e, g=g: e.copy(out=Ssb[:, g * 512:(g + 1) * 512], in_=Ss[:, g * 512:(g + 1) * 512]), reads=[("Ss", g)], writes=[("Ssb", g)])
        r4 = "p (a b) -> p a b"
        for tc in range(8):
            b = tc % 2
            tsl = slice(tc * 128, (tc + 1) * 128)
            tg = tc // 4
            pg.dma("sp", lambda e, b=b, tsl=tsl: e.dma_start(out=Xtm[b], in_=XTM[tsl, :].rearrange("p (h f) -> p h f", h=16)), reads=[("xTM", c, tg) for c in range(8)], writes=[("Xtm", b)])
            pg.dma("sp", lambda e, b=b, tsl=tsl: e.dma_start(out=Btm[b], in_=BTM[tsl, :]), reads=[("BTM", c, tg) for c in range(2)], writes=[("Btm", b)])
            if own:
                pg.dma("sp", lambda e, b=b, tsl=tsl: e.dma_start(out=BTc[b], in_=BT[:, :, tsl].rearrange("h p t -> p h t")), reads=[("BT", c) for c in range(2)], writes=[("BTc", b)])
                pg.dma("sp", lambda e, b=b, tsl=tsl: e.dma_start(out=CTc[b], in_=CT[:, :, tsl].rearrange("h p t -> p h t")), reads=[("CT", c) for c in range(2)], writes=[("CTc", b)])
                pg.dma("sp", lambda e, b=b, tsl=tsl: e.dma_start(out=XTc[b], in_=XT[:, :, tsl].rearrange("h p t -> p h t")), reads=[("xT", c) for c in range(8)], writes=[("XTc", b)])
                pg.dma("sp", lambda e, b=b, tsl=tsl: e.dma_start(out=ZTc[b], in_=ZT[:, :, tsl].rearrange("h p t -> p h t")), reads=[("zT", c) for c in range(8)], writes=[("ZTc", b)])
            ab_ = nb()
            pg.pe(lambda e, ab_=ab_, tc=tc: e.matmul(banks[ab_][:, 0:16], lhsT=tri, rhs=Las[:, tc, :], start=True, stop=True), reads=["Las", "cst"], writes=[bk(ab_)])
            pg.pe(lambda e, ab_=ab_, tc=tc: e.matmul(banks[ab_][:, 16:32], lhsT=ones, rhs=Las[:, tc, :], start=True, stop=True), reads=["Las", "cst"], writes=[bk(ab_)])
            pg.act(lambda e, ab_=ab_: e.copy(out=ssm_[:, 0:16], in_=banks[ab_][:, 0:16]), reads=[bk(ab_)], writes=["acs"])
            pg.dve(lambda e, ab_=ab_: e.tensor_tensor(out=ssm_[:, 48:64], in0=banks[ab_][:, 16:32], in1=ssm_[:, 0:16], op=ALU.subtract), reads=[bk(ab_), "acs"], writes=["stmp"])
            pg.act(lambda e: e.activation(out=ssm_[:, 16:32], in_=ssm_[:, 48:64], func=AF.Exp), reads=["stmp"], writes=["dte"])
            pg.act(lambda e, ab_=ab_: e.activation(out=ssm_[:, 32:48], in_=banks[ab_][:, 16:32], func=AF.Exp), reads=[bk(ab_)], writes=["cdb"])
            pg.dve(lambda e, tc=tc: e.tensor_tensor(out=ssm_[:, 64:80], in0=ssm_[:, 16:32], in1=Dts[:, tc, :], op=ALU.mult), reads=["dte", "Dts"], writes=["dtdte"])
            pg.pool(lambda e, b=b: e.tensor_tensor(out=XCD, in0=Xtm[b], in1=ssm_[:, 64:80].unsqueeze(2).broadcast_to([128, 16, 64]), op=ALU.mult), reads=[("Xtm", b), "dtdte"], writes=["XCD"])
            if own:
                pg.dve(lambda e, b=b, tc=tc: e.tensor_tensor(out=XC, in0=Xtm[b], in1=Dts[:, tc, :].unsqueeze(2).broadcast_to([128, 16, 64]), op=ALU.mult), reads=[("Xtm", b), "Dts"], writes=["XC"])
                pg.dve(lambda e, tc=tc: e.tensor_tensor(out=RL, in0=tri.unsqueeze(1).broadcast_to([128, 16, 128]), in1=Las[:, tc, :].unsqueeze(2).broadcast_to([128, 16, 128]), op=ALU.mult), reads=["Las", "cst"], writes=["RL"])
                for g in range(2):
                    cb_ = nb()
                    pg.pe(lambda e, cb_=cb_, g=g, b=b: e.matmul(banks[cb_][:, 0:128], lhsT=BTc[b][:, g, :], rhs=CTc[b][:, g, :], start=True, stop=True), reads=[("BTc", b), ("CTc", b)], writes=[bk(cb_)])
                    pg.dve(lambda e, cb_=cb_, g=g: e.tensor_tensor(out=CBm[:, g, :], in0=banks[cb_][:, 0:128], in1=tri, op=ALU.mult), reads=[bk(cb_), "cst"], writes=[("CBm", g)])
                pg.dve(lambda e, b=b: e.tensor_tensor(out=ySq, in0=XTc[b], in1=smc("sd").unsqueeze(2).broadcast_to([128, 8, 128]), op=ALU.mult), reads=[("XTc", b), "sm"], writes=["XD", "ySq"])

                def S1(q4):
                    g = q4 // 2
                    k2 = q4 % 2
                    se_ = nb()
                    ac_ = nb()
                    for j in range(4):
                        hh = q4 * 4 + j
                        pg.pe(lambda e, se_=se_, j=j, hh=hh: e.matmul(banks[se_][:, j * 128:(j + 1) * 128], lhsT=sgt, rhs=RL[:, hh, :], start=True, stop=True), reads=["RL", "cst"], writes=[bk(se_)])
                    for j in range(4):
                        hh = q4 * 4 + j
                        pg.pe(lambda e, ac_=ac_, j=j, hh=hh: e.matmul(banks[ac_][:, j * 128:(j + 1) * 128], lhsT=ones, rhs=RL[:, hh, :], start=True, stop=True), reads=["RL", "cst"], writes=[bk(ac_)])
                    pg.act(lambda e, se_=se_, k2=k2: e.activation(out=SEG[k2], in_=banks[se_][:].rearrange(r4, a=4), func=AF.Exp), reads=[bk(se_)], writes=[("SEG", k2)])
                    pg.act(lambda e, ac_=ac_, k2=k2: e.activation(out=EA[k2], in_=banks[ac_][:].rearrange(r4, a=4), func=AF.Exp), reads=[bk(ac_)], writes=[("EA", k2)])
                    pg.dve(lambda e, g=g, k2=k2: e.tensor_tensor(out=Mh[k2], in0=SEG[k2], in1=CBm[:, g, :].unsqueeze(1).broadcast_to([128, 4, 128]), op=ALU.mult), reads=[("SEG", k2), ("CBm", g)], writes=[("Mh", k2)])
                    pg.dve(lambda e, g=g, k2=k2, b=b: e.tensor_tensor(out=CE[k2], in0=EA[k2], in1=CTc[b][:, g, :].unsqueeze(1).broadcast_to([128, 4, 128]), op=ALU.mult), reads=[("EA", k2), ("CTc", b)], writes=[("CE", k2)])

                def S2(q4):
                    k2 = q4 % 2
                    yb_ = nb()
                    for j in range(4):
                        hh = q4 * 4 + j
                        pr = hh // 2
                        pg.pe(lambda e, yb_=yb_, j=j, pr=pr, k2=k2: e.matmul(banks[yb_][:, j * 128:(j + 1) * 128], lhsT=XC[:, 2 * pr:2 * pr + 2, :].rearrange("p a b -> p (a b)"), rhs=Mh[k2][:, j, :], start=True, stop=False), reads=["XC", ("Mh", k2)], writes=[bk(yb_)])
                        pg.pe(lambda e, yb_=yb_, j=j, pr=pr, k2=k2: e.matmul(banks[yb_][:, j * 128:(j + 1) * 128], lhsT=Ssb[:, pr * 128:(pr + 1) * 128], rhs=CE[k2][:, j, :], start=False, stop=True), reads=[("Ssb", pr // 4), ("CE", k2)], writes=[bk(yb_)])
                    p0 = q4 * 2
                    ybv = banks[yb_][:].rearrange("p (a b c) -> p a b c", a=2, b=2)
                    pg.dve(lambda e, ybv=ybv, p0=p0: e.tensor_tensor(out=yT[0:64, p0:p0 + 2, :], in0=ybv[0:64, :, 0, :], in1=ySq[0:64, p0:p0 + 2, :], op=ALU.add), reads=[bk(yb_), "XD"], writes=[("yT", p0, 0)])
                    pg.dve(lambda e, ybv=ybv, p0=p0: e.tensor_tensor(out=yT[64:128, p0:p0 + 2, :], in0=ybv[64:128, :, 1, :], in1=ySq[64:128, p0:p0 + 2, :], op=ALU.add), reads=[bk(yb_), "XD"], writes=[("yT", p0, 1)])

                S1(0)
                for q4 in range(4):
                    if q4 + 1 < 4:
                        S1(q4 + 1)
                    S2(q4)
                yr = [("yT", p0, k) for p0 in (0, 2, 4, 6) for k in (0, 1)]
                pg.dve(lambda e, b=b: e.tensor_tensor(out=yT, in0=yT, in1=ZTc[b], op=ALU.mult), reads=yr + [("ZTc", b)], writes=["yz"])
                pg.dve(lambda e: e.tensor_tensor(out=ySq, in0=yT, in1=yT, op=ALU.mult), reads=["yz", "XD"], writes=["ySq", "XD"])
                nb_ = nb()
                for g in range(2):
                    for k in range(4):
                        pg.pe(lambda e, nb_=nb_, g=g, k=k: e.matmul(banks[nb_][:, g * 128:(g + 1) * 128], lhsT=ones, rhs=ySq[:, g * 4 + k, :], start=(k == 0), stop=(k == 3)), reads=["ySq", "cst"], writes=[bk(nb_)])
                pg.act(lambda e, nb_=nb_: e.activation(out=rr, in_=banks[nb_][:, 0:256].rearrange("p (a b) -> p a b", a=2), func=AF.Ln, bias=EPS, scale=1.0 / 512), reads=[bk(nb_)], writes=["rr0", "rr"])
                pg.act(lambda e: e.activation(out=rr, in_=rr, func=AF.Exp, scale=-0.5), reads=["rr0"], writes=["rr"])
                for g in range(2):
                    pg.dve(lambda e, g=g: e.tensor_tensor(out=yT[:, g * 4:(g + 1) * 4, :], in0=yT[:, g * 4:(g + 1) * 4, :], in1=rr[:, g, :].unsqueeze(1).broadcast_to([128, 4, 128]), op=ALU.mult), reads=["yz", "rr"] + ([("yn", 0)] if g else []), writes=[("yn", g)])
                pg.dve(lambda e, tsl=tsl: e.tensor_tensor(out=xn[:, 8:16, tsl], in0=yT, in1=smc("son").unsqueeze(2).broadcast_to([128, 8, 128]), op=ALU.mult), reads=[("yn", 0), ("yn", 1), "sm"], writes=[("xn", c_) for c_ in range(8, 16)])
            for g in range(2):
                sb_ = nb()
                pg.pe(lambda e, sb_=sb_, g=g, b=b: e.matmul(banks[sb_][:], lhsT=Btm[b][:, g * 128:(g + 1) * 128], rhs=XCD[:, g * 8:(g + 1) * 8, :].rearrange("p a b -> p (a b)"), start=True, stop=True), reads=[("Btm", b), "XCD"], writes=[bk(sb_)])
                sv = Ss[:, g * 512:(g + 1) * 512].rearrange("p (a b) -> p a b", a=8)
                pg.dve(lambda e, sv=sv, g=g: e.tensor_tensor(out=sv, in0=sv, in1=ssm_[:, 32 + g * 8:40 + g * 8].unsqueeze(2).broadcast_to([128, 8, 64]), op=ALU.mult), reads=[("Ss", g), "cdb"], writes=[("Ss", g)])
                pg.dve(lambda e, sb_=sb_, g=g: e.tensor_tensor(out=Ss[:, g * 512:(g + 1) * 512], in0=Ss[:, g * 512:(g + 1) * 512], in1=banks[sb_][:], op=ALU.add), reads=[("Ss", g), bk(sb_)], writes=[("Ss", g)])
                if own:
                    pg.act(lambda e, g=g: e.copy(out=Ssb[:, g * 512:(g + 1) * 512], in_=Ss[:, g * 512:(g + 1) * 512]), reads=[("Ss", g)], writes=[("Ssb", g)])
            yield

    def dense_residual(wname, gate_fn=None):
        ar.reset()
        wsl = [ar.alloc([128, NKC, 128], BF16) for _ in range(3)]
        w = W[wname]
        for oc in range(NKC):
            sl = oc % 3
            pg.dma("pool", lambda e, sl=sl, oc=oc: e.dma_start(out=wsl[sl], in_=w[:, oc * 128:(oc + 1) * 128].rearrange("(kc p) f -> p kc f", p=128)), writes=[("wsl", sl)])
            for tt in range(2):
                bi = nb()
                for kc in range(NKC):
                    pg.pe(lambda e, bi=bi, sl=sl, kc=kc, tt=tt: e.matmul(banks[bi][:], lhsT=wsl[sl][:, kc, :], rhs=xn[:, kc, tt * 512:(tt + 1) * 512], start=(kc == 0), stop=(kc == NKC - 1)), reads=[("wsl", sl), ("xn", kc)], writes=[bk(bi)])
                pg.dve(lambda e, bi=bi, oc=oc, tt=tt: e.tensor_tensor(out=h[:, oc, tt * 512:(tt + 1) * 512], in0=h[:, oc, tt * 512:(tt + 1) * 512], in1=banks[bi][:], op=ALU.add), reads=[bk(bi), ("h", oc, tt)], writes=[("h", oc, tt)])

    def ple_phase():
        ar.reset()
        rmsnorm_to_xn(3)
        wsl = [ar.alloc([128, NKC, 128], BF16) for _ in range(3)]
        wpr = ar.alloc([128, 2, D], BF16)
        pT = ar.alloc([128, 2, T], BF16)
        gs = [ar.alloc([128, 512]) for _ in range(2)]
        pt_ = [ar.alloc([128, 512]) for _ in range(2)]
        sq2 = [ar.alloc([128, 512], BF16) for _ in range(4)]
        pg.dma("pool", lambda e: e.dma_start(out=wpr, in_=W["ple_w_proj"].rearrange("(kc p) f -> p kc f", p=128)), writes=["wpr"])
        pg.dma("pool", lambda e: e.dma_start(out=pT, in_=pin.rearrange("(kc p) t -> p kc t", p=128)), writes=["pT"])
        ptr = [["pT"], ["pT"]]
        ssb = [nb(), nb()]
        reserved.update(ssb)
        its = [(oc, tt) for oc in range(NKC) for tt in range(2)]

        def X1(n):
            oc, tt = its[n]
            bi = nb()
            q2 = n % 4
            for kc in range(2):
                pg.pe(lambda e, bi=bi, kc=kc, oc=oc, tt=tt: e.matmul(banks[bi][:], lhsT=wpr[:, kc, oc * 128:(oc + 1) * 128], rhs=pT[:, kc, tt * 512:(tt + 1) * 512], start=(kc == 0), stop=(kc == 1)),
                      reads=["wpr"] + ptr[tt], writes=[bk(bi)])
            pg.act(lambda e, bi=bi, q2=q2: e.activation(out=sq2[q2], in_=banks[bi][:], func=AF.Square), reads=[bk(bi)], writes=[("sq2", q2)])

        def Y1(n):
            oc, tt = its[n]
            q2 = n % 4
            pg.pe(lambda e, q2=q2, tt=tt, oc=oc: e.matmul(banks[ssb[tt]][:], lhsT=ones_bf[:], rhs=sq2[q2], start=(oc == 0), stop=(oc == NKC - 1)), reads=[("sq2", q2), "ones_bf"], writes=[bk(ssb[tt])])

        X1(0)
        X1(1)
        for n in range(len(its)):
            if n + 2 < len(its):
                X1(n + 2)
            Y1(n)
        for tt in range(2):
            pg.act(lambda e, tt=tt: e.activation(out=rstd[:, tt * 512:(tt + 1) * 512], in_=banks[ssb[tt]][:], func=AF.Ln, bias=EPS, scale=1.0 / D), reads=[bk(ssb[tt])], writes=[("rstd0", tt), ("rstd", tt)])
            pg.act(lambda e, tt=tt: e.activation(out=rstd[:, tt * 512:(tt + 1) * 512], in_=rstd[:, tt * 512:(tt + 1) * 512], func=AF.Exp, scale=-0.5), reads=[("rstd0", tt)], writes=[("rstd", tt)])
        reserved.clear()
        w = W["ple_w_gate"]
        for oc in range(NKC):
            sl = oc % 3
            pg.dma("pool", lambda e, sl=sl, oc=oc: e.dma_start(out=wsl[sl], in_=w[:, oc * 128:(oc + 1) * 128].rearrange("(kc p) f -> p kc f", p=128)), writes=[("wsl", sl)])
            for tt in range(2):
                bi = nb()
                for kc in range(NKC):
                    pg.pe(lambda e, bi=bi, sl=sl, kc=kc, tt=tt: e.matmul(banks[bi][:], lhsT=wsl[sl][:, kc, :], rhs=xn[:, kc, tt * 512:(tt + 1) * 512], start=(kc == 0), stop=(kc == NKC - 1)), reads=[("wsl", sl), ("xn", kc)], writes=[bk(bi)])
                pg.act(lambda e, bi=bi, tt=tt: e.activation(out=gs[tt], in_=banks[bi][:], func=AF.Sigmoid), reads=[bk(bi)], writes=[("gs", tt)])
                b2 = nb()
                for kc in range(2):
                    pg.pe(lambda e, b2=b2, kc=kc, oc=oc, tt=tt: e.matmul(banks[b2][:], lhsT=wpr[:, kc, oc * 128:(oc + 1) * 128], rhs=pT[:, kc, tt * 512:(tt + 1) * 512], start=(kc == 0), stop=(kc == 1)),
                          reads=["wpr"] + ptr[tt], writes=[bk(b2)])
                ts_ = slice(tt * 512, (tt + 1) * 512)
                pg.dve(lambda e, oc=oc, ts_=ts_, b2=b2, tt=tt: e.scalar_tensor_tensor(out=pt_[tt], in0=banks[b2][:], scalar=sm[:, 4 * 16 + oc:4 * 16 + oc + 1], in1=rstd[:, ts_], op0=ALU.mult, op1=ALU.mult),
                       reads=[bk(b2), ("rstd", tt), "sm"], writes=[("pt", tt)])
                pg.dve(lambda e, tt=tt: e.tensor_tensor(out=pt_[tt], in0=pt_[tt], in1=gs[tt], op=ALU.mult), reads=[("pt", tt), ("gs", tt)], writes=[("pt", tt)])
                pg.dve(lambda e, oc=oc, ts_=ts_, tt=tt: e.tensor_tensor(out=h[:, oc, ts_], in0=h[:, oc, ts_], in1=pt_[tt], op=ALU.add), reads=[("pt", tt), ("h", oc, tt)], writes=[("h", oc, tt)])

    def final_phase():
        rmsnorm_to_xn(5, dst=None, dst_is_xn=False)
        fins = []
        for cg in range(4):
            for j in range(4):
                c = cg * 4 + j
                pg.dve(lambda e, c=c: e.scalar_tensor_tensor(out=h[:, c, :], in0=h[:, c, :], scalar=sm[:, 5 * 16 + c:5 * 16 + c + 1], in1=rstd[:], op0=ALU.mult, op1=ALU.mult),
                       reads=[("h", c, 0), ("h", c, 1), ("rstd", 0), ("rstd", 1), "sm"], writes=[("h", c, 0), ("h", c, 1)])
            fins.append(pg.dma("sp", lambda e, cg=cg: e.dma_start(out=out[cg * 512:(cg + 1) * 512, :].rearrange("(c p) t -> p c t", p=128), in_=h[:, cg * 4:(cg + 1) * 4, :]),
                               reads=[("h", cg * 4 + j, tt) for j in range(4) for tt in range(2)], writes=[("out", cg)]))
        return fins

    stop = debug or ""
    fins = None
    preloaded = False
    for s in range(nslot):
        own = (s == nslot - 1)
        if not preloaded:
            run_all([load_x_gen(s)])
        preloaded = False
        ffn(W["ffn1_w_gate"], W["ffn1_w_up"], W["ffn1_w_down"], 0)
        if stop == "ffn1":
            continue
        proj_phase(s, own)
        if not own:
            ar.reset()
            run_all([gdn_gen(False, reset=False), ssd_gen(False, reset=False), load_x_gen(s + 1, reset=False, act_only=True)])
            preloaded = True
        else:
            run_all([gdn_gen(own)])
            run_all([ssd_gen(own)])
    if stop not in ("ffn1",):
        if stop != "nomix":
            dense_residual("w_out")
        if stop != "mix":
            ffn(W["ffn2_w_gate"], W["ffn2_w_up"], W["ffn2_w_down"], 2)
            ple_phase()
    if stop in ("ffn1", "mix"):
        fins = []
        for cg in range(4):
            fins.append(pg.dma("sp", lambda e, cg=cg: e.dma_start(out=out[cg * 512:(cg + 1) * 512, :].rearrange("(c p) t -> p c t", p=128), in_=h[:, cg * 4:(cg + 1) * 4, :]),
                               reads=[("h", cg * 4 + j, tt) for j in range(4) for tt in range(2)], writes=[("out", cg)]))
    else:
        fins = final_phase()
    pg.emit(final_wait_ops=fins)
    return nc


def _smalls(inp):
    sm = np.zeros((128, NSM), np.float32)

    def put(name, arr):
        o, w = SM[name]
        assert arr.shape == (128, w), (name, arr.shape)
        sm[:, o:o + w] = arr

    nws = [inp["ffn1_norm"][0], inp["mix_norm"][0], inp["ffn2_norm"][0], inp["ple_norm"][0], inp["ple_post_norm"][0], inp["final_norm"]]
    put("nw", np.concatenate([np.asarray(w).reshape(16, 128).T for w in nws], axis=1))
    put("gcw", np.asarray(inp["gdn_conv_w"][0]).reshape(4, 24, 128).transpose(2, 1, 0).reshape(128, 96))
    put("scw", np.asarray(inp["ssm_conv_w"][0]).reshape(4, 12, 128).transpose(2, 1, 0).reshape(128, 48))
    put("scb", np.asarray(inp["ssm_conv_b"][0]).reshape(12, 128).T)
    put("gon", np.asarray(inp["gdn_out_norm"][0]).reshape(128, 1))
    put("son", np.asarray(inp["ssm_out_norm"][0]).reshape(8, 128).T)
    put("sd", np.repeat(np.asarray(inp["ssm_d"][0]), 64).reshape(8, 128).T)
    put("galog", np.broadcast_to(np.asarray(inp["gdn_a_log"][0])[None, :], (128, 8)))
    put("gdtb", np.broadcast_to(np.asarray(inp["gdn_dt_bias"][0])[None, :], (128, 8)))
    put("salog", np.broadcast_to(np.asarray(inp["ssm_a_log"][0])[None, :], (128, 16)))
    put("sdtb", np.broadcast_to(np.asarray(inp["ssm_dt_bias"][0])[None, :], (128, 16)))
    return sm


def _consts():
    a = np.arange(128)[:, None]
    b = np.arange(128)[None, :]
    c = np.zeros((128, 5, 128), np.float32)
    c[:, 0] = (a == b)
    c[:, 1] = (a <= b)
    c[:, 2] = (a > b)
    c[:, 3] = (a < b)
    c[:, 4] = 1.0
    return c


_NC_CACHE = {}


def kernel(_nslot=NSLOT, _debug=None, _cores=8, **inputs):
    inp = {k: np.asarray(v) for k, v in inputs.items()}
    x = inp["x"]
    p = inp["p"][0]
    key = (_nslot, _debug)
    if key not in _NC_CACHE:
        _NC_CACHE[key] = build_nc(_nslot, _debug)
    nc = _NC_CACHE[key]
    sm = _smalls(inp)
    cst = _consts()
    wmap = {nm: np.ascontiguousarray(inp[nm][0]) for nm in ("ffn1_w_gate", "ffn1_w_up", "ffn1_w_down", "w_in", "w_out",
                                                           "ffn2_w_gate", "ffn2_w_up", "ffn2_w_down", "ple_w_gate", "ple_w_proj")}
    in_maps = []
    for r in range(_cores):
        b, q = r // 4, r % 4
        xs = np.zeros((_nslot, D, T), np.float32)
        cmask = np.zeros((128, 4), np.float32)
        for j in range(_nslot):
            seg = q - (_nslot - 1) + j
            if seg >= 0:
                xs[j] = x[b, seg * T:(seg + 1) * T].T
                cmask[:, j] = 1.0
        m = {"xs": xs, "pin": np.ascontiguousarray(p[b, q * T:(q + 1) * T].T), "cmask": cmask, "smalls": sm, "consts": cst}
        m.update(wmap)
        in_maps.append(m)
    res = run_bass_kernel_spmd(nc, in_maps, core_ids=list(range(_cores)))
    outp = np.zeros((2, 4 * T, D), np.float32)
    for r in range(_cores):
        b, q = r // 4, r % 4
        outp[b, q * T:(q + 1) * T] = res.results[r]["out"].T
    return outp
```
